# Optimizing a Trainium2 kernel written in Bass

```python
import jax, jax.numpy as jnp
from jax import lax
import numpy as np

D_MODEL = 2048
BATCH = 4
SEQ = 4096
DEPTH = 4

N_MIXERS = 3
N_RWKV = (DEPTH + 2) // 3
N_SWA = (DEPTH + 1) // 3
N_CONV = DEPTH // 3
RMS_EPS = 1e-6

RWKV_HEAD = 64
RWKV_HEADS = D_MODEL // RWKV_HEAD
DECAY_LORA = 96
AAA_LORA = 96
MV_LORA = 64
GATE_LORA = 256
GN_EPS = 64e-5
N_MIX_COEF = 6

ATT_HEAD = 64
N_Q_HEADS = D_MODEL // ATT_HEAD
N_KV_HEADS = max(4, N_Q_HEADS // 8)
Q_PER_KV = N_Q_HEADS // N_KV_HEADS
WINDOW = 128
BLOCK = 128

CONV_W = 3

D_FF = 4 * D_MODEL

kernel_name = "hybrid_rwkv7_swa_sink_shortconv_trunk"


def rms_norm(x, g):
    xf = x.astype(jnp.float32)
    y = xf * lax.rsqrt(jnp.mean(xf * xf, axis=-1, keepdims=True) + RMS_EPS)
    return (y * g.astype(jnp.float32)).astype(x.dtype)


def token_shift(x):
    return jnp.pad(x, ((0, 0), (1, 0), (0, 0)))[:, :-1]


def rwkv7_scan(r, w, k, v, a, b):
    bsz, _, h, n = r.shape

    def step(S, inp):
        r_t, w_t, k_t, v_t, a_t, b_t = inp
        sa = jnp.einsum('bhij,bhj->bhi', S, a_t)
        S = S * w_t[:, :, None, :] + sa[..., None] * b_t[:, :, None, :] + v_t[..., None] * k_t[:, :, None, :]
        return S, jnp.einsum('bhij,bhj->bhi', S, r_t)

    S0 = jnp.zeros((bsz, h, n, n), jnp.float32)
    seq = tuple(jnp.moveaxis(t, 1, 0) for t in (r, w, k, v, a, b))
    _, y = lax.scan(step, S0, seq)
    return jnp.moveaxis(y, 0, 1)


def rwkv7_time_mix(x, v_first, mu, w_rkv, w_o, w0, w1, w2, a0, a1, a2, g1, g2,
                   k_k, k_a, r_k, lnx_w, lnx_b, vres):
    B, T, D = x.shape
    H, N = RWKV_HEADS, RWKV_HEAD
    f32 = jnp.float32
    xx = token_shift(x) - x
    xr, xw, xk, xv, xa, xg = [x + xx * mu[i] for i in range(N_MIX_COEF)]
    r, k, v = jnp.einsum('cbtd,cde->cbte', jnp.stack([xr, xk, xv]), w_rkv)
    w = -jax.nn.softplus(-(w0 + jnp.tanh(xw @ w1) @ w2).astype(f32)) - 0.5
    decay = jnp.exp(-jnp.exp(w))
    if vres is None:
        v_first = v
    else:
        v0, v1, v2 = vres
        v = v + (v_first - v) * jax.nn.sigmoid(v0 + (xv @ v1) @ v2)
    a = jax.nn.sigmoid(a0 + (xa @ a1) @ a2)
    g = jax.nn.sigmoid(xg @ g1) @ g2
    kk = (k * k_k).reshape(B, T, H, N).astype(f32)
    kk = kk / jnp.maximum(jnp.linalg.norm(kk, axis=-1, keepdims=True), 1e-12)
    k = k * (1.0 + (a - 1.0) * k_a)
    rh = r.reshape(B, T, H, N).astype(f32)
    kh = k.reshape(B, T, H, N).astype(f32)
    vh = v.reshape(B, T, H, N).astype(f32)
    ah = a.reshape(B, T, H, N).astype(f32)
    y = rwkv7_scan(rh, decay.reshape(B, T, H, N), kh, vh, -kk, kk * ah)
    mean = jnp.mean(y, axis=-1, keepdims=True)
    var = jnp.mean(jnp.square(y - mean), axis=-1, keepdims=True)
    y = (y - mean) * lax.rsqrt(var + GN_EPS)
    y = y.reshape(B, T, D) * lnx_w.astype(f32) + lnx_b.astype(f32)
    bonus = jnp.sum(rh * kh * r_k.astype(f32), axis=-1, keepdims=True) * vh
    y = (y + bonus.reshape(B, T, D)).astype(x.dtype)
    return (y * g) @ w_o, v_first


def swa_sink_attention(x, w_qkv, b_qkv, w_o, b_o, sinks):
    B, T, D = x.shape
    nb = T // BLOCK
    kvd = N_KV_HEADS * ATT_HEAD
    qkv = x @ w_qkv + b_qkv
    q = qkv[..., :D].reshape(B, nb, BLOCK, N_KV_HEADS, Q_PER_KV, ATT_HEAD) * (ATT_HEAD ** -0.5)
    k = qkv[..., D:D + kvd].reshape(B, nb, BLOCK, N_KV_HEADS, ATT_HEAD)
    v = qkv[..., D + kvd:].reshape(B, nb, BLOCK, N_KV_HEADS, ATT_HEAD)

    def with_prev(t):
        prev = jnp.pad(t, ((0, 0), (1, 0), (0, 0), (0, 0), (0, 0)))[:, :-1]
        return jnp.concatenate([prev, t], axis=2)

    kb, vb = with_prev(k), with_prev(v)
    logits = jnp.einsum('bnqhgd,bnkhd->bnhgqk', q, kb).astype(jnp.float32)
    blk = jnp.arange(nb)[:, None, None] * BLOCK
    qpos = blk + jnp.arange(BLOCK)[None, :, None]
    kpos = blk - BLOCK + jnp.arange(2 * BLOCK)[None, None, :]
    rel = qpos - kpos
    mask = (rel >= 0) & (rel < WINDOW) & (kpos >= 0)
    logits = jnp.where(mask[None, :, None, None], logits, -jnp.inf)
    sink = sinks.astype(jnp.float32).reshape(N_KV_HEADS, Q_PER_KV)[None, None, :, :, None]
    m = jnp.maximum(jnp.max(logits, axis=-1), sink)
    p = jnp.exp(logits - m[..., None])
    denom = jnp.sum(p, axis=-1) + jnp.exp(sink - m)
    probs = (p / denom[..., None]).astype(x.dtype)
    o = jnp.einsum('bnhgqk,bnkhd->bnqhgd', probs, vb).reshape(B, T, D)
    return o @ w_o + b_o


def short_gated_conv(x, w_in, conv_w, w_out):
    D = x.shape[-1]
    bch = x @ w_in
    bg, cg, h = bch[..., :D], bch[..., D:2 * D], bch[..., 2 * D:]
    u = cg * h
    uc = lax.conv_general_dilated(u, conv_w[:, None, :], window_strides=(1,),
                                  padding=((CONV_W - 1, 0),),
                                  dimension_numbers=('NWC', 'WIO', 'NWC'),
                                  feature_group_count=D)
    return (bg * uc) @ w_out


def sqrelu_mlp(x, w_up, w_down):
    h = jax.nn.relu(x @ w_up)
    return (h * h) @ w_down


def setup_inputs(seed: int = 0) -> dict:
    key = jax.random.key(seed)
    ks = iter(jax.random.split(key, 40))
    nrm = lambda shape, s: jax.random.normal(next(ks), shape, jnp.float32) * s
    uni = lambda shape, lo, hi: jax.random.uniform(next(ks), shape, jnp.float32, lo, hi)
    D, H, N = D_MODEL, RWKV_HEADS, RWKV_HEAD
    dsc = D ** -0.5
    kvd = N_KV_HEADS * ATT_HEAD
    return {
        "x": nrm((BATCH, SEQ, D), 1.0),
        "norm_mix": 1.0 + nrm((DEPTH, D), 0.02),
        "norm_ffn": 1.0 + nrm((DEPTH, D), 0.02),
        "norm_final": 1.0 + nrm((D,), 0.02),
        "rwkv_mu": uni((N_RWKV, N_MIX_COEF, D), 0.0, 1.0),
        "rwkv_w_rkv": nrm((N_RWKV, 3, D, D), dsc),
        "rwkv_w_o": nrm((N_RWKV, D, D), dsc),
        "rwkv_w0": uni((N_RWKV, D), -6.0, -1.0),
        "rwkv_w1": nrm((N_RWKV, D, DECAY_LORA), dsc),
        "rwkv_w2": nrm((N_RWKV, DECAY_LORA, D), 0.1 * DECAY_LORA ** -0.5),
        "rwkv_a0": nrm((N_RWKV, D), 0.5),
        "rwkv_a1": nrm((N_RWKV, D, AAA_LORA), dsc),
        "rwkv_a2": nrm((N_RWKV, AAA_LORA, D), 0.5 * AAA_LORA ** -0.5),
        "rwkv_v0": nrm((N_RWKV - 1, D), 0.5),
        "rwkv_v1": nrm((N_RWKV - 1, D, MV_LORA), dsc),
        "rwkv_v2": nrm((N_RWKV - 1, MV_LORA, D), 0.5 * MV_LORA ** -0.5),
        "rwkv_g1": nrm((N_RWKV, D, GATE_LORA), dsc),
        "rwkv_g2": nrm((N_RWKV, GATE_LORA, D), GATE_LORA ** -0.5),
        "rwkv_k_k": 1.0 + nrm((N_RWKV, D), 0.1),
        "rwkv_k_a": uni((N_RWKV, D), 0.0, 1.0),
        "rwkv_r_k": nrm((N_RWKV, H, N), 0.1),
        "rwkv_lnx_w": 1.0 + nrm((N_RWKV, D), 0.02),
        "rwkv_lnx_b": nrm((N_RWKV, D), 0.02),
        "swa_w_qkv": nrm((N_SWA, D, D + 2 * kvd), dsc),
        "swa_b_qkv": nrm((N_SWA, D + 2 * kvd), 0.02),
        "swa_w_o": nrm((N_SWA, D, D), dsc),
        "swa_b_o": nrm((N_SWA, D), 0.02),
        "swa_sinks": nrm((N_SWA, N_Q_HEADS), 1.0),
        "conv_w_in": nrm((N_CONV, D, 3 * D), dsc),
        "conv_w": nrm((N_CONV, CONV_W, D), CONV_W ** -0.5),
        "conv_w_out": nrm((N_CONV, D, D), dsc),
        "mlp_w_up": nrm((DEPTH, D, D_FF), dsc),
        "mlp_w_down": nrm((DEPTH, D_FF, D), D_FF ** -0.5),
    }


def reference(x, norm_mix, norm_ffn, norm_final,
              rwkv_mu, rwkv_w_rkv, rwkv_w_o, rwkv_w0, rwkv_w1, rwkv_w2,
              rwkv_a0, rwkv_a1, rwkv_a2, rwkv_v0, rwkv_v1, rwkv_v2,
              rwkv_g1, rwkv_g2, rwkv_k_k, rwkv_k_a, rwkv_r_k, rwkv_lnx_w, rwkv_lnx_b,
              swa_w_qkv, swa_b_qkv, swa_w_o, swa_b_o, swa_sinks,
              conv_w_in, conv_w, conv_w_out,
              mlp_w_up, mlp_w_down):
    v_first = None
    ia = ib = ic = 0
    for i in range(DEPTH):
        h = rms_norm(x, norm_mix[i])
        kind = i % N_MIXERS
        if kind == 0:
            vres = None if ia == 0 else (rwkv_v0[ia - 1], rwkv_v1[ia - 1], rwkv_v2[ia - 1])
            out, v_first = rwkv7_time_mix(
                h, v_first, rwkv_mu[ia], rwkv_w_rkv[ia], rwkv_w_o[ia],
                rwkv_w0[ia], rwkv_w1[ia], rwkv_w2[ia],
                rwkv_a0[ia], rwkv_a1[ia], rwkv_a2[ia],
                rwkv_g1[ia], rwkv_g2[ia], rwkv_k_k[ia], rwkv_k_a[ia], rwkv_r_k[ia],
                rwkv_lnx_w[ia], rwkv_lnx_b[ia], vres)
            ia += 1
        elif kind == 1:
            out = swa_sink_attention(h, swa_w_qkv[ib], swa_b_qkv[ib], swa_w_o[ib],
                                     swa_b_o[ib], swa_sinks[ib])
            ib += 1
        else:
            out = short_gated_conv(h, conv_w_in[ic], conv_w[ic], conv_w_out[ic])
            ic += 1
        x = x + out
        x = x + sqrelu_mlp(rms_norm(x, norm_ffn[i]), mlp_w_up[i], mlp_w_down[i])
    return rms_norm(x, norm_final)
```

```python
import contextlib
import functools
import math
import numpy as np
import concourse.bass as bass
import concourse.mybir as mybir
from concourse.bass_utils import run_bass_kernel_spmd

F32 = mybir.dt.float32
BF16 = mybir.dt.bfloat16
AF = mybir.ActivationFunctionType
ALU = mybir.AluOpType
AX = mybir.AxisListType

D = 2048
KC = 16
DFF = 8192
NH = 32
HD = 64
TB = 1024
TT = 512
RMS_EPS = 1e-6
GN_EPS = 64e-5


class Buf:
    __slots__ = ("name", "w", "r", "dsem", "dcount", "excl")

    def __init__(self, name, excl=False):
        self.excl = excl
        self.name = name
        self.w = None
        self.r = []
        self.dsem = None
        self.dcount = 0


class Op:
    __slots__ = ("eng", "fn", "reads", "writes", "dma", "dbuf", "tile_idx", "uses_tile",
                 "need_inc", "sem", "val", "ndma")

    def __init__(self, eng, fn, reads, writes, dma=False, dbuf=None, tile_idx=None, uses_tile=None, ndma=1):
        self.eng = eng
        self.fn = fn
        self.reads = reads
        self.writes = writes
        self.dma = dma
        self.dbuf = dbuf
        self.tile_idx = tile_idx
        self.uses_tile = uses_tile
        self.need_inc = False
        self.sem = None
        self.val = None
        self.ndma = ndma


class Prog:
    ENGS = ("pe", "act", "dve", "pool", "sp")

    def __init__(self):
        self.nc = bass.Bass("TRN2", target_bir_lowering=False)
        self.es = contextlib.ExitStack()
        self.ops = []
        self.cur_tile = None
        nc = self.nc
        self.eng = {"pe": nc.tensor, "act": nc.scalar, "dve": nc.vector, "pool": nc.gpsimd, "sp": nc.sync}
        self.esem = {e: self.es.enter_context(nc.semaphore("sem_" + e)) for e in ("pe", "act", "dve", "pool")}
        self.dsems = []

    def op(self, eng, fn, reads=(), writes=(), uses_tile=None):
        reads = list(reads)
        writes = list(writes)
        for b in reads:
            if b.excl and b not in writes:
                writes.append(b)
        o = Op(eng, fn, list(reads), list(writes),
               uses_tile=uses_tile if uses_tile is not None else self.cur_tile)
        self.ops.append(o)
        return o

    def dma(self, eng, fn, dbuf, reads=(), writes=(), tile_idx=None, ndma=1):
        o = Op(eng, fn, list(reads), list(writes), dma=True, dbuf=dbuf, tile_idx=tile_idx, ndma=ndma)
        self.ops.append(o)
        return o

    def barrier(self):
        self.ops.append("BARRIER")

    def _hoist(self, dist):
        loads = {}
        rest = []
        for o in self.ops:
            if o != "BARRIER" and o.dma and o.tile_idx is not None:
                loads[o.tile_idx] = o
            else:
                rest.append(o)
        if not loads:
            return
        ntiles = max(loads) + 1
        first_use = {}
        for i, o in enumerate(rest):
            if o != "BARRIER" and o.uses_tile is not None and o.uses_tile not in first_use:
                first_use[o.uses_tile] = i
        inserts = {}
        for j in range(ntiles):
            t = max(0, j - dist)
            while t not in first_use and t < ntiles:
                t += 1
            pos = first_use.get(t, len(rest))
            inserts.setdefault(pos, []).append(loads[j])
        out = []
        for i, o in enumerate(rest):
            if i in inserts:
                out.extend(inserts[i])
            out.append(o)
        if len(rest) in inserts:
            out.extend(inserts[len(rest)])
        self.ops = out

    def finalize(self, hoist_dist=2):
        self._hoist(hoist_dist)
        nc = self.nc
        deps = []
        bufs_seen = {}
        all_bufs = []

        def reg(b):
            if id(b) not in bufs_seen:
                bufs_seen[id(b)] = b
                all_bufs.append(b)

        for o in self.ops:
            if o == "BARRIER":
                deps.append(None)
                continue
            d = []
            for b in o.reads:
                reg(b)
                if b.w is not None:
                    d.append(b.w)
            for b in o.writes:
                reg(b)
                if b.w is not None:
                    d.append(b.w)
                d.extend(b.r)
            dd = []
            seen = set()
            for x in d:
                if id(x) not in seen and x is not o:
                    seen.add(id(x))
                    dd.append(x)
            for x in dd:
                if not x.dma and not (x.eng == "pe" and o.eng == "pe"):
                    x.need_inc = True
            deps.append(dd)
            for b in o.reads:
                if not o.dma:
                    b.r = [x for x in b.r if x.dma or x.eng != o.eng]
                b.r.append(o)
            for b in o.writes:
                b.w = o
                b.r = []
            if o.dma:
                reg(o.dbuf)
        class DS:
            def __init__(self, sem):
                self.sem = sem
                self.count = 0
        ds_by_name = {}
        for o in self.ops:
            if o != "BARRIER" and o.dma and o.dbuf.dsem is None:
                nm = o.dbuf.name
                if nm not in ds_by_name:
                    ds_by_name[nm] = DS(self.es.enter_context(nc.semaphore("ds%d" % len(self.dsems))))
                    self.dsems.append(ds_by_name[nm])
                o.dbuf.dsem = ds_by_name[nm]
        all_ds = list(ds_by_name.values())
        cnt = {e: 0 for e in self.esem}
        known = {e: {} for e in self.ENGS}
        n_wait = 0
        for o, dd in zip(self.ops, deps):
            if o == "BARRIER":
                for e in self.ENGS:
                    eng = self.eng[e]
                    for e2 in self.esem:
                        v = cnt[e2]
                        if v > known[e].get(id(self.esem[e2]), 0):
                            eng.wait_ge(self.esem[e2], v)
                            known[e][id(self.esem[e2])] = v
                    for d_ in all_ds:
                        if d_.count > known[e].get(id(d_.sem), 0):
                            eng.wait_ge(d_.sem, d_.count)
                            known[e][id(d_.sem)] = d_.count
                continue
            e = o.eng
            eng = self.eng[e]
            for x in dd:
                if x.dma:
                    sem, val = x.dbuf.dsem.sem, x.dbuf.dsem.count
                else:
                    if x.eng == e and e == "pe":
                        continue
                    sem, val = x.sem, x.val
                if val > known[e].get(id(sem), 0):
                    eng.wait_ge(sem, val)
                    known[e][id(sem)] = val
                    n_wait += 1
            r = o.fn()
            if o.dma:
                insts = r if isinstance(r, (list, tuple)) else [r]
                assert len(insts) == o.ndma, (len(insts), o.ndma)
                for ins in insts:
                    ins.then_inc(o.dbuf.dsem.sem, 16)
                o.dbuf.dsem.count += 16 * len(insts)
            elif o.need_inc:
                cnt[e] += 1
                r.then_inc(self.esem[e], 1)
                o.sem, o.val = self.esem[e], cnt[e]
        for d_ in all_ds:
            if d_.count > known["sp"].get(id(d_.sem), 0):
                nc.sync.wait_ge(d_.sem, d_.count)
        for e2 in self.esem:
            if cnt[e2] > known["sp"].get(id(self.esem[e2]), 0):
                nc.sync.wait_ge(self.esem[e2], cnt[e2])
        self.stats = dict(nops=len(self.ops), nwait=n_wait, cnt=dict(cnt), ndsem=len(self.dsems))
        self.es.close()
        return nc


class R:
    __slots__ = ("ap", "b")

    def __init__(self, ap, b):
        self.ap = ap
        self.b = b


class Builder:
    def __init__(self, tok, layers, final_norm, nblk=None):
        self.P = Prog()
        self.nc = self.P.nc
        self.tok = tok
        self.nblk = tok // TB
        self.layers = layers
        self.final_norm = final_norm
        nc = self.nc
        P = self.P
        self.xT = nc.dram_tensor("xT", [KC, 128, tok], F32, kind="ExternalInput").ap()
        self.yT = nc.dram_tensor("yT", [KC, 128, tok], F32, kind="ExternalOutput").ap()
        self.xs = nc.dram_tensor("xs", [KC, 128, tok], F32).ap()
        self.xT_b = Buf("xT")
        self.xs_b = Buf("xs")
        self.yT_b = Buf("yT")
        self.wgetters = []
        self.wkeys = {}
        self.woff = 0
        self.pgetters = []
        self.pkeys = {}
        self.pcol = 0
        self.cgetters = []
        self.ccol = 0
        self.NPCOL = 1024
        self.NCCOL = 1280
        self.AW = 53200
        self.arena = P.es.enter_context(nc.sbuf_tensor("arena", [128, self.AW], F32))
        self.atop = 0
        self.nbufs = 0
        self.psA = P.es.enter_context(nc.psum_tensor("psA", [128, 2048], F32))
        self.psB = P.es.enter_context(nc.psum_tensor("psB", [128, 1024], F32))
        self.psT = P.es.enter_context(nc.psum_tensor("psT", [128, 1024], BF16))
        self.psS = P.es.enter_context(nc.psum_tensor("psS", [128, 512], F32))
        self.psA_r = [R(self.psA[:, i * 512:(i + 1) * 512], Buf("psA%d" % i, True)) for i in range(4)]
        self.psB_r = [R(self.psB[:, i * 512:(i + 1) * 512], Buf("psB%d" % i, True)) for i in range(2)]
        self.psT_r = R(self.psT[:, :], Buf("psT", True))
        self.psS_r = R(self.psS[:, :], Buf("psS", True))
        self.rotA = 0
        self.rotB = 0
        self.pp = self.alloc("pp", self.NPCOL, F32)
        self.cb = self.alloc("cb", self.NCCOL, BF16)
        self.wslots = [self.alloc("wslot%d" % i, 8192, BF16) for i in range(3)]
        self.wtile_n = 0
        self.built = False

    def alloc(self, name, nelem, dt, npart=128):
        words = nelem if dt == F32 else (nelem + 1) // 2
        off = self.atop
        self.atop += words
        assert self.atop <= self.AW, ("SBUF arena overflow", name, self.atop)
        ap = self.arena[:, off:off + words]
        if dt != F32:
            ap = ap.bitcast(dt)
        if npart != 128:
            ap = ap[0:npart]
        return R(ap, Buf(name))

    def mark(self):
        return self.atop

    def release(self, m):
        self.P.barrier()
        self.atop = m

    def psa(self):
        r = self.psA_r[self.rotA % 4]
        self.rotA += 1
        return r

    def psb(self):
        r = self.psB_r[self.rotB % 2]
        self.rotB += 1
        return r

    def param(self, key, ncols, getter):
        if key not in self.pkeys:
            self.pkeys[key] = self.pcol
            self.pgetters.append((self.pcol, ncols, getter))
            self.pcol += ncols
            assert self.pcol <= self.NPCOL
        c = self.pkeys[key]
        return self.pp.ap[:, c:c + ncols]

    def dvec(self, key, getter):
        return self.param(key, KC, lambda inp, g=getter: np.ascontiguousarray(
            np.asarray(g(inp), np.float32).reshape(KC, 128).T))

    def const(self, ncols, arr):
        c = self.ccol
        self.cgetters.append((c, ncols, np.asarray(arr, np.float32)))
        self.ccol += ncols
        assert self.ccol <= self.NCCOL
        return self.cb.ap[:, c:c + ncols]

    def wtile(self, key, npart, nelem, getter):
        assert nelem <= 8192
        if key not in self.wkeys:
            self.wkeys[key] = self.woff
            self.wgetters.append((self.woff, npart, nelem, getter))
            self.woff += npart * nelem
        off = self.wkeys[key]
        idx = self.wtile_n
        self.wtile_n += 1
        slot = self.wslots[idx % 3]
        dst = slot.ap[0:npart, 0:nelem]

        def fn(off=off, npart=npart, nelem=nelem, dst=dst):
            src = self.wpack[off:off + npart * nelem].rearrange("(p n) -> p n", p=npart)
            return self.nc.gpsimd.dma_start(out=dst, in_=src)

        self.P.dma("pool", fn, slot.b, writes=[slot.b], tile_idx=idx)
        self.P.cur_tile = idx
        return R(dst, slot.b)

    def setup_consts(self):
        P, nc = self.P, self.nc
        self.ones = self.const(128, np.ones((128, 128)))
        self.ident = self.const(128, np.eye(128))
        bd = np.zeros((128, 128))
        bd[:64, :64] = 1
        bd[64:, 64:] = 1
        self.blk64 = self.const(128, bd)

    def load_consts(self):
        P, nc = self.P, self.nc

        def f1():
            return nc.sync.dma_start(out=self.pp.ap, in_=self.ppack)

        P.dma("sp", f1, self.pp.b, writes=[self.pp.b])

        def f2():
            return nc.gpsimd.dma_start(out=self.cb.ap, in_=self.cpack)

        P.dma("pool", f2, self.cb.b, writes=[self.cb.b])

    def x_src(self, first):
        return (self.xT, self.xT_b) if first else (self.xs, self.xs_b)

    def load_x(self, dst, src, srcb, t0, n):
        nc = self.nc
        d3 = dst.ap.rearrange("p (k t) -> p k t", k=KC)

        def fn():
            return nc.sync.dma_start(out=d3, in_=src.rearrange("k p t -> p k t")[:, :, t0:t0 + n])

        self.P.dma("sp", fn, dst.b, reads=[srcb], writes=[dst.b])

    def store_x(self, srcr, dst, dstb, t0, n):
        nc = self.nc
        s3 = srcr.ap.rearrange("p (k t) -> p k t", k=KC)

        def fn():
            return nc.sync.dma_start(out=dst.rearrange("k p t -> p k t")[:, :, t0:t0 + n], in_=s3)

        self.P.dma("sp", fn, dstb, reads=[srcr.b], writes=[dstb])

    def rmsnorm(self, xr, n, gain, hr, hoff=0, hstride=None):
        P, nc = self.P, self.nc
        hstride = hstride or n
        x3 = xr.ap.rearrange("p (k t) -> p k t", k=KC)
        h3 = hr.ap.rearrange("p (k t) -> p k t", k=KC)
        m = self.mark()
        sq = [self.alloc("sq%d" % i, TT, BF16) for i in range(3)]
        rs = self.alloc("rs", TT, F32)
        rs2 = self.alloc("rs2", TT, F32)
        ntt = (n + TT - 1) // TT
        for tt in range(ntt):
            w = min(TT, n - tt * TT)
            sl = slice(tt * TT, tt * TT + w)
            for kc in range(KC):
                s = sq[kc % 3]
                P.op("act", functools.partial(nc.scalar.activation,
                    out=s.ap[:, 0:w], in_=x3[:, kc, sl], func=AF.Square), reads=[xr.b], writes=[s.b])
                P.op("pe", functools.partial(nc.tensor.matmul,
                    self.psS[:, 0:w], lhsT=self.ones, rhs=s.ap[:, 0:w], start=(kc == 0), stop=(kc == KC - 1)),
                    reads=[s.b, self.cb.b], writes=[self.psS_r.b])
            P.op("dve", functools.partial(nc.vector.tensor_scalar,
                out=rs.ap[:, 0:w], in0=self.psS[:, 0:w], scalar1=1.0 / D, scalar2=RMS_EPS,
                op0=ALU.mult, op1=ALU.add), reads=[self.psS_r.b], writes=[rs.b])
            P.op("act", functools.partial(nc.scalar.activation, out=rs2.ap[:, 0:w], in_=rs.ap[:, 0:w], func=AF.Sqrt),
                 reads=[rs.b], writes=[rs2.b])
            P.op("dve", functools.partial(nc.vector.reciprocal, out=rs.ap[:, 0:w], in_=rs2.ap[:, 0:w]),
                 reads=[rs2.b], writes=[rs.b])
            for kc in range(KC):
                P.op("dve", functools.partial(nc.vector.scalar_tensor_tensor,
                    out=h3[:, kc, hoff + sl.start:hoff + sl.start + w], in0=x3[:, kc, sl],
                    scalar=gain[:, kc:kc + 1], in1=rs.ap[:, 0:w], op0=ALU.mult, op1=ALU.mult),
                    reads=[xr.b, rs.b, self.pp.b], writes=[hr.b])
        self.release(m)

    def mlp_phase(self, L, blk, first):
        P, nc = self.P, self.nc
        t0 = blk * TB
        m = self.mark()
        xb = self.alloc("xblk", KC * TB, F32)
        h = self.alloc("h", KC * TB, BF16)
        hh = [self.alloc("hh%d" % i, 4 * TB, BF16) for i in range(2)]
        tmp = [self.alloc("rl%d" % i, TT, F32) for i in range(2)]
        src, srcb = self.x_src(first)
        self.load_x(xb, src, srcb, t0, TB)
        gain = self.dvec(("norm_ffn", L), lambda inp, L=L: inp["norm_ffn"][L])
        self.rmsnorm(xb, TB, gain, h)
        x3 = xb.ap.rearrange("p (k t) -> p k t", k=KC)
        h3 = h.ap.rearrange("p (k t) -> p k t", k=KC)
        NG = DFF // 512
        ntt = TB // TT
        nrl = [0]

        def up(g):
            wt = self.wtile(("up", L, g), 128, KC * 512,
                            lambda inp, L=L, g=g: inp["mlp_w_up"][L][:, g * 512:(g + 1) * 512]
                            .reshape(KC, 128, 512).transpose(1, 0, 2).reshape(128, KC * 512))
            w3 = wt.ap.rearrange("p (k m) -> p k m", k=KC)
            hg = hh[g % 2]
            hg3 = hg.ap.rearrange("p (j t) -> p j t", j=4)
            for j in range(4):
                for tt in range(ntt):
                    ps = self.psa()
                    for kc in range(KC):
                        P.op("pe", functools.partial(nc.tensor.matmul,
                            ps.ap, lhsT=w3[:, kc, j * 128:(j + 1) * 128], rhs=h3[:, kc, tt * TT:(tt + 1) * TT],
                            start=(kc == 0), stop=(kc == KC - 1)), reads=[wt.b, h.b], writes=[ps.b])
                    tm = tmp[nrl[0] % 2]
                    nrl[0] += 1
                    P.op("act", functools.partial(nc.scalar.activation, out=tm.ap, in_=ps.ap, func=AF.Relu),
                         reads=[ps.b], writes=[tm.b])
                    P.op("pool", functools.partial(nc.gpsimd.tensor_tensor,
                        out=hg3[:, j, tt * TT:(tt + 1) * TT], in0=tm.ap, in1=tm.ap, op=ALU.mult),
                        reads=[tm.b], writes=[hg.b])

        def down(g):
            wt = self.wtile(("dn", L, g), 128, 4 * D,
                            lambda inp, L=L, g=g: inp["mlp_w_down"][L][g * 512:(g + 1) * 512, :]
                            .reshape(4, 128, D).transpose(1, 0, 2).reshape(128, 4 * D))
            w3 = wt.ap.rearrange("p (j d) -> p j d", j=4)
            hg = hh[g % 2]
            hg3 = hg.ap.rearrange("p (j t) -> p j t", j=4)
            for dc in range(KC):
                for tt in range(ntt):
                    ps = self.psb()
                    for j in range(4):
                        P.op("pe", functools.partial(nc.tensor.matmul,
                            ps.ap, lhsT=w3[:, j, dc * 128:(dc + 1) * 128], rhs=hg3[:, j, tt * TT:(tt + 1) * TT],
                            start=(j == 0), stop=(j == 3)), reads=[wt.b, hg.b], writes=[ps.b])
                    P.op("dve", functools.partial(nc.vector.tensor_tensor,
                        out=x3[:, dc, tt * TT:(tt + 1) * TT], in0=ps.ap, in1=x3[:, dc, tt * TT:(tt + 1) * TT],
                        op=ALU.add), reads=[ps.b, xb.b], writes=[xb.b])

        up(0)
        for g in range(NG):
            if g + 1 < NG:
                up(g + 1)
            down(g)
        return xb, m

    def finish_block(self, xb, m, blk, last):
        P, nc = self.P, self.nc
        t0 = blk * TB
        if last and self.final_norm:
            gain = self.dvec(("norm_final",), lambda inp: inp["norm_final"])
            self.final_rms(xb, gain, t0)
        elif last:
            self.store_x(xb, self.yT, self.yT_b, t0, TB)
        else:
            self.store_x(xb, self.xs, self.xs_b, t0, TB)
        self.release(m)

    def final_rms(self, xb, gain, t0):
        P, nc = self.P, self.nc
        x3 = xb.ap.rearrange("p (k t) -> p k t", k=KC)
        m = self.mark()
        sq = [self.alloc("fsq%d" % i, TT, BF16) for i in range(3)]
        rs = self.alloc("frs", TT, F32)
        rs2 = self.alloc("frs2", TT, F32)
        for tt in range(TB // TT):
            sl = slice(tt * TT, (tt + 1) * TT)
            for kc in range(KC):
                s = sq[kc % 3]
                P.op("act", functools.partial(nc.scalar.activation,
                    out=s.ap, in_=x3[:, kc, sl], func=AF.Square), reads=[xb.b], writes=[s.b])
                P.op("pe", functools.partial(nc.tensor.matmul,
                    self.psS[:, :], lhsT=self.ones, rhs=s.ap, start=(kc == 0), stop=(kc == KC - 1)),
                    reads=[s.b, self.cb.b], writes=[self.psS_r.b])
            P.op("dve", functools.partial(nc.vector.tensor_scalar,
                out=rs.ap, in0=self.psS[:, :], scalar1=1.0 / D, scalar2=RMS_EPS,
                op0=ALU.mult, op1=ALU.add), reads=[self.psS_r.b], writes=[rs.b])
            P.op("act", functools.partial(nc.scalar.activation, out=rs2.ap, in_=rs.ap, func=AF.Sqrt),
                 reads=[rs.b], writes=[rs2.b])
            P.op("dve", functools.partial(nc.vector.reciprocal, out=rs.ap, in_=rs2.ap), reads=[rs2.b], writes=[rs.b])
            for kc in range(KC):
                P.op("dve", functools.partial(nc.vector.scalar_tensor_tensor,
                    out=x3[:, kc, sl], in0=x3[:, kc, sl], scalar=gain[:, kc:kc + 1], in1=rs.ap,
                    op0=ALU.mult, op1=ALU.mult), reads=[xb.b, rs.b, self.pp.b], writes=[xb.b])
        self.store_x(xb, self.yT, self.yT_b, t0, TB)
        self.release(m)

    def build(self):
        self.setup_consts()
        self.load_consts()
        self.setup_persist()
        first_layer = True
        for li, spec in enumerate(self.layers):
            last_layer = (li == len(self.layers) - 1)
            mlp_only = isinstance(spec, tuple)
            L = spec[1] if mlp_only else spec
            for blk in range(self.nblk):
                kind = L % 3
                if mlp_only:
                    pass
                elif kind == 0:
                    self.rwkv_phase(L, blk, first_layer)
                elif kind == 1:
                    self.swa_phase(L, blk, first_layer)
                elif kind == 2:
                    self.conv_phase(L, blk, first_layer)
                xb, m = self.mlp_phase(L, blk, first=(first_layer and mlp_only))
                self.finish_block(xb, m, blk, last_layer)
            first_layer = False
        nc = self.nc
        self.wpack = nc.dram_tensor("wpack", [max(self.woff, 128)], F32, kind="ExternalInput").ap()
        self.ppack = nc.dram_tensor("ppack", [128, self.NPCOL], F32, kind="ExternalInput").ap()
        self.cpack = nc.dram_tensor("cpack", [128, self.NCCOL], F32, kind="ExternalInput").ap()
        self.P.finalize(hoist_dist=2)
        self.built = True
        return nc

    def pack(self, inp):
        wp = np.zeros(max(self.woff, 128), np.float32)
        for off, npart, nelem, g in self.wgetters:
            a = np.asarray(g(inp), np.float32)
            assert a.shape == (npart, nelem), (a.shape, npart, nelem)
            wp[off:off + npart * nelem] = a.reshape(-1)
        pp = np.zeros((128, self.NPCOL), np.float32)
        for c, n, g in self.pgetters:
            a = np.asarray(g(inp), np.float32)
            assert a.shape == (128, n), (a.shape, n)
            pp[:, c:c + n] = a
        cp = np.zeros((128, self.NCCOL), np.float32)
        for c, n, a in self.cgetters:
            cp[:, c:c + n] = a
        return wp, pp, cp

    def out_proj(self, key, L, z, n, t0, wget, first, bias=None):
        P, nc = self.P, self.nc
        z3 = z.ap.rearrange("p (k t) -> p k t", k=KC)
        src, srcb = self.x_src(first)
        m = self.mark()
        st = [self.alloc("ost%d" % i, TT, F32) for i in range(4)]
        ntt = n // TT
        k = 0
        for g in range(4):
            wt = self.wtile((key, L, g), 128, KC * 512,
                            lambda inp, g=g: wget(inp)[:, g * 512:(g + 1) * 512]
                            .reshape(KC, 128, 512).transpose(1, 0, 2).reshape(128, KC * 512))
            w3 = wt.ap.rearrange("p (k m) -> p k m", k=KC)
            for j in range(4):
                mc = g * 4 + j
                for tt in range(ntt):
                    s_ = st[k % 4]
                    k += 1
                    sl = slice(t0 + tt * TT, t0 + (tt + 1) * TT)
                    P.dma("sp", functools.partial(nc.sync.dma_start, out=s_.ap, in_=src[mc, :, sl]),
                          s_.b, reads=[srcb], writes=[s_.b])
                    ps = self.psa()
                    for kc in range(KC):
                        P.op("pe", functools.partial(nc.tensor.matmul,
                            ps.ap, lhsT=w3[:, kc, j * 128:(j + 1) * 128], rhs=z3[:, kc, tt * TT:(tt + 1) * TT],
                            start=(kc == 0), stop=(kc == KC - 1)), reads=[wt.b, z.b], writes=[ps.b])
                    if bias is None:
                        P.op("dve", functools.partial(nc.vector.tensor_tensor,
                            out=s_.ap, in0=ps.ap, in1=s_.ap, op=ALU.add), reads=[ps.b, s_.b], writes=[s_.b])
                    else:
                        P.op("dve", functools.partial(nc.vector.scalar_tensor_tensor,
                            out=s_.ap, in0=ps.ap, scalar=bias[:, mc:mc + 1], in1=s_.ap, op0=ALU.add, op1=ALU.add),
                            reads=[ps.b, s_.b, self.pp.b], writes=[s_.b])
                    P.dma("sp", functools.partial(nc.sync.dma_start, out=self.xs[mc, :, sl], in_=s_.ap),
                          self.xs_b, reads=[s_.b], writes=[self.xs_b])
        self.release(m)

    def norm_block(self, L, key, t0, n, first, h, hoff=0, hstride=None):
        gain = self.dvec((key, L), lambda inp, L=L, key=key: inp[key][L])
        src, srcb = self.x_src(first)
        for s0 in range(0, n, TT):
            w = min(TT, n - s0)
            m = self.mark()
            xb = self.alloc("xnb", KC * w, F32)
            self.load_x(xb, src, srcb, t0 + s0, w)
            self.rmsnorm(xb, w, gain, h, hoff=hoff + s0, hstride=hstride)
            self.release(m)

    def conv_phase(self, L, blk, first):
        P, nc = self.P, self.nc
        t0 = blk * TB
        ic = L // 3
        if not hasattr(self, "uhalo"):
            raise RuntimeError("persist not set up")
        m = self.mark()
        h = self.alloc("h", KC * TB, BF16)
        self.norm_block(L, "norm_mix", t0, TB, first, h)
        z = self.alloc("z", KC * TB, BF16)
        h3 = h.ap.rearrange("p (k t) -> p k t", k=KC)
        z3 = z.ap.rearrange("p (k t) -> p k t", k=KC)
        u = [self.alloc("u%d" % i, TB + 2, F32) for i in range(2)]
        uc = [self.alloc("uc%d" % i, TB, F32) for i in range(2)]
        tcg = [self.alloc("tcg%d" % i, TB, F32) for i in range(2)]
        cw = [self.dvec(("conv_w", ic, tap), lambda inp, ic=ic, tap=tap: inp["conv_w"][ic][tap]) for tap in range(3)]
        uh3 = self.uhalo.ap.rearrange("p (k t) -> p k t", k=KC)
        ntt = TB // TT
        for j in range(KC):
            def getter(inp, j=j, ic=ic):
                W = inp["conv_w_in"][ic]
                cols = np.concatenate([W[:, D + j * 128:D + (j + 1) * 128], W[:, 2 * D + j * 128:2 * D + (j + 1) * 128],
                                       W[:, j * 128:(j + 1) * 128]], axis=1)
                return cols.reshape(KC, 128, 384).transpose(1, 0, 2).reshape(128, KC * 384)
            wt = self.wtile(("cin", L, j), 128, KC * 384, getter)
            w3 = wt.ap.rearrange("p (k m) -> p k m", k=KC)
            uj, ucj, tj = u[j % 2], uc[j % 2], tcg[j % 2]
            P.op("pool", functools.partial(nc.gpsimd.tensor_copy, out=uj.ap[:, 0:2], in_=uh3[:, j, :]),
                 reads=[self.uhalo.b], writes=[uj.b])
            for part in range(3):
                if part == 2:
                    P.op("pool", functools.partial(nc.gpsimd.tensor_scalar,
                        out=ucj.ap, in0=uj.ap[:, 2:2 + TB], scalar1=cw[2][:, j:j + 1], scalar2=None, op0=ALU.mult),
                        reads=[uj.b, self.pp.b], writes=[ucj.b])
                    P.op("dve", functools.partial(nc.vector.scalar_tensor_tensor,
                        out=ucj.ap, in0=uj.ap[:, 1:1 + TB], scalar=cw[1][:, j:j + 1], in1=ucj.ap,
                        op0=ALU.mult, op1=ALU.add), reads=[uj.b, ucj.b, self.pp.b], writes=[ucj.b])
                    P.op("dve", functools.partial(nc.vector.scalar_tensor_tensor,
                        out=ucj.ap, in0=uj.ap[:, 0:TB], scalar=cw[0][:, j:j + 1], in1=ucj.ap,
                        op0=ALU.mult, op1=ALU.add), reads=[uj.b, ucj.b, self.pp.b], writes=[ucj.b])
                    P.op("pool", functools.partial(nc.gpsimd.tensor_copy, out=uh3[:, j, :], in_=uj.ap[:, TB:TB + 2]),
                         reads=[uj.b], writes=[self.uhalo.b])
                for tt in range(ntt):
                    ps = self.psa()
                    sl = slice(tt * TT, (tt + 1) * TT)
                    for kc in range(KC):
                        P.op("pe", functools.partial(nc.tensor.matmul,
                            ps.ap, lhsT=w3[:, kc, part * 128:(part + 1) * 128], rhs=h3[:, kc, sl],
                            start=(kc == 0), stop=(kc == KC - 1)), reads=[wt.b, h.b], writes=[ps.b])
                    if part == 0:
                        P.op("act", functools.partial(nc.scalar.copy, out=tj.ap[:, sl], in_=ps.ap),
                             reads=[ps.b], writes=[tj.b])
                    elif part == 1:
                        P.op("dve", functools.partial(nc.vector.tensor_tensor,
                            out=uj.ap[:, 2 + sl.start:2 + sl.stop], in0=ps.ap, in1=tj.ap[:, sl], op=ALU.mult),
                            reads=[ps.b, tj.b], writes=[uj.b])
                    else:
                        P.op("dve", functools.partial(nc.vector.tensor_tensor,
                            out=z3[:, j, sl], in0=ps.ap, in1=ucj.ap[:, sl], op=ALU.mult),
                            reads=[ps.b, ucj.b], writes=[z.b])
        self.out_proj("cout", L, z, TB, t0, lambda inp, ic=ic: inp["conv_w_out"][ic], first)
        self.release(m)

    def setup_persist(self):
        P, nc = self.P, self.nc
        kinds = set((l[1] if isinstance(l, tuple) else l) % 3 for l in self.layers if not isinstance(l, tuple))
        if 2 in kinds:
            self.uhalo = self.alloc("uhalo", KC * 2, F32)
            P.op("pool", functools.partial(nc.gpsimd.memset, self.uhalo.ap, 0.0), writes=[self.uhalo.b])
        if 0 in kinds:
            self.hlast = self.alloc("hlast", KC, BF16)

    def swa_phase(self, L, blk, first):
        P, nc = self.P, self.nc
        t0 = blk * TB
        ib = L // 3
        NQB = TB // 128
        if not hasattr(self, "swa_k_st"):
            self.swa_k_st = nc.dram_tensor("swa_k_st", [128, 4 * 128], BF16).ap()
            self.swa_v_st = nc.dram_tensor("swa_v_st", [128, 8 * 128], BF16).ap()
            self.swa_st_b = Buf("swa_st")
            NEG = -30000.0
            qi = np.arange(128)[:, None]
            kj = np.arange(256)[None, :]
            ok = (kj > qi) & (kj <= qi + 128)
            self.c_mask = self.const(256, np.where(ok, 0.0, NEG))
            self.c_mask0 = self.const(256, np.where(ok & (kj >= 128), 0.0, NEG))
        Wq = lambda inp: inp["swa_w_qkv"][ib]
        bq = lambda inp: inp["swa_b_qkv"][ib]
        b_q = self.param(("swa_bq", ib), KC, lambda inp: np.ascontiguousarray(bq(inp)[:D].reshape(KC, 128).T))
        b_k = self.param(("swa_bk", ib), 4, lambda inp: np.stack(
            [np.concatenate([bq(inp)[D + j * 64:D + (j + 1) * 64]] * 2) for j in range(4)], axis=1))
        b_v = self.param(("swa_bv", ib), 2, lambda inp: np.ascontiguousarray(bq(inp)[D + 256:D + 512].reshape(2, 128).T))
        b_o = self.dvec(("swa_bo", ib), lambda inp: inp["swa_b_o"][ib])
        sink = self.param(("swa_sink", ib), NH, lambda inp: np.tile(inp["swa_sinks"][ib][None, :], (128, 1)))
        m = self.mark()
        q_all = self.alloc("q_all", KC * TB, BF16)
        kbuf = self.alloc("kbuf", 4 * (128 + TB), BF16)
        vtp = self.alloc("vtp", (NQB + 1) * 8 * 128, BF16)
        q3 = q_all.ap.rearrange("p (k t) -> p k t", k=KC)
        k3 = kbuf.ap.rearrange("p (j t) -> p j t", j=4)
        v5 = vtp.ap.rearrange("p (b j v d) -> p b j v d", b=NQB + 1, j=4, v=2)
        if blk == 0:
            P.op("dve", functools.partial(nc.vector.memset, vtp.ap, 0.0), writes=[vtp.b])
            P.op("dve", functools.partial(nc.vector.memset, k3[:, :, 0:128], 0.0), writes=[kbuf.b])
        else:
            P.op("dve", functools.partial(nc.vector.memset, vtp.ap[:, 8 * 128:], 0.0), writes=[vtp.b])
            P.dma("sp", functools.partial(nc.sync.dma_start, out=vtp.ap[:, 0:8 * 128], in_=self.swa_v_st), vtp.b,
                  reads=[self.swa_st_b], writes=[vtp.b])
            P.dma("sp", functools.partial(nc.sync.dma_start, out=k3[:, :, 0:128],
                                                  in_=self.swa_k_st.rearrange("p (j t) -> p j t", j=4)), kbuf.b,
                  reads=[self.swa_st_b], writes=[kbuf.b])
        m2 = self.mark()
        h = self.alloc("h", KC * TB, BF16)
        self.norm_block(L, "norm_mix", t0, TB, first, h)
        vfm = self.alloc("vfm", 2 * TB, BF16)
        h3 = h.ap.rearrange("p (k t) -> p k t", k=KC)
        vf3 = vfm.ap.rearrange("p (c t) -> p c t", c=2)
        ntt = TB // TT

        def proj(key, ncols, getter, epi):
            wt = self.wtile((key, L), 128, KC * ncols,
                            lambda inp: getter(inp).reshape(KC, 128, ncols).transpose(1, 0, 2).reshape(128, KC * ncols))
            w3 = wt.ap.rearrange("p (k m) -> p k m", k=KC)
            for j in range(ncols // 128):
                for tt in range(ntt):
                    ps = self.psa()
                    sl = slice(tt * TT, (tt + 1) * TT)
                    for kc in range(KC):
                        P.op("pe", functools.partial(nc.tensor.matmul,
                            ps.ap, lhsT=w3[:, kc, j * 128:(j + 1) * 128], rhs=h3[:, kc, sl],
                            start=(kc == 0), stop=(kc == KC - 1)), reads=[wt.b, h.b], writes=[ps.b])
                    epi(j, sl, ps)

        for g in range(4):
            def epi_q(j, sl, ps, g=g):
                mc = g * 4 + j
                P.op("dve", functools.partial(nc.vector.tensor_scalar,
                    out=q3[:, mc, sl], in0=ps.ap, scalar1=b_q[:, mc:mc + 1], scalar2=HD ** -0.5,
                    op0=ALU.add, op1=ALU.mult), reads=[ps.b, self.pp.b], writes=[q_all.b])
            proj(("swa_q", g), 512, lambda inp, g=g: Wq(inp)[:, g * 512:(g + 1) * 512], epi_q)

        def epi_k(j, sl, ps):
            P.op("dve", functools.partial(nc.vector.tensor_scalar,
                out=k3[:, j, 128 + sl.start:128 + sl.stop], in0=ps.ap, scalar1=b_k[:, j:j + 1], scalar2=None,
                op0=ALU.add), reads=[ps.b, self.pp.b], writes=[kbuf.b])
        proj("swa_k", 512, lambda inp: np.concatenate(
            [Wq(inp)[:, D + (j // 2) * 64:D + (j // 2 + 1) * 64] for j in range(8)], axis=1), epi_k)

        def epi_v(j, sl, ps):
            P.op("dve", functools.partial(nc.vector.tensor_scalar,
                out=vf3[:, j, sl], in0=ps.ap, scalar1=b_v[:, j:j + 1], scalar2=None, op0=ALU.add),
                reads=[ps.b, self.pp.b], writes=[vfm.b])
        proj("swa_v", 256, lambda inp: Wq(inp)[:, D + 256:D + 512], epi_v)
        import os
        DBG = int(os.environ.get('SWA_DBG', '99'))
        for bb in range(NQB if DBG >= 1 else 0):
            for c in range(2):
                P.op("pe", functools.partial(nc.tensor.transpose,
                    self.psT[:, c * 128:(c + 1) * 128], vf3[:, c, bb * 128:(bb + 1) * 128], self.ident),
                    reads=[vfm.b, self.cb.b], writes=[self.psT_r.b])
            src4 = self.psT[:, 0:256].rearrange("p (j d) -> p j d", j=4)
            P.op("act", functools.partial(nc.scalar.copy, out=v5[:, bb + 1, :, 0, 0:64], in_=src4),
                 reads=[self.psT_r.b], writes=[vtp.b])
            P.op("dve", functools.partial(nc.vector.tensor_copy, out=v5[:, bb + 1, :, 1, 64:128], in_=src4),
                 reads=[self.psT_r.b], writes=[vtp.b])
        self.release(m2)
        o_all = self.alloc("o_all", KC * TB, BF16)
        o3 = o_all.ap.rearrange("p (k t) -> p k t", k=KC)
        NR = 3
        lm = [self.alloc("lm%d" % i, 512, F32) for i in range(NR)]
        pe_ = [self.alloc("pe%d" % i, 512, F32) for i in range(NR)]
        pn = [self.alloc("pn%d" % i, 512, BF16) for i in range(NR)]
        pT = [self.alloc("pT%d" % i, 512, BF16) for i in range(NR)]
        sm = [self.alloc("sm%d" % i, 16, F32) for i in range(NR)]
        it = 0
        for c in range(KC if DBG >= 2 else 0):
            kvh = c // 4
            for bb in range(NQB):
                i = it % NR
                it += 1
                lmr, per, pnr, pTr, smr = lm[i], pe_[i], pn[i], pT[i], sm[i]
                if self.rotA % 2:
                    self.rotA += 1
                psl0 = self.psa()
                psl1 = self.psa()
                pbase = ((self.rotA - 2) % 4) * 512
                for hh, psl in ((0, psl0), (1, psl1)):
                    pr = slice(hh * 64, (hh + 1) * 64)
                    P.op("pe", functools.partial(nc.tensor.matmul,
                        psl.ap[:, 0:256], lhsT=q3[pr, c, bb * 128:(bb + 1) * 128],
                        rhs=k3[pr, kvh, bb * 128:bb * 128 + 256], start=True, stop=True),
                        reads=[q_all.b, kbuf.b], writes=[psl.b])
                mk = self.c_mask0 if (blk == 0 and bb == 0) else self.c_mask
                l3 = lmr.ap.rearrange("p (h k) -> p h k", h=2)
                pl3 = self.psA[:, pbase:pbase + 1024].rearrange("p (h k) -> p h k", h=2)[:, :, 0:256]
                P.op("dve", functools.partial(nc.vector.tensor_tensor,
                    out=l3, in0=pl3, in1=mk.unsqueeze(1).to_broadcast([128, 2, 256]), op=ALU.add),
                    reads=[psl0.b, psl1.b, self.cb.b], writes=[lmr.b])
                s_ = smr.ap
                P.op("dve", functools.partial(nc.vector.tensor_reduce,
                    out=s_[:, 0:2], in_=l3, axis=AX.X, op=ALU.max), reads=[lmr.b], writes=[smr.b])
                P.op("dve", functools.partial(nc.vector.tensor_tensor,
                    out=s_[:, 2:4], in0=s_[:, 0:2], in1=sink[:, 2 * c:2 * c + 2], op=ALU.max),
                    reads=[smr.b, self.pp.b], writes=[smr.b])
                P.op("dve", functools.partial(nc.vector.tensor_scalar,
                    out=s_[:, 4:6], in0=s_[:, 2:4], scalar1=-1.0, scalar2=None, op0=ALU.mult),
                    reads=[smr.b], writes=[smr.b])
                p3 = per.ap.rearrange("p (h k) -> p h k", h=2)
                for hh in range(2):
                    P.op("act", functools.partial(nc.scalar.activation,
                        out=p3[:, hh, :], in_=l3[:, hh, :], func=AF.Exp, bias=s_[:, 4 + hh:5 + hh], scale=1.0,
                        accum_out=s_[:, 6 + hh:7 + hh]), reads=[lmr.b, smr.b], writes=[per.b, smr.b])
                P.op("dve", functools.partial(nc.vector.tensor_tensor,
                    out=s_[:, 8:10], in0=s_[:, 4:6], in1=sink[:, 2 * c:2 * c + 2], op=ALU.add),
                    reads=[smr.b, self.pp.b], writes=[smr.b])
                P.op("act", functools.partial(nc.scalar.activation, out=s_[:, 10:12], in_=s_[:, 8:10], func=AF.Exp),
                     reads=[smr.b], writes=[smr.b])
                P.op("dve", functools.partial(nc.vector.tensor_tensor,
                    out=s_[:, 12:14], in0=s_[:, 10:12], in1=s_[:, 6:8], op=ALU.add),
                    reads=[smr.b], writes=[smr.b])
                P.op("dve", functools.partial(nc.vector.reciprocal, out=s_[:, 14:16], in_=s_[:, 12:14]),
                     reads=[smr.b], writes=[smr.b])
                pn3 = pnr.ap.rearrange("p (h k) -> p h k", h=2)
                P.op("dve", functools.partial(nc.vector.tensor_tensor,
                    out=pn3, in0=p3, in1=s_[:, 14:16].unsqueeze(2).to_broadcast([128, 2, 256]), op=ALU.mult),
                    reads=[per.b, smr.b], writes=[pnr.b])
                for hh in range(2):
                    for kb in range(2):
                        P.op("pe", functools.partial(nc.tensor.transpose,
                            self.psT[:, (hh * 2 + kb) * 128:(hh * 2 + kb + 1) * 128],
                            pn3[:, hh, kb * 128:(kb + 1) * 128], self.ident),
                            reads=[pnr.b, self.cb.b], writes=[self.psT_r.b])
                P.op("act", functools.partial(nc.scalar.copy, out=pTr.ap, in_=self.psT[:, 0:512]),
                     reads=[self.psT_r.b], writes=[pTr.b])
                pso = self.psb()
                n_ = 0
                for hh in range(2):
                    for kb in range(2):
                        P.op("pe", functools.partial(nc.tensor.matmul,
                            pso.ap[:, 0:128], lhsT=v5[:, bb + kb, kvh, hh, :],
                            rhs=pTr.ap[:, (hh * 2 + kb) * 128:(hh * 2 + kb + 1) * 128],
                            start=(n_ == 0), stop=(n_ == 3)), reads=[vtp.b, pTr.b], writes=[pso.b])
                        n_ += 1
                P.op("act", functools.partial(nc.scalar.copy,
                    out=o3[:, c, bb * 128:(bb + 1) * 128], in_=pso.ap[:, 0:128]), reads=[pso.b], writes=[o_all.b])
        if blk + 1 < self.nblk:
            P.dma("sp", functools.partial(nc.sync.dma_start, out=self.swa_v_st, in_=vtp.ap[:, NQB * 8 * 128:]), self.swa_st_b,
                  reads=[vtp.b], writes=[self.swa_st_b])
            P.dma("sp", functools.partial(nc.sync.dma_start, out=self.swa_k_st.rearrange("p (j t) -> p j t", j=4),
                                                  in_=k3[:, :, TB:TB + 128]), self.swa_st_b,
                  reads=[kbuf.b], writes=[self.swa_st_b])
        self.out_proj("swa_o", L, o_all, TB, t0, lambda inp: inp["swa_w_o"][ib], first, bias=b_o)
        self.release(m)

    def rwkv_phase(self, L, blk, first):
        T2 = 512
        for sub in range(TB // T2):
            self.rwkv_sub(L, blk * TB + sub * T2, T2, first, seq_start=(blk == 0 and sub == 0))

    def rwkv_sub(self, L, t0, T2, first, seq_start):
        P, nc = self.P, self.nc
        ia = L // 3
        has_vres = ia > 0
        NCH = T2 // 64
        LD = 0.6065306597126334
        if not hasattr(self, "zst"):
            self.zst = nc.dram_tensor("zst", [KC, 128, 128], F32).ap()
            self.zst_b = Buf("zst")
            self.vfirst = nc.dram_tensor("vfirst", [KC, 128, self.tok], F32).ap()
            self.vfirst_b = Buf("vfirst")
            si = np.arange(64)[:, None]
            tj = np.arange(64)[None, :]
            mab = np.zeros((128, 128))
            mab[:64, :64] = (si < tj)
            mab[:64, 64:] = (si <= tj)
            self.c_mab = self.const(128, mab)
            msl = np.zeros((128, 64))
            msl[:64, :] = (si > tj)
            self.c_msl = self.const(64, msl)
            bd = np.zeros((128, 128))
            bd[:64, :64] = 1.0 / 64
            bd[64:, 64:] = 1.0 / 64
            self.c_blkm = self.const(128, bd)

        def A(fn, reads, writes):
            P.op("act", fn, [x.b for x in reads], [x.b for x in writes])

        def V(fn, reads, writes):
            P.op("dve", fn, [x.b for x in reads], [x.b for x in writes])

        def G(fn, reads, writes):
            P.op("pool", fn, [x.b for x in reads], [x.b for x in writes])

        def M(fn, reads, writes):
            P.op("pe", fn, [x.b for x in reads], [x.b for x in writes])

        CB, PP = self.cb, self.pp
        pv = lambda name: self.dvec((name, ia), lambda inp, name=name: inp[name][ia].reshape(-1))
        mu = [self.dvec(("rwkv_mu", ia, i), lambda inp, i=i: inp["rwkv_mu"][ia][i]) for i in range(6)]
        w0c, a0c, kkc, kac, rkc, lwc, lbc = (pv("rwkv_w0"), pv("rwkv_a0"), pv("rwkv_k_k"), pv("rwkv_k_a"),
                                             pv("rwkv_r_k"), pv("rwkv_lnx_w"), pv("rwkv_lnx_b"))
        if has_vres:
            v0c = self.dvec(("rwkv_v0", ia), lambda inp: inp["rwkv_v0"][ia - 1])
        m = self.mark()
        xr = self.alloc("xr", KC * T2, BF16)
        xk = self.alloc("xk", KC * T2, BF16)
        xv = self.alloc("xv", KC * T2, BF16)
        inw = self.alloc("inw", T2, BF16)
        ina = self.alloc("ina", T2, BF16)
        ing = self.alloc("ing", 2 * T2, BF16)
        inv = self.alloc("inv", T2, BF16) if has_vres else None
        yg = self.alloc("yg", KC * T2, BF16)
        yg3 = yg.ap.rearrange("p (k t) -> p k t", k=KC)
        xr3, xk3, xv3 = [x.ap.rearrange("p (k t) -> p k t", k=KC) for x in (xr, xk, xv)]
        m2 = self.mark()
        hb = self.alloc("hb", KC * (T2 + 1), BF16)
        h3 = hb.ap.rearrange("p (k t) -> p k t", k=KC)
        if seq_start:
            V(functools.partial(nc.vector.memset, h3[:, :, 0:1], 0.0), [], [hb])
        else:
            V(functools.partial(nc.vector.tensor_copy, out=h3[:, :, 0:1], in_=self.hlast.ap.unsqueeze(2)), [self.hlast], [hb])
        self.norm_block(L, "norm_mix", t0, T2, first, hb, hoff=1, hstride=T2 + 1)
        V(functools.partial(nc.vector.tensor_copy, out=self.hlast.ap.unsqueeze(2), in_=h3[:, :, T2:T2 + 1]), [hb], [self.hlast])
        dx = self.alloc("dx", KC * T2, BF16)
        dx3 = dx.ap.rearrange("p (k t) -> p k t", k=KC)
        V(functools.partial(nc.vector.tensor_tensor, out=dx3, in0=h3[:, :, 0:T2], in1=h3[:, :, 1:T2 + 1], op=ALU.subtract),
          [hb], [dx])

        def mix(i, outr, out3):
            for kc in range(KC):
                V(functools.partial(nc.vector.scalar_tensor_tensor,
                    out=out3[:, kc, :], in0=dx3[:, kc, :], scalar=mu[i][:, kc:kc + 1], in1=h3[:, kc, 1:T2 + 1],
                    op0=ALU.mult, op1=ALU.add), [dx, hb, PP], [outr])

        xm = self.alloc("xm", KC * T2, BF16)
        xm3 = xm.ap.rearrange("p (k t) -> p k t", k=KC)

        def lora1(mi, key, wname, widx, Rr, outr, func):
            mix(mi, xm, xm3)
            wt = self.wtile((key, L), 128, KC * Rr,
                            lambda inp: inp[wname][widx].reshape(KC, 128, Rr).transpose(1, 0, 2).reshape(128, KC * Rr))
            w3 = wt.ap.rearrange("p (k m) -> p k m", k=KC)
            for rc in range((Rr + 127) // 128):
                rr = min(128, Rr - rc * 128)
                ps = self.psa()
                for kc in range(KC):
                    M(functools.partial(nc.tensor.matmul,
                        ps.ap[0:rr, 0:T2], lhsT=w3[:, kc, rc * 128:rc * 128 + rr], rhs=xm3[:, kc, :],
                        start=(kc == 0), stop=(kc == KC - 1)), [wt, xm], [ps])
                A(functools.partial(nc.scalar.activation,
                    out=outr.ap[0:rr, rc * T2:(rc + 1) * T2], in_=ps.ap[0:rr, 0:T2], func=func), [ps], [outr])

        lora1(1, "rw1", "rwkv_w1", ia, 96, inw, AF.Tanh)
        lora1(4, "ra1", "rwkv_a1", ia, 96, ina, AF.Copy)
        lora1(5, "rg1", "rwkv_g1", ia, 256, ing, AF.Sigmoid)
        if has_vres:
            lora1(3, "rv1", "rwkv_v1", ia - 1, 64, inv, AF.Copy)
        mix(0, xr, xr3)
        mix(2, xk, xk3)
        mix(3, xv, xv3)
        self.release(m2)
        f32 = lambda name: self.alloc(name, T2, F32)
        m3 = self.mark()
        r32, k32, v32, sg, alr, kkn, kmod, cs, epos, bonus, t1, t2 = [
            f32(n) for n in ("r32", "k32", "v32", "sg", "alr", "kkn", "kmod", "cs", "epos", "bonus", "t1", "t2")]
        y32, eexc, eneg = r32, k32, v32
        b1 = self.alloc("b1", T2, BF16)
        vb = self.alloc("vb", T2, BF16)
        ar = self.alloc("ar", T2 * 2, BF16)
        bk = self.alloc("bk", T2 * 2, BF16)
        ar3 = ar.ap.rearrange("p (c x t) -> p c x t", c=NCH, x=2)
        bk3 = bk.ap.rearrange("p (c x t) -> p c x t", c=NCH, x=2)
        tT = self.alloc("tT", NCH * 4 * 128, BF16)
        tT4 = tT.ap[0:64].rearrange("p (c k d) -> p c k d", c=NCH, k=4)
        PMb = [self.alloc("PMb%d" % i, NCH * 128, BF16) for i in range(2)]
        PMk = [self.alloc("PMk%d" % i, NCH * 128, BF16) for i in range(2)]
        Lb = [self.alloc("Lb%d" % i, NCH * 64, BF16) for i in range(2)]
        Nb = [self.alloc("Nb%d" % i, NCH * 64, BF16) for i in range(2)]
        NI = self.alloc("NI", NCH * 64, BF16)
        Xb = [self.alloc("Xb%d" % i, NCH * 128, BF16) for i in range(2)]
        Wp = self.alloc("Wp", NCH * 128, BF16)
        U0p = self.alloc("U0p", NCH * 128, BF16)
        Wpad = [self.alloc("Wpad%d" % i, NCH * 128, BF16) for i in range(2)]
        U0pad = [self.alloc("U0pad%d" % i, NCH * 128, BF16) for i in range(2)]
        vTpad = [self.alloc("vTpad%d" % i, NCH * 128, BF16) for i in range(2)]
        GT = self.alloc("GT", NCH * 128, BF16)
        Hh = self.alloc("Hh", NCH * 128, F32)
        QT = self.alloc("QT", NCH * 64, BF16)
        Zall = self.alloc("Zall", (NCH + 1) * 128, BF16)
        Z32 = self.alloc("Z32", 128, F32)
        tz = self.alloc("tz", 128, F32)
        v3c = lambda r_, w=128: r_.ap[0:64].rearrange("p (c d) -> p c d", c=NCH)
        v3f = lambda r_: r_.ap.rearrange("p (c d) -> p c d", c=NCH)
        for z_ in Wpad + U0pad + vTpad + [GT]:
            V(functools.partial(nc.vector.memset, z_.ap, 0.0), [], [z_])
        V(functools.partial(nc.vector.memset, Hh.ap, 0.0), [], [Hh])
        psA01 = (self.psA_r[0], self.psA_r[1])
        psA23 = (self.psA_r[2], self.psA_r[3])
        pA01 = self.psA[:, 0:1024]
        pA23 = self.psA[:, 1024:2048]
        pB0, pB1 = self.psB_r[0], self.psB_r[1]
        blk64, blkm, ident, ones = self.blk64, self.c_blkm, self.ident, self.ones
        id64 = ident[0:64, 0:64]
        mab = self.c_mab[0:64, :]
        msl = self.c_msl[0:64, :]
        W_rkv = lambda inp: inp["rwkv_w_rkv"][ia]
        for p in range(KC):
            cs_ = slice(p * 128, (p + 1) * 128)

            def getter(inp, cs_=cs_, p=p):
                W = W_rkv(inp)
                main = np.concatenate([W[0][:, cs_], W[1][:, cs_], W[2][:, cs_]], axis=1)
                main = main.reshape(KC, 128, 384).transpose(1, 0, 2).reshape(128, KC * 384)
                ext = np.zeros((128, 640), np.float32)
                ext[:96, 0:128] = inp["rwkv_w2"][ia][:, cs_]
                ext[:96, 128:256] = inp["rwkv_a2"][ia][:, cs_]
                g2 = inp["rwkv_g2"][ia][:, cs_]
                ext[:, 256:384] = g2[0:128]
                ext[:, 384:512] = g2[128:256]
                if has_vres:
                    ext[:64, 512:640] = inp["rwkv_v2"][ia - 1][:, cs_]
                return np.concatenate([main, ext], axis=1)

            wt = self.wtile(("rkv", L, p), 128, KC * 384 + 640, getter)
            w3 = wt.ap[:, 0:KC * 384].rearrange("p (k m) -> p k m", k=KC)
            E0 = KC * 384
            w2p = wt.ap[0:96, E0:E0 + 128]
            a2p = wt.ap[0:96, E0 + 128:E0 + 256]
            g2p = [wt.ap[:, E0 + 256:E0 + 384], wt.ap[:, E0 + 384:E0 + 512]]
            v2p = wt.ap[0:64, E0 + 512:E0 + 640]
            for part, (xin, xin3, dst) in enumerate(((xr, xr3, r32), (xk, xk3, k32), (xv, xv3, v32))):
                ps = self.psa()
                for kc in range(KC):
                    M(functools.partial(nc.tensor.matmul,
                        ps.ap, lhsT=w3[:, kc, part * 128:(part + 1) * 128], rhs=xin3[:, kc, :],
                        start=(kc == 0), stop=(kc == KC - 1)), [wt, xin], [ps])
                A(functools.partial(nc.scalar.copy, out=dst.ap, in_=ps.ap), [ps], [dst])
            ps = self.psa()
            M(functools.partial(nc.tensor.matmul, ps.ap, lhsT=w2p, rhs=inw.ap[0:96, :], start=True, stop=True), [wt, inw], [ps])
            A(functools.partial(nc.scalar.activation, out=sg.ap, in_=ps.ap, func=AF.Sigmoid, bias=w0c[:, p:p + 1], scale=1.0),
              [ps, PP], [sg])
            ps = self.psa()
            M(functools.partial(nc.tensor.matmul, ps.ap, lhsT=a2p, rhs=ina.ap[0:96, :], start=True, stop=True), [wt, ina], [ps])
            A(functools.partial(nc.scalar.activation, out=alr.ap, in_=ps.ap, func=AF.Sigmoid, bias=a0c[:, p:p + 1], scale=1.0),
              [ps, PP], [alr])
            if has_vres:
                ps = self.psa()
                M(functools.partial(nc.tensor.matmul, ps.ap, lhsT=v2p, rhs=inv.ap[0:64, :], start=True, stop=True), [wt, inv], [ps])
                A(functools.partial(nc.scalar.activation, out=t1.ap, in_=ps.ap, func=AF.Sigmoid, bias=v0c[:, p:p + 1], scale=1.0),
                  [ps, PP], [t1])
                P.dma("sp", functools.partial(nc.sync.dma_start, out=t2.ap, in_=self.vfirst[p, :, t0:t0 + T2]), t2.b,
                      reads=[self.vfirst_b], writes=[t2.b])
                V(functools.partial(nc.vector.tensor_tensor, out=t2.ap, in0=t2.ap, in1=v32.ap, op=ALU.subtract), [t2, v32], [t2])
                V(functools.partial(nc.vector.tensor_tensor, out=t2.ap, in0=t2.ap, in1=t1.ap, op=ALU.mult), [t2, t1], [t2])
                V(functools.partial(nc.vector.tensor_tensor, out=v32.ap, in0=v32.ap, in1=t2.ap, op=ALU.add), [t2, v32], [v32])
            else:
                P.dma("sp", functools.partial(nc.sync.dma_start, out=self.vfirst[p, :, t0:t0 + T2], in_=v32.ap), self.vfirst_b,
                      reads=[v32.b], writes=[self.vfirst_b])
            V(functools.partial(nc.vector.tensor_scalar, out=t1.ap, in0=k32.ap, scalar1=kkc[:, p:p + 1], scalar2=None, op0=ALU.mult),
              [k32, PP], [t1])
            A(functools.partial(nc.scalar.activation, out=b1.ap, in_=t1.ap, func=AF.Square), [t1], [b1])
            M(functools.partial(nc.tensor.matmul, self.psS[:, :], lhsT=blk64, rhs=b1.ap, start=True, stop=True), [b1, CB], [self.psS_r])
            A(functools.partial(nc.scalar.activation, out=t2.ap, in_=self.psS[:, :], func=AF.Sqrt), [self.psS_r], [t2])
            V(functools.partial(nc.vector.tensor_scalar, out=t2.ap, in0=t2.ap, scalar1=1e-12, scalar2=None, op0=ALU.max), [t2], [t2])
            V(functools.partial(nc.vector.reciprocal, out=t2.ap, in_=t2.ap), [t2], [t2])
            V(functools.partial(nc.vector.tensor_tensor, out=kkn.ap, in0=t1.ap, in1=t2.ap, op=ALU.mult), [t1, t2], [kkn])
            V(functools.partial(nc.vector.tensor_scalar, out=t1.ap, in0=alr.ap, scalar1=-1.0, scalar2=kac[:, p:p + 1],
                                                  op0=ALU.add, op1=ALU.mult), [alr, PP], [t1])
            V(functools.partial(nc.vector.scalar_tensor_tensor, out=kmod.ap, in0=t1.ap, scalar=1.0, in1=k32.ap,
                                                     op0=ALU.add, op1=ALU.mult), [t1, k32], [kmod])
            V(functools.partial(nc.vector.tensor_tensor, out=t1.ap, in0=r32.ap, in1=kmod.ap, op=ALU.mult), [r32, kmod], [t1])
            V(functools.partial(nc.vector.tensor_scalar, out=b1.ap, in0=t1.ap, scalar1=rkc[:, p:p + 1], scalar2=None, op0=ALU.mult),
              [t1, PP], [b1])
            M(functools.partial(nc.tensor.matmul, self.psS[:, :], lhsT=blk64, rhs=b1.ap, start=True, stop=True), [b1, CB], [self.psS_r])
            V(functools.partial(nc.vector.tensor_tensor, out=bonus.ap, in0=self.psS[:, :], in1=v32.ap, op=ALU.mult),
              [self.psS_r, v32], [bonus])
            A(functools.partial(nc.scalar.copy, out=vb.ap, in_=v32.ap), [v32], [vb])
            for c in range(NCH):
                sl = slice(c * 64, (c + 1) * 64)
                V(functools.partial(nc.vector.tensor_tensor_scan, out=cs.ap[:, sl], data0=ones[:, 0:64], data1=sg.ap[:, sl],
                                                             initial=0.0, op0=ALU.mult, op1=ALU.add), [sg, CB], [cs])
            A(functools.partial(nc.scalar.activation, out=eneg.ap, in_=cs.ap, func=AF.Exp, scale=LD), [cs], [eneg])
            A(functools.partial(nc.scalar.activation, out=epos.ap, in_=cs.ap, func=AF.Exp, scale=-LD), [cs], [epos])
            V(functools.partial(nc.vector.tensor_tensor, out=t2.ap, in0=cs.ap, in1=sg.ap, op=ALU.subtract), [cs, sg], [t2])
            A(functools.partial(nc.scalar.activation, out=eexc.ap, in_=t2.ap, func=AF.Exp, scale=-LD), [t2], [eexc])
            c64 = lambda r_: r_.ap.rearrange("p (c t) -> p c t", c=NCH)
            V(functools.partial(nc.vector.scalar_tensor_tensor, out=ar3[:, :, 0, :], in0=c64(kkn), scalar=-1.0, in1=c64(eexc),
                                                     op0=ALU.mult, op1=ALU.mult), [kkn, eexc], [ar])
            V(functools.partial(nc.vector.tensor_tensor, out=ar3[:, :, 1, :], in0=c64(r32), in1=c64(epos), op=ALU.mult),
              [r32, epos], [ar])
            V(functools.partial(nc.vector.tensor_tensor, out=t1.ap, in0=kkn.ap, in1=alr.ap, op=ALU.mult), [kkn, alr], [t1])
            V(functools.partial(nc.vector.tensor_tensor, out=bk3[:, :, 0, :], in0=c64(t1), in1=c64(eneg), op=ALU.mult), [t1, eneg], [bk])
            V(functools.partial(nc.vector.tensor_tensor, out=bk3[:, :, 1, :], in0=c64(kmod), in1=c64(eneg), op=ALU.mult),
              [kmod, eneg], [bk])
            gC = c64(epos)[:, :, 63]
            for c2 in range(NCH // 2):
                for cc in range(2):
                    c = c2 * 2 + cc
                    srcs = (ar3[:, c, 0, :], bk3[:, c, 0, :], bk3[:, c, 1, :], vb.ap[:, c * 64:(c + 1) * 64])
                    for kind in range(4):
                        o_ = (cc * 4 + kind) * 128
                        M(functools.partial(nc.tensor.transpose, self.psT[0:64, o_:o_ + 128], srcs[kind], ident),
                          [ar, bk, vb, CB], [self.psT_r])
                A(functools.partial(nc.scalar.copy,
                    out=tT4[:, c2 * 2:c2 * 2 + 2, :, :],
                    in_=self.psT[0:64, :].rearrange("p (c k d) -> p c k d", c=2, k=4)), [self.psT_r], [tT])
            for hh in range(2):
                hp = slice(hh * 64, (hh + 1) * 64)
                hc = slice(hh * 64, (hh + 1) * 64)
                for c in range(NCH):
                    M(functools.partial(nc.tensor.matmul, pA01[0:64, c * 128:(c + 1) * 128], lhsT=bk3[hp, c, 0, :],
                                                  rhs=ar3[hp, c, :, :], start=True, stop=True), [bk, ar], psA01)
                for c in range(NCH):
                    M(functools.partial(nc.tensor.matmul, pA23[0:64, c * 128:(c + 1) * 128], lhsT=bk3[hp, c, 1, :],
                                                  rhs=ar3[hp, c, :, :], start=True, stop=True), [bk, ar], psA23)
                pmb, pmk = PMb[hh], PMk[hh]
                mab_b = mab.unsqueeze(1).to_broadcast([64, NCH, 128])
                V(functools.partial(nc.vector.tensor_tensor,
                    out=v3c(pmb), in0=pA01[0:64, :].rearrange("p (c d) -> p c d", c=NCH), in1=mab_b, op=ALU.mult),
                  list(psA01) + [CB], [pmb])
                V(functools.partial(nc.vector.tensor_tensor,
                    out=v3c(pmk), in0=pA23[0:64, :].rearrange("p (c d) -> p c d", c=NCH), in1=mab_b, op=ALU.mult),
                  list(psA23) + [CB], [pmk])
                for c in range(NCH):
                    M(functools.partial(nc.tensor.matmul, pB0.ap[0:64, c * 64:(c + 1) * 64], lhsT=ar3[hp, c, 0, :],
                                                  rhs=bk3[hp, c, 0, :], start=True, stop=True), [bk, ar], [pB0])
                Lc, Nc = Lb[0], pmb
                V(functools.partial(nc.vector.tensor_tensor,
                    out=v3c(Lc), in0=pB0.ap[0:64, :].rearrange("p (c d) -> p c d", c=NCH),
                    in1=msl.unsqueeze(1).to_broadcast([64, NCH, 64]), op=ALU.mult), [pB0, CB], [Lc])
                idb = id64.unsqueeze(1).to_broadcast([64, NCH, 64])
                V(functools.partial(nc.vector.tensor_tensor, out=v3c(NI), in0=v3c(pmb)[:, :, 0:64], in1=idb, op=ALU.add),
                  [pmb, CB], [NI])
                for c in range(NCH):
                    M(functools.partial(nc.tensor.matmul, pB1.ap[0:64, c * 64:(c + 1) * 64], lhsT=v3c(pmk)[:, c, 0:64],
                                                           rhs=tT4[:, c, 3, hc], start=True, stop=True), [pmk, tT], [pB1])
                X = Xb[0]
                A(functools.partial(nc.scalar.copy, out=v3c(X)[:, :, 64:128],
                                             in_=pB1.ap[0:64, :].rearrange("p (c d) -> p c d", c=NCH)), [pB1], [X])
                G(functools.partial(nc.gpsimd.tensor_copy, out=v3c(X)[:, :, 0:64], in_=tT4[:, :, 0, hc]), [tT], [X])
                Nview = lambda r_, isP: (v3c(r_)[:, :, 0:64] if isP else v3c(r_))
                n_isP = True
                for lvl in range(6):
                    for c in range(NCH):
                        M(functools.partial(nc.tensor.matmul, pA01[0:64, c * 128:(c + 1) * 128], lhsT=v3c(NI)[:, c, :],
                                                           rhs=v3c(X)[:, c, :], start=True, stop=True), [NI, X], psA01)
                    px3 = pA01[0:64, :].rearrange("p (c d) -> p c d", c=NCH)
                    if lvl < 5:
                        Xn = Xb[(lvl + 1) % 2]
                        A(functools.partial(nc.scalar.copy, out=v3c(Xn), in_=px3), list(psA01), [Xn])
                        X = Xn
                        nv = Nview(Nc, n_isP)
                        for c in range(NCH):
                            M(functools.partial(nc.tensor.matmul, pB0.ap[0:64, c * 64:(c + 1) * 64], lhsT=v3c(Lc)[:, c, :],
                                                                        rhs=nv[:, c, :], start=True, stop=True), [Lc, Nc], [pB0])
                        if lvl < 4:
                            for c in range(NCH):
                                M(functools.partial(nc.tensor.matmul, pB1.ap[0:64, c * 64:(c + 1) * 64], lhsT=nv[:, c, :],
                                                                            rhs=v3c(Lc)[:, c, :], start=True, stop=True), [Lc, Nc], [pB1])
                        Nn = Nb[lvl % 2]
                        pn3 = pB0.ap[0:64, :].rearrange("p (c d) -> p c d", c=NCH)
                        V(functools.partial(nc.vector.tensor_copy, out=v3c(Nn), in_=pn3), [pB0], [Nn])
                        V(functools.partial(nc.vector.tensor_tensor, out=v3c(NI), in0=pn3, in1=idb, op=ALU.add), [pB0, CB], [NI])
                        if lvl < 4:
                            Ln = Lb[(lvl + 1) % 2]
                            A(functools.partial(nc.scalar.copy, out=v3c(Ln), in_=pB1.ap[0:64, :].rearrange("p (c d) -> p c d", c=NCH)),
                              [pB1], [Ln])
                            Lc = Ln
                        Nc, n_isP = Nn, False
                    else:
                        A(functools.partial(nc.scalar.copy, out=v3c(Wp)[:, :, hc], in_=px3[:, :, 0:64]), list(psA01), [Wp])
                        V(functools.partial(nc.vector.tensor_copy, out=v3c(U0p)[:, :, hc], in_=px3[:, :, 64:128]), list(psA01), [U0p])
                        A(functools.partial(nc.scalar.copy, out=v3c(Wpad[hh])[:, :, hc], in_=px3[:, :, 0:64]),
                          list(psA01), [Wpad[hh]])
                        V(functools.partial(nc.vector.tensor_copy, out=v3c(U0pad[hh])[:, :, hc], in_=px3[:, :, 64:128]),
                          list(psA01), [U0pad[hh]])
                G(functools.partial(nc.gpsimd.tensor_copy, out=v3c(vTpad[hh])[:, :, hc], in_=tT4[:, :, 3, hc]), [tT], [vTpad[hh]])
            for c in range(NCH):
                M(functools.partial(nc.tensor.matmul, pA01[:, c * 128:(c + 1) * 128], lhsT=v3c(Wp)[:, c, :], rhs=tT4[:, c, 1, :],
                                              start=True, stop=True), [Wp, tT], psA01)
            pg3 = pA01.rearrange("p (c d) -> p c d", c=NCH)
            V(functools.partial(nc.vector.tensor_copy, out=v3f(GT)[0:64, :, 0:64], in_=pg3[0:64, :, 0:64]), list(psA01), [GT])
            A(functools.partial(nc.scalar.copy, out=v3f(GT)[64:128, :, 64:128], in_=pg3[64:128, :, 64:128]), list(psA01), [GT])
            for c in range(NCH):
                M(functools.partial(nc.tensor.matmul, pA23[:, c * 128:(c + 1) * 128], lhsT=tT4[:, c, 1, :], rhs=v3c(U0p)[:, c, :],
                                              start=True, stop=False), [U0p, tT], psA23)
                M(functools.partial(nc.tensor.matmul, pA23[:, c * 128:(c + 1) * 128], lhsT=tT4[:, c, 2, :], rhs=tT4[:, c, 3, :],
                                              start=False, stop=True), [tT], psA23)
            ph3 = pA23.rearrange("p (c d) -> p c d", c=NCH)
            for hh in range(2):
                hp = slice(hh * 64, (hh + 1) * 64)
                V(functools.partial(nc.vector.tensor_tensor,
                    out=v3f(Hh)[hp, :, hp], in0=ph3[hp, :, hp], in1=gC[hp, :].unsqueeze(2).to_broadcast([64, NCH, 64]),
                    op=ALU.mult), list(psA23) + [epos], [Hh])
            for c in range(NCH):
                for hh in range(2):
                    M(functools.partial(nc.tensor.matmul, pB0.ap[:, c * 64:(c + 1) * 64], lhsT=v3c(Wpad[hh])[:, c, :],
                                                         rhs=v3c(PMb[hh])[:, c, 64:128], start=(hh == 0), stop=(hh == 1)),
                      [Wpad[hh], PMb[hh]], [pB0])
            V(functools.partial(nc.vector.tensor_tensor, out=QT.ap.rearrange("p (c t) -> p c t", c=NCH),
                                              in0=pB0.ap.rearrange("p (c t) -> p c t", c=NCH), in1=ar3[:, :, 1, :],
                                              op=ALU.add), [pB0, ar], [QT])
            if seq_start:
                V(functools.partial(nc.vector.memset, Z32.ap, 0.0), [], [Z32])
            else:
                P.dma("sp", functools.partial(nc.sync.dma_start, out=Z32.ap, in_=self.zst[p]), Z32.b,
                      reads=[self.zst_b], writes=[Z32.b])
            za3 = Zall.ap.rearrange("p (c d) -> p c d", c=NCH + 1)
            A(functools.partial(nc.scalar.copy, out=za3[:, 0, :], in_=Z32.ap), [Z32], [Zall])
            for c in range(NCH):
                M(functools.partial(nc.tensor.matmul, pB1.ap[:, 0:128], lhsT=v3f(GT)[:, c, :], rhs=za3[:, c, :], start=True, stop=True),
                  [GT, Zall], [pB1])
                V(functools.partial(nc.vector.tensor_tensor, out=tz.ap, in0=pB1.ap[:, 0:128], in1=Z32.ap, op=ALU.add), [pB1, Z32], [tz])
                V(functools.partial(nc.vector.scalar_tensor_tensor, out=Z32.ap, in0=tz.ap, scalar=gC[:, c:c + 1], in1=v3f(Hh)[:, c, :],
                                                             op0=ALU.mult, op1=ALU.add), [tz, epos, Hh], [Z32])
                A(functools.partial(nc.scalar.copy, out=za3[:, c + 1, :], in_=Z32.ap), [Z32], [Zall])
            P.dma("sp", functools.partial(nc.sync.dma_start, out=self.zst[p], in_=Z32.ap), self.zst_b,
                  reads=[Z32.b], writes=[self.zst_b])
            q3_ = QT.ap.rearrange("p (c t) -> p c t", c=NCH)
            for c in range(NCH):
                M(functools.partial(nc.tensor.matmul, pB0.ap[:, c * 64:(c + 1) * 64], lhsT=za3[:, c, :], rhs=q3_[:, c, :],
                                              start=True, stop=False), [Zall, QT], [pB0])
                for hh in range(2):
                    M(functools.partial(nc.tensor.matmul, pB0.ap[:, c * 64:(c + 1) * 64], lhsT=v3c(U0pad[hh])[:, c, :],
                                                         rhs=v3c(PMb[hh])[:, c, 64:128], start=False, stop=False),
                      [U0pad[hh], PMb[hh]], [pB0])
                for hh in range(2):
                    M(functools.partial(nc.tensor.matmul, pB0.ap[:, c * 64:(c + 1) * 64], lhsT=v3c(vTpad[hh])[:, c, :],
                                                         rhs=v3c(PMk[hh])[:, c, 64:128], start=False, stop=(hh == 1)),
                      [vTpad[hh], PMk[hh]], [pB0])
            A(functools.partial(nc.scalar.copy, out=y32.ap, in_=pB0.ap), [pB0], [y32])
            A(functools.partial(nc.scalar.copy, out=b1.ap, in_=y32.ap), [y32], [b1])
            M(functools.partial(nc.tensor.matmul, self.psS[:, :], lhsT=blkm, rhs=b1.ap, start=True, stop=True), [b1, CB], [self.psS_r])
            V(functools.partial(nc.vector.tensor_tensor, out=y32.ap, in0=y32.ap, in1=self.psS[:, :], op=ALU.subtract),
              [y32, self.psS_r], [y32])
            A(functools.partial(nc.scalar.activation, out=b1.ap, in_=y32.ap, func=AF.Square), [y32], [b1])
            M(functools.partial(nc.tensor.matmul, self.psS[:, :], lhsT=blkm, rhs=b1.ap, start=True, stop=True), [b1, CB], [self.psS_r])
            V(functools.partial(nc.vector.tensor_scalar, out=t1.ap, in0=self.psS[:, :], scalar1=GN_EPS, scalar2=None, op0=ALU.add),
              [self.psS_r], [t1])
            A(functools.partial(nc.scalar.activation, out=t2.ap, in_=t1.ap, func=AF.Sqrt), [t1], [t2])
            V(functools.partial(nc.vector.reciprocal, out=t1.ap, in_=t2.ap), [t2], [t1])
            V(functools.partial(nc.vector.tensor_tensor, out=y32.ap, in0=y32.ap, in1=t1.ap, op=ALU.mult), [y32, t1], [y32])
            V(functools.partial(nc.vector.tensor_scalar, out=y32.ap, in0=y32.ap, scalar1=lwc[:, p:p + 1], scalar2=lbc[:, p:p + 1],
                                                  op0=ALU.mult, op1=ALU.add), [y32, PP], [y32])
            V(functools.partial(nc.vector.tensor_tensor, out=y32.ap, in0=y32.ap, in1=bonus.ap, op=ALU.add), [y32, bonus], [y32])
            ps = self.psa()
            for kc2 in range(2):
                M(functools.partial(nc.tensor.matmul, ps.ap, lhsT=g2p[kc2], rhs=ing.ap[:, kc2 * T2:(kc2 + 1) * T2],
                                                         start=(kc2 == 0), stop=(kc2 == 1)), [wt, ing], [ps])
            V(functools.partial(nc.vector.tensor_tensor, out=yg3[:, p, :], in0=ps.ap, in1=y32.ap, op=ALU.mult),
              [ps, y32], [yg])
        self.release(m3)
        self.out_proj("rwo", L, yg, T2, t0, lambda inp: inp["rwkv_w_o"][ia], first)
        self.release(m)


N_CORES = 4
SEQ = 4096
_CACHE = {}


def kernel(**inputs):
    inp = {k: np.asarray(v) for k, v in inputs.items()}
    x = inp["x"].astype(np.float32, copy=False)
    B, T, Dm = x.shape
    assert (B, T, Dm) == (N_CORES, SEQ, D)
    if "b" not in _CACHE:
        b = Builder(SEQ, [0, 1, 2, 3], final_norm=True)
        b.build()
        _CACHE["b"] = b
    b = _CACHE["b"]
    wp, pp, cp = b.pack(inp)
    in_maps = []
    for i in range(B):
        xT = np.ascontiguousarray(x[i].T).reshape(KC, 128, T)
        in_maps.append({"xT": xT, "wpack": wp, "ppack": pp, "cpack": cp})
    res = run_bass_kernel_spmd(b.nc, in_maps, core_ids=list(range(B)))
    out = np.empty((B, T, Dm), np.float32)
    for i in range(B):
        out[i] = np.asarray(res.results[i]["yT"]).reshape(Dm, T).T
    return out
```

```python
import contextlib
import functools
import math
import numpy as np
import concourse.bass as bass
import concourse.mybir as mybir
from concourse.bass_utils import run_bass_kernel_spmd

F32 = mybir.dt.float32
BF16 = mybir.dt.bfloat16
AF = mybir.ActivationFunctionType
ALU = mybir.AluOpType
AX = mybir.AxisListType

D = 2048
KC = 16
DFF = 8192
NH = 32
HD = 64
TB = 1024
TT = 512
RMS_EPS = 1e-6
GN_EPS = 64e-5


class Buf:
    __slots__ = ("name", "w", "r", "dsem", "dcount", "excl")

    def __init__(self, name, excl=False):
        self.excl = excl
        self.name = name
        self.w = None
        self.r = []
        self.dsem = None
        self.dcount = 0


class Op:
    __slots__ = ("eng", "fn", "reads", "writes", "dma", "dbuf", "tile_idx", "uses_tile",
                 "need_inc", "sem", "val", "ndma", "inc")

    def __init__(self, eng, fn, reads, writes, dma=False, dbuf=None, tile_idx=None, uses_tile=None, ndma=1, inc=16):
        self.inc = inc
        self.eng = eng
        self.fn = fn
        self.reads = reads
        self.writes = writes
        self.dma = dma
        self.dbuf = dbuf
        self.tile_idx = tile_idx
        self.uses_tile = uses_tile
        self.need_inc = False
        self.sem = None
        self.val = None
        self.ndma = ndma


class Prog:
    ENGS = ("pe", "act", "dve", "pool", "sp")

    def __init__(self):
        self.nc = bass.Bass("TRN2", target_bir_lowering=False)
        self.es = contextlib.ExitStack()
        self.ops = []
        self.cur_tile = None
        nc = self.nc
        self.eng = {"pe": nc.tensor, "act": nc.scalar, "dve": nc.vector, "pool": nc.gpsimd, "sp": nc.sync}
        self.esem = {e: self.es.enter_context(nc.semaphore("sem_" + e)) for e in ("pe", "act", "dve", "pool")}
        self.dsems = []

    def op(self, eng, fn, reads=(), writes=(), uses_tile=None):
        reads = list(reads)
        writes = list(writes)
        for b in reads:
            if b.excl and b not in writes:
                writes.append(b)
        o = Op(eng, fn, list(reads), list(writes),
               uses_tile=uses_tile if uses_tile is not None else self.cur_tile)
        self.ops.append(o)
        return o

    def dma(self, eng, fn, dbuf, reads=(), writes=(), tile_idx=None, ndma=1, inc=16):
        o = Op(eng, fn, list(reads), list(writes), dma=True, dbuf=dbuf, tile_idx=tile_idx, ndma=ndma, inc=inc)
        self.ops.append(o)
        return o

    def barrier(self):
        self.ops.append("BARRIER")

    def _hoist(self, dist):
        loads = {}
        rest = []
        for o in self.ops:
            if o != "BARRIER" and o.dma and o.tile_idx is not None:
                loads[o.tile_idx] = o
            else:
                rest.append(o)
        if not loads:
            return
        ntiles = max(loads) + 1
        first_use = {}
        for i, o in enumerate(rest):
            if o != "BARRIER" and o.uses_tile is not None and o.uses_tile not in first_use:
                first_use[o.uses_tile] = i
        inserts = {}
        for j in range(ntiles):
            t = max(0, j - dist)
            while t not in first_use and t < ntiles:
                t += 1
            pos = first_use.get(t, len(rest))
            inserts.setdefault(pos, []).append(loads[j])
        out = []
        for i, o in enumerate(rest):
            if i in inserts:
                out.extend(inserts[i])
            out.append(o)
        if len(rest) in inserts:
            out.extend(inserts[len(rest)])
        self.ops = out

    def finalize(self, hoist_dist=2):
        self._hoist(hoist_dist)
        nc = self.nc
        deps = []
        bufs_seen = {}
        all_bufs = []

        def reg(b):
            if id(b) not in bufs_seen:
                bufs_seen[id(b)] = b
                all_bufs.append(b)

        for o in self.ops:
            if o == "BARRIER":
                deps.append(None)
                continue
            d = []
            for b in o.reads:
                reg(b)
                if b.w is not None:
                    d.append(b.w)
            for b in o.writes:
                reg(b)
                if b.w is not None:
                    d.append(b.w)
                d.extend(b.r)
            dd = []
            seen = set()
            for x in d:
                if id(x) not in seen and x is not o:
                    seen.add(id(x))
                    dd.append(x)
            for x in dd:
                if not x.dma and not (x.eng == "pe" and o.eng == "pe"):
                    x.need_inc = True
            deps.append(dd)
            for b in o.reads:
                if not o.dma:
                    b.r = [x for x in b.r if x.dma or x.eng != o.eng]
                b.r.append(o)
            for b in o.writes:
                b.w = o
                b.r = []
            if o.dma:
                reg(o.dbuf)
        class DS:
            def __init__(self, sem):
                self.sem = sem
                self.count = 0
        ds_by_name = {}
        for o in self.ops:
            if o != "BARRIER" and o.dma and o.dbuf.dsem is None:
                nm = o.dbuf.name
                if nm not in ds_by_name:
                    ds_by_name[nm] = DS(self.es.enter_context(nc.semaphore("ds%d" % len(self.dsems))))
                    self.dsems.append(ds_by_name[nm])
                o.dbuf.dsem = ds_by_name[nm]
        all_ds = list(ds_by_name.values())
        cnt = {e: 0 for e in self.esem}
        known = {e: {} for e in self.ENGS}
        n_wait = 0
        for o, dd in zip(self.ops, deps):
            if o == "BARRIER":
                for e in self.ENGS:
                    eng = self.eng[e]
                    for e2 in self.esem:
                        v = cnt[e2]
                        if v > known[e].get(id(self.esem[e2]), 0):
                            eng.wait_ge(self.esem[e2], v)
                            known[e][id(self.esem[e2])] = v
                    for d_ in all_ds:
                        if d_.count > known[e].get(id(d_.sem), 0):
                            eng.wait_ge(d_.sem, d_.count)
                            known[e][id(d_.sem)] = d_.count
                continue
            e = o.eng
            eng = self.eng[e]
            for x in dd:
                if x.dma:
                    sem, val = x.dbuf.dsem.sem, x.dbuf.dsem.count
                else:
                    if x.eng == e and e == "pe":
                        continue
                    sem, val = x.sem, x.val
                if val > known[e].get(id(sem), 0):
                    eng.wait_ge(sem, val)
                    known[e][id(sem)] = val
                    n_wait += 1
            r = o.fn()
            if o.dma:
                insts = r if isinstance(r, (list, tuple)) else [r]
                assert len(insts) == o.ndma, (len(insts), o.ndma)
                for ins in insts:
                    ins.then_inc(o.dbuf.dsem.sem, o.inc)
                o.dbuf.dsem.count += o.inc * len(insts)
            elif o.need_inc:
                cnt[e] += 1
                r.then_inc(self.esem[e], 1)
                o.sem, o.val = self.esem[e], cnt[e]
        for d_ in all_ds:
            if d_.count > known["sp"].get(id(d_.sem), 0):
                nc.sync.wait_ge(d_.sem, d_.count)
        for e2 in self.esem:
            if cnt[e2] > known["sp"].get(id(self.esem[e2]), 0):
                nc.sync.wait_ge(self.esem[e2], cnt[e2])
        self.stats = dict(nops=len(self.ops), nwait=n_wait, cnt=dict(cnt), ndsem=len(self.dsems))
        self.es.close()
        return nc


class R:
    __slots__ = ("ap", "b")

    def __init__(self, ap, b):
        self.ap = ap
        self.b = b


class Builder:
    def __init__(self, tok, layers, final_norm, pair_mode=False):
        self.P = Prog()
        self.nc = self.P.nc
        self.tok = tok
        self.nblk = tok // TB
        self.layers = layers
        self.final_norm = final_norm
        self.pair_mode = pair_mode
        nc = self.nc
        P = self.P
        if pair_mode:
            self.xfullT = nc.dram_tensor("xfullT", [KC, 128, 2 * tok], F32, kind="ExternalInput").ap()
            self.xfull_b = Buf("xfullT")
            self.xg3 = nc.dram_tensor("xg3", [2 * KC * 128, tok], F32).ap()
            self.xg3_b = Buf("xg3")
            self.tail_send = nc.dram_tensor("tail_send", [KC * 128, 128], F32).ap()
            self.tail_send_b = Buf("tail_send")
            self.tail_recv = nc.dram_tensor("tail_recv", [2 * KC * 128, 128], F32).ap()
            self.tail_recv_b = Buf("tail_recv")
            self.NSB = 2 * tok // 512
            self.ygs = nc.dram_tensor("ygs", [self.NSB * 8 * 128, 512], BF16).ap()
            self.ygs_b = Buf("ygs")
            self.ygr = nc.dram_tensor("ygr", [2 * self.NSB * 8 * 128, 512], BF16).ap()
            self.ygr_b = Buf("ygr")
            self.groups = [[0, 1], [2, 3], [4, 5], [6, 7]]
        self.xT = nc.dram_tensor("xT", [KC, 128, tok], F32, kind="ExternalInput").ap()
        self.yT = nc.dram_tensor("yT", [KC, 128, tok], F32, kind="ExternalOutput").ap()
        self.xs = nc.dram_tensor("xs", [KC, 128, tok], F32).ap()
        self.xT_b = Buf("xT")
        self.xs_b = Buf("xs")
        self.yT_b = Buf("yT")
        self.wgetters = []
        self.wkeys = {}
        self.woff = 0
        self.pgetters = []
        self.pkeys = {}
        self.pcol = 0
        self.cgetters = []
        self.ccol = 0
        self.NPCOL = 1024
        self.NCCOL = 1280
        self.AW = 53200
        self.arena = P.es.enter_context(nc.sbuf_tensor("arena", [128, self.AW], F32))
        self.atop = 0
        self.nbufs = 0
        self.psA = P.es.enter_context(nc.psum_tensor("psA", [128, 2048], F32))
        self.psB = P.es.enter_context(nc.psum_tensor("psB", [128, 1024], F32))
        self.psT = P.es.enter_context(nc.psum_tensor("psT", [128, 1024], BF16))
        self.psS = P.es.enter_context(nc.psum_tensor("psS", [128, 512], F32))
        self.psA_r = [R(self.psA[:, i * 512:(i + 1) * 512], Buf("psA%d" % i, True)) for i in range(4)]
        self.psB_r = [R(self.psB[:, i * 512:(i + 1) * 512], Buf("psB%d" % i, True)) for i in range(2)]
        self.psT_r = R(self.psT[:, :], Buf("psT", True))
        self.psS_r = R(self.psS[:, :], Buf("psS", True))
        self.rotA = 0
        self.rotB = 0
        self.pp = self.alloc("pp", self.NPCOL, F32)
        self.cb = self.alloc("cb", self.NCCOL, BF16)
        self.wslots = [self.alloc("wslot%d" % i, 8192, BF16) for i in range(3)]
        self.wtile_n = 0
        self.built = False

    def alloc(self, name, nelem, dt, npart=128):
        words = nelem if dt == F32 else (nelem + 1) // 2
        off = self.atop
        self.atop += words
        assert self.atop <= self.AW, ("SBUF arena overflow", name, self.atop)
        ap = self.arena[:, off:off + words]
        if dt != F32:
            ap = ap.bitcast(dt)
        if npart != 128:
            ap = ap[0:npart]
        return R(ap, Buf(name))

    def mark(self):
        return self.atop

    def release(self, m):
        self.P.barrier()
        self.atop = m

    def psa(self):
        r = self.psA_r[self.rotA % 4]
        self.rotA += 1
        return r

    def psb(self):
        r = self.psB_r[self.rotB % 2]
        self.rotB += 1
        return r

    def param(self, key, ncols, getter):
        if key not in self.pkeys:
            self.pkeys[key] = self.pcol
            self.pgetters.append((self.pcol, ncols, getter))
            self.pcol += ncols
            assert self.pcol <= self.NPCOL
        c = self.pkeys[key]
        return self.pp.ap[:, c:c + ncols]

    def dvec(self, key, getter):
        return self.param(key, KC, lambda inp, g=getter: np.ascontiguousarray(
            np.asarray(g(inp), np.float32).reshape(KC, 128).T))

    def const(self, ncols, arr):
        c = self.ccol
        self.cgetters.append((c, ncols, arr if callable(arr) else np.asarray(arr, np.float32)))
        self.ccol += ncols
        assert self.ccol <= self.NCCOL
        return self.cb.ap[:, c:c + ncols]

    def wtile(self, key, npart, nelem, getter):
        assert nelem <= 8192
        if key not in self.wkeys:
            self.wkeys[key] = self.woff
            self.wgetters.append((self.woff, npart, nelem, getter))
            self.woff += npart * nelem
        off = self.wkeys[key]
        idx = self.wtile_n
        self.wtile_n += 1
        slot = self.wslots[idx % 3]
        dst = slot.ap[0:npart, 0:nelem]

        def fn(off=off, npart=npart, nelem=nelem, dst=dst):
            src = self.wpack[off:off + npart * nelem].rearrange("(p n) -> p n", p=npart)
            return self.nc.gpsimd.dma_start(out=dst, in_=src)

        self.P.dma("pool", fn, slot.b, writes=[slot.b], tile_idx=idx)
        self.P.cur_tile = idx
        return R(dst, slot.b)

    def setup_consts(self):
        P, nc = self.P, self.nc
        self.ones = self.const(128, np.ones((128, 128)))
        self.ident = self.const(128, np.eye(128))
        bd = np.zeros((128, 128))
        bd[:64, :64] = 1
        bd[64:, 64:] = 1
        self.blk64 = self.const(128, bd)

    def load_consts(self):
        P, nc = self.P, self.nc

        def f1():
            return nc.sync.dma_start(out=self.pp.ap, in_=self.ppack)

        P.dma("sp", f1, self.pp.b, writes=[self.pp.b])

        def f2():
            return nc.gpsimd.dma_start(out=self.cb.ap, in_=self.cpack)

        P.dma("pool", f2, self.cb.b, writes=[self.cb.b])

    def x_src(self, first):
        return (self.xT, self.xT_b) if first else (self.xs, self.xs_b)

    def load_x(self, dst, src, srcb, t0, n):
        nc = self.nc
        d3 = dst.ap.rearrange("p (k t) -> p k t", k=KC)

        def fn():
            return nc.sync.dma_start(out=d3, in_=src.rearrange("k p t -> p k t")[:, :, t0:t0 + n])

        self.P.dma("sp", fn, dst.b, reads=[srcb], writes=[dst.b])

    def store_x(self, srcr, dst, dstb, t0, n):
        nc = self.nc
        s3 = srcr.ap.rearrange("p (k t) -> p k t", k=KC)

        def fn():
            return nc.sync.dma_start(out=dst.rearrange("k p t -> p k t")[:, :, t0:t0 + n], in_=s3)

        self.P.dma("sp", fn, dstb, reads=[srcr.b], writes=[dstb])

    def rmsnorm(self, xr, n, gain, hr, hoff=0, hstride=None):
        P, nc = self.P, self.nc
        hstride = hstride or n
        x3 = xr.ap.rearrange("p (k t) -> p k t", k=KC)
        h3 = hr.ap.rearrange("p (k t) -> p k t", k=KC)
        m = self.mark()
        sq = [self.alloc("sq%d" % i, TT, BF16) for i in range(3)]
        rs = self.alloc("rs", TT, F32)
        rs2 = self.alloc("rs2", TT, F32)
        ntt = (n + TT - 1) // TT
        for tt in range(ntt):
            w = min(TT, n - tt * TT)
            sl = slice(tt * TT, tt * TT + w)
            for kc in range(KC):
                s = sq[kc % 3]
                P.op("act", functools.partial(nc.scalar.activation,
                    out=s.ap[:, 0:w], in_=x3[:, kc, sl], func=AF.Square), reads=[xr.b], writes=[s.b])
                P.op("pe", functools.partial(nc.tensor.matmul,
                    self.psS[:, 0:w], lhsT=self.ones, rhs=s.ap[:, 0:w], start=(kc == 0), stop=(kc == KC - 1)),
                    reads=[s.b, self.cb.b], writes=[self.psS_r.b])
            P.op("dve", functools.partial(nc.vector.tensor_scalar,
                out=rs.ap[:, 0:w], in0=self.psS[:, 0:w], scalar1=1.0 / D, scalar2=RMS_EPS,
                op0=ALU.mult, op1=ALU.add), reads=[self.psS_r.b], writes=[rs.b])
            P.op("act", functools.partial(nc.scalar.activation, out=rs2.ap[:, 0:w], in_=rs.ap[:, 0:w], func=AF.Sqrt),
                 reads=[rs.b], writes=[rs2.b])
            P.op("dve", functools.partial(nc.vector.reciprocal, out=rs.ap[:, 0:w], in_=rs2.ap[:, 0:w]),
                 reads=[rs2.b], writes=[rs.b])
            for kc in range(KC):
                P.op("dve", functools.partial(nc.vector.scalar_tensor_tensor,
                    out=h3[:, kc, hoff + sl.start:hoff + sl.start + w], in0=x3[:, kc, sl],
                    scalar=gain[:, kc:kc + 1], in1=rs.ap[:, 0:w], op0=ALU.mult, op1=ALU.mult),
                    reads=[xr.b, rs.b, self.pp.b], writes=[hr.b])
        self.release(m)

    def mlp_phase(self, L, blk, first):
        P, nc = self.P, self.nc
        t0 = blk * TB
        m = self.mark()
        xb = self.alloc("xblk", KC * TB, F32)
        h = self.alloc("h", KC * TB, BF16)
        hh = [self.alloc("hh%d" % i, 4 * TB, BF16) for i in range(2)]
        tmp = [self.alloc("rl%d" % i, TT, F32) for i in range(2)]
        src, srcb = self.x_src(first)
        self.load_x(xb, src, srcb, t0, TB)
        gain = self.dvec(("norm_ffn", L), lambda inp, L=L: inp["norm_ffn"][L])
        self.rmsnorm(xb, TB, gain, h)
        x3 = xb.ap.rearrange("p (k t) -> p k t", k=KC)
        h3 = h.ap.rearrange("p (k t) -> p k t", k=KC)
        NG = DFF // 512
        ntt = TB // TT
        nrl = [0]

        def up(g):
            wt = self.wtile(("up", L, g), 128, KC * 512,
                            lambda inp, L=L, g=g: inp["mlp_w_up"][L][:, g * 512:(g + 1) * 512]
                            .reshape(KC, 128, 512).transpose(1, 0, 2).reshape(128, KC * 512))
            w3 = wt.ap.rearrange("p (k m) -> p k m", k=KC)
            hg = hh[g % 2]
            hg3 = hg.ap.rearrange("p (j t) -> p j t", j=4)
            for j in range(4):
                for tt in range(ntt):
                    ps = self.psa()
                    for kc in range(KC):
                        P.op("pe", functools.partial(nc.tensor.matmul,
                            ps.ap, lhsT=w3[:, kc, j * 128:(j + 1) * 128], rhs=h3[:, kc, tt * TT:(tt + 1) * TT],
                            start=(kc == 0), stop=(kc == KC - 1)), reads=[wt.b, h.b], writes=[ps.b])
                    tm = tmp[nrl[0] % 2]
                    nrl[0] += 1
                    P.op("act", functools.partial(nc.scalar.activation, out=tm.ap, in_=ps.ap, func=AF.Relu),
                         reads=[ps.b], writes=[tm.b])
                    P.op("pool", functools.partial(nc.gpsimd.tensor_tensor,
                        out=hg3[:, j, tt * TT:(tt + 1) * TT], in0=tm.ap, in1=tm.ap, op=ALU.mult),
                        reads=[tm.b], writes=[hg.b])

        def down(g):
            wt = self.wtile(("dn", L, g), 128, 4 * D,
                            lambda inp, L=L, g=g: inp["mlp_w_down"][L][g * 512:(g + 1) * 512, :]
                            .reshape(4, 128, D).transpose(1, 0, 2).reshape(128, 4 * D))
            w3 = wt.ap.rearrange("p (j d) -> p j d", j=4)
            hg = hh[g % 2]
            hg3 = hg.ap.rearrange("p (j t) -> p j t", j=4)
            for dc in range(KC):
                for tt in range(ntt):
                    ps = self.psb()
                    for j in range(4):
                        P.op("pe", functools.partial(nc.tensor.matmul,
                            ps.ap, lhsT=w3[:, j, dc * 128:(dc + 1) * 128], rhs=hg3[:, j, tt * TT:(tt + 1) * TT],
                            start=(j == 0), stop=(j == 3)), reads=[wt.b, hg.b], writes=[ps.b])
                    P.op("dve", functools.partial(nc.vector.tensor_tensor,
                        out=x3[:, dc, tt * TT:(tt + 1) * TT], in0=ps.ap, in1=x3[:, dc, tt * TT:(tt + 1) * TT],
                        op=ALU.add), reads=[ps.b, xb.b], writes=[xb.b])

        up(0)
        for g in range(NG):
            if g + 1 < NG:
                up(g + 1)
            down(g)
        return xb, m

    def finish_block(self, xb, m, blk, last):
        P, nc = self.P, self.nc
        t0 = blk * TB
        if last and self.final_norm:
            gain = self.dvec(("norm_final",), lambda inp: inp["norm_final"])
            self.final_rms(xb, gain, t0)
        elif last:
            self.store_x(xb, self.yT, self.yT_b, t0, TB)
        else:
            self.store_x(xb, self.xs, self.xs_b, t0, TB)
        self.release(m)

    def final_rms(self, xb, gain, t0):
        P, nc = self.P, self.nc
        x3 = xb.ap.rearrange("p (k t) -> p k t", k=KC)
        m = self.mark()
        sq = [self.alloc("fsq%d" % i, TT, BF16) for i in range(3)]
        rs = self.alloc("frs", TT, F32)
        rs2 = self.alloc("frs2", TT, F32)
        for tt in range(TB // TT):
            sl = slice(tt * TT, (tt + 1) * TT)
            for kc in range(KC):
                s = sq[kc % 3]
                P.op("act", functools.partial(nc.scalar.activation,
                    out=s.ap, in_=x3[:, kc, sl], func=AF.Square), reads=[xb.b], writes=[s.b])
                P.op("pe", functools.partial(nc.tensor.matmul,
                    self.psS[:, :], lhsT=self.ones, rhs=s.ap, start=(kc == 0), stop=(kc == KC - 1)),
                    reads=[s.b, self.cb.b], writes=[self.psS_r.b])
            P.op("dve", functools.partial(nc.vector.tensor_scalar,
                out=rs.ap, in0=self.psS[:, :], scalar1=1.0 / D, scalar2=RMS_EPS,
                op0=ALU.mult, op1=ALU.add), reads=[self.psS_r.b], writes=[rs.b])
            P.op("act", functools.partial(nc.scalar.activation, out=rs2.ap, in_=rs.ap, func=AF.Sqrt),
                 reads=[rs.b], writes=[rs2.b])
            P.op("dve", functools.partial(nc.vector.reciprocal, out=rs.ap, in_=rs2.ap), reads=[rs2.b], writes=[rs.b])
            for kc in range(KC):
                P.op("dve", functools.partial(nc.vector.scalar_tensor_tensor,
                    out=x3[:, kc, sl], in0=x3[:, kc, sl], scalar=gain[:, kc:kc + 1], in1=rs.ap,
                    op0=ALU.mult, op1=ALU.mult), reads=[xb.b, rs.b, self.pp.b], writes=[xb.b])
        self.store_x(xb, self.yT, self.yT_b, t0, TB)
        self.release(m)

    def build(self):
        self.setup_consts()
        self.load_consts()
        self.setup_persist()
        first_layer = True
        for li, spec in enumerate(self.layers):
            last_layer = (li == len(self.layers) - 1)
            mlp_only = isinstance(spec, tuple)
            L = spec[1] if mlp_only else spec
            kind = L % 3
            if self.pair_mode and not mlp_only and kind == 0:
                self.rwkv_layer_pair(L, first_layer)
            for blk in range(self.nblk):
                if mlp_only:
                    pass
                elif kind == 0:
                    if not self.pair_mode:
                        self.rwkv_phase(L, blk, first_layer)
                elif kind == 1:
                    self.swa_phase(L, blk, first_layer)
                elif kind == 2:
                    self.conv_phase(L, blk, first_layer)
                xb, m = self.mlp_phase(L, blk, first=(first_layer and mlp_only))
                self.finish_block(xb, m, blk, last_layer)
            if self.pair_mode and not last_layer:
                nc_, P_ = self.nc, self.P
                nxt = self.layers[li + 1]
                nxt = (nxt[1] if isinstance(nxt, tuple) else nxt) % 3
                if nxt == 0:
                    for kc in range(KC):
                        P_.dma("pool", functools.partial(nc_.gpsimd.collective_compute, "AllGather", ALU.bypass,
                                                         replica_groups=self.groups,
                                                         ins=[self.xs[kc]], outs=[self.xg3[kc * 256:(kc + 1) * 256, :]]),
                               self.xg3_b, reads=[self.xs_b], writes=[self.xg3_b], inc=1)
                else:
                    P_.dma("sp", functools.partial(nc_.sync.dma_start,
                                                   out=self.tail_send.rearrange("(k p) t -> k p t", k=KC),
                                                   in_=self.xs[:, :, self.tok - 128:self.tok]),
                           self.tail_send_b, reads=[self.xs_b], writes=[self.tail_send_b])
                    P_.dma("pool", functools.partial(nc_.gpsimd.collective_compute, "AllGather", ALU.bypass,
                                                     replica_groups=self.groups,
                                                     ins=[self.tail_send], outs=[self.tail_recv]),
                           self.tail_recv_b, reads=[self.tail_send_b], writes=[self.tail_recv_b], inc=1)
            first_layer = False
        nc = self.nc
        self.wpack = nc.dram_tensor("wpack", [max(self.woff, 128)], F32, kind="ExternalInput").ap()
        self.ppack = nc.dram_tensor("ppack", [128, self.NPCOL], F32, kind="ExternalInput").ap()
        self.cpack = nc.dram_tensor("cpack", [128, self.NCCOL], F32, kind="ExternalInput").ap()
        self.P.finalize(hoist_dist=2)
        self.built = True
        return nc

    def pack(self, inp, core=0):
        inp = dict(inp)
        inp["_core"] = core
        wp = np.zeros(max(self.woff, 128), np.float32)
        for off, npart, nelem, g in self.wgetters:
            a = np.asarray(g(inp), np.float32)
            assert a.shape == (npart, nelem), (a.shape, npart, nelem)
            wp[off:off + npart * nelem] = a.reshape(-1)
        pp = np.zeros((128, self.NPCOL), np.float32)
        for c, n, g in self.pgetters:
            a = np.asarray(g(inp), np.float32)
            assert a.shape == (128, n), (a.shape, n)
            pp[:, c:c + n] = a
        cp = np.zeros((128, self.NCCOL), np.float32)
        for c, n, a in self.cgetters:
            cp[:, c:c + n] = a(core) if callable(a) else a
        return wp, pp, cp

    def out_proj(self, key, L, z, n, t0, wget, first, bias=None):
        P, nc = self.P, self.nc
        z3 = z.ap.rearrange("p (k t) -> p k t", k=KC)
        src, srcb = self.x_src(first)
        m = self.mark()
        st = [self.alloc("ost%d" % i, TT, F32) for i in range(4)]
        ntt = n // TT
        k = 0
        for g in range(4):
            wt = self.wtile((key, L, g), 128, KC * 512,
                            lambda inp, g=g: wget(inp)[:, g * 512:(g + 1) * 512]
                            .reshape(KC, 128, 512).transpose(1, 0, 2).reshape(128, KC * 512))
            w3 = wt.ap.rearrange("p (k m) -> p k m", k=KC)
            for j in range(4):
                mc = g * 4 + j
                for tt in range(ntt):
                    s_ = st[k % 4]
                    k += 1
                    sl = slice(t0 + tt * TT, t0 + (tt + 1) * TT)
                    P.dma("sp", functools.partial(nc.sync.dma_start, out=s_.ap, in_=src[mc, :, sl]),
                          s_.b, reads=[srcb], writes=[s_.b])
                    ps = self.psa()
                    for kc in range(KC):
                        P.op("pe", functools.partial(nc.tensor.matmul,
                            ps.ap, lhsT=w3[:, kc, j * 128:(j + 1) * 128], rhs=z3[:, kc, tt * TT:(tt + 1) * TT],
                            start=(kc == 0), stop=(kc == KC - 1)), reads=[wt.b, z.b], writes=[ps.b])
                    if bias is None:
                        P.op("dve", functools.partial(nc.vector.tensor_tensor,
                            out=s_.ap, in0=ps.ap, in1=s_.ap, op=ALU.add), reads=[ps.b, s_.b], writes=[s_.b])
                    else:
                        P.op("dve", functools.partial(nc.vector.scalar_tensor_tensor,
                            out=s_.ap, in0=ps.ap, scalar=bias[:, mc:mc + 1], in1=s_.ap, op0=ALU.add, op1=ALU.add),
                            reads=[ps.b, s_.b, self.pp.b], writes=[s_.b])
                    P.dma("sp", functools.partial(nc.sync.dma_start, out=self.xs[mc, :, sl], in_=s_.ap),
                          self.xs_b, reads=[s_.b], writes=[self.xs_b])
        self.release(m)

    def norm_block(self, L, key, t0, n, first, h, hoff=0, hstride=None, src=None, srcb=None):
        gain = self.dvec((key, L), lambda inp, L=L, key=key: inp[key][L])
        if src is None:
            src, srcb = self.x_src(first)
        for s0 in range(0, n, TT):
            w = min(TT, n - s0)
            m = self.mark()
            xb = self.alloc("xnb", KC * w, F32)
            self.load_x(xb, src, srcb, t0 + s0, w)
            self.rmsnorm(xb, w, gain, h, hoff=hoff + s0, hstride=hstride)
            self.release(m)

    def conv_phase(self, L, blk, first):
        P, nc = self.P, self.nc
        t0 = blk * TB
        ic = L // 3
        HAL = 2 if (self.pair_mode and blk == 0) else 0
        m = self.mark()
        h = self.alloc("h", KC * (HAL + TB), BF16)
        self.norm_block(L, "norm_mix", t0, TB, first, h, hoff=HAL, hstride=HAL + TB)
        if HAL:
            tsrc = self.tail_recv.rearrange("(r k p) t -> r k p t", r=2, k=KC)[0][:, :, 126:128]
            self.norm_block(L, "norm_mix", 0, HAL, first, h, hoff=0, hstride=HAL + TB, src=tsrc, srcb=self.tail_recv_b)
            m01 = self.param(("mask01",), 1, lambda inp: np.full((128, 1), float(inp["_core"] % 2), np.float32))
        z = self.alloc("z", KC * TB, BF16)
        h3 = h.ap.rearrange("p (k t) -> p k t", k=KC)
        z3 = z.ap.rearrange("p (k t) -> p k t", k=KC)
        u = [self.alloc("u%d" % i, TB + 2, F32) for i in range(2)]
        uc = [self.alloc("uc%d" % i, TB, F32) for i in range(2)]
        tcg = [self.alloc("tcg%d" % i, TB + 2, F32) for i in range(2)]
        cw = [self.dvec(("conv_w", ic, tap), lambda inp, ic=ic, tap=tap: inp["conv_w"][ic][tap]) for tap in range(3)]
        uh3 = self.uhalo.ap.rearrange("p (k t) -> p k t", k=KC)
        ntt = TB // TT
        for j in range(KC):
            def getter(inp, j=j, ic=ic):
                W = inp["conv_w_in"][ic]
                cols = np.concatenate([W[:, D + j * 128:D + (j + 1) * 128], W[:, 2 * D + j * 128:2 * D + (j + 1) * 128],
                                       W[:, j * 128:(j + 1) * 128]], axis=1)
                return cols.reshape(KC, 128, 384).transpose(1, 0, 2).reshape(128, KC * 384)
            wt = self.wtile(("cin", L, j), 128, KC * 384, getter)
            w3 = wt.ap.rearrange("p (k m) -> p k m", k=KC)
            uj, ucj, tj = u[j % 2], uc[j % 2], tcg[j % 2]
            if not HAL:
                P.op("pool", functools.partial(nc.gpsimd.tensor_copy, out=uj.ap[:, 0:2], in_=uh3[:, j, :]),
                     reads=[self.uhalo.b], writes=[uj.b])
            for part in range(3):
                if part == 2:
                    P.op("pool", functools.partial(nc.gpsimd.tensor_scalar,
                        out=ucj.ap, in0=uj.ap[:, 2:2 + TB], scalar1=cw[2][:, j:j + 1], scalar2=None, op0=ALU.mult),
                        reads=[uj.b, self.pp.b], writes=[ucj.b])
                    P.op("dve", functools.partial(nc.vector.scalar_tensor_tensor,
                        out=ucj.ap, in0=uj.ap[:, 1:1 + TB], scalar=cw[1][:, j:j + 1], in1=ucj.ap,
                        op0=ALU.mult, op1=ALU.add), reads=[uj.b, ucj.b, self.pp.b], writes=[ucj.b])
                    P.op("dve", functools.partial(nc.vector.scalar_tensor_tensor,
                        out=ucj.ap, in0=uj.ap[:, 0:TB], scalar=cw[0][:, j:j + 1], in1=ucj.ap,
                        op0=ALU.mult, op1=ALU.add), reads=[uj.b, ucj.b, self.pp.b], writes=[ucj.b])
                    P.op("pool", functools.partial(nc.gpsimd.tensor_copy, out=uh3[:, j, :], in_=uj.ap[:, TB:TB + 2]),
                         reads=[uj.b], writes=[self.uhalo.b])
                tiles = [(HAL + tt * TT, TT) for tt in range(ntt)]
                if HAL and part < 2:
                    tiles = [(0, HAL)] + tiles
                for (c0, cw_) in tiles:
                    ps = self.psa()
                    for kc in range(KC):
                        P.op("pe", functools.partial(nc.tensor.matmul,
                            ps.ap[:, 0:cw_], lhsT=w3[:, kc, part * 128:(part + 1) * 128], rhs=h3[:, kc, c0:c0 + cw_],
                            start=(kc == 0), stop=(kc == KC - 1)), reads=[wt.b, h.b], writes=[ps.b])
                    r0 = c0 - HAL
                    if part == 0:
                        P.op("act", functools.partial(nc.scalar.copy, out=tj.ap[:, 2 + r0:2 + r0 + cw_], in_=ps.ap[:, 0:cw_]),
                             reads=[ps.b], writes=[tj.b])
                    elif part == 1:
                        if r0 < 0:
                            P.op("dve", functools.partial(nc.vector.scalar_tensor_tensor,
                                out=uj.ap[:, 0:2], in0=ps.ap[:, 0:2], scalar=m01[:, 0:1], in1=tj.ap[:, 0:2],
                                op0=ALU.mult, op1=ALU.mult), reads=[ps.b, tj.b, self.pp.b], writes=[uj.b])
                        else:
                            P.op("dve", functools.partial(nc.vector.tensor_tensor,
                                out=uj.ap[:, 2 + r0:2 + r0 + cw_], in0=ps.ap[:, 0:cw_], in1=tj.ap[:, 2 + r0:2 + r0 + cw_],
                                op=ALU.mult), reads=[ps.b, tj.b], writes=[uj.b])
                    else:
                        P.op("dve", functools.partial(nc.vector.tensor_tensor,
                            out=z3[:, j, r0:r0 + cw_], in0=ps.ap[:, 0:cw_], in1=ucj.ap[:, r0:r0 + cw_], op=ALU.mult),
                            reads=[ps.b, ucj.b], writes=[z.b])
        self.out_proj("cout", L, z, TB, t0, lambda inp, ic=ic: inp["conv_w_out"][ic], first)
        self.release(m)

    def setup_persist(self):
        P, nc = self.P, self.nc
        kinds = set((l[1] if isinstance(l, tuple) else l) % 3 for l in self.layers if not isinstance(l, tuple))
        if 2 in kinds:
            self.uhalo = self.alloc("uhalo", KC * 2, F32)
            P.op("pool", functools.partial(nc.gpsimd.memset, self.uhalo.ap, 0.0), writes=[self.uhalo.b])
        if 0 in kinds:
            self.hlast = self.alloc("hlast", KC, BF16)

    def swa_phase(self, L, blk, first):
        P, nc = self.P, self.nc
        t0 = blk * TB
        ib = L // 3
        NQB = TB // 128
        if not hasattr(self, "swa_k_st"):
            self.swa_k_st = nc.dram_tensor("swa_k_st", [128, 4 * 128], BF16).ap()
            self.swa_v_st = nc.dram_tensor("swa_v_st", [128, 8 * 128], BF16).ap()
            self.swa_st_b = Buf("swa_st")
            NEG = -30000.0
            qi = np.arange(128)[:, None]
            kj = np.arange(256)[None, :]
            ok = (kj > qi) & (kj <= qi + 128)
            self.c_mask = self.const(256, np.where(ok, 0.0, NEG))
            m_all = np.where(ok, 0.0, NEG)
            m_first = np.where(ok & (kj >= 128), 0.0, NEG)
            if self.pair_mode:
                self.c_mask0 = self.const(256, lambda core: m_first if core % 2 == 0 else m_all)
            else:
                self.c_mask0 = self.const(256, m_first)
        Wq = lambda inp: inp["swa_w_qkv"][ib]
        bq = lambda inp: inp["swa_b_qkv"][ib]
        b_q = self.param(("swa_bq", ib), KC, lambda inp: np.ascontiguousarray(bq(inp)[:D].reshape(KC, 128).T))
        b_k = self.param(("swa_bk", ib), 4, lambda inp: np.stack(
            [np.concatenate([bq(inp)[D + j * 64:D + (j + 1) * 64]] * 2) for j in range(4)], axis=1))
        b_v = self.param(("swa_bv", ib), 2, lambda inp: np.ascontiguousarray(bq(inp)[D + 256:D + 512].reshape(2, 128).T))
        b_o = self.dvec(("swa_bo", ib), lambda inp: inp["swa_b_o"][ib])
        sink = self.param(("swa_sink", ib), NH, lambda inp: np.tile(inp["swa_sinks"][ib][None, :], (128, 1)))
        m = self.mark()
        q_all = self.alloc("q_all", KC * TB, BF16)
        kbuf = self.alloc("kbuf", 4 * (128 + TB), BF16)
        vtp = self.alloc("vtp", (NQB + 1) * 8 * 128, BF16)
        q3 = q_all.ap.rearrange("p (k t) -> p k t", k=KC)
        k3 = kbuf.ap.rearrange("p (j t) -> p j t", j=4)
        v5 = vtp.ap.rearrange("p (b j v d) -> p b j v d", b=NQB + 1, j=4, v=2)
        HAL = 128 if (self.pair_mode and blk == 0) else 0
        if blk == 0:
            P.op("dve", functools.partial(nc.vector.memset, vtp.ap, 0.0), writes=[vtp.b])
            if not HAL:
                P.op("dve", functools.partial(nc.vector.memset, k3[:, :, 0:128], 0.0), writes=[kbuf.b])
        else:
            P.op("dve", functools.partial(nc.vector.memset, vtp.ap[:, 8 * 128:], 0.0), writes=[vtp.b])
            P.dma("sp", functools.partial(nc.sync.dma_start, out=vtp.ap[:, 0:8 * 128], in_=self.swa_v_st), vtp.b,
                  reads=[self.swa_st_b], writes=[vtp.b])
            P.dma("sp", functools.partial(nc.sync.dma_start, out=k3[:, :, 0:128],
                                                  in_=self.swa_k_st.rearrange("p (j t) -> p j t", j=4)), kbuf.b,
                  reads=[self.swa_st_b], writes=[kbuf.b])
        m2 = self.mark()
        h = self.alloc("h", KC * (HAL + TB), BF16)
        self.norm_block(L, "norm_mix", t0, TB, first, h, hoff=HAL, hstride=HAL + TB)
        if HAL:
            tsrc = self.tail_recv.rearrange("(r k p) t -> r k p t", r=2, k=KC)[0]
            self.norm_block(L, "norm_mix", 0, HAL, first, h, hoff=0, hstride=HAL + TB, src=tsrc, srcb=self.tail_recv_b)
        vfm = self.alloc("vfm", 2 * (128 + TB), BF16)
        h3 = h.ap.rearrange("p (k t) -> p k t", k=KC)
        vf3 = vfm.ap.rearrange("p (c t) -> p c t", c=2)
        ntt = TB // TT
        main_tiles = [(HAL + tt * TT, TT) for tt in range(ntt)]
        kv_tiles = ([(0, HAL)] if HAL else []) + main_tiles

        def proj(key, ncols, getter, epi, tiles):
            wt = self.wtile((key, L), 128, KC * ncols,
                            lambda inp: getter(inp).reshape(KC, 128, ncols).transpose(1, 0, 2).reshape(128, KC * ncols))
            w3 = wt.ap.rearrange("p (k m) -> p k m", k=KC)
            for j in range(ncols // 128):
                for (c0, cw) in tiles:
                    ps = self.psa()
                    for kc in range(KC):
                        P.op("pe", functools.partial(nc.tensor.matmul,
                            ps.ap[:, 0:cw], lhsT=w3[:, kc, j * 128:(j + 1) * 128], rhs=h3[:, kc, c0:c0 + cw],
                            start=(kc == 0), stop=(kc == KC - 1)), reads=[wt.b, h.b], writes=[ps.b])
                    epi(j, c0 - HAL, cw, ps)

        for g in range(4):
            def epi_q(j, c0, cw, ps, g=g):
                mc = g * 4 + j
                P.op("dve", functools.partial(nc.vector.tensor_scalar,
                    out=q3[:, mc, c0:c0 + cw], in0=ps.ap[:, 0:cw], scalar1=b_q[:, mc:mc + 1], scalar2=HD ** -0.5,
                    op0=ALU.add, op1=ALU.mult), reads=[ps.b, self.pp.b], writes=[q_all.b])
            proj(("swa_q", g), 512, lambda inp, g=g: Wq(inp)[:, g * 512:(g + 1) * 512], epi_q, main_tiles)

        def epi_k(j, c0, cw, ps):
            P.op("dve", functools.partial(nc.vector.tensor_scalar,
                out=k3[:, j, 128 + c0:128 + c0 + cw], in0=ps.ap[:, 0:cw], scalar1=b_k[:, j:j + 1], scalar2=None,
                op0=ALU.add), reads=[ps.b, self.pp.b], writes=[kbuf.b])
        proj("swa_k", 512, lambda inp: np.concatenate(
            [Wq(inp)[:, D + (j // 2) * 64:D + (j // 2 + 1) * 64] for j in range(8)], axis=1), epi_k, kv_tiles)

        def epi_v(j, c0, cw, ps):
            P.op("dve", functools.partial(nc.vector.tensor_scalar,
                out=vf3[:, j, 128 + c0:128 + c0 + cw], in0=ps.ap[:, 0:cw], scalar1=b_v[:, j:j + 1], scalar2=None,
                op0=ALU.add), reads=[ps.b, self.pp.b], writes=[vfm.b])
        proj("swa_v", 256, lambda inp: Wq(inp)[:, D + 256:D + 512], epi_v, kv_tiles)
        for bb in range(-1 if HAL else 0, NQB):
            for c in range(2):
                P.op("pe", functools.partial(nc.tensor.transpose,
                    self.psT[:, c * 128:(c + 1) * 128], vf3[:, c, 128 + bb * 128:128 + (bb + 1) * 128], self.ident),
                    reads=[vfm.b, self.cb.b], writes=[self.psT_r.b])
            src4 = self.psT[:, 0:256].rearrange("p (j d) -> p j d", j=4)
            P.op("act", functools.partial(nc.scalar.copy, out=v5[:, bb + 1, :, 0, 0:64], in_=src4),
                 reads=[self.psT_r.b], writes=[vtp.b])
            P.op("dve", functools.partial(nc.vector.tensor_copy, out=v5[:, bb + 1, :, 1, 64:128], in_=src4),
                 reads=[self.psT_r.b], writes=[vtp.b])
        self.release(m2)
        o_all = self.alloc("o_all", KC * TB, BF16)
        o3 = o_all.ap.rearrange("p (k t) -> p k t", k=KC)
        NR = 3
        lm = [self.alloc("lm%d" % i, 512, F32) for i in range(NR)]
        pe_ = [self.alloc("pe%d" % i, 512, F32) for i in range(NR)]
        pn = [self.alloc("pn%d" % i, 512, BF16) for i in range(NR)]
        pT = [self.alloc("pT%d" % i, 512, BF16) for i in range(NR)]
        sm = [self.alloc("sm%d" % i, 16, F32) for i in range(NR)]
        it = 0
        for c in range(KC):
            kvh = c // 4
            for bb in range(NQB):
                i = it % NR
                it += 1
                lmr, per, pnr, pTr, smr = lm[i], pe_[i], pn[i], pT[i], sm[i]
                if self.rotA % 2:
                    self.rotA += 1
                psl0 = self.psa()
                psl1 = self.psa()
                pbase = ((self.rotA - 2) % 4) * 512
                for hh, psl in ((0, psl0), (1, psl1)):
                    pr = slice(hh * 64, (hh + 1) * 64)
                    P.op("pe", functools.partial(nc.tensor.matmul,
                        psl.ap[:, 0:256], lhsT=q3[pr, c, bb * 128:(bb + 1) * 128],
                        rhs=k3[pr, kvh, bb * 128:bb * 128 + 256], start=True, stop=True),
                        reads=[q_all.b, kbuf.b], writes=[psl.b])
                mk = self.c_mask0 if (blk == 0 and bb == 0) else self.c_mask
                l3 = lmr.ap.rearrange("p (h k) -> p h k", h=2)
                pl3 = self.psA[:, pbase:pbase + 1024].rearrange("p (h k) -> p h k", h=2)[:, :, 0:256]
                P.op("dve", functools.partial(nc.vector.tensor_tensor,
                    out=l3, in0=pl3, in1=mk.unsqueeze(1).to_broadcast([128, 2, 256]), op=ALU.add),
                    reads=[psl0.b, psl1.b, self.cb.b], writes=[lmr.b])
                s_ = smr.ap
                P.op("dve", functools.partial(nc.vector.tensor_reduce,
                    out=s_[:, 0:2], in_=l3, axis=AX.X, op=ALU.max), reads=[lmr.b], writes=[smr.b])
                P.op("dve", functools.partial(nc.vector.tensor_tensor,
                    out=s_[:, 2:4], in0=s_[:, 0:2], in1=sink[:, 2 * c:2 * c + 2], op=ALU.max),
                    reads=[smr.b, self.pp.b], writes=[smr.b])
                P.op("dve", functools.partial(nc.vector.tensor_scalar,
                    out=s_[:, 4:6], in0=s_[:, 2:4], scalar1=-1.0, scalar2=None, op0=ALU.mult),
                    reads=[smr.b], writes=[smr.b])
                p3 = per.ap.rearrange("p (h k) -> p h k", h=2)
                for hh in range(2):
                    P.op("act", functools.partial(nc.scalar.activation,
                        out=p3[:, hh, :], in_=l3[:, hh, :], func=AF.Exp, bias=s_[:, 4 + hh:5 + hh], scale=1.0,
                        accum_out=s_[:, 6 + hh:7 + hh]), reads=[lmr.b, smr.b], writes=[per.b, smr.b])
                P.op("dve", functools.partial(nc.vector.tensor_tensor,
                    out=s_[:, 8:10], in0=s_[:, 4:6], in1=sink[:, 2 * c:2 * c + 2], op=ALU.add),
                    reads=[smr.b, self.pp.b], writes=[smr.b])
                P.op("act", functools.partial(nc.scalar.activation, out=s_[:, 10:12], in_=s_[:, 8:10], func=AF.Exp),
                     reads=[smr.b], writes=[smr.b])
                P.op("dve", functools.partial(nc.vector.tensor_tensor,
                    out=s_[:, 12:14], in0=s_[:, 10:12], in1=s_[:, 6:8], op=ALU.add),
                    reads=[smr.b], writes=[smr.b])
                P.op("dve", functools.partial(nc.vector.reciprocal, out=s_[:, 14:16], in_=s_[:, 12:14]),
                     reads=[smr.b], writes=[smr.b])
                pn3 = pnr.ap.rearrange("p (h k) -> p h k", h=2)
                P.op("dve", functools.partial(nc.vector.tensor_tensor,
                    out=pn3, in0=p3, in1=s_[:, 14:16].unsqueeze(2).to_broadcast([128, 2, 256]), op=ALU.mult),
                    reads=[per.b, smr.b], writes=[pnr.b])
                for hh in range(2):
                    for kb in range(2):
                        P.op("pe", functools.partial(nc.tensor.transpose,
                            self.psT[:, (hh * 2 + kb) * 128:(hh * 2 + kb + 1) * 128],
                            pn3[:, hh, kb * 128:(kb + 1) * 128], self.ident),
                            reads=[pnr.b, self.cb.b], writes=[self.psT_r.b])
                P.op("act", functools.partial(nc.scalar.copy, out=pTr.ap, in_=self.psT[:, 0:512]),
                     reads=[self.psT_r.b], writes=[pTr.b])
                pso = self.psb()
                n_ = 0
                for hh in range(2):
                    for kb in range(2):
                        P.op("pe", functools.partial(nc.tensor.matmul,
                            pso.ap[:, 0:128], lhsT=v5[:, bb + kb, kvh, hh, :],
                            rhs=pTr.ap[:, (hh * 2 + kb) * 128:(hh * 2 + kb + 1) * 128],
                            start=(n_ == 0), stop=(n_ == 3)), reads=[vtp.b, pTr.b], writes=[pso.b])
                        n_ += 1
                P.op("act", functools.partial(nc.scalar.copy,
                    out=o3[:, c, bb * 128:(bb + 1) * 128], in_=pso.ap[:, 0:128]), reads=[pso.b], writes=[o_all.b])
        if blk + 1 < self.nblk:
            P.dma("sp", functools.partial(nc.sync.dma_start, out=self.swa_v_st, in_=vtp.ap[:, NQB * 8 * 128:]), self.swa_st_b,
                  reads=[vtp.b], writes=[self.swa_st_b])
            P.dma("sp", functools.partial(nc.sync.dma_start, out=self.swa_k_st.rearrange("p (j t) -> p j t", j=4),
                                                  in_=k3[:, :, TB:TB + 128]), self.swa_st_b,
                  reads=[kbuf.b], writes=[self.swa_st_b])
        self.out_proj("swa_o", L, o_all, TB, t0, lambda inp: inp["swa_w_o"][ib], first, bias=b_o)
        self.release(m)

    def rwkv_phase(self, L, blk, first):
        T2 = 512
        for sub in range(TB // T2):
            self.rwkv_sub(L, blk * TB + sub * T2, T2, first, seq_start=(blk == 0 and sub == 0))

    def rwkv_layer_pair(self, L, first):
        P, nc = self.P, self.nc
        T2 = 512
        for sb in range(self.NSB):
            if first:
                src, srcb = self.xfullT[:, :, sb * T2:(sb + 1) * T2], self.xfull_b
            else:
                half, off = divmod(sb, self.NSB // 2)
                src = self.xg3.rearrange("(k r p) t -> r k p t", k=KC, r=2)[half][:, :, off * T2:(off + 1) * T2]
                srcb = self.xg3_b
            self.rwkv_sub(L, sb * T2, T2, first, seq_start=(sb == 0), xsrc=src, xsrcb=srcb, sb=sb)
        m01 = self.param(("mask01",), 1, lambda inp: np.full((128, 1), float(inp["_core"] % 2), np.float32))
        om01 = self.param(("omask01",), 1, lambda inp: np.full((128, 1), 1.0 - float(inp["_core"] % 2), np.float32))
        ygv = self.ygr.rearrange("(s r q p) t -> s r p q t", s=self.NSB, r=2, q=8)
        hsb = self.NSB // 2
        for blk in range(self.nblk):
            m = self.mark()
            cand = [self.alloc("ygc%d" % i, KC * TB, BF16) for i in range(2)]
            ygb = self.alloc("ygb", KC * TB, BF16)
            for hc in range(2):
                c3 = cand[hc].ap.rearrange("p (k t) -> p k t", k=KC)
                for r in range(2):
                    for s2 in range(TB // T2):
                        sbg = hc * hsb + blk * (TB // T2) + s2
                        P.dma("sp", functools.partial(nc.sync.dma_start, out=c3[:, r * 8:(r + 1) * 8, s2 * T2:(s2 + 1) * T2],
                                                      in_=ygv[sbg][r]), cand[hc].b, reads=[self.ygr_b], writes=[cand[hc].b])
            P.op("dve", functools.partial(nc.vector.tensor_scalar, out=cand[0].ap, in0=cand[0].ap, scalar1=om01[:, 0:1],
                                          scalar2=None, op0=ALU.mult), reads=[cand[0].b, self.pp.b], writes=[cand[0].b])
            P.op("dve", functools.partial(nc.vector.scalar_tensor_tensor, out=ygb.ap, in0=cand[1].ap, scalar=m01[:, 0:1],
                                          in1=cand[0].ap, op0=ALU.mult, op1=ALU.add),
                 reads=[cand[0].b, cand[1].b, self.pp.b], writes=[ygb.b])
            ia = L // 3
            self.out_proj("rwo", L, ygb, TB, blk * TB, lambda inp, ia=ia: inp["rwkv_w_o"][ia], first)
            self.release(m)

    def rwkv_sub(self, L, t0, T2, first, seq_start, xsrc=None, xsrcb=None, sb=None):
        P, nc = self.P, self.nc
        ia = L // 3
        has_vres = ia > 0
        NCH = T2 // 64
        PAIRM = self.pair_mode
        NPAIR = 8 if PAIRM else KC
        LD = 0.6065306597126334
        if not hasattr(self, "zst"):
            self.zst = nc.dram_tensor("zst", [KC, 128, 128], F32).ap()
            self.zst_b = Buf("zst")
            self.vfirst = nc.dram_tensor("vfirst", [KC, 128, self.tok * (2 if PAIRM else 1)], F32).ap()
            self.vfirst_b = Buf("vfirst")
            si = np.arange(64)[:, None]
            tj = np.arange(64)[None, :]
            mab = np.zeros((128, 128))
            mab[:64, :64] = (si < tj)
            mab[:64, 64:] = (si <= tj)
            self.c_mab = self.const(128, mab)
            msl = np.zeros((128, 64))
            msl[:64, :] = (si > tj)
            self.c_msl = self.const(64, msl)
            bd = np.zeros((128, 128))
            bd[:64, :64] = 1.0 / 64
            bd[64:, 64:] = 1.0 / 64
            self.c_blkm = self.const(128, bd)

        def A(fn, reads, writes):
            P.op("act", fn, [x.b for x in reads], [x.b for x in writes])

        def V(fn, reads, writes):
            P.op("dve", fn, [x.b for x in reads], [x.b for x in writes])

        def G(fn, reads, writes):
            P.op("pool", fn, [x.b for x in reads], [x.b for x in writes])

        def M(fn, reads, writes):
            P.op("pe", fn, [x.b for x in reads], [x.b for x in writes])

        CB, PP = self.cb, self.pp
        if PAIRM:
            def pv(name, idx=ia):
                return self.param((name, idx, "half"), 8, lambda inp, name=name, idx=idx: np.ascontiguousarray(
                    np.asarray(inp[name][idx], np.float32).reshape(KC, 128).T[:, (inp["_core"] % 2) * 8:(inp["_core"] % 2) * 8 + 8]))
        else:
            pv = lambda name, idx=ia: self.dvec((name, idx), lambda inp, name=name, idx=idx: inp[name][idx].reshape(-1))
        mu = [self.dvec(("rwkv_mu", ia, i), lambda inp, i=i: inp["rwkv_mu"][ia][i]) for i in range(6)]
        w0c, a0c, kkc, kac, rkc, lwc, lbc = (pv("rwkv_w0"), pv("rwkv_a0"), pv("rwkv_k_k"), pv("rwkv_k_a"),
                                             pv("rwkv_r_k"), pv("rwkv_lnx_w"), pv("rwkv_lnx_b"))
        if has_vres:
            v0c = pv("rwkv_v0", ia - 1)
        m = self.mark()
        xr = self.alloc("xr", KC * T2, BF16)
        xk = self.alloc("xk", KC * T2, BF16)
        xv = self.alloc("xv", KC * T2, BF16)
        inw = self.alloc("inw", T2, BF16)
        ina = self.alloc("ina", T2, BF16)
        ing = self.alloc("ing", 2 * T2, BF16)
        inv = self.alloc("inv", T2, BF16) if has_vres else None
        yg = self.alloc("yg", NPAIR * T2, BF16)
        yg3 = yg.ap.rearrange("p (k t) -> p k t", k=NPAIR)
        xr3, xk3, xv3 = [x.ap.rearrange("p (k t) -> p k t", k=KC) for x in (xr, xk, xv)]
        m2 = self.mark()
        hb = self.alloc("hb", KC * (T2 + 1), BF16)
        h3 = hb.ap.rearrange("p (k t) -> p k t", k=KC)
        if seq_start:
            V(functools.partial(nc.vector.memset, h3[:, :, 0:1], 0.0), [], [hb])
        else:
            V(functools.partial(nc.vector.tensor_copy, out=h3[:, :, 0:1], in_=self.hlast.ap.unsqueeze(2)), [self.hlast], [hb])
        if xsrc is not None:
            self.norm_block(L, "norm_mix", 0, T2, first, hb, hoff=1, hstride=T2 + 1, src=xsrc, srcb=xsrcb)
        else:
            self.norm_block(L, "norm_mix", t0, T2, first, hb, hoff=1, hstride=T2 + 1)
        V(functools.partial(nc.vector.tensor_copy, out=self.hlast.ap.unsqueeze(2), in_=h3[:, :, T2:T2 + 1]), [hb], [self.hlast])
        dx = self.alloc("dx", KC * T2, BF16)
        dx3 = dx.ap.rearrange("p (k t) -> p k t", k=KC)
        V(functools.partial(nc.vector.tensor_tensor, out=dx3, in0=h3[:, :, 0:T2], in1=h3[:, :, 1:T2 + 1], op=ALU.subtract),
          [hb], [dx])

        def mix(i, outr, out3):
            for kc in range(KC):
                V(functools.partial(nc.vector.scalar_tensor_tensor,
                    out=out3[:, kc, :], in0=dx3[:, kc, :], scalar=mu[i][:, kc:kc + 1], in1=h3[:, kc, 1:T2 + 1],
                    op0=ALU.mult, op1=ALU.add), [dx, hb, PP], [outr])

        xm = self.alloc("xm", KC * T2, BF16)
        xm3 = xm.ap.rearrange("p (k t) -> p k t", k=KC)

        def lora1(mi, key, wname, widx, Rr, outr, func):
            mix(mi, xm, xm3)
            wt = self.wtile((key, L), 128, KC * Rr,
                            lambda inp: inp[wname][widx].reshape(KC, 128, Rr).transpose(1, 0, 2).reshape(128, KC * Rr))
            w3 = wt.ap.rearrange("p (k m) -> p k m", k=KC)
            for rc in range((Rr + 127) // 128):
                rr = min(128, Rr - rc * 128)
                ps = self.psa()
                for kc in range(KC):
                    M(functools.partial(nc.tensor.matmul,
                        ps.ap[0:rr, 0:T2], lhsT=w3[:, kc, rc * 128:rc * 128 + rr], rhs=xm3[:, kc, :],
                        start=(kc == 0), stop=(kc == KC - 1)), [wt, xm], [ps])
                A(functools.partial(nc.scalar.activation,
                    out=outr.ap[0:rr, rc * T2:(rc + 1) * T2], in_=ps.ap[0:rr, 0:T2], func=func), [ps], [outr])

        lora1(1, "rw1", "rwkv_w1", ia, 96, inw, AF.Tanh)
        lora1(4, "ra1", "rwkv_a1", ia, 96, ina, AF.Copy)
        lora1(5, "rg1", "rwkv_g1", ia, 256, ing, AF.Sigmoid)
        if has_vres:
            lora1(3, "rv1", "rwkv_v1", ia - 1, 64, inv, AF.Copy)
        mix(0, xr, xr3)
        mix(2, xk, xk3)
        mix(3, xv, xv3)
        self.release(m2)
        f32 = lambda name: self.alloc(name, T2, F32)
        m3 = self.mark()
        r32, k32, v32, sg, alr, kkn, kmod, cs, epos, bonus, t1, t2 = [
            f32(n) for n in ("r32", "k32", "v32", "sg", "alr", "kkn", "kmod", "cs", "epos", "bonus", "t1", "t2")]
        y32, eexc, eneg = r32, k32, v32
        b1 = self.alloc("b1", T2, BF16)
        vb = self.alloc("vb", T2, BF16)
        ar = self.alloc("ar", T2 * 2, BF16)
        bk = self.alloc("bk", T2 * 2, BF16)
        ar3 = ar.ap.rearrange("p (c x t) -> p c x t", c=NCH, x=2)
        bk3 = bk.ap.rearrange("p (c x t) -> p c x t", c=NCH, x=2)
        tT = self.alloc("tT", NCH * 4 * 128, BF16)
        tT4 = tT.ap[0:64].rearrange("p (c k d) -> p c k d", c=NCH, k=4)
        PMb = [self.alloc("PMb%d" % i, NCH * 128, BF16) for i in range(2)]
        PMk = [self.alloc("PMk%d" % i, NCH * 128, BF16) for i in range(2)]
        Lb = [self.alloc("Lb%d" % i, NCH * 64, BF16) for i in range(2)]
        Nb = [self.alloc("Nb%d" % i, NCH * 64, BF16) for i in range(2)]
        NI = self.alloc("NI", NCH * 64, BF16)
        Xb = [self.alloc("Xb%d" % i, NCH * 128, BF16) for i in range(2)]
        Wp = self.alloc("Wp", NCH * 128, BF16)
        U0p = self.alloc("U0p", NCH * 128, BF16)
        Wpad = [self.alloc("Wpad%d" % i, NCH * 128, BF16) for i in range(2)]
        U0pad = [self.alloc("U0pad%d" % i, NCH * 128, BF16) for i in range(2)]
        vTpad = [self.alloc("vTpad%d" % i, NCH * 128, BF16) for i in range(2)]
        GT = self.alloc("GT", NCH * 128, BF16)
        Hh = self.alloc("Hh", NCH * 128, F32)
        QT = self.alloc("QT", NCH * 64, BF16)
        Zall = self.alloc("Zall", (NCH + 1) * 128, BF16)
        Z32 = self.alloc("Z32", 128, F32)
        tz = self.alloc("tz", 128, F32)
        v3c = lambda r_, w=128: r_.ap[0:64].rearrange("p (c d) -> p c d", c=NCH)
        v3f = lambda r_: r_.ap.rearrange("p (c d) -> p c d", c=NCH)
        for z_ in Wpad + U0pad + vTpad + [GT]:
            V(functools.partial(nc.vector.memset, z_.ap, 0.0), [], [z_])
        V(functools.partial(nc.vector.memset, Hh.ap, 0.0), [], [Hh])
        psA01 = (self.psA_r[0], self.psA_r[1])
        psA23 = (self.psA_r[2], self.psA_r[3])
        pA01 = self.psA[:, 0:1024]
        pA23 = self.psA[:, 1024:2048]
        pB0, pB1 = self.psB_r[0], self.psB_r[1]
        blk64, blkm, ident, ones = self.blk64, self.c_blkm, self.ident, self.ones
        id64 = ident[0:64, 0:64]
        mab = self.c_mab[0:64, :]
        msl = self.c_msl[0:64, :]
        W_rkv = lambda inp: inp["rwkv_w_rkv"][ia]
        for p in range(NPAIR):
            def getter(inp, p=p):
                pg = (inp["_core"] % 2) * 8 + p if PAIRM else p
                cs_ = slice(pg * 128, (pg + 1) * 128)
                W = W_rkv(inp)
                main = np.concatenate([W[0][:, cs_], W[1][:, cs_], W[2][:, cs_]], axis=1)
                main = main.reshape(KC, 128, 384).transpose(1, 0, 2).reshape(128, KC * 384)
                ext = np.zeros((128, 640), np.float32)
                ext[:96, 0:128] = inp["rwkv_w2"][ia][:, cs_]
                ext[:96, 128:256] = inp["rwkv_a2"][ia][:, cs_]
                g2 = inp["rwkv_g2"][ia][:, cs_]
                ext[:, 256:384] = g2[0:128]
                ext[:, 384:512] = g2[128:256]
                if has_vres:
                    ext[:64, 512:640] = inp["rwkv_v2"][ia - 1][:, cs_]
                return np.concatenate([main, ext], axis=1)

            wt = self.wtile(("rkv", L, p), 128, KC * 384 + 640, getter)
            w3 = wt.ap[:, 0:KC * 384].rearrange("p (k m) -> p k m", k=KC)
            E0 = KC * 384
            w2p = wt.ap[0:96, E0:E0 + 128]
            a2p = wt.ap[0:96, E0 + 128:E0 + 256]
            g2p = [wt.ap[:, E0 + 256:E0 + 384], wt.ap[:, E0 + 384:E0 + 512]]
            v2p = wt.ap[0:64, E0 + 512:E0 + 640]
            for part, (xin, xin3, dst) in enumerate(((xr, xr3, r32), (xk, xk3, k32), (xv, xv3, v32))):
                ps = self.psa()
                for kc in range(KC):
                    M(functools.partial(nc.tensor.matmul,
                        ps.ap, lhsT=w3[:, kc, part * 128:(part + 1) * 128], rhs=xin3[:, kc, :],
                        start=(kc == 0), stop=(kc == KC - 1)), [wt, xin], [ps])
                A(functools.partial(nc.scalar.copy, out=dst.ap, in_=ps.ap), [ps], [dst])
            ps = self.psa()
            M(functools.partial(nc.tensor.matmul, ps.ap, lhsT=w2p, rhs=inw.ap[0:96, :], start=True, stop=True), [wt, inw], [ps])
            A(functools.partial(nc.scalar.activation, out=sg.ap, in_=ps.ap, func=AF.Sigmoid, bias=w0c[:, p:p + 1], scale=1.0),
              [ps, PP], [sg])
            ps = self.psa()
            M(functools.partial(nc.tensor.matmul, ps.ap, lhsT=a2p, rhs=ina.ap[0:96, :], start=True, stop=True), [wt, ina], [ps])
            A(functools.partial(nc.scalar.activation, out=alr.ap, in_=ps.ap, func=AF.Sigmoid, bias=a0c[:, p:p + 1], scale=1.0),
              [ps, PP], [alr])
            if has_vres:
                ps = self.psa()
                M(functools.partial(nc.tensor.matmul, ps.ap, lhsT=v2p, rhs=inv.ap[0:64, :], start=True, stop=True), [wt, inv], [ps])
                A(functools.partial(nc.scalar.activation, out=t1.ap, in_=ps.ap, func=AF.Sigmoid, bias=v0c[:, p:p + 1], scale=1.0),
                  [ps, PP], [t1])
                P.dma("sp", functools.partial(nc.sync.dma_start, out=t2.ap, in_=self.vfirst[p, :, t0:t0 + T2]), t2.b,
                      reads=[self.vfirst_b], writes=[t2.b])
                V(functools.partial(nc.vector.tensor_tensor, out=t2.ap, in0=t2.ap, in1=v32.ap, op=ALU.subtract), [t2, v32], [t2])
                V(functools.partial(nc.vector.tensor_tensor, out=t2.ap, in0=t2.ap, in1=t1.ap, op=ALU.mult), [t2, t1], [t2])
                V(functools.partial(nc.vector.tensor_tensor, out=v32.ap, in0=v32.ap, in1=t2.ap, op=ALU.add), [t2, v32], [v32])
            else:
                P.dma("sp", functools.partial(nc.sync.dma_start, out=self.vfirst[p, :, t0:t0 + T2], in_=v32.ap), self.vfirst_b,
                      reads=[v32.b], writes=[self.vfirst_b])
            V(functools.partial(nc.vector.tensor_scalar, out=t1.ap, in0=k32.ap, scalar1=kkc[:, p:p + 1], scalar2=None, op0=ALU.mult),
              [k32, PP], [t1])
            A(functools.partial(nc.scalar.activation, out=b1.ap, in_=t1.ap, func=AF.Square), [t1], [b1])
            M(functools.partial(nc.tensor.matmul, self.psS[:, :], lhsT=blk64, rhs=b1.ap, start=True, stop=True), [b1, CB], [self.psS_r])
            A(functools.partial(nc.scalar.activation, out=t2.ap, in_=self.psS[:, :], func=AF.Sqrt), [self.psS_r], [t2])
            V(functools.partial(nc.vector.tensor_scalar, out=t2.ap, in0=t2.ap, scalar1=1e-12, scalar2=None, op0=ALU.max), [t2], [t2])
            V(functools.partial(nc.vector.reciprocal, out=t2.ap, in_=t2.ap), [t2], [t2])
            V(functools.partial(nc.vector.tensor_tensor, out=kkn.ap, in0=t1.ap, in1=t2.ap, op=ALU.mult), [t1, t2], [kkn])
            V(functools.partial(nc.vector.tensor_scalar, out=t1.ap, in0=alr.ap, scalar1=-1.0, scalar2=kac[:, p:p + 1],
                                                  op0=ALU.add, op1=ALU.mult), [alr, PP], [t1])
            V(functools.partial(nc.vector.scalar_tensor_tensor, out=kmod.ap, in0=t1.ap, scalar=1.0, in1=k32.ap,
                                                     op0=ALU.add, op1=ALU.mult), [t1, k32], [kmod])
            V(functools.partial(nc.vector.tensor_tensor, out=t1.ap, in0=r32.ap, in1=kmod.ap, op=ALU.mult), [r32, kmod], [t1])
            V(functools.partial(nc.vector.tensor_scalar, out=b1.ap, in0=t1.ap, scalar1=rkc[:, p:p + 1], scalar2=None, op0=ALU.mult),
              [t1, PP], [b1])
            M(functools.partial(nc.tensor.matmul, self.psS[:, :], lhsT=blk64, rhs=b1.ap, start=True, stop=True), [b1, CB], [self.psS_r])
            V(functools.partial(nc.vector.tensor_tensor, out=bonus.ap, in0=self.psS[:, :], in1=v32.ap, op=ALU.mult),
              [self.psS_r, v32], [bonus])
            A(functools.partial(nc.scalar.copy, out=vb.ap, in_=v32.ap), [v32], [vb])
            for c in range(NCH):
                sl = slice(c * 64, (c + 1) * 64)
                V(functools.partial(nc.vector.tensor_tensor_scan, out=cs.ap[:, sl], data0=ones[:, 0:64], data1=sg.ap[:, sl],
                                                             initial=0.0, op0=ALU.mult, op1=ALU.add), [sg, CB], [cs])
            A(functools.partial(nc.scalar.activation, out=eneg.ap, in_=cs.ap, func=AF.Exp, scale=LD), [cs], [eneg])
            A(functools.partial(nc.scalar.activation, out=epos.ap, in_=cs.ap, func=AF.Exp, scale=-LD), [cs], [epos])
            V(functools.partial(nc.vector.tensor_tensor, out=t2.ap, in0=cs.ap, in1=sg.ap, op=ALU.subtract), [cs, sg], [t2])
            A(functools.partial(nc.scalar.activation, out=eexc.ap, in_=t2.ap, func=AF.Exp, scale=-LD), [t2], [eexc])
            c64 = lambda r_: r_.ap.rearrange("p (c t) -> p c t", c=NCH)
            V(functools.partial(nc.vector.scalar_tensor_tensor, out=ar3[:, :, 0, :], in0=c64(kkn), scalar=-1.0, in1=c64(eexc),
                                                     op0=ALU.mult, op1=ALU.mult), [kkn, eexc], [ar])
            V(functools.partial(nc.vector.tensor_tensor, out=ar3[:, :, 1, :], in0=c64(r32), in1=c64(epos), op=ALU.mult),
              [r32, epos], [ar])
            V(functools.partial(nc.vector.tensor_tensor, out=t1.ap, in0=kkn.ap, in1=alr.ap, op=ALU.mult), [kkn, alr], [t1])
            V(functools.partial(nc.vector.tensor_tensor, out=bk3[:, :, 0, :], in0=c64(t1), in1=c64(eneg), op=ALU.mult), [t1, eneg], [bk])
            V(functools.partial(nc.vector.tensor_tensor, out=bk3[:, :, 1, :], in0=c64(kmod), in1=c64(eneg), op=ALU.mult),
              [kmod, eneg], [bk])
            gC = c64(epos)[:, :, 63]
            for c2 in range(NCH // 2):
                for cc in range(2):
                    c = c2 * 2 + cc
                    srcs = (ar3[:, c, 0, :], bk3[:, c, 0, :], bk3[:, c, 1, :], vb.ap[:, c * 64:(c + 1) * 64])
                    for kind in range(4):
                        o_ = (cc * 4 + kind) * 128
                        M(functools.partial(nc.tensor.transpose, self.psT[0:64, o_:o_ + 128], srcs[kind], ident),
                          [ar, bk, vb, CB], [self.psT_r])
                A(functools.partial(nc.scalar.copy,
                    out=tT4[:, c2 * 2:c2 * 2 + 2, :, :],
                    in_=self.psT[0:64, :].rearrange("p (c k d) -> p c k d", c=2, k=4)), [self.psT_r], [tT])
            for hh in range(2):
                hp = slice(hh * 64, (hh + 1) * 64)
                hc = slice(hh * 64, (hh + 1) * 64)
                for c in range(NCH):
                    M(functools.partial(nc.tensor.matmul, pA01[0:64, c * 128:(c + 1) * 128], lhsT=bk3[hp, c, 0, :],
                                                  rhs=ar3[hp, c, :, :], start=True, stop=True), [bk, ar], psA01)
                for c in range(NCH):
                    M(functools.partial(nc.tensor.matmul, pA23[0:64, c * 128:(c + 1) * 128], lhsT=bk3[hp, c, 1, :],
                                                  rhs=ar3[hp, c, :, :], start=True, stop=True), [bk, ar], psA23)
                pmb, pmk = PMb[hh], PMk[hh]
                mab_b = mab.unsqueeze(1).to_broadcast([64, NCH, 128])
                V(functools.partial(nc.vector.tensor_tensor,
                    out=v3c(pmb), in0=pA01[0:64, :].rearrange("p (c d) -> p c d", c=NCH), in1=mab_b, op=ALU.mult),
                  list(psA01) + [CB], [pmb])
                V(functools.partial(nc.vector.tensor_tensor,
                    out=v3c(pmk), in0=pA23[0:64, :].rearrange("p (c d) -> p c d", c=NCH), in1=mab_b, op=ALU.mult),
                  list(psA23) + [CB], [pmk])
                for c in range(NCH):
                    M(functools.partial(nc.tensor.matmul, pB0.ap[0:64, c * 64:(c + 1) * 64], lhsT=ar3[hp, c, 0, :],
                                                  rhs=bk3[hp, c, 0, :], start=True, stop=True), [bk, ar], [pB0])
                Lc, Nc = Lb[0], pmb
                V(functools.partial(nc.vector.tensor_tensor,
                    out=v3c(Lc), in0=pB0.ap[0:64, :].rearrange("p (c d) -> p c d", c=NCH),
                    in1=msl.unsqueeze(1).to_broadcast([64, NCH, 64]), op=ALU.mult), [pB0, CB], [Lc])
                idb = id64.unsqueeze(1).to_broadcast([64, NCH, 64])
                V(functools.partial(nc.vector.tensor_tensor, out=v3c(NI), in0=v3c(pmb)[:, :, 0:64], in1=idb, op=ALU.add),
                  [pmb, CB], [NI])
                for c in range(NCH):
                    M(functools.partial(nc.tensor.matmul, pB1.ap[0:64, c * 64:(c + 1) * 64], lhsT=v3c(pmk)[:, c, 0:64],
                                                           rhs=tT4[:, c, 3, hc], start=True, stop=True), [pmk, tT], [pB1])
                X = Xb[0]
                A(functools.partial(nc.scalar.copy, out=v3c(X)[:, :, 64:128],
                                             in_=pB1.ap[0:64, :].rearrange("p (c d) -> p c d", c=NCH)), [pB1], [X])
                G(functools.partial(nc.gpsimd.tensor_copy, out=v3c(X)[:, :, 0:64], in_=tT4[:, :, 0, hc]), [tT], [X])
                Nview = lambda r_, isP: (v3c(r_)[:, :, 0:64] if isP else v3c(r_))
                n_isP = True
                for lvl in range(6):
                    for c in range(NCH):
                        M(functools.partial(nc.tensor.matmul, pA01[0:64, c * 128:(c + 1) * 128], lhsT=v3c(NI)[:, c, :],
                                                           rhs=v3c(X)[:, c, :], start=True, stop=True), [NI, X], psA01)
                    px3 = pA01[0:64, :].rearrange("p (c d) -> p c d", c=NCH)
                    if lvl < 5:
                        Xn = Xb[(lvl + 1) % 2]
                        A(functools.partial(nc.scalar.copy, out=v3c(Xn), in_=px3), list(psA01), [Xn])
                        X = Xn
                        nv = Nview(Nc, n_isP)
                        for c in range(NCH):
                            M(functools.partial(nc.tensor.matmul, pB0.ap[0:64, c * 64:(c + 1) * 64], lhsT=v3c(Lc)[:, c, :],
                                                                        rhs=nv[:, c, :], start=True, stop=True), [Lc, Nc], [pB0])
                        if lvl < 4:
                            for c in range(NCH):
                                M(functools.partial(nc.tensor.matmul, pB1.ap[0:64, c * 64:(c + 1) * 64], lhsT=nv[:, c, :],
                                                                            rhs=v3c(Lc)[:, c, :], start=True, stop=True), [Lc, Nc], [pB1])
                        Nn = Nb[lvl % 2]
                        pn3 = pB0.ap[0:64, :].rearrange("p (c d) -> p c d", c=NCH)
                        V(functools.partial(nc.vector.tensor_copy, out=v3c(Nn), in_=pn3), [pB0], [Nn])
                        V(functools.partial(nc.vector.tensor_tensor, out=v3c(NI), in0=pn3, in1=idb, op=ALU.add), [pB0, CB], [NI])
                        if lvl < 4:
                            Ln = Lb[(lvl + 1) % 2]
                            A(functools.partial(nc.scalar.copy, out=v3c(Ln), in_=pB1.ap[0:64, :].rearrange("p (c d) -> p c d", c=NCH)),
                              [pB1], [Ln])
                            Lc = Ln
                        Nc, n_isP = Nn, False
                    else:
                        A(functools.partial(nc.scalar.copy, out=v3c(Wp)[:, :, hc], in_=px3[:, :, 0:64]), list(psA01), [Wp])
                        V(functools.partial(nc.vector.tensor_copy, out=v3c(U0p)[:, :, hc], in_=px3[:, :, 64:128]), list(psA01), [U0p])
                        A(functools.partial(nc.scalar.copy, out=v3c(Wpad[hh])[:, :, hc], in_=px3[:, :, 0:64]),
                          list(psA01), [Wpad[hh]])
                        V(functools.partial(nc.vector.tensor_copy, out=v3c(U0pad[hh])[:, :, hc], in_=px3[:, :, 64:128]),
                          list(psA01), [U0pad[hh]])
                G(functools.partial(nc.gpsimd.tensor_copy, out=v3c(vTpad[hh])[:, :, hc], in_=tT4[:, :, 3, hc]), [tT], [vTpad[hh]])
            for c in range(NCH):
                M(functools.partial(nc.tensor.matmul, pA01[:, c * 128:(c + 1) * 128], lhsT=v3c(Wp)[:, c, :], rhs=tT4[:, c, 1, :],
                                              start=True, stop=True), [Wp, tT], psA01)
            pg3 = pA01.rearrange("p (c d) -> p c d", c=NCH)
            V(functools.partial(nc.vector.tensor_copy, out=v3f(GT)[0:64, :, 0:64], in_=pg3[0:64, :, 0:64]), list(psA01), [GT])
            A(functools.partial(nc.scalar.copy, out=v3f(GT)[64:128, :, 64:128], in_=pg3[64:128, :, 64:128]), list(psA01), [GT])
            for c in range(NCH):
                M(functools.partial(nc.tensor.matmul, pA23[:, c * 128:(c + 1) * 128], lhsT=tT4[:, c, 1, :], rhs=v3c(U0p)[:, c, :],
                                              start=True, stop=False), [U0p, tT], psA23)
                M(functools.partial(nc.tensor.matmul, pA23[:, c * 128:(c + 1) * 128], lhsT=tT4[:, c, 2, :], rhs=tT4[:, c, 3, :],
                                              start=False, stop=True), [tT], psA23)
            ph3 = pA23.rearrange("p (c d) -> p c d", c=NCH)
            for hh in range(2):
                hp = slice(hh * 64, (hh + 1) * 64)
                V(functools.partial(nc.vector.tensor_tensor,
                    out=v3f(Hh)[hp, :, hp], in0=ph3[hp, :, hp], in1=gC[hp, :].unsqueeze(2).to_broadcast([64, NCH, 64]),
                    op=ALU.mult), list(psA23) + [epos], [Hh])
            for c in range(NCH):
                for hh in range(2):
                    M(functools.partial(nc.tensor.matmul, pB0.ap[:, c * 64:(c + 1) * 64], lhsT=v3c(Wpad[hh])[:, c, :],
                                                         rhs=v3c(PMb[hh])[:, c, 64:128], start=(hh == 0), stop=(hh == 1)),
                      [Wpad[hh], PMb[hh]], [pB0])
            V(functools.partial(nc.vector.tensor_tensor, out=QT.ap.rearrange("p (c t) -> p c t", c=NCH),
                                              in0=pB0.ap.rearrange("p (c t) -> p c t", c=NCH), in1=ar3[:, :, 1, :],
                                              op=ALU.add), [pB0, ar], [QT])
            if seq_start:
                V(functools.partial(nc.vector.memset, Z32.ap, 0.0), [], [Z32])
            else:
                P.dma("sp", functools.partial(nc.sync.dma_start, out=Z32.ap, in_=self.zst[p]), Z32.b,
                      reads=[self.zst_b], writes=[Z32.b])
            za3 = Zall.ap.rearrange("p (c d) -> p c d", c=NCH + 1)
            A(functools.partial(nc.scalar.copy, out=za3[:, 0, :], in_=Z32.ap), [Z32], [Zall])
            for c in range(NCH):
                M(functools.partial(nc.tensor.matmul, pB1.ap[:, 0:128], lhsT=v3f(GT)[:, c, :], rhs=za3[:, c, :], start=True, stop=True),
                  [GT, Zall], [pB1])
                V(functools.partial(nc.vector.tensor_tensor, out=tz.ap, in0=pB1.ap[:, 0:128], in1=Z32.ap, op=ALU.add), [pB1, Z32], [tz])
                V(functools.partial(nc.vector.scalar_tensor_tensor, out=Z32.ap, in0=tz.ap, scalar=gC[:, c:c + 1], in1=v3f(Hh)[:, c, :],
                                                             op0=ALU.mult, op1=ALU.add), [tz, epos, Hh], [Z32])
                A(functools.partial(nc.scalar.copy, out=za3[:, c + 1, :], in_=Z32.ap), [Z32], [Zall])
            P.dma("sp", functools.partial(nc.sync.dma_start, out=self.zst[p], in_=Z32.ap), self.zst_b,
                  reads=[Z32.b], writes=[self.zst_b])
            q3_ = QT.ap.rearrange("p (c t) -> p c t", c=NCH)
            for c in range(NCH):
                M(functools.partial(nc.tensor.matmul, pB0.ap[:, c * 64:(c + 1) * 64], lhsT=za3[:, c, :], rhs=q3_[:, c, :],
                                              start=True, stop=False), [Zall, QT], [pB0])
                for hh in range(2):
                    M(functools.partial(nc.tensor.matmul, pB0.ap[:, c * 64:(c + 1) * 64], lhsT=v3c(U0pad[hh])[:, c, :],
                                                         rhs=v3c(PMb[hh])[:, c, 64:128], start=False, stop=False),
                      [U0pad[hh], PMb[hh]], [pB0])
                for hh in range(2):
                    M(functools.partial(nc.tensor.matmul, pB0.ap[:, c * 64:(c + 1) * 64], lhsT=v3c(vTpad[hh])[:, c, :],
                                                         rhs=v3c(PMk[hh])[:, c, 64:128], start=False, stop=(hh == 1)),
                      [vTpad[hh], PMk[hh]], [pB0])
            A(functools.partial(nc.scalar.copy, out=y32.ap, in_=pB0.ap), [pB0], [y32])
            A(functools.partial(nc.scalar.copy, out=b1.ap, in_=y32.ap), [y32], [b1])
            M(functools.partial(nc.tensor.matmul, self.psS[:, :], lhsT=blkm, rhs=b1.ap, start=True, stop=True), [b1, CB], [self.psS_r])
            V(functools.partial(nc.vector.tensor_tensor, out=y32.ap, in0=y32.ap, in1=self.psS[:, :], op=ALU.subtract),
              [y32, self.psS_r], [y32])
            A(functools.partial(nc.scalar.activation, out=b1.ap, in_=y32.ap, func=AF.Square), [y32], [b1])
            M(functools.partial(nc.tensor.matmul, self.psS[:, :], lhsT=blkm, rhs=b1.ap, start=True, stop=True), [b1, CB], [self.psS_r])
            V(functools.partial(nc.vector.tensor_scalar, out=t1.ap, in0=self.psS[:, :], scalar1=GN_EPS, scalar2=None, op0=ALU.add),
              [self.psS_r], [t1])
            A(functools.partial(nc.scalar.activation, out=t2.ap, in_=t1.ap, func=AF.Sqrt), [t1], [t2])
            V(functools.partial(nc.vector.reciprocal, out=t1.ap, in_=t2.ap), [t2], [t1])
            V(functools.partial(nc.vector.tensor_tensor, out=y32.ap, in0=y32.ap, in1=t1.ap, op=ALU.mult), [y32, t1], [y32])
            V(functools.partial(nc.vector.tensor_scalar, out=y32.ap, in0=y32.ap, scalar1=lwc[:, p:p + 1], scalar2=lbc[:, p:p + 1],
                                                  op0=ALU.mult, op1=ALU.add), [y32, PP], [y32])
            V(functools.partial(nc.vector.tensor_tensor, out=y32.ap, in0=y32.ap, in1=bonus.ap, op=ALU.add), [y32, bonus], [y32])
            ps = self.psa()
            for kc2 in range(2):
                M(functools.partial(nc.tensor.matmul, ps.ap, lhsT=g2p[kc2], rhs=ing.ap[:, kc2 * T2:(kc2 + 1) * T2],
                                                         start=(kc2 == 0), stop=(kc2 == 1)), [wt, ing], [ps])
            V(functools.partial(nc.vector.tensor_tensor, out=yg3[:, p, :], in0=ps.ap, in1=y32.ap, op=ALU.mult),
              [ps, y32], [yg])
        self.release(m3)
        if PAIRM:
            dst = self.ygs.rearrange("(s q p) t -> s p q t", s=self.NSB, q=8)[sb]
            P.dma("sp", functools.partial(nc.sync.dma_start, out=dst, in_=yg3), self.ygs_b, reads=[yg.b], writes=[self.ygs_b])
            P.dma("pool", functools.partial(nc.gpsimd.collective_compute, "AllGather", ALU.bypass, replica_groups=self.groups,
                                            ins=[self.ygs[sb * 1024:(sb + 1) * 1024, :]],
                                            outs=[self.ygr[sb * 2048:(sb + 1) * 2048, :]]), self.ygr_b,
                  reads=[self.ygs_b], writes=[self.ygr_b], inc=1)
        else:
            self.out_proj("rwo", L, yg, T2, t0, lambda inp: inp["rwkv_w_o"][ia], first)
        self.release(m)


N_CORES = 8
SEQ = 4096
_CACHE = {}


def kernel(**inputs):
    inp = {k: np.asarray(v) for k, v in inputs.items()}
    x = inp["x"].astype(np.float32, copy=False)
    B, T, Dm = x.shape
    assert (2 * B, T, Dm) == (N_CORES, SEQ, D)
    half_t = T // 2
    if "b" not in _CACHE:
        b = Builder(half_t, [0, 1, 2, 3], final_norm=True, pair_mode=True)
        b.build()
        _CACHE["b"] = b
    b = _CACHE["b"]
    packs = [b.pack(inp, core=h) for h in range(2)]
    in_maps = []
    for c in range(N_CORES):
        bi, h = divmod(c, 2)
        xfull = np.ascontiguousarray(x[bi].T).reshape(KC, 128, T)
        xT = np.ascontiguousarray(xfull[:, :, h * half_t:(h + 1) * half_t])
        wp, pp, cp = packs[h]
        in_maps.append({"xT": xT, "xfullT": xfull, "wpack": wp, "ppack": pp, "cpack": cp})
    res = run_bass_kernel_spmd(b.nc, in_maps, core_ids=list(range(N_CORES)))
    out = np.empty((B, T, Dm), np.float32)
    for c in range(N_CORES):
        bi, h = divmod(c, 2)
        out[bi, h * half_t:(h + 1) * half_t] = np.asarray(res.results[c]["yT"]).reshape(Dm, half_t).T
    return out
```

```python
import contextlib
import functools
import math
import numpy as np
import concourse.bass as bass
import concourse.mybir as mybir
from concourse.bass_utils import run_bass_kernel_spmd

F32 = mybir.dt.float32
BF16 = mybir.dt.bfloat16
AF = mybir.ActivationFunctionType
ALU = mybir.AluOpType
AX = mybir.AxisListType

D = 2048
KC = 16
DFF = 8192
NH = 32
HD = 64
TB = 1024
TT = 512
RMS_EPS = 1e-6
import os as _os
SAME_ENGINE_WAITS = bool(int(_os.environ.get('SAME_ENGINE_WAITS', '1')))
GN_EPS = 64e-5


class Buf:
    __slots__ = ("name", "w", "r", "dsem", "dcount", "excl")

    def __init__(self, name, excl=False):
        self.excl = excl
        self.name = name
        self.w = None
        self.r = []
        self.dsem = None
        self.dcount = 0


class Op:
    __slots__ = ("eng", "fn", "reads", "writes", "dma", "dbuf", "tile_idx", "uses_tile",
                 "need_inc", "sem", "val", "ndma", "inc")

    def __init__(self, eng, fn, reads, writes, dma=False, dbuf=None, tile_idx=None, uses_tile=None, ndma=1, inc=16):
        self.inc = inc
        self.eng = eng
        self.fn = fn
        self.reads = reads
        self.writes = writes
        self.dma = dma
        self.dbuf = dbuf
        self.tile_idx = tile_idx
        self.uses_tile = uses_tile
        self.need_inc = False
        self.sem = None
        self.val = None
        self.ndma = ndma


class Prog:
    ENGS = ("pe", "act", "dve", "pool", "sp")

    def __init__(self):
        self.nc = bass.Bass("TRN2", target_bir_lowering=False)
        self.es = contextlib.ExitStack()
        self.ops = []
        self.cur_tile = None
        nc = self.nc
        self.eng = {"pe": nc.tensor, "act": nc.scalar, "dve": nc.vector, "pool": nc.gpsimd, "sp": nc.sync}
        self.esem = {e: self.es.enter_context(nc.semaphore("sem_" + e)) for e in ("pe", "act", "dve", "pool")}
        self.dsems = []

    def op(self, eng, fn, reads=(), writes=(), uses_tile=None):
        reads = list(reads)
        writes = list(writes)
        for b in reads:
            if b.excl and b not in writes:
                writes.append(b)
        o = Op(eng, fn, list(reads), list(writes),
               uses_tile=uses_tile if uses_tile is not None else self.cur_tile)
        self.ops.append(o)
        return o

    def dma(self, eng, fn, dbuf, reads=(), writes=(), tile_idx=None, ndma=1, inc=16):
        o = Op(eng, fn, list(reads), list(writes), dma=True, dbuf=dbuf, tile_idx=tile_idx, ndma=ndma, inc=inc)
        self.ops.append(o)
        return o

    def barrier(self):
        self.ops.append("BARRIER")

    def _hoist(self, dist):
        loads = {}
        rest = []
        for o in self.ops:
            if o != "BARRIER" and o.dma and o.tile_idx is not None:
                loads[o.tile_idx] = o
            else:
                rest.append(o)
        if not loads:
            return
        ntiles = max(loads) + 1
        first_use = {}
        for i, o in enumerate(rest):
            if o != "BARRIER" and o.uses_tile is not None and o.uses_tile not in first_use:
                first_use[o.uses_tile] = i
        inserts = {}
        for j in range(ntiles):
            t = max(0, j - dist)
            while t not in first_use and t < ntiles:
                t += 1
            pos = first_use.get(t, len(rest))
            inserts.setdefault(pos, []).append(loads[j])
        out = []
        for i, o in enumerate(rest):
            if i in inserts:
                out.extend(inserts[i])
            out.append(o)
        if len(rest) in inserts:
            out.extend(inserts[len(rest)])
        self.ops = out

    def finalize(self, hoist_dist=2):
        self._hoist(hoist_dist)
        nc = self.nc
        deps = []
        bufs_seen = {}
        all_bufs = []

        def reg(b):
            if id(b) not in bufs_seen:
                bufs_seen[id(b)] = b
                all_bufs.append(b)

        for o in self.ops:
            if o == "BARRIER":
                deps.append(None)
                continue
            d = []
            for b in o.reads:
                reg(b)
                if b.w is not None:
                    d.append(b.w)
            for b in o.writes:
                reg(b)
                if b.w is not None:
                    d.append(b.w)
                d.extend(b.r)
            dd = []
            seen = set()
            for x in d:
                if id(x) not in seen and x is not o:
                    seen.add(id(x))
                    dd.append(x)
            for x in dd:
                if not x.dma and not (x.eng == o.eng and (o.eng == "pe" or not SAME_ENGINE_WAITS)):
                    x.need_inc = True
            deps.append(dd)
            for b in o.reads:
                if not o.dma:
                    b.r = [x for x in b.r if x.dma or x.eng != o.eng]
                b.r.append(o)
            for b in o.writes:
                b.w = o
                b.r = []
            if o.dma:
                reg(o.dbuf)
        class DS:
            def __init__(self, sem):
                self.sem = sem
                self.count = 0
        ds_by_name = {}
        for o in self.ops:
            if o != "BARRIER" and o.dma and o.dbuf.dsem is None:
                nm = o.dbuf.name
                if nm not in ds_by_name:
                    ds_by_name[nm] = DS(self.es.enter_context(nc.semaphore("ds%d" % len(self.dsems))))
                    self.dsems.append(ds_by_name[nm])
                o.dbuf.dsem = ds_by_name[nm]
        all_ds = list(ds_by_name.values())
        cnt = {e: 0 for e in self.esem}
        known = {e: {} for e in self.ENGS}
        n_wait = 0
        for o, dd in zip(self.ops, deps):
            if o == "BARRIER":
                for e in self.ENGS:
                    eng = self.eng[e]
                    for e2 in self.esem:
                        v = cnt[e2]
                        if v > known[e].get(id(self.esem[e2]), 0):
                            eng.wait_ge(self.esem[e2], v)
                            known[e][id(self.esem[e2])] = v
                    for d_ in all_ds:
                        if d_.count > known[e].get(id(d_.sem), 0):
                            eng.wait_ge(d_.sem, d_.count)
                            known[e][id(d_.sem)] = d_.count
                continue
            e = o.eng
            eng = self.eng[e]
            for x in dd:
                if x.dma:
                    sem, val = x.dbuf.dsem.sem, x.dbuf.dsem.count
                else:
                    if x.eng == e and (e == "pe" or not SAME_ENGINE_WAITS):
                        continue
                    sem, val = x.sem, x.val
                if val > known[e].get(id(sem), 0):
                    eng.wait_ge(sem, val)
                    known[e][id(sem)] = val
                    n_wait += 1
            r = o.fn()
            if o.dma:
                insts = r if isinstance(r, (list, tuple)) else [r]
                assert len(insts) == o.ndma, (len(insts), o.ndma)
                for ins in insts:
                    ins.then_inc(o.dbuf.dsem.sem, o.inc)
                o.dbuf.dsem.count += o.inc * len(insts)
            elif o.need_inc:
                cnt[e] += 1
                r.then_inc(self.esem[e], 1)
                o.sem, o.val = self.esem[e], cnt[e]
        for d_ in all_ds:
            if d_.count > known["sp"].get(id(d_.sem), 0):
                nc.sync.wait_ge(d_.sem, d_.count)
        for e2 in self.esem:
            if cnt[e2] > known["sp"].get(id(self.esem[e2]), 0):
                nc.sync.wait_ge(self.esem[e2], cnt[e2])
        self.stats = dict(nops=len(self.ops), nwait=n_wait, cnt=dict(cnt), ndsem=len(self.dsems))
        self.es.close()
        return nc


class R:
    __slots__ = ("ap", "b")

    def __init__(self, ap, b):
        self.ap = ap
        self.b = b


class Builder:
    def __init__(self, tok, layers, final_norm, pair_mode=False):
        self.P = Prog()
        self.nc = self.P.nc
        self.tok = tok
        self.nblk = tok // TB
        self.layers = layers
        self.final_norm = final_norm
        self.pair_mode = pair_mode
        nc = self.nc
        P = self.P
        if pair_mode:
            self.xfullT = nc.dram_tensor("xfullT", [KC, 128, 2 * tok], F32, kind="ExternalInput").ap()
            self.xfull_b = Buf("xfullT")
            self.xg3 = nc.dram_tensor("xg3", [2 * KC * 128, tok], F32).ap()
            self.xg3_b = Buf("xg3")
            self.tail_send = nc.dram_tensor("tail_send", [KC * 128, 128], F32).ap()
            self.tail_send_b = Buf("tail_send")
            self.tail_recv = nc.dram_tensor("tail_recv", [2 * KC * 128, 128], F32).ap()
            self.tail_recv_b = Buf("tail_recv")
            self.NSB = 2 * tok // 512
            self.ygs = nc.dram_tensor("ygs", [self.NSB * 8 * 128, 512], BF16).ap()
            self.ygs_b = Buf("ygs")
            self.ygr = nc.dram_tensor("ygr", [2 * self.NSB * 8 * 128, 512], BF16).ap()
            self.ygr_b = Buf("ygr")
            self.groups = [[0, 1], [2, 3], [4, 5], [6, 7]]
        self.xT = nc.dram_tensor("xT", [KC, 128, tok], F32, kind="ExternalInput").ap()
        self.yT = nc.dram_tensor("yT", [KC, 128, tok], F32, kind="ExternalOutput").ap()
        self.xs = nc.dram_tensor("xs", [KC, 128, tok], F32).ap()
        self.xT_b = Buf("xT")
        self.xs_b = Buf("xs")
        self.yT_b = Buf("yT")
        self.wgetters = []
        self.wkeys = {}
        self.woff = 0
        self.pgetters = []
        self.pkeys = {}
        self.pcol = 0
        self.cgetters = []
        self.ccol = 0
        self.NPCOL = 1024
        self.NCCOL = 1280
        self.AW = 53200
        self.arena = P.es.enter_context(nc.sbuf_tensor("arena", [128, self.AW], F32))
        self.atop = 0
        self.nbufs = 0
        self.psA = P.es.enter_context(nc.psum_tensor("psA", [128, 2048], F32))
        self.psB = P.es.enter_context(nc.psum_tensor("psB", [128, 1024], F32))
        self.psT = P.es.enter_context(nc.psum_tensor("psT", [128, 1024], BF16))
        self.psS = P.es.enter_context(nc.psum_tensor("psS", [128, 512], F32))
        self.psA_r = [R(self.psA[:, i * 512:(i + 1) * 512], Buf("psA%d" % i, True)) for i in range(4)]
        self.psB_r = [R(self.psB[:, i * 512:(i + 1) * 512], Buf("psB%d" % i, True)) for i in range(2)]
        self.psT_r = R(self.psT[:, :], Buf("psT", True))
        self.psS_r = R(self.psS[:, :], Buf("psS", True))
        self.rotA = 0
        self.rotB = 0
        self.pp = self.alloc("pp", self.NPCOL, F32)
        self.cb = self.alloc("cb", self.NCCOL, BF16)
        self.wslots = [self.alloc("wslot%d" % i, 8192, BF16) for i in range(3)]
        self.wtile_n = 0
        self.built = False

    def alloc(self, name, nelem, dt, npart=128):
        words = nelem if dt == F32 else (nelem + 1) // 2
        off = self.atop
        self.atop += words
        assert self.atop <= self.AW, ("SBUF arena overflow", name, self.atop)
        ap = self.arena[:, off:off + words]
        if dt != F32:
            ap = ap.bitcast(dt)
        if npart != 128:
            ap = ap[0:npart]
        return R(ap, Buf(name))

    def mark(self):
        return self.atop

    def release(self, m):
        self.P.barrier()
        self.atop = m

    def psa(self):
        r = self.psA_r[self.rotA % 4]
        self.rotA += 1
        return r

    def psb(self):
        r = self.psB_r[self.rotB % 2]
        self.rotB += 1
        return r

    def param(self, key, ncols, getter):
        if key not in self.pkeys:
            self.pkeys[key] = self.pcol
            self.pgetters.append((self.pcol, ncols, getter))
            self.pcol += ncols
            assert self.pcol <= self.NPCOL
        c = self.pkeys[key]
        return self.pp.ap[:, c:c + ncols]

    def dvec(self, key, getter):
        return self.param(key, KC, lambda inp, g=getter: np.ascontiguousarray(
            np.asarray(g(inp), np.float32).reshape(KC, 128).T))

    def const(self, ncols, arr):
        c = self.ccol
        self.cgetters.append((c, ncols, arr if callable(arr) else np.asarray(arr, np.float32)))
        self.ccol += ncols
        assert self.ccol <= self.NCCOL
        return self.cb.ap[:, c:c + ncols]

    def wtile(self, key, npart, nelem, getter):
        assert nelem <= 8192
        if key not in self.wkeys:
            self.wkeys[key] = self.woff
            self.wgetters.append((self.woff, npart, nelem, getter))
            self.woff += npart * nelem
        off = self.wkeys[key]
        idx = self.wtile_n
        self.wtile_n += 1
        slot = self.wslots[idx % 3]
        dst = slot.ap[0:npart, 0:nelem]

        def fn(off=off, npart=npart, nelem=nelem, dst=dst):
            src = self.wpack[off:off + npart * nelem].rearrange("(p n) -> p n", p=npart)
            return self.nc.gpsimd.dma_start(out=dst, in_=src)

        self.P.dma("pool", fn, slot.b, writes=[slot.b], tile_idx=idx)
        self.P.cur_tile = idx
        return R(dst, slot.b)

    def setup_consts(self):
        P, nc = self.P, self.nc
        self.ones = self.const(128, np.ones((128, 128)))
        self.ident = self.const(128, np.eye(128))
        bd = np.zeros((128, 128))
        bd[:64, :64] = 1
        bd[64:, 64:] = 1
        self.blk64 = self.const(128, bd)

    def load_consts(self):
        P, nc = self.P, self.nc

        def f1():
            return nc.sync.dma_start(out=self.pp.ap, in_=self.ppack)

        P.dma("sp", f1, self.pp.b, writes=[self.pp.b])

        def f2():
            return nc.gpsimd.dma_start(out=self.cb.ap, in_=self.cpack)

        P.dma("pool", f2, self.cb.b, writes=[self.cb.b])

    def x_src(self, first):
        return (self.xT, self.xT_b) if first else (self.xs, self.xs_b)

    def load_x(self, dst, src, srcb, t0, n):
        nc = self.nc
        d3 = dst.ap.rearrange("p (k t) -> p k t", k=KC)

        def fn():
            return nc.sync.dma_start(out=d3, in_=src.rearrange("k p t -> p k t")[:, :, t0:t0 + n])

        self.P.dma("sp", fn, dst.b, reads=[srcb], writes=[dst.b])

    def store_x(self, srcr, dst, dstb, t0, n):
        nc = self.nc
        s3 = srcr.ap.rearrange("p (k t) -> p k t", k=KC)

        def fn():
            return nc.sync.dma_start(out=dst.rearrange("k p t -> p k t")[:, :, t0:t0 + n], in_=s3)

        self.P.dma("sp", fn, dstb, reads=[srcr.b], writes=[dstb])

    def rmsnorm(self, xr, n, gain, hr, hoff=0, hstride=None):
        P, nc = self.P, self.nc
        hstride = hstride or n
        x3 = xr.ap.rearrange("p (k t) -> p k t", k=KC)
        h3 = hr.ap.rearrange("p (k t) -> p k t", k=KC)
        m = self.mark()
        sq = [self.alloc("sq%d" % i, TT, BF16) for i in range(3)]
        rs = self.alloc("rs", TT, F32)
        rs2 = self.alloc("rs2", TT, F32)
        ntt = (n + TT - 1) // TT
        for tt in range(ntt):
            w = min(TT, n - tt * TT)
            sl = slice(tt * TT, tt * TT + w)
            for kc in range(KC):
                s = sq[kc % 3]
                P.op("act", functools.partial(nc.scalar.activation,
                    out=s.ap[:, 0:w], in_=x3[:, kc, sl], func=AF.Square), reads=[xr.b], writes=[s.b])
                P.op("pe", functools.partial(nc.tensor.matmul,
                    self.psS[:, 0:w], lhsT=self.ones, rhs=s.ap[:, 0:w], start=(kc == 0), stop=(kc == KC - 1)),
                    reads=[s.b, self.cb.b], writes=[self.psS_r.b])
            P.op("dve", functools.partial(nc.vector.tensor_scalar,
                out=rs.ap[:, 0:w], in0=self.psS[:, 0:w], scalar1=1.0 / D, scalar2=RMS_EPS,
                op0=ALU.mult, op1=ALU.add), reads=[self.psS_r.b], writes=[rs.b])
            P.op("act", functools.partial(nc.scalar.activation, out=rs2.ap[:, 0:w], in_=rs.ap[:, 0:w], func=AF.Sqrt),
                 reads=[rs.b], writes=[rs2.b])
            P.op("dve", functools.partial(nc.vector.reciprocal, out=rs.ap[:, 0:w], in_=rs2.ap[:, 0:w]),
                 reads=[rs2.b], writes=[rs.b])
            for kc in range(KC):
                P.op("dve", functools.partial(nc.vector.scalar_tensor_tensor,
                    out=h3[:, kc, hoff + sl.start:hoff + sl.start + w], in0=x3[:, kc, sl],
                    scalar=gain[:, kc:kc + 1], in1=rs.ap[:, 0:w], op0=ALU.mult, op1=ALU.mult),
                    reads=[xr.b, rs.b, self.pp.b], writes=[hr.b])
        self.release(m)

    def mlp_phase(self, L, blk, first):
        P, nc = self.P, self.nc
        t0 = blk * TB
        m = self.mark()
        xb = self.alloc("xblk", KC * TB, F32)
        h = self.alloc("h", KC * TB, BF16)
        hh = [self.alloc("hh%d" % i, 4 * TB, BF16) for i in range(2)]
        tmp = [self.alloc("rl%d" % i, TT, F32) for i in range(2)]
        src, srcb = self.x_src(first)
        self.load_x(xb, src, srcb, t0, TB)
        gain = self.dvec(("norm_ffn", L), lambda inp, L=L: inp["norm_ffn"][L])
        self.rmsnorm(xb, TB, gain, h)
        x3 = xb.ap.rearrange("p (k t) -> p k t", k=KC)
        h3 = h.ap.rearrange("p (k t) -> p k t", k=KC)
        NG = DFF // 512
        ntt = TB // TT
        nrl = [0]

        def up(g):
            wt = self.wtile(("up", L, g), 128, KC * 512,
                            lambda inp, L=L, g=g: inp["mlp_w_up"][L][:, g * 512:(g + 1) * 512]
                            .reshape(KC, 128, 512).transpose(1, 0, 2).reshape(128, KC * 512))
            w3 = wt.ap.rearrange("p (k m) -> p k m", k=KC)
            hg = hh[g % 2]
            hg3 = hg.ap.rearrange("p (j t) -> p j t", j=4)
            for j in range(4):
                for tt in range(ntt):
                    ps = self.psb()
                    for kc in range(KC):
                        P.op("pe", functools.partial(nc.tensor.matmul,
                            ps.ap, lhsT=w3[:, kc, j * 128:(j + 1) * 128], rhs=h3[:, kc, tt * TT:(tt + 1) * TT],
                            start=(kc == 0), stop=(kc == KC - 1)), reads=[wt.b, h.b], writes=[ps.b])
                    tm = tmp[nrl[0] % 2]
                    nrl[0] += 1
                    P.op("act", functools.partial(nc.scalar.activation, out=tm.ap, in_=ps.ap, func=AF.Relu),
                         reads=[ps.b], writes=[tm.b])
                    P.op("pool", functools.partial(nc.gpsimd.tensor_tensor,
                        out=hg3[:, j, tt * TT:(tt + 1) * TT], in0=tm.ap, in1=tm.ap, op=ALU.mult),
                        reads=[tm.b], writes=[hg.b])

        def down(g):
            wt = self.wtile(("dn", L, g), 128, 4 * D,
                            lambda inp, L=L, g=g: inp["mlp_w_down"][L][g * 512:(g + 1) * 512, :]
                            .reshape(4, 128, D).transpose(1, 0, 2).reshape(128, 4 * D))
            w3 = wt.ap.rearrange("p (j d) -> p j d", j=4)
            hg = hh[g % 2]
            hg3 = hg.ap.rearrange("p (j t) -> p j t", j=4)
            for dc in range(KC):
                for tt in range(ntt):
                    ps = self.psa()
                    for j in range(4):
                        P.op("pe", functools.partial(nc.tensor.matmul,
                            ps.ap, lhsT=w3[:, j, dc * 128:(dc + 1) * 128], rhs=hg3[:, j, tt * TT:(tt + 1) * TT],
                            start=(j == 0), stop=(j == 3)), reads=[wt.b, hg.b], writes=[ps.b])
                    P.op("dve", functools.partial(nc.vector.tensor_tensor,
                        out=x3[:, dc, tt * TT:(tt + 1) * TT], in0=ps.ap, in1=x3[:, dc, tt * TT:(tt + 1) * TT],
                        op=ALU.add), reads=[ps.b, xb.b], writes=[xb.b])

        up(0)
        for g in range(NG):
            if g + 1 < NG:
                up(g + 1)
            down(g)
        return xb, m

    def finish_block(self, xb, m, blk, last):
        P, nc = self.P, self.nc
        t0 = blk * TB
        if last and self.final_norm:
            gain = self.dvec(("norm_final",), lambda inp: inp["norm_final"])
            self.final_rms(xb, gain, t0)
        elif last:
            self.store_x(xb, self.yT, self.yT_b, t0, TB)
        else:
            self.store_x(xb, self.xs, self.xs_b, t0, TB)
        self.release(m)

    def final_rms(self, xb, gain, t0):
        P, nc = self.P, self.nc
        x3 = xb.ap.rearrange("p (k t) -> p k t", k=KC)
        m = self.mark()
        sq = [self.alloc("fsq%d" % i, TT, BF16) for i in range(3)]
        rs = self.alloc("frs", TT, F32)
        rs2 = self.alloc("frs2", TT, F32)
        for tt in range(TB // TT):
            sl = slice(tt * TT, (tt + 1) * TT)
            for kc in range(KC):
                s = sq[kc % 3]
                P.op("act", functools.partial(nc.scalar.activation,
                    out=s.ap, in_=x3[:, kc, sl], func=AF.Square), reads=[xb.b], writes=[s.b])
                P.op("pe", functools.partial(nc.tensor.matmul,
                    self.psS[:, :], lhsT=self.ones, rhs=s.ap, start=(kc == 0), stop=(kc == KC - 1)),
                    reads=[s.b, self.cb.b], writes=[self.psS_r.b])
            P.op("dve", functools.partial(nc.vector.tensor_scalar,
                out=rs.ap, in0=self.psS[:, :], scalar1=1.0 / D, scalar2=RMS_EPS,
                op0=ALU.mult, op1=ALU.add), reads=[self.psS_r.b], writes=[rs.b])
            P.op("act", functools.partial(nc.scalar.activation, out=rs2.ap, in_=rs.ap, func=AF.Sqrt),
                 reads=[rs.b], writes=[rs2.b])
            P.op("dve", functools.partial(nc.vector.reciprocal, out=rs.ap, in_=rs2.ap), reads=[rs2.b], writes=[rs.b])
            for kc in range(KC):
                P.op("dve", functools.partial(nc.vector.scalar_tensor_tensor,
                    out=x3[:, kc, sl], in0=x3[:, kc, sl], scalar=gain[:, kc:kc + 1], in1=rs.ap,
                    op0=ALU.mult, op1=ALU.mult), reads=[xb.b, rs.b, self.pp.b], writes=[xb.b])
        self.store_x(xb, self.yT, self.yT_b, t0, TB)
        self.release(m)

    def build(self):
        self.setup_consts()
        self.load_consts()
        self.setup_persist()
        first_layer = True
        for li, spec in enumerate(self.layers):
            last_layer = (li == len(self.layers) - 1)
            mlp_only = isinstance(spec, tuple)
            L = spec[1] if mlp_only else spec
            kind = L % 3
            if self.pair_mode and not mlp_only and kind == 0:
                self.rwkv_layer_pair(L, first_layer)
            for blk in range(self.nblk):
                if mlp_only:
                    pass
                elif kind == 0:
                    if not self.pair_mode:
                        self.rwkv_phase(L, blk, first_layer)
                elif kind == 1:
                    self.swa_phase(L, blk, first_layer)
                elif kind == 2:
                    self.conv_phase(L, blk, first_layer)
                xb, m = self.mlp_phase(L, blk, first=(first_layer and mlp_only))
                self.finish_block(xb, m, blk, last_layer)
            if self.pair_mode and not last_layer:
                nc_, P_ = self.nc, self.P
                nxt = self.layers[li + 1]
                nxt = (nxt[1] if isinstance(nxt, tuple) else nxt) % 3
                if nxt == 0:
                    for kc in range(KC):
                        P_.dma("pool", functools.partial(nc_.gpsimd.collective_compute, "AllGather", ALU.bypass,
                                                         replica_groups=self.groups,
                                                         ins=[self.xs[kc]], outs=[self.xg3[kc * 256:(kc + 1) * 256, :]]),
                               self.xg3_b, reads=[self.xs_b], writes=[self.xg3_b], inc=1)
                else:
                    P_.dma("sp", functools.partial(nc_.sync.dma_start,
                                                   out=self.tail_send.rearrange("(k p) t -> k p t", k=KC),
                                                   in_=self.xs[:, :, self.tok - 128:self.tok]),
                           self.tail_send_b, reads=[self.xs_b], writes=[self.tail_send_b])
                    P_.dma("pool", functools.partial(nc_.gpsimd.collective_compute, "AllGather", ALU.bypass,
                                                     replica_groups=self.groups,
                                                     ins=[self.tail_send], outs=[self.tail_recv]),
                           self.tail_recv_b, reads=[self.tail_send_b], writes=[self.tail_recv_b], inc=1)
            first_layer = False
        nc = self.nc
        self.wpack = nc.dram_tensor("wpack", [max(self.woff, 128)], F32, kind="ExternalInput").ap()
        self.ppack = nc.dram_tensor("ppack", [128, self.NPCOL], F32, kind="ExternalInput").ap()
        self.cpack = nc.dram_tensor("cpack", [128, self.NCCOL], F32, kind="ExternalInput").ap()
        self.P.finalize(hoist_dist=2)
        self.built = True
        return nc

    def pack(self, inp, core=0):
        inp = dict(inp)
        inp["_core"] = core
        wp = np.zeros(max(self.woff, 128), np.float32)
        for off, npart, nelem, g in self.wgetters:
            a = np.asarray(g(inp), np.float32)
            assert a.shape == (npart, nelem), (a.shape, npart, nelem)
            wp[off:off + npart * nelem] = a.reshape(-1)
        pp = np.zeros((128, self.NPCOL), np.float32)
        for c, n, g in self.pgetters:
            a = np.asarray(g(inp), np.float32)
            assert a.shape == (128, n), (a.shape, n)
            pp[:, c:c + n] = a
        cp = np.zeros((128, self.NCCOL), np.float32)
        for c, n, a in self.cgetters:
            cp[:, c:c + n] = a(core) if callable(a) else a
        return wp, pp, cp

    def out_proj(self, key, L, z, n, t0, wget, first, bias=None):
        P, nc = self.P, self.nc
        z3 = z.ap.rearrange("p (k t) -> p k t", k=KC)
        src, srcb = self.x_src(first)
        m = self.mark()
        st = [self.alloc("ost%d" % i, TT, F32) for i in range(4)]
        ntt = n // TT
        k = 0
        for g in range(4):
            wt = self.wtile((key, L, g), 128, KC * 512,
                            lambda inp, g=g: wget(inp)[:, g * 512:(g + 1) * 512]
                            .reshape(KC, 128, 512).transpose(1, 0, 2).reshape(128, KC * 512))
            w3 = wt.ap.rearrange("p (k m) -> p k m", k=KC)
            for j in range(4):
                mc = g * 4 + j
                for tt in range(ntt):
                    s_ = st[k % 4]
                    k += 1
                    sl = slice(t0 + tt * TT, t0 + (tt + 1) * TT)
                    P.dma("sp", functools.partial(nc.sync.dma_start, out=s_.ap, in_=src[mc, :, sl]),
                          s_.b, reads=[srcb], writes=[s_.b])
                    ps = self.psa()
                    for kc in range(KC):
                        P.op("pe", functools.partial(nc.tensor.matmul,
                            ps.ap, lhsT=w3[:, kc, j * 128:(j + 1) * 128], rhs=z3[:, kc, tt * TT:(tt + 1) * TT],
                            start=(kc == 0), stop=(kc == KC - 1)), reads=[wt.b, z.b], writes=[ps.b])
                    if bias is None:
                        P.op("dve", functools.partial(nc.vector.tensor_tensor,
                            out=s_.ap, in0=ps.ap, in1=s_.ap, op=ALU.add), reads=[ps.b, s_.b], writes=[s_.b])
                    else:
                        P.op("dve", functools.partial(nc.vector.scalar_tensor_tensor,
                            out=s_.ap, in0=ps.ap, scalar=bias[:, mc:mc + 1], in1=s_.ap, op0=ALU.add, op1=ALU.add),
                            reads=[ps.b, s_.b, self.pp.b], writes=[s_.b])
                    P.dma("sp", functools.partial(nc.sync.dma_start, out=self.xs[mc, :, sl], in_=s_.ap),
                          self.xs_b, reads=[s_.b], writes=[self.xs_b])
        self.release(m)

    def norm_block(self, L, key, t0, n, first, h, hoff=0, hstride=None, src=None, srcb=None):
        gain = self.dvec((key, L), lambda inp, L=L, key=key: inp[key][L])
        if src is None:
            src, srcb = self.x_src(first)
        for s0 in range(0, n, TT):
            w = min(TT, n - s0)
            m = self.mark()
            xb = self.alloc("xnb", KC * w, F32)
            self.load_x(xb, src, srcb, t0 + s0, w)
            self.rmsnorm(xb, w, gain, h, hoff=hoff + s0, hstride=hstride)
            self.release(m)

    def conv_phase(self, L, blk, first):
        P, nc = self.P, self.nc
        t0 = blk * TB
        ic = L // 3
        HAL = 2 if (self.pair_mode and blk == 0) else 0
        m = self.mark()
        h = self.alloc("h", KC * (HAL + TB), BF16)
        self.norm_block(L, "norm_mix", t0, TB, first, h, hoff=HAL, hstride=HAL + TB)
        if HAL:
            tsrc = self.tail_recv.rearrange("(r k p) t -> r k p t", r=2, k=KC)[0][:, :, 126:128]
            self.norm_block(L, "norm_mix", 0, HAL, first, h, hoff=0, hstride=HAL + TB, src=tsrc, srcb=self.tail_recv_b)
            m01 = self.param(("mask01",), 1, lambda inp: np.full((128, 1), float(inp["_core"] % 2), np.float32))
        z = self.alloc("z", KC * TB, BF16)
        h3 = h.ap.rearrange("p (k t) -> p k t", k=KC)
        z3 = z.ap.rearrange("p (k t) -> p k t", k=KC)
        u = [self.alloc("u%d" % i, TB + 2, F32) for i in range(2)]
        uc = [self.alloc("uc%d" % i, TB, F32) for i in range(2)]
        tcg = [self.alloc("tcg%d" % i, TB + 2, F32) for i in range(2)]
        cw = [self.dvec(("conv_w", ic, tap), lambda inp, ic=ic, tap=tap: inp["conv_w"][ic][tap]) for tap in range(3)]
        uh3 = self.uhalo.ap.rearrange("p (k t) -> p k t", k=KC)
        ntt = TB // TT
        for j in range(KC):
            def getter(inp, j=j, ic=ic):
                W = inp["conv_w_in"][ic]
                cols = np.concatenate([W[:, D + j * 128:D + (j + 1) * 128], W[:, 2 * D + j * 128:2 * D + (j + 1) * 128],
                                       W[:, j * 128:(j + 1) * 128]], axis=1)
                return cols.reshape(KC, 128, 384).transpose(1, 0, 2).reshape(128, KC * 384)
            wt = self.wtile(("cin", L, j), 128, KC * 384, getter)
            w3 = wt.ap.rearrange("p (k m) -> p k m", k=KC)
            uj, ucj, tj = u[j % 2], uc[j % 2], tcg[j % 2]
            if not HAL:
                P.op("pool", functools.partial(nc.gpsimd.tensor_copy, out=uj.ap[:, 0:2], in_=uh3[:, j, :]),
                     reads=[self.uhalo.b], writes=[uj.b])
            for part in range(3):
                if part == 2:
                    P.op("pool", functools.partial(nc.gpsimd.tensor_scalar,
                        out=ucj.ap, in0=uj.ap[:, 2:2 + TB], scalar1=cw[2][:, j:j + 1], scalar2=None, op0=ALU.mult),
                        reads=[uj.b, self.pp.b], writes=[ucj.b])
                    P.op("dve", functools.partial(nc.vector.scalar_tensor_tensor,
                        out=ucj.ap, in0=uj.ap[:, 1:1 + TB], scalar=cw[1][:, j:j + 1], in1=ucj.ap,
                        op0=ALU.mult, op1=ALU.add), reads=[uj.b, ucj.b, self.pp.b], writes=[ucj.b])
                    P.op("dve", functools.partial(nc.vector.scalar_tensor_tensor,
                        out=ucj.ap, in0=uj.ap[:, 0:TB], scalar=cw[0][:, j:j + 1], in1=ucj.ap,
                        op0=ALU.mult, op1=ALU.add), reads=[uj.b, ucj.b, self.pp.b], writes=[ucj.b])
                    P.op("pool", functools.partial(nc.gpsimd.tensor_copy, out=uh3[:, j, :], in_=uj.ap[:, TB:TB + 2]),
                         reads=[uj.b], writes=[self.uhalo.b])
                tiles = [(HAL + tt * TT, TT) for tt in range(ntt)]
                if HAL and part < 2:
                    tiles = [(0, HAL)] + tiles
                for (c0, cw_) in tiles:
                    ps = self.psa()
                    for kc in range(KC):
                        P.op("pe", functools.partial(nc.tensor.matmul,
                            ps.ap[:, 0:cw_], lhsT=w3[:, kc, part * 128:(part + 1) * 128], rhs=h3[:, kc, c0:c0 + cw_],
                            start=(kc == 0), stop=(kc == KC - 1)), reads=[wt.b, h.b], writes=[ps.b])
                    r0 = c0 - HAL
                    if part == 0:
                        P.op("act", functools.partial(nc.scalar.copy, out=tj.ap[:, 2 + r0:2 + r0 + cw_], in_=ps.ap[:, 0:cw_]),
                             reads=[ps.b], writes=[tj.b])
                    elif part == 1:
                        if r0 < 0:
                            P.op("dve", functools.partial(nc.vector.scalar_tensor_tensor,
                                out=uj.ap[:, 0:2], in0=ps.ap[:, 0:2], scalar=m01[:, 0:1], in1=tj.ap[:, 0:2],
                                op0=ALU.mult, op1=ALU.mult), reads=[ps.b, tj.b, self.pp.b], writes=[uj.b])
                        else:
                            P.op("dve", functools.partial(nc.vector.tensor_tensor,
                                out=uj.ap[:, 2 + r0:2 + r0 + cw_], in0=ps.ap[:, 0:cw_], in1=tj.ap[:, 2 + r0:2 + r0 + cw_],
                                op=ALU.mult), reads=[ps.b, tj.b], writes=[uj.b])
                    else:
                        P.op("dve", functools.partial(nc.vector.tensor_tensor,
                            out=z3[:, j, r0:r0 + cw_], in0=ps.ap[:, 0:cw_], in1=ucj.ap[:, r0:r0 + cw_], op=ALU.mult),
                            reads=[ps.b, ucj.b], writes=[z.b])
        self.out_proj("cout", L, z, TB, t0, lambda inp, ic=ic: inp["conv_w_out"][ic], first)
        self.release(m)

    def setup_persist(self):
        P, nc = self.P, self.nc
        kinds = set((l[1] if isinstance(l, tuple) else l) % 3 for l in self.layers if not isinstance(l, tuple))
        if 2 in kinds:
            self.uhalo = self.alloc("uhalo", KC * 2, F32)
            P.op("pool", functools.partial(nc.gpsimd.memset, self.uhalo.ap, 0.0), writes=[self.uhalo.b])
        if 0 in kinds:
            self.hlast = self.alloc("hlast", KC, BF16)

    def swa_phase(self, L, blk, first):
        P, nc = self.P, self.nc
        t0 = blk * TB
        ib = L // 3
        NQB = TB // 128
        if not hasattr(self, "swa_k_st"):
            self.swa_k_st = nc.dram_tensor("swa_k_st", [128, 4 * 128], BF16).ap()
            self.swa_v_st = nc.dram_tensor("swa_v_st", [128, 8 * 128], BF16).ap()
            self.swa_st_b = Buf("swa_st")
            NEG = -30000.0
            qi = np.arange(128)[:, None]
            kj = np.arange(256)[None, :]
            ok = (kj > qi) & (kj <= qi + 128)
            self.c_mask = self.const(256, np.where(ok, 0.0, NEG))
            m_all = np.where(ok, 0.0, NEG)
            m_first = np.where(ok & (kj >= 128), 0.0, NEG)
            if self.pair_mode:
                self.c_mask0 = self.const(256, lambda core: m_first if core % 2 == 0 else m_all)
            else:
                self.c_mask0 = self.const(256, m_first)
        Wq = lambda inp: inp["swa_w_qkv"][ib]
        bq = lambda inp: inp["swa_b_qkv"][ib]
        b_q = self.param(("swa_bq", ib), KC, lambda inp: np.ascontiguousarray(bq(inp)[:D].reshape(KC, 128).T))
        b_k = self.param(("swa_bk", ib), 4, lambda inp: np.stack(
            [np.concatenate([bq(inp)[D + j * 64:D + (j + 1) * 64]] * 2) for j in range(4)], axis=1))
        b_v = self.param(("swa_bv", ib), 2, lambda inp: np.ascontiguousarray(bq(inp)[D + 256:D + 512].reshape(2, 128).T))
        b_o = self.dvec(("swa_bo", ib), lambda inp: inp["swa_b_o"][ib])
        sink = self.param(("swa_sink", ib), NH, lambda inp: np.tile(inp["swa_sinks"][ib][None, :], (128, 1)))
        m = self.mark()
        q_all = self.alloc("q_all", KC * TB, BF16)
        kbuf = self.alloc("kbuf", 4 * (128 + TB), BF16)
        vtp = self.alloc("vtp", (NQB + 1) * 8 * 128, BF16)
        q3 = q_all.ap.rearrange("p (k t) -> p k t", k=KC)
        k3 = kbuf.ap.rearrange("p (j t) -> p j t", j=4)
        v5 = vtp.ap.rearrange("p (b j v d) -> p b j v d", b=NQB + 1, j=4, v=2)
        HAL = 128 if (self.pair_mode and blk == 0) else 0
        if blk == 0:
            P.op("dve", functools.partial(nc.vector.memset, vtp.ap, 0.0), writes=[vtp.b])
            if not HAL:
                P.op("dve", functools.partial(nc.vector.memset, k3[:, :, 0:128], 0.0), writes=[kbuf.b])
        else:
            P.op("dve", functools.partial(nc.vector.memset, vtp.ap[:, 8 * 128:], 0.0), writes=[vtp.b])
            P.dma("sp", functools.partial(nc.sync.dma_start, out=vtp.ap[:, 0:8 * 128], in_=self.swa_v_st), vtp.b,
                  reads=[self.swa_st_b], writes=[vtp.b])
            P.dma("sp", functools.partial(nc.sync.dma_start, out=k3[:, :, 0:128],
                                                  in_=self.swa_k_st.rearrange("p (j t) -> p j t", j=4)), kbuf.b,
                  reads=[self.swa_st_b], writes=[kbuf.b])
        m2 = self.mark()
        h = self.alloc("h", KC * (HAL + TB), BF16)
        self.norm_block(L, "norm_mix", t0, TB, first, h, hoff=HAL, hstride=HAL + TB)
        if HAL:
            tsrc = self.tail_recv.rearrange("(r k p) t -> r k p t", r=2, k=KC)[0]
            self.norm_block(L, "norm_mix", 0, HAL, first, h, hoff=0, hstride=HAL + TB, src=tsrc, srcb=self.tail_recv_b)
        vfm = self.alloc("vfm", 2 * (128 + TB), BF16)
        h3 = h.ap.rearrange("p (k t) -> p k t", k=KC)
        vf3 = vfm.ap.rearrange("p (c t) -> p c t", c=2)
        ntt = TB // TT
        main_tiles = [(HAL + tt * TT, TT) for tt in range(ntt)]
        kv_tiles = ([(0, HAL)] if HAL else []) + main_tiles

        def proj(key, ncols, getter, epi, tiles):
            wt = self.wtile((key, L), 128, KC * ncols,
                            lambda inp: getter(inp).reshape(KC, 128, ncols).transpose(1, 0, 2).reshape(128, KC * ncols))
            w3 = wt.ap.rearrange("p (k m) -> p k m", k=KC)
            for j in range(ncols // 128):
                for (c0, cw) in tiles:
                    ps = self.psa()
                    for kc in range(KC):
                        P.op("pe", functools.partial(nc.tensor.matmul,
                            ps.ap[:, 0:cw], lhsT=w3[:, kc, j * 128:(j + 1) * 128], rhs=h3[:, kc, c0:c0 + cw],
                            start=(kc == 0), stop=(kc == KC - 1)), reads=[wt.b, h.b], writes=[ps.b])
                    epi(j, c0 - HAL, cw, ps)

        for g in range(4):
            def epi_q(j, c0, cw, ps, g=g):
                mc = g * 4 + j
                P.op("dve", functools.partial(nc.vector.tensor_scalar,
                    out=q3[:, mc, c0:c0 + cw], in0=ps.ap[:, 0:cw], scalar1=b_q[:, mc:mc + 1], scalar2=HD ** -0.5,
                    op0=ALU.add, op1=ALU.mult), reads=[ps.b, self.pp.b], writes=[q_all.b])
            proj(("swa_q", g), 512, lambda inp, g=g: Wq(inp)[:, g * 512:(g + 1) * 512], epi_q, main_tiles)

        def epi_k(j, c0, cw, ps):
            P.op("dve", functools.partial(nc.vector.tensor_scalar,
                out=k3[:, j, 128 + c0:128 + c0 + cw], in0=ps.ap[:, 0:cw], scalar1=b_k[:, j:j + 1], scalar2=None,
                op0=ALU.add), reads=[ps.b, self.pp.b], writes=[kbuf.b])
        proj("swa_k", 512, lambda inp: np.concatenate(
            [Wq(inp)[:, D + (j // 2) * 64:D + (j // 2 + 1) * 64] for j in range(8)], axis=1), epi_k, kv_tiles)

        def epi_v(j, c0, cw, ps):
            P.op("dve", functools.partial(nc.vector.tensor_scalar,
                out=vf3[:, j, 128 + c0:128 + c0 + cw], in0=ps.ap[:, 0:cw], scalar1=b_v[:, j:j + 1], scalar2=None,
                op0=ALU.add), reads=[ps.b, self.pp.b], writes=[vfm.b])
        proj("swa_v", 256, lambda inp: Wq(inp)[:, D + 256:D + 512], epi_v, kv_tiles)
        for bb in range(-1 if HAL else 0, NQB):
            for c in range(2):
                P.op("pe", functools.partial(nc.tensor.transpose,
                    self.psT[:, c * 128:(c + 1) * 128], vf3[:, c, 128 + bb * 128:128 + (bb + 1) * 128], self.ident),
                    reads=[vfm.b, self.cb.b], writes=[self.psT_r.b])
            src4 = self.psT[:, 0:256].rearrange("p (j d) -> p j d", j=4)
            P.op("act", functools.partial(nc.scalar.copy, out=v5[:, bb + 1, :, 0, 0:64], in_=src4),
                 reads=[self.psT_r.b], writes=[vtp.b])
            P.op("dve", functools.partial(nc.vector.tensor_copy, out=v5[:, bb + 1, :, 1, 64:128], in_=src4),
                 reads=[self.psT_r.b], writes=[vtp.b])
        self.release(m2)
        o_all = self.alloc("o_all", KC * TB, BF16)
        o3 = o_all.ap.rearrange("p (k t) -> p k t", k=KC)
        NR = 4
        lm = [self.alloc("lm%d" % i, 512, F32) for i in range(NR)]
        pe_ = [self.alloc("pe%d" % i, 512, F32) for i in range(NR)]
        pn = [self.alloc("pn%d" % i, 512, BF16) for i in range(NR)]
        pT = [self.alloc("pT%d" % i, 512, BF16) for i in range(NR)]
        sm = [self.alloc("sm%d" % i, 16, F32) for i in range(NR)]
        def unit(c, bb, it):
            kvh = c // 4
            if True:
                i = it % NR
                u0 = (it % 2) * 512
                lmr, per, pnr, pTr, smr = lm[i], pe_[i], pn[i], pT[i], sm[i]
                if self.rotA % 2:
                    self.rotA += 1
                psl0 = self.psa()
                psl1 = self.psa()
                pbase = ((self.rotA - 2) % 4) * 512
                for hh, psl in ((0, psl0), (1, psl1)):
                    pr = slice(hh * 64, (hh + 1) * 64)
                    P.op("pe", functools.partial(nc.tensor.matmul,
                        psl.ap[:, 0:256], lhsT=q3[pr, c, bb * 128:(bb + 1) * 128],
                        rhs=k3[pr, kvh, bb * 128:bb * 128 + 256], start=True, stop=True),
                        reads=[q_all.b, kbuf.b], writes=[psl.b])
                mk = self.c_mask0 if (blk == 0 and bb == 0) else self.c_mask
                l3 = lmr.ap.rearrange("p (h k) -> p h k", h=2)
                pl3 = self.psA[:, pbase:pbase + 1024].rearrange("p (h k) -> p h k", h=2)[:, :, 0:256]
                P.op("dve", functools.partial(nc.vector.tensor_tensor,
                    out=l3, in0=pl3, in1=mk.unsqueeze(1).to_broadcast([128, 2, 256]), op=ALU.add),
                    reads=[psl0.b, psl1.b, self.cb.b], writes=[lmr.b])
                s_ = smr.ap
                P.op("dve", functools.partial(nc.vector.tensor_reduce,
                    out=s_[:, 0:2], in_=l3, axis=AX.X, op=ALU.max), reads=[lmr.b], writes=[smr.b])
                P.op("dve", functools.partial(nc.vector.tensor_tensor,
                    out=s_[:, 2:4], in0=s_[:, 0:2], in1=sink[:, 2 * c:2 * c + 2], op=ALU.max),
                    reads=[smr.b, self.pp.b], writes=[smr.b])
                P.op("dve", functools.partial(nc.vector.tensor_scalar,
                    out=s_[:, 4:6], in0=s_[:, 2:4], scalar1=-1.0, scalar2=None, op0=ALU.mult),
                    reads=[smr.b], writes=[smr.b])
                p3 = per.ap.rearrange("p (h k) -> p h k", h=2)
                for hh in range(2):
                    P.op("act", functools.partial(nc.scalar.activation,
                        out=p3[:, hh, :], in_=l3[:, hh, :], func=AF.Exp, bias=s_[:, 4 + hh:5 + hh], scale=1.0,
                        accum_out=s_[:, 6 + hh:7 + hh]), reads=[lmr.b, smr.b], writes=[per.b, smr.b])
                P.op("dve", functools.partial(nc.vector.tensor_tensor,
                    out=s_[:, 8:10], in0=s_[:, 4:6], in1=sink[:, 2 * c:2 * c + 2], op=ALU.add),
                    reads=[smr.b, self.pp.b], writes=[smr.b])
                P.op("act", functools.partial(nc.scalar.activation, out=s_[:, 10:12], in_=s_[:, 8:10], func=AF.Exp),
                     reads=[smr.b], writes=[smr.b])
                P.op("dve", functools.partial(nc.vector.tensor_tensor,
                    out=s_[:, 12:14], in0=s_[:, 10:12], in1=s_[:, 6:8], op=ALU.add),
                    reads=[smr.b], writes=[smr.b])
                P.op("dve", functools.partial(nc.vector.reciprocal, out=s_[:, 14:16], in_=s_[:, 12:14]),
                     reads=[smr.b], writes=[smr.b])
                pn3 = pnr.ap.rearrange("p (h k) -> p h k", h=2)
                P.op("dve", functools.partial(nc.vector.tensor_tensor,
                    out=pn3, in0=p3, in1=s_[:, 14:16].unsqueeze(2).to_broadcast([128, 2, 256]), op=ALU.mult),
                    reads=[per.b, smr.b], writes=[pnr.b])
                for hh in range(2):
                    for kb in range(2):
                        P.op("pe", functools.partial(nc.tensor.transpose,
                            self.psT[:, u0 + (hh * 2 + kb) * 128:u0 + (hh * 2 + kb + 1) * 128],
                            pn3[:, hh, kb * 128:(kb + 1) * 128], self.ident),
                            reads=[pnr.b, self.cb.b], writes=[self.psT_r.b])
                P.op("act", functools.partial(nc.scalar.copy, out=pTr.ap, in_=self.psT[:, u0:u0 + 512]),
                     reads=[self.psT_r.b], writes=[pTr.b])
                pso = self.psb()
                n_ = 0
                for hh in range(2):
                    for kb in range(2):
                        P.op("pe", functools.partial(nc.tensor.matmul,
                            pso.ap[:, 0:128], lhsT=v5[:, bb + kb, kvh, hh, :],
                            rhs=pTr.ap[:, (hh * 2 + kb) * 128:(hh * 2 + kb + 1) * 128],
                            start=(n_ == 0), stop=(n_ == 3)), reads=[vtp.b, pTr.b], writes=[pso.b])
                        n_ += 1
                P.op("act", functools.partial(nc.scalar.copy,
                    out=o3[:, c, bb * 128:(bb + 1) * 128], in_=pso.ap[:, 0:128]), reads=[pso.b], writes=[o_all.b])
        units = [(c, bb) for c in range(KC) for bb in range(NQB)]
        for u in range(0, len(units), 2):
            lists = []
            for k_ in range(2):
                saved = P.ops
                P.ops = []
                unit(units[u + k_][0], units[u + k_][1], u + k_)
                lists.append(P.ops)
                P.ops = saved
            for oa, ob in zip(lists[0], lists[1]):
                P.ops.append(oa)
                P.ops.append(ob)
            assert len(lists[0]) == len(lists[1])
        if blk + 1 < self.nblk:
            P.dma("sp", functools.partial(nc.sync.dma_start, out=self.swa_v_st, in_=vtp.ap[:, NQB * 8 * 128:]), self.swa_st_b,
                  reads=[vtp.b], writes=[self.swa_st_b])
            P.dma("sp", functools.partial(nc.sync.dma_start, out=self.swa_k_st.rearrange("p (j t) -> p j t", j=4),
                                                  in_=k3[:, :, TB:TB + 128]), self.swa_st_b,
                  reads=[kbuf.b], writes=[self.swa_st_b])
        self.out_proj("swa_o", L, o_all, TB, t0, lambda inp: inp["swa_w_o"][ib], first, bias=b_o)
        self.release(m)

    def rwkv_phase(self, L, blk, first):
        T2 = 512
        for sub in range(TB // T2):
            self.rwkv_sub(L, blk * TB + sub * T2, T2, first, seq_start=(blk == 0 and sub == 0))

    def rwkv_layer_pair(self, L, first):
        P, nc = self.P, self.nc
        T2 = 512
        for sb in range(self.NSB):
            if first:
                src, srcb = self.xfullT[:, :, sb * T2:(sb + 1) * T2], self.xfull_b
            else:
                half, off = divmod(sb, self.NSB // 2)
                src = self.xg3.rearrange("(k r p) t -> r k p t", k=KC, r=2)[half][:, :, off * T2:(off + 1) * T2]
                srcb = self.xg3_b
            self.rwkv_sub(L, sb * T2, T2, first, seq_start=(sb == 0), xsrc=src, xsrcb=srcb, sb=sb)
        m01 = self.param(("mask01",), 1, lambda inp: np.full((128, 1), float(inp["_core"] % 2), np.float32))
        om01 = self.param(("omask01",), 1, lambda inp: np.full((128, 1), 1.0 - float(inp["_core"] % 2), np.float32))
        ygv = self.ygr.rearrange("(s r q p) t -> s r p q t", s=self.NSB, r=2, q=8)
        hsb = self.NSB // 2
        for blk in range(self.nblk):
            m = self.mark()
            cand = [self.alloc("ygc%d" % i, KC * TB, BF16) for i in range(2)]
            ygb = self.alloc("ygb", KC * TB, BF16)
            for hc in range(2):
                c3 = cand[hc].ap.rearrange("p (k t) -> p k t", k=KC)
                for r in range(2):
                    for s2 in range(TB // T2):
                        sbg = hc * hsb + blk * (TB // T2) + s2
                        P.dma("sp", functools.partial(nc.sync.dma_start, out=c3[:, r * 8:(r + 1) * 8, s2 * T2:(s2 + 1) * T2],
                                                      in_=ygv[sbg][r]), cand[hc].b, reads=[self.ygr_b], writes=[cand[hc].b])
            P.op("dve", functools.partial(nc.vector.tensor_scalar, out=cand[0].ap, in0=cand[0].ap, scalar1=om01[:, 0:1],
                                          scalar2=None, op0=ALU.mult), reads=[cand[0].b, self.pp.b], writes=[cand[0].b])
            P.op("dve", functools.partial(nc.vector.scalar_tensor_tensor, out=ygb.ap, in0=cand[1].ap, scalar=m01[:, 0:1],
                                          in1=cand[0].ap, op0=ALU.mult, op1=ALU.add),
                 reads=[cand[0].b, cand[1].b, self.pp.b], writes=[ygb.b])
            ia = L // 3
            self.out_proj("rwo", L, ygb, TB, blk * TB, lambda inp, ia=ia: inp["rwkv_w_o"][ia], first)
            self.release(m)

    def rwkv_sub(self, L, t0, T2, first, seq_start, xsrc=None, xsrcb=None, sb=None):
        P, nc = self.P, self.nc
        ia = L // 3
        has_vres = ia > 0
        NCH = T2 // 64
        PAIRM = self.pair_mode
        NPAIR = 8 if PAIRM else KC
        LD = 0.6065306597126334
        if not hasattr(self, "zst"):
            self.zst = nc.dram_tensor("zst", [KC, 128, 128], F32).ap()
            self.zst_b = Buf("zst")
            self.vfirst = nc.dram_tensor("vfirst", [KC, 128, self.tok * (2 if PAIRM else 1)], F32).ap()
            self.vfirst_b = Buf("vfirst")
            si = np.arange(64)[:, None]
            tj = np.arange(64)[None, :]
            mab = np.zeros((128, 128))
            mab[:64, :64] = (si < tj)
            mab[:64, 64:] = (si <= tj)
            self.c_mab = self.const(128, mab)
            msl = np.zeros((128, 64))
            msl[:64, :] = (si > tj)
            self.c_msl = self.const(64, msl)
            bd = np.zeros((128, 128))
            bd[:64, :64] = 1.0 / 64
            bd[64:, 64:] = 1.0 / 64
            self.c_blkm = self.const(128, bd)

        def A(fn, reads, writes):
            P.op("act", fn, [x.b for x in reads], [x.b for x in writes])

        def V(fn, reads, writes):
            P.op("dve", fn, [x.b for x in reads], [x.b for x in writes])

        def G(fn, reads, writes):
            P.op("pool", fn, [x.b for x in reads], [x.b for x in writes])

        def M(fn, reads, writes):
            P.op("pe", fn, [x.b for x in reads], [x.b for x in writes])

        CB, PP = self.cb, self.pp
        if PAIRM:
            def pv(name, idx=ia):
                return self.param((name, idx, "half"), 8, lambda inp, name=name, idx=idx: np.ascontiguousarray(
                    np.asarray(inp[name][idx], np.float32).reshape(KC, 128).T[:, (inp["_core"] % 2) * 8:(inp["_core"] % 2) * 8 + 8]))
        else:
            pv = lambda name, idx=ia: self.dvec((name, idx), lambda inp, name=name, idx=idx: inp[name][idx].reshape(-1))
        mu = [self.dvec(("rwkv_mu", ia, i), lambda inp, i=i: inp["rwkv_mu"][ia][i]) for i in range(6)]
        w0c, a0c, kkc, kac, rkc, lwc, lbc = (pv("rwkv_w0"), pv("rwkv_a0"), pv("rwkv_k_k"), pv("rwkv_k_a"),
                                             pv("rwkv_r_k"), pv("rwkv_lnx_w"), pv("rwkv_lnx_b"))
        if has_vres:
            v0c = pv("rwkv_v0", ia - 1)
        m = self.mark()
        xr = self.alloc("xr", KC * T2, BF16)
        xk = self.alloc("xk", KC * T2, BF16)
        xv = self.alloc("xv", KC * T2, BF16)
        inw = self.alloc("inw", T2, BF16)
        ina = self.alloc("ina", T2, BF16)
        ing = self.alloc("ing", 2 * T2, BF16)
        inv = self.alloc("inv", T2, BF16) if has_vres else None
        yg = self.alloc("yg", NPAIR * T2, BF16)
        yg3 = yg.ap.rearrange("p (k t) -> p k t", k=NPAIR)
        xr3, xk3, xv3 = [x.ap.rearrange("p (k t) -> p k t", k=KC) for x in (xr, xk, xv)]
        m2 = self.mark()
        hb = self.alloc("hb", KC * (T2 + 1), BF16)
        h3 = hb.ap.rearrange("p (k t) -> p k t", k=KC)
        if seq_start:
            V(functools.partial(nc.vector.memset, h3[:, :, 0:1], 0.0), [], [hb])
        else:
            V(functools.partial(nc.vector.tensor_copy, out=h3[:, :, 0:1], in_=self.hlast.ap.unsqueeze(2)), [self.hlast], [hb])
        if xsrc is not None:
            self.norm_block(L, "norm_mix", 0, T2, first, hb, hoff=1, hstride=T2 + 1, src=xsrc, srcb=xsrcb)
        else:
            self.norm_block(L, "norm_mix", t0, T2, first, hb, hoff=1, hstride=T2 + 1)
        V(functools.partial(nc.vector.tensor_copy, out=self.hlast.ap.unsqueeze(2), in_=h3[:, :, T2:T2 + 1]), [hb], [self.hlast])
        dx = self.alloc("dx", KC * T2, BF16)
        dx3 = dx.ap.rearrange("p (k t) -> p k t", k=KC)
        V(functools.partial(nc.vector.tensor_tensor, out=dx3, in0=h3[:, :, 0:T2], in1=h3[:, :, 1:T2 + 1], op=ALU.subtract),
          [hb], [dx])

        def mix(i, outr, out3):
            for kc in range(KC):
                V(functools.partial(nc.vector.scalar_tensor_tensor,
                    out=out3[:, kc, :], in0=dx3[:, kc, :], scalar=mu[i][:, kc:kc + 1], in1=h3[:, kc, 1:T2 + 1],
                    op0=ALU.mult, op1=ALU.add), [dx, hb, PP], [outr])

        xm = self.alloc("xm", KC * T2, BF16)
        xm3 = xm.ap.rearrange("p (k t) -> p k t", k=KC)

        def lora1(mi, key, wname, widx, Rr, outr, func):
            mix(mi, xm, xm3)
            wt = self.wtile((key, L), 128, KC * Rr,
                            lambda inp: inp[wname][widx].reshape(KC, 128, Rr).transpose(1, 0, 2).reshape(128, KC * Rr))
            w3 = wt.ap.rearrange("p (k m) -> p k m", k=KC)
            for rc in range((Rr + 127) // 128):
                rr = min(128, Rr - rc * 128)
                ps = self.psa()
                for kc in range(KC):
                    M(functools.partial(nc.tensor.matmul,
                        ps.ap[0:rr, 0:T2], lhsT=w3[:, kc, rc * 128:rc * 128 + rr], rhs=xm3[:, kc, :],
                        start=(kc == 0), stop=(kc == KC - 1)), [wt, xm], [ps])
                A(functools.partial(nc.scalar.activation,
                    out=outr.ap[0:rr, rc * T2:(rc + 1) * T2], in_=ps.ap[0:rr, 0:T2], func=func), [ps], [outr])

        lora1(1, "rw1", "rwkv_w1", ia, 96, inw, AF.Tanh)
        lora1(4, "ra1", "rwkv_a1", ia, 96, ina, AF.Copy)
        lora1(5, "rg1", "rwkv_g1", ia, 256, ing, AF.Sigmoid)
        if has_vres:
            lora1(3, "rv1", "rwkv_v1", ia - 1, 64, inv, AF.Copy)
        mix(0, xr, xr3)
        mix(2, xk, xk3)
        mix(3, xv, xv3)
        self.release(m2)
        f32 = lambda name: self.alloc(name, T2, F32)
        m3 = self.mark()
        r32, k32, v32, sg, alr, kkn, kmod, cs, epos, bonus, t1, t2 = [
            f32(n) for n in ("r32", "k32", "v32", "sg", "alr", "kkn", "kmod", "cs", "epos", "bonus", "t1", "t2")]
        y32, eexc, eneg = r32, k32, v32
        b1 = self.alloc("b1", T2, BF16)
        vb = self.alloc("vb", T2, BF16)
        ar = self.alloc("ar", T2 * 2, BF16)
        bk = self.alloc("bk", T2 * 2, BF16)
        ar3 = ar.ap.rearrange("p (c x t) -> p c x t", c=NCH, x=2)
        bk3 = bk.ap.rearrange("p (c x t) -> p c x t", c=NCH, x=2)
        tT = self.alloc("tT", NCH * 4 * 128, BF16)
        tT4 = tT.ap[0:64].rearrange("p (c k d) -> p c k d", c=NCH, k=4)
        PMb = [self.alloc("PMb%d" % i, NCH * 128, BF16) for i in range(2)]
        PMk = [self.alloc("PMk%d" % i, NCH * 128, BF16) for i in range(2)]
        Lb = [self.alloc("Lb%d" % i, NCH * 64, BF16) for i in range(2)]
        Nb = [self.alloc("Nb%d" % i, NCH * 64, BF16) for i in range(2)]
        NI = self.alloc("NI", NCH * 64, BF16)
        Xb = [self.alloc("Xb%d" % i, NCH * 128, BF16) for i in range(2)]
        Wp = self.alloc("Wp", NCH * 128, BF16)
        U0p = self.alloc("U0p", NCH * 128, BF16)
        Wpad = [self.alloc("Wpad%d" % i, NCH * 128, BF16) for i in range(2)]
        U0pad = [self.alloc("U0pad%d" % i, NCH * 128, BF16) for i in range(2)]
        vTpad = [self.alloc("vTpad%d" % i, NCH * 128, BF16) for i in range(2)]
        GT = self.alloc("GT", NCH * 128, BF16)
        Hh = self.alloc("Hh", NCH * 128, F32)
        QT = self.alloc("QT", NCH * 64, BF16)
        Zall = self.alloc("Zall", (NCH + 1) * 128, BF16)
        Z32 = self.alloc("Z32", 128, F32)
        tz = self.alloc("tz", 128, F32)
        v3c = lambda r_, w=128: r_.ap[0:64].rearrange("p (c d) -> p c d", c=NCH)
        v3f = lambda r_: r_.ap.rearrange("p (c d) -> p c d", c=NCH)
        for z_ in Wpad + U0pad + vTpad + [GT]:
            V(functools.partial(nc.vector.memset, z_.ap, 0.0), [], [z_])
        V(functools.partial(nc.vector.memset, Hh.ap, 0.0), [], [Hh])
        psA01 = (self.psA_r[0], self.psA_r[1])
        psA23 = (self.psA_r[2], self.psA_r[3])
        pA01 = self.psA[:, 0:1024]
        pA23 = self.psA[:, 1024:2048]
        pB0, pB1 = self.psB_r[0], self.psB_r[1]
        blk64, blkm, ident, ones = self.blk64, self.c_blkm, self.ident, self.ones
        id64 = ident[0:64, 0:64]
        mab = self.c_mab[0:64, :]
        msl = self.c_msl[0:64, :]
        W_rkv = lambda inp: inp["rwkv_w_rkv"][ia]
        for p in range(NPAIR):
            def getter(inp, p=p):
                pg = (inp["_core"] % 2) * 8 + p if PAIRM else p
                cs_ = slice(pg * 128, (pg + 1) * 128)
                W = W_rkv(inp)
                main = np.concatenate([W[0][:, cs_], W[1][:, cs_], W[2][:, cs_]], axis=1)
                main = main.reshape(KC, 128, 384).transpose(1, 0, 2).reshape(128, KC * 384)
                ext = np.zeros((128, 640), np.float32)
                ext[:96, 0:128] = inp["rwkv_w2"][ia][:, cs_]
                ext[:96, 128:256] = inp["rwkv_a2"][ia][:, cs_]
                g2 = inp["rwkv_g2"][ia][:, cs_]
                ext[:, 256:384] = g2[0:128]
                ext[:, 384:512] = g2[128:256]
                if has_vres:
                    ext[:64, 512:640] = inp["rwkv_v2"][ia - 1][:, cs_]
                return np.concatenate([main, ext], axis=1)

            wt = self.wtile(("rkv", L, p), 128, KC * 384 + 640, getter)
            w3 = wt.ap[:, 0:KC * 384].rearrange("p (k m) -> p k m", k=KC)
            E0 = KC * 384
            w2p = wt.ap[0:96, E0:E0 + 128]
            a2p = wt.ap[0:96, E0 + 128:E0 + 256]
            g2p = [wt.ap[:, E0 + 256:E0 + 384], wt.ap[:, E0 + 384:E0 + 512]]
            v2p = wt.ap[0:64, E0 + 512:E0 + 640]
            for part, (xin, xin3, dst) in enumerate(((xr, xr3, r32), (xk, xk3, k32), (xv, xv3, v32))):
                ps = self.psa()
                for kc in range(KC):
                    M(functools.partial(nc.tensor.matmul,
                        ps.ap, lhsT=w3[:, kc, part * 128:(part + 1) * 128], rhs=xin3[:, kc, :],
                        start=(kc == 0), stop=(kc == KC - 1)), [wt, xin], [ps])
                A(functools.partial(nc.scalar.copy, out=dst.ap, in_=ps.ap), [ps], [dst])
            ps = self.psa()
            M(functools.partial(nc.tensor.matmul, ps.ap, lhsT=w2p, rhs=inw.ap[0:96, :], start=True, stop=True), [wt, inw], [ps])
            A(functools.partial(nc.scalar.activation, out=sg.ap, in_=ps.ap, func=AF.Sigmoid, bias=w0c[:, p:p + 1], scale=1.0),
              [ps, PP], [sg])
            ps = self.psa()
            M(functools.partial(nc.tensor.matmul, ps.ap, lhsT=a2p, rhs=ina.ap[0:96, :], start=True, stop=True), [wt, ina], [ps])
            A(functools.partial(nc.scalar.activation, out=alr.ap, in_=ps.ap, func=AF.Sigmoid, bias=a0c[:, p:p + 1], scale=1.0),
              [ps, PP], [alr])
            if has_vres:
                ps = self.psa()
                M(functools.partial(nc.tensor.matmul, ps.ap, lhsT=v2p, rhs=inv.ap[0:64, :], start=True, stop=True), [wt, inv], [ps])
                A(functools.partial(nc.scalar.activation, out=t1.ap, in_=ps.ap, func=AF.Sigmoid, bias=v0c[:, p:p + 1], scale=1.0),
                  [ps, PP], [t1])
                P.dma("sp", functools.partial(nc.sync.dma_start, out=t2.ap, in_=self.vfirst[p, :, t0:t0 + T2]), t2.b,
                      reads=[self.vfirst_b], writes=[t2.b])
                V(functools.partial(nc.vector.tensor_tensor, out=t2.ap, in0=t2.ap, in1=v32.ap, op=ALU.subtract), [t2, v32], [t2])
                V(functools.partial(nc.vector.tensor_tensor, out=t2.ap, in0=t2.ap, in1=t1.ap, op=ALU.mult), [t2, t1], [t2])
                V(functools.partial(nc.vector.tensor_tensor, out=v32.ap, in0=v32.ap, in1=t2.ap, op=ALU.add), [t2, v32], [v32])
            else:
                P.dma("sp", functools.partial(nc.sync.dma_start, out=self.vfirst[p, :, t0:t0 + T2], in_=v32.ap), self.vfirst_b,
                      reads=[v32.b], writes=[self.vfirst_b])
            V(functools.partial(nc.vector.tensor_scalar, out=t1.ap, in0=k32.ap, scalar1=kkc[:, p:p + 1], scalar2=None, op0=ALU.mult),
              [k32, PP], [t1])
            A(functools.partial(nc.scalar.activation, out=b1.ap, in_=t1.ap, func=AF.Square), [t1], [b1])
            M(functools.partial(nc.tensor.matmul, self.psS[:, :], lhsT=blk64, rhs=b1.ap, start=True, stop=True), [b1, CB], [self.psS_r])
            A(functools.partial(nc.scalar.activation, out=t2.ap, in_=self.psS[:, :], func=AF.Sqrt), [self.psS_r], [t2])
            V(functools.partial(nc.vector.tensor_scalar, out=t2.ap, in0=t2.ap, scalar1=1e-12, scalar2=None, op0=ALU.max), [t2], [t2])
            V(functools.partial(nc.vector.reciprocal, out=t2.ap, in_=t2.ap), [t2], [t2])
            V(functools.partial(nc.vector.tensor_tensor, out=kkn.ap, in0=t1.ap, in1=t2.ap, op=ALU.mult), [t1, t2], [kkn])
            V(functools.partial(nc.vector.tensor_scalar, out=t1.ap, in0=alr.ap, scalar1=-1.0, scalar2=kac[:, p:p + 1],
                                                  op0=ALU.add, op1=ALU.mult), [alr, PP], [t1])
            V(functools.partial(nc.vector.scalar_tensor_tensor, out=kmod.ap, in0=t1.ap, scalar=1.0, in1=k32.ap,
                                                     op0=ALU.add, op1=ALU.mult), [t1, k32], [kmod])
            V(functools.partial(nc.vector.tensor_tensor, out=t1.ap, in0=r32.ap, in1=kmod.ap, op=ALU.mult), [r32, kmod], [t1])
            V(functools.partial(nc.vector.tensor_scalar, out=b1.ap, in0=t1.ap, scalar1=rkc[:, p:p + 1], scalar2=None, op0=ALU.mult),
              [t1, PP], [b1])
            M(functools.partial(nc.tensor.matmul, self.psS[:, :], lhsT=blk64, rhs=b1.ap, start=True, stop=True), [b1, CB], [self.psS_r])
            V(functools.partial(nc.vector.tensor_tensor, out=bonus.ap, in0=self.psS[:, :], in1=v32.ap, op=ALU.mult),
              [self.psS_r, v32], [bonus])
            A(functools.partial(nc.scalar.copy, out=vb.ap, in_=v32.ap), [v32], [vb])
            for c in range(NCH):
                sl = slice(c * 64, (c + 1) * 64)
                V(functools.partial(nc.vector.tensor_tensor_scan, out=cs.ap[:, sl], data0=ones[:, 0:64], data1=sg.ap[:, sl],
                                                             initial=0.0, op0=ALU.mult, op1=ALU.add), [sg, CB], [cs])
            A(functools.partial(nc.scalar.activation, out=eneg.ap, in_=cs.ap, func=AF.Exp, scale=LD), [cs], [eneg])
            A(functools.partial(nc.scalar.activation, out=epos.ap, in_=cs.ap, func=AF.Exp, scale=-LD), [cs], [epos])
            V(functools.partial(nc.vector.tensor_tensor, out=t2.ap, in0=cs.ap, in1=sg.ap, op=ALU.subtract), [cs, sg], [t2])
            A(functools.partial(nc.scalar.activation, out=eexc.ap, in_=t2.ap, func=AF.Exp, scale=-LD), [t2], [eexc])
            c64 = lambda r_: r_.ap.rearrange("p (c t) -> p c t", c=NCH)
            V(functools.partial(nc.vector.scalar_tensor_tensor, out=ar3[:, :, 0, :], in0=c64(kkn), scalar=-1.0, in1=c64(eexc),
                                                     op0=ALU.mult, op1=ALU.mult), [kkn, eexc], [ar])
            V(functools.partial(nc.vector.tensor_tensor, out=ar3[:, :, 1, :], in0=c64(r32), in1=c64(epos), op=ALU.mult),
              [r32, epos], [ar])
            V(functools.partial(nc.vector.tensor_tensor, out=t1.ap, in0=kkn.ap, in1=alr.ap, op=ALU.mult), [kkn, alr], [t1])
            V(functools.partial(nc.vector.tensor_tensor, out=bk3[:, :, 0, :], in0=c64(t1), in1=c64(eneg), op=ALU.mult), [t1, eneg], [bk])
            V(functools.partial(nc.vector.tensor_tensor, out=bk3[:, :, 1, :], in0=c64(kmod), in1=c64(eneg), op=ALU.mult),
              [kmod, eneg], [bk])
            gC = c64(epos)[:, :, 63]
            for c2 in range(NCH // 2):
                for cc in range(2):
                    c = c2 * 2 + cc
                    srcs = (ar3[:, c, 0, :], bk3[:, c, 0, :], bk3[:, c, 1, :], vb.ap[:, c * 64:(c + 1) * 64])
                    for kind in range(4):
                        o_ = (cc * 4 + kind) * 128
                        M(functools.partial(nc.tensor.transpose, self.psT[0:64, o_:o_ + 128], srcs[kind], ident),
                          [ar, bk, vb, CB], [self.psT_r])
                A(functools.partial(nc.scalar.copy,
                    out=tT4[:, c2 * 2:c2 * 2 + 2, :, :],
                    in_=self.psT[0:64, :].rearrange("p (c k d) -> p c k d", c=2, k=4)), [self.psT_r], [tT])
            for hh in range(2):
                hp = slice(hh * 64, (hh + 1) * 64)
                hc = slice(hh * 64, (hh + 1) * 64)
                for c in range(NCH):
                    M(functools.partial(nc.tensor.matmul, pA01[0:64, c * 128:(c + 1) * 128], lhsT=bk3[hp, c, 0, :],
                                                  rhs=ar3[hp, c, :, :], start=True, stop=True), [bk, ar], psA01)
                for c in range(NCH):
                    M(functools.partial(nc.tensor.matmul, pA23[0:64, c * 128:(c + 1) * 128], lhsT=bk3[hp, c, 1, :],
                                                  rhs=ar3[hp, c, :, :], start=True, stop=True), [bk, ar], psA23)
                pmb, pmk = PMb[hh], PMk[hh]
                mab_b = mab.unsqueeze(1).to_broadcast([64, NCH, 128])
                V(functools.partial(nc.vector.tensor_tensor,
                    out=v3c(pmb), in0=pA01[0:64, :].rearrange("p (c d) -> p c d", c=NCH), in1=mab_b, op=ALU.mult),
                  list(psA01) + [CB], [pmb])
                V(functools.partial(nc.vector.tensor_tensor,
                    out=v3c(pmk), in0=pA23[0:64, :].rearrange("p (c d) -> p c d", c=NCH), in1=mab_b, op=ALU.mult),
                  list(psA23) + [CB], [pmk])
                for c in range(NCH):
                    M(functools.partial(nc.tensor.matmul, pB0.ap[0:64, c * 64:(c + 1) * 64], lhsT=ar3[hp, c, 0, :],
                                                  rhs=bk3[hp, c, 0, :], start=True, stop=True), [bk, ar], [pB0])
                Lc, Nc = Lb[0], pmb
                V(functools.partial(nc.vector.tensor_tensor,
                    out=v3c(Lc), in0=pB0.ap[0:64, :].rearrange("p (c d) -> p c d", c=NCH),
                    in1=msl.unsqueeze(1).to_broadcast([64, NCH, 64]), op=ALU.mult), [pB0, CB], [Lc])
                idb = id64.unsqueeze(1).to_broadcast([64, NCH, 64])
                V(functools.partial(nc.vector.tensor_tensor, out=v3c(NI), in0=v3c(pmb)[:, :, 0:64], in1=idb, op=ALU.add),
                  [pmb, CB], [NI])
                for c in range(NCH):
                    M(functools.partial(nc.tensor.matmul, pB1.ap[0:64, c * 64:(c + 1) * 64], lhsT=v3c(pmk)[:, c, 0:64],
                                                           rhs=tT4[:, c, 3, hc], start=True, stop=True), [pmk, tT], [pB1])
                X = Xb[0]
                A(functools.partial(nc.scalar.copy, out=v3c(X)[:, :, 64:128],
                                             in_=pB1.ap[0:64, :].rearrange("p (c d) -> p c d", c=NCH)), [pB1], [X])
                G(functools.partial(nc.gpsimd.tensor_copy, out=v3c(X)[:, :, 0:64], in_=tT4[:, :, 0, hc]), [tT], [X])
                Nview = lambda r_, isP: (v3c(r_)[:, :, 0:64] if isP else v3c(r_))
                n_isP = True
                for lvl in range(6):
                    for c in range(NCH):
                        M(functools.partial(nc.tensor.matmul, pA01[0:64, c * 128:(c + 1) * 128], lhsT=v3c(NI)[:, c, :],
                                                           rhs=v3c(X)[:, c, :], start=True, stop=True), [NI, X], psA01)
                    px3 = pA01[0:64, :].rearrange("p (c d) -> p c d", c=NCH)
                    if lvl < 5:
                        Xn = Xb[(lvl + 1) % 2]
                        A(functools.partial(nc.scalar.copy, out=v3c(Xn), in_=px3), list(psA01), [Xn])
                        X = Xn
                        nv = Nview(Nc, n_isP)
                        for c in range(NCH):
                            M(functools.partial(nc.tensor.matmul, pB0.ap[0:64, c * 64:(c + 1) * 64], lhsT=v3c(Lc)[:, c, :],
                                                                        rhs=nv[:, c, :], start=True, stop=True), [Lc, Nc], [pB0])
                        if lvl < 4:
                            for c in range(NCH):
                                M(functools.partial(nc.tensor.matmul, pB1.ap[0:64, c * 64:(c + 1) * 64], lhsT=nv[:, c, :],
                                                                            rhs=v3c(Lc)[:, c, :], start=True, stop=True), [Lc, Nc], [pB1])
                        Nn = Nb[lvl % 2]
                        pn3 = pB0.ap[0:64, :].rearrange("p (c d) -> p c d", c=NCH)
                        V(functools.partial(nc.vector.tensor_copy, out=v3c(Nn), in_=pn3), [pB0], [Nn])
                        V(functools.partial(nc.vector.tensor_tensor, out=v3c(NI), in0=pn3, in1=idb, op=ALU.add), [pB0, CB], [NI])
                        if lvl < 4:
                            Ln = Lb[(lvl + 1) % 2]
                            A(functools.partial(nc.scalar.copy, out=v3c(Ln), in_=pB1.ap[0:64, :].rearrange("p (c d) -> p c d", c=NCH)),
                              [pB1], [Ln])
                            Lc = Ln
                        Nc, n_isP = Nn, False
                    else:
                        A(functools.partial(nc.scalar.copy, out=v3c(Wp)[:, :, hc], in_=px3[:, :, 0:64]), list(psA01), [Wp])
                        V(functools.partial(nc.vector.tensor_copy, out=v3c(U0p)[:, :, hc], in_=px3[:, :, 64:128]), list(psA01), [U0p])
                        A(functools.partial(nc.scalar.copy, out=v3c(Wpad[hh])[:, :, hc], in_=px3[:, :, 0:64]),
                          list(psA01), [Wpad[hh]])
                        V(functools.partial(nc.vector.tensor_copy, out=v3c(U0pad[hh])[:, :, hc], in_=px3[:, :, 64:128]),
                          list(psA01), [U0pad[hh]])
                G(functools.partial(nc.gpsimd.tensor_copy, out=v3c(vTpad[hh])[:, :, hc], in_=tT4[:, :, 3, hc]), [tT], [vTpad[hh]])
            for c in range(NCH):
                M(functools.partial(nc.tensor.matmul, pA01[:, c * 128:(c + 1) * 128], lhsT=v3c(Wp)[:, c, :], rhs=tT4[:, c, 1, :],
                                              start=True, stop=True), [Wp, tT], psA01)
            pg3 = pA01.rearrange("p (c d) -> p c d", c=NCH)
            V(functools.partial(nc.vector.tensor_copy, out=v3f(GT)[0:64, :, 0:64], in_=pg3[0:64, :, 0:64]), list(psA01), [GT])
            A(functools.partial(nc.scalar.copy, out=v3f(GT)[64:128, :, 64:128], in_=pg3[64:128, :, 64:128]), list(psA01), [GT])
            for c in range(NCH):
                M(functools.partial(nc.tensor.matmul, pA23[:, c * 128:(c + 1) * 128], lhsT=tT4[:, c, 1, :], rhs=v3c(U0p)[:, c, :],
                                              start=True, stop=False), [U0p, tT], psA23)
                M(functools.partial(nc.tensor.matmul, pA23[:, c * 128:(c + 1) * 128], lhsT=tT4[:, c, 2, :], rhs=tT4[:, c, 3, :],
                                              start=False, stop=True), [tT], psA23)
            ph3 = pA23.rearrange("p (c d) -> p c d", c=NCH)
            for hh in range(2):
                hp = slice(hh * 64, (hh + 1) * 64)
                V(functools.partial(nc.vector.tensor_tensor,
                    out=v3f(Hh)[hp, :, hp], in0=ph3[hp, :, hp], in1=gC[hp, :].unsqueeze(2).to_broadcast([64, NCH, 64]),
                    op=ALU.mult), list(psA23) + [epos], [Hh])
            for c in range(NCH):
                for hh in range(2):
                    M(functools.partial(nc.tensor.matmul, pB0.ap[:, c * 64:(c + 1) * 64], lhsT=v3c(Wpad[hh])[:, c, :],
                                                         rhs=v3c(PMb[hh])[:, c, 64:128], start=(hh == 0), stop=(hh == 1)),
                      [Wpad[hh], PMb[hh]], [pB0])
            V(functools.partial(nc.vector.tensor_tensor, out=QT.ap.rearrange("p (c t) -> p c t", c=NCH),
                                              in0=pB0.ap.rearrange("p (c t) -> p c t", c=NCH), in1=ar3[:, :, 1, :],
                                              op=ALU.add), [pB0, ar], [QT])
            if seq_start:
                V(functools.partial(nc.vector.memset, Z32.ap, 0.0), [], [Z32])
            else:
                P.dma("sp", functools.partial(nc.sync.dma_start, out=Z32.ap, in_=self.zst[p]), Z32.b,
                      reads=[self.zst_b], writes=[Z32.b])
            za3 = Zall.ap.rearrange("p (c d) -> p c d", c=NCH + 1)
            A(functools.partial(nc.scalar.copy, out=za3[:, 0, :], in_=Z32.ap), [Z32], [Zall])
            for c in range(NCH):
                M(functools.partial(nc.tensor.matmul, pB1.ap[:, 0:128], lhsT=v3f(GT)[:, c, :], rhs=za3[:, c, :], start=True, stop=True),
                  [GT, Zall], [pB1])
                V(functools.partial(nc.vector.tensor_tensor, out=tz.ap, in0=pB1.ap[:, 0:128], in1=Z32.ap, op=ALU.add), [pB1, Z32], [tz])
                V(functools.partial(nc.vector.scalar_tensor_tensor, out=Z32.ap, in0=tz.ap, scalar=gC[:, c:c + 1], in1=v3f(Hh)[:, c, :],
                                                             op0=ALU.mult, op1=ALU.add), [tz, epos, Hh], [Z32])
                A(functools.partial(nc.scalar.copy, out=za3[:, c + 1, :], in_=Z32.ap), [Z32], [Zall])
            P.dma("sp", functools.partial(nc.sync.dma_start, out=self.zst[p], in_=Z32.ap), self.zst_b,
                  reads=[Z32.b], writes=[self.zst_b])
            q3_ = QT.ap.rearrange("p (c t) -> p c t", c=NCH)
            for c in range(NCH):
                M(functools.partial(nc.tensor.matmul, pB0.ap[:, c * 64:(c + 1) * 64], lhsT=za3[:, c, :], rhs=q3_[:, c, :],
                                              start=True, stop=False), [Zall, QT], [pB0])
                for hh in range(2):
                    M(functools.partial(nc.tensor.matmul, pB0.ap[:, c * 64:(c + 1) * 64], lhsT=v3c(U0pad[hh])[:, c, :],
                                                         rhs=v3c(PMb[hh])[:, c, 64:128], start=False, stop=False),
                      [U0pad[hh], PMb[hh]], [pB0])
                for hh in range(2):
                    M(functools.partial(nc.tensor.matmul, pB0.ap[:, c * 64:(c + 1) * 64], lhsT=v3c(vTpad[hh])[:, c, :],
                                                         rhs=v3c(PMk[hh])[:, c, 64:128], start=False, stop=(hh == 1)),
                      [vTpad[hh], PMk[hh]], [pB0])
            A(functools.partial(nc.scalar.copy, out=y32.ap, in_=pB0.ap), [pB0], [y32])
            A(functools.partial(nc.scalar.copy, out=b1.ap, in_=y32.ap), [y32], [b1])
            M(functools.partial(nc.tensor.matmul, self.psS[:, :], lhsT=blkm, rhs=b1.ap, start=True, stop=True), [b1, CB], [self.psS_r])
            V(functools.partial(nc.vector.tensor_tensor, out=y32.ap, in0=y32.ap, in1=self.psS[:, :], op=ALU.subtract),
              [y32, self.psS_r], [y32])
            A(functools.partial(nc.scalar.activation, out=b1.ap, in_=y32.ap, func=AF.Square), [y32], [b1])
            M(functools.partial(nc.tensor.matmul, self.psS[:, :], lhsT=blkm, rhs=b1.ap, start=True, stop=True), [b1, CB], [self.psS_r])
            V(functools.partial(nc.vector.tensor_scalar, out=t1.ap, in0=self.psS[:, :], scalar1=GN_EPS, scalar2=None, op0=ALU.add),
              [self.psS_r], [t1])
            A(functools.partial(nc.scalar.activation, out=t2.ap, in_=t1.ap, func=AF.Sqrt), [t1], [t2])
            V(functools.partial(nc.vector.reciprocal, out=t1.ap, in_=t2.ap), [t2], [t1])
            V(functools.partial(nc.vector.tensor_tensor, out=y32.ap, in0=y32.ap, in1=t1.ap, op=ALU.mult), [y32, t1], [y32])
            V(functools.partial(nc.vector.tensor_scalar, out=y32.ap, in0=y32.ap, scalar1=lwc[:, p:p + 1], scalar2=lbc[:, p:p + 1],
                                                  op0=ALU.mult, op1=ALU.add), [y32, PP], [y32])
            V(functools.partial(nc.vector.tensor_tensor, out=y32.ap, in0=y32.ap, in1=bonus.ap, op=ALU.add), [y32, bonus], [y32])
            ps = self.psa()
            for kc2 in range(2):
                M(functools.partial(nc.tensor.matmul, ps.ap, lhsT=g2p[kc2], rhs=ing.ap[:, kc2 * T2:(kc2 + 1) * T2],
                                                         start=(kc2 == 0), stop=(kc2 == 1)), [wt, ing], [ps])
            V(functools.partial(nc.vector.tensor_tensor, out=yg3[:, p, :], in0=ps.ap, in1=y32.ap, op=ALU.mult),
              [ps, y32], [yg])
        self.release(m3)
        if PAIRM:
            dst = self.ygs.rearrange("(s q p) t -> s p q t", s=self.NSB, q=8)[sb]
            P.dma("sp", functools.partial(nc.sync.dma_start, out=dst, in_=yg3), self.ygs_b, reads=[yg.b], writes=[self.ygs_b])
            P.dma("pool", functools.partial(nc.gpsimd.collective_compute, "AllGather", ALU.bypass, replica_groups=self.groups,
                                            ins=[self.ygs[sb * 1024:(sb + 1) * 1024, :]],
                                            outs=[self.ygr[sb * 2048:(sb + 1) * 2048, :]]), self.ygr_b,
                  reads=[self.ygs_b], writes=[self.ygr_b], inc=1)
        else:
            self.out_proj("rwo", L, yg, T2, t0, lambda inp: inp["rwkv_w_o"][ia], first)
        self.release(m)


N_CORES = 8
SEQ = 4096
_CACHE = {}


def kernel(**inputs):
    inp = {k: np.asarray(v) for k, v in inputs.items()}
    x = inp["x"].astype(np.float32, copy=False)
    B, T, Dm = x.shape
    assert (2 * B, T, Dm) == (N_CORES, SEQ, D)
    half_t = T // 2
    if "b" not in _CACHE:
        b = Builder(half_t, [0, 1, 2, 3], final_norm=True, pair_mode=True)
        b.build()
        _CACHE["b"] = b
    b = _CACHE["b"]
    packs = [b.pack(inp, core=h) for h in range(2)]
    in_maps = []
    for c in range(N_CORES):
        bi, h = divmod(c, 2)
        xfull = np.ascontiguousarray(x[bi].T).reshape(KC, 128, T)
        xT = np.ascontiguousarray(xfull[:, :, h * half_t:(h + 1) * half_t])
        wp, pp, cp = packs[h]
        in_maps.append({"xT": xT, "xfullT": xfull, "wpack": wp, "ppack": pp, "cpack": cp})
    res = run_bass_kernel_spmd(b.nc, in_maps, core_ids=list(range(N_CORES)))
    out = np.empty((B, T, Dm), np.float32)
    for c in range(N_CORES):
        bi, h = divmod(c, 2)
        out[bi, h * half_t:(h + 1) * half_t] = np.asarray(res.results[c]["yT"]).reshape(Dm, half_t).T
    return out
```

```python
import contextlib
import functools
import math
import numpy as np
import concourse.bass as bass
import concourse.mybir as mybir
from concourse.bass_utils import run_bass_kernel_spmd

F32 = mybir.dt.float32
BF16 = mybir.dt.bfloat16
AF = mybir.ActivationFunctionType
ALU = mybir.AluOpType
AX = mybir.AxisListType

D = 2048
KC = 16
DFF = 8192
NH = 32
HD = 64
TB = 1024
TT = 512
RMS_EPS = 1e-6
import os as _os
SAME_ENGINE_WAITS = bool(int(_os.environ.get('SAME_ENGINE_WAITS', '1')))
GN_EPS = 64e-5


class Buf:
    __slots__ = ("name", "w", "r", "dsem", "dcount", "excl")

    def __init__(self, name, excl=False):
        self.excl = excl
        self.name = name
        self.w = None
        self.r = []
        self.dsem = None
        self.dcount = 0


class Op:
    __slots__ = ("eng", "fn", "reads", "writes", "dma", "dbuf", "tile_idx", "uses_tile",
                 "need_inc", "sem", "val", "ndma", "inc")

    def __init__(self, eng, fn, reads, writes, dma=False, dbuf=None, tile_idx=None, uses_tile=None, ndma=1, inc=16):
        self.inc = inc
        self.eng = eng
        self.fn = fn
        self.reads = reads
        self.writes = writes
        self.dma = dma
        self.dbuf = dbuf
        self.tile_idx = tile_idx
        self.uses_tile = uses_tile
        self.need_inc = False
        self.sem = None
        self.val = None
        self.ndma = ndma


class Prog:
    ENGS = ("pe", "act", "dve", "pool", "sp")

    def __init__(self):
        self.nc = bass.Bass("TRN2", target_bir_lowering=False)
        self.es = contextlib.ExitStack()
        self.ops = []
        self.cur_tile = None
        nc = self.nc
        self.eng = {"pe": nc.tensor, "act": nc.scalar, "dve": nc.vector, "pool": nc.gpsimd, "sp": nc.sync}
        self.esem = {e: self.es.enter_context(nc.semaphore("sem_" + e)) for e in ("pe", "act", "dve", "pool")}
        self.dsems = []

    def op(self, eng, fn, reads=(), writes=(), uses_tile=None):
        reads = list(reads)
        writes = list(writes)
        for b in reads:
            if b.excl and b not in writes:
                writes.append(b)
        o = Op(eng, fn, list(reads), list(writes),
               uses_tile=uses_tile if uses_tile is not None else self.cur_tile)
        self.ops.append(o)
        return o

    def dma(self, eng, fn, dbuf, reads=(), writes=(), tile_idx=None, ndma=1, inc=16):
        o = Op(eng, fn, list(reads), list(writes), dma=True, dbuf=dbuf, tile_idx=tile_idx, ndma=ndma, inc=inc)
        self.ops.append(o)
        return o

    def barrier(self):
        self.ops.append("BARRIER")

    def _hoist(self, dist):
        loads = {}
        rest = []
        for o in self.ops:
            if o != "BARRIER" and o.dma and o.tile_idx is not None:
                loads[o.tile_idx] = o
            else:
                rest.append(o)
        if not loads:
            return
        ntiles = max(loads) + 1
        first_use = {}
        for i, o in enumerate(rest):
            if o != "BARRIER" and o.uses_tile is not None and o.uses_tile not in first_use:
                first_use[o.uses_tile] = i
        inserts = {}
        for j in range(ntiles):
            t = max(0, j - dist)
            while t not in first_use and t < ntiles:
                t += 1
            pos = first_use.get(t, len(rest))
            inserts.setdefault(pos, []).append(loads[j])
        out = []
        for i, o in enumerate(rest):
            if i in inserts:
                out.extend(inserts[i])
            out.append(o)
        if len(rest) in inserts:
            out.extend(inserts[len(rest)])
        self.ops = out

    def finalize(self, hoist_dist=2):
        self._hoist(hoist_dist)
        nc = self.nc
        deps = []
        bufs_seen = {}
        all_bufs = []

        def reg(b):
            if id(b) not in bufs_seen:
                bufs_seen[id(b)] = b
                all_bufs.append(b)

        for o in self.ops:
            if o == "BARRIER":
                deps.append(None)
                continue
            d = []
            for b in o.reads:
                reg(b)
                if b.w is not None:
                    d.append(b.w)
            for b in o.writes:
                reg(b)
                if b.w is not None:
                    d.append(b.w)
                d.extend(b.r)
            dd = []
            seen = set()
            for x in d:
                if id(x) not in seen and x is not o:
                    seen.add(id(x))
                    dd.append(x)
            for x in dd:
                if not x.dma and not (x.eng == o.eng and (o.eng == "pe" or not SAME_ENGINE_WAITS)):
                    x.need_inc = True
            deps.append(dd)
            for b in o.reads:
                if not o.dma:
                    b.r = [x for x in b.r if x.dma or x.eng != o.eng]
                b.r.append(o)
            for b in o.writes:
                b.w = o
                b.r = []
            if o.dma:
                reg(o.dbuf)
        class DS:
            def __init__(self, sem):
                self.sem = sem
                self.count = 0
        ds_by_name = {}
        for o in self.ops:
            if o != "BARRIER" and o.dma and o.dbuf.dsem is None:
                nm = o.dbuf.name
                if nm not in ds_by_name:
                    ds_by_name[nm] = DS(self.es.enter_context(nc.semaphore("ds%d" % len(self.dsems))))
                    self.dsems.append(ds_by_name[nm])
                o.dbuf.dsem = ds_by_name[nm]
        all_ds = list(ds_by_name.values())
        cnt = {e: 0 for e in self.esem}
        known = {e: {} for e in self.ENGS}
        n_wait = 0
        for o, dd in zip(self.ops, deps):
            if o == "BARRIER":
                for e in self.ENGS:
                    eng = self.eng[e]
                    for e2 in self.esem:
                        v = cnt[e2]
                        if v > known[e].get(id(self.esem[e2]), 0):
                            eng.wait_ge(self.esem[e2], v)
                            known[e][id(self.esem[e2])] = v
                    for d_ in all_ds:
                        if d_.count > known[e].get(id(d_.sem), 0):
                            eng.wait_ge(d_.sem, d_.count)
                            known[e][id(d_.sem)] = d_.count
                continue
            e = o.eng
            eng = self.eng[e]
            for x in dd:
                if x.dma:
                    sem, val = x.dbuf.dsem.sem, x.dbuf.dsem.count
                else:
                    if x.eng == e and (e == "pe" or not SAME_ENGINE_WAITS):
                        continue
                    sem, val = x.sem, x.val
                if val > known[e].get(id(sem), 0):
                    eng.wait_ge(sem, val)
                    known[e][id(sem)] = val
                    n_wait += 1
            r = o.fn()
            if o.dma:
                insts = r if isinstance(r, (list, tuple)) else [r]
                assert len(insts) == o.ndma, (len(insts), o.ndma)
                for ins in insts:
                    ins.then_inc(o.dbuf.dsem.sem, o.inc)
                o.dbuf.dsem.count += o.inc * len(insts)
            elif o.need_inc:
                cnt[e] += 1
                r.then_inc(self.esem[e], 1)
                o.sem, o.val = self.esem[e], cnt[e]
        for d_ in all_ds:
            if d_.count > known["sp"].get(id(d_.sem), 0):
                nc.sync.wait_ge(d_.sem, d_.count)
        for e2 in self.esem:
            if cnt[e2] > known["sp"].get(id(self.esem[e2]), 0):
                nc.sync.wait_ge(self.esem[e2], cnt[e2])
        self.stats = dict(nops=len(self.ops), nwait=n_wait, cnt=dict(cnt), ndsem=len(self.dsems))
        self.es.close()
        return nc


class R:
    __slots__ = ("ap", "b")

    def __init__(self, ap, b):
        self.ap = ap
        self.b = b


class Builder:
    def __init__(self, tok, layers, final_norm, pair_mode=False):
        self.P = Prog()
        self.nc = self.P.nc
        self.tok = tok
        self.nblk = tok // TB
        self.layers = layers
        self.final_norm = final_norm
        self.pair_mode = pair_mode
        nc = self.nc
        P = self.P
        if pair_mode:
            self.xfullT = nc.dram_tensor("xfullT", [KC, 128, 2 * tok], F32, kind="ExternalInput").ap()
            self.xfull_b = Buf("xfullT")
            self.xg3 = nc.dram_tensor("xg3", [2 * KC * 128, tok], F32).ap()
            self.xg3_b = Buf("xg3")
            self.tail_send = nc.dram_tensor("tail_send", [KC * 128, 128], F32).ap()
            self.tail_send_b = Buf("tail_send")
            self.tail_recv = nc.dram_tensor("tail_recv", [2 * KC * 128, 128], F32).ap()
            self.tail_recv_b = Buf("tail_recv")
            self.NSB = 2 * tok // 512
            self.ygs = nc.dram_tensor("ygs", [self.NSB * 8 * 128, 512], BF16).ap()
            self.ygs_b = Buf("ygs")
            self.ygr = nc.dram_tensor("ygr", [2 * self.NSB * 8 * 128, 512], BF16).ap()
            self.ygr_b = Buf("ygr")
            self.groups = [[0, 1], [2, 3], [4, 5], [6, 7]]
        self.xT = nc.dram_tensor("xT", [KC, 128, tok], F32, kind="ExternalInput").ap()
        self.yT = nc.dram_tensor("yT", [KC, 128, tok], F32, kind="ExternalOutput").ap()
        self.xs = nc.dram_tensor("xs", [KC, 128, tok], F32).ap()
        self.xT_b = Buf("xT")
        self.xs_b = Buf("xs")
        self.yT_b = Buf("yT")
        self.wgetters = []
        self.wkeys = {}
        self.woff = 0
        self.pgetters = []
        self.pkeys = {}
        self.pcol = 0
        self.cgetters = []
        self.ccol = 0
        self.NPCOL = 1024
        self.NCCOL = 1280
        self.AW = 53200
        self.arena = P.es.enter_context(nc.sbuf_tensor("arena", [128, self.AW], F32))
        self.atop = 0
        self.nbufs = 0
        self.psA = P.es.enter_context(nc.psum_tensor("psA", [128, 2048], F32))
        self.psB = P.es.enter_context(nc.psum_tensor("psB", [128, 1024], F32))
        self.psT = P.es.enter_context(nc.psum_tensor("psT", [128, 1024], BF16))
        self.psS = P.es.enter_context(nc.psum_tensor("psS", [128, 512], F32))
        self.psA_r = [R(self.psA[:, i * 512:(i + 1) * 512], Buf("psA%d" % i, True)) for i in range(4)]
        self.psB_r = [R(self.psB[:, i * 512:(i + 1) * 512], Buf("psB%d" % i, True)) for i in range(2)]
        self.psT_r = R(self.psT[:, :], Buf("psT", True))
        self.psS_r = R(self.psS[:, :], Buf("psS", True))
        self.rotA = 0
        self.rotB = 0
        self.pp = self.alloc("pp", self.NPCOL, F32)
        self.cb = self.alloc("cb", self.NCCOL, BF16)
        self.wslots = [self.alloc("wslot%d" % i, 8192, BF16) for i in range(3)]
        self.wtile_n = 0
        self.built = False

    def alloc(self, name, nelem, dt, npart=128):
        words = nelem if dt == F32 else (nelem + 1) // 2
        off = self.atop
        self.atop += words
        assert self.atop <= self.AW, ("SBUF arena overflow", name, self.atop)
        ap = self.arena[:, off:off + words]
        if dt != F32:
            ap = ap.bitcast(dt)
        if npart != 128:
            ap = ap[0:npart]
        return R(ap, Buf(name))

    def mark(self):
        return self.atop

    def release(self, m):
        self.P.barrier()
        self.atop = m

    def psa(self):
        r = self.psA_r[self.rotA % 4]
        self.rotA += 1
        return r

    def psb(self):
        r = self.psB_r[self.rotB % 2]
        self.rotB += 1
        return r

    def param(self, key, ncols, getter):
        if key not in self.pkeys:
            self.pkeys[key] = self.pcol
            self.pgetters.append((self.pcol, ncols, getter))
            self.pcol += ncols
            assert self.pcol <= self.NPCOL
        c = self.pkeys[key]
        return self.pp.ap[:, c:c + ncols]

    def dvec(self, key, getter):
        return self.param(key, KC, lambda inp, g=getter: np.ascontiguousarray(
            np.asarray(g(inp), np.float32).reshape(KC, 128).T))

    def const(self, ncols, arr):
        c = self.ccol
        self.cgetters.append((c, ncols, arr if callable(arr) else np.asarray(arr, np.float32)))
        self.ccol += ncols
        assert self.ccol <= self.NCCOL
        return self.cb.ap[:, c:c + ncols]

    def wtile(self, key, npart, nelem, getter):
        assert nelem <= 8192
        if key not in self.wkeys:
            self.wkeys[key] = self.woff
            self.wgetters.append((self.woff, npart, nelem, getter))
            self.woff += npart * nelem
        off = self.wkeys[key]
        idx = self.wtile_n
        self.wtile_n += 1
        slot = self.wslots[idx % 3]
        dst = slot.ap[0:npart, 0:nelem]

        def fn(off=off, npart=npart, nelem=nelem, dst=dst):
            src = self.wpack[off:off + npart * nelem].rearrange("(p n) -> p n", p=npart)
            return self.nc.gpsimd.dma_start(out=dst, in_=src)

        self.P.dma("pool", fn, slot.b, writes=[slot.b], tile_idx=idx)
        self.P.cur_tile = idx
        return R(dst, slot.b)

    def setup_consts(self):
        P, nc = self.P, self.nc
        self.ones = self.const(128, np.ones((128, 128)))
        self.ident = self.const(128, np.eye(128))
        bd = np.zeros((128, 128))
        bd[:64, :64] = 1
        bd[64:, 64:] = 1
        self.blk64 = self.const(128, bd)

    def load_consts(self):
        P, nc = self.P, self.nc

        def f1():
            return nc.sync.dma_start(out=self.pp.ap, in_=self.ppack)

        P.dma("sp", f1, self.pp.b, writes=[self.pp.b])

        def f2():
            return nc.gpsimd.dma_start(out=self.cb.ap, in_=self.cpack)

        P.dma("pool", f2, self.cb.b, writes=[self.cb.b])

    def x_src(self, first):
        return (self.xT, self.xT_b) if first else (self.xs, self.xs_b)

    def load_x(self, dst, src, srcb, t0, n):
        nc = self.nc
        d3 = dst.ap.rearrange("p (k t) -> p k t", k=KC)

        def fn():
            return nc.sync.dma_start(out=d3, in_=src.rearrange("k p t -> p k t")[:, :, t0:t0 + n])

        self.P.dma("sp", fn, dst.b, reads=[srcb], writes=[dst.b])

    def store_x(self, srcr, dst, dstb, t0, n):
        nc = self.nc
        s3 = srcr.ap.rearrange("p (k t) -> p k t", k=KC)

        def fn():
            return nc.sync.dma_start(out=dst.rearrange("k p t -> p k t")[:, :, t0:t0 + n], in_=s3)

        self.P.dma("sp", fn, dstb, reads=[srcr.b], writes=[dstb])

    def rmsnorm(self, xr, n, gain, hr, hoff=0, hstride=None):
        P, nc = self.P, self.nc
        hstride = hstride or n
        x3 = xr.ap.rearrange("p (k t) -> p k t", k=KC)
        h3 = hr.ap.rearrange("p (k t) -> p k t", k=KC)
        m = self.mark()
        sq = [self.alloc("sq%d" % i, TT, BF16) for i in range(3)]
        rs = self.alloc("rs", TT, F32)
        rs2 = self.alloc("rs2", TT, F32)
        ntt = (n + TT - 1) // TT
        for tt in range(ntt):
            w = min(TT, n - tt * TT)
            sl = slice(tt * TT, tt * TT + w)
            for kc in range(KC):
                s = sq[kc % 3]
                P.op("act", functools.partial(nc.scalar.activation,
                    out=s.ap[:, 0:w], in_=x3[:, kc, sl], func=AF.Square), reads=[xr.b], writes=[s.b])
                P.op("pe", functools.partial(nc.tensor.matmul,
                    self.psS[:, 0:w], lhsT=self.ones, rhs=s.ap[:, 0:w], start=(kc == 0), stop=(kc == KC - 1)),
                    reads=[s.b, self.cb.b], writes=[self.psS_r.b])
            P.op("dve", functools.partial(nc.vector.tensor_scalar,
                out=rs.ap[:, 0:w], in0=self.psS[:, 0:w], scalar1=1.0 / D, scalar2=RMS_EPS,
                op0=ALU.mult, op1=ALU.add), reads=[self.psS_r.b], writes=[rs.b])
            P.op("act", functools.partial(nc.scalar.activation, out=rs2.ap[:, 0:w], in_=rs.ap[:, 0:w], func=AF.Sqrt),
                 reads=[rs.b], writes=[rs2.b])
            P.op("dve", functools.partial(nc.vector.reciprocal, out=rs.ap[:, 0:w], in_=rs2.ap[:, 0:w]),
                 reads=[rs2.b], writes=[rs.b])
            for kc in range(KC):
                P.op("dve", functools.partial(nc.vector.scalar_tensor_tensor,
                    out=h3[:, kc, hoff + sl.start:hoff + sl.start + w], in0=x3[:, kc, sl],
                    scalar=gain[:, kc:kc + 1], in1=rs.ap[:, 0:w], op0=ALU.mult, op1=ALU.mult),
                    reads=[xr.b, rs.b, self.pp.b], writes=[hr.b])
        self.release(m)

    def mlp_phase(self, L, blk, first):
        P, nc = self.P, self.nc
        t0 = blk * TB
        m = self.mark()
        xb = self.alloc("xblk", KC * TB, F32)
        h = self.alloc("h", KC * TB, BF16)
        hh = [self.alloc("hh%d" % i, 4 * TB, BF16) for i in range(2)]
        tmp = [self.alloc("rl%d" % i, TT, F32) for i in range(2)]
        src, srcb = self.x_src(first)
        self.load_x(xb, src, srcb, t0, TB)
        gain = self.dvec(("norm_ffn", L), lambda inp, L=L: inp["norm_ffn"][L])
        self.rmsnorm(xb, TB, gain, h)
        x3 = xb.ap.rearrange("p (k t) -> p k t", k=KC)
        h3 = h.ap.rearrange("p (k t) -> p k t", k=KC)
        NG = DFF // 512
        ntt = TB // TT
        nrl = [0]

        def up(g):
            wt = self.wtile(("up", L, g), 128, KC * 512,
                            lambda inp, L=L, g=g: inp["mlp_w_up"][L][:, g * 512:(g + 1) * 512]
                            .reshape(KC, 128, 512).transpose(1, 0, 2).reshape(128, KC * 512))
            w3 = wt.ap.rearrange("p (k m) -> p k m", k=KC)
            hg = hh[g % 2]
            hg3 = hg.ap.rearrange("p (j t) -> p j t", j=4)
            for j in range(4):
                for tt in range(ntt):
                    ps = self.psb()
                    for kc in range(KC):
                        P.op("pe", functools.partial(nc.tensor.matmul,
                            ps.ap, lhsT=w3[:, kc, j * 128:(j + 1) * 128], rhs=h3[:, kc, tt * TT:(tt + 1) * TT],
                            start=(kc == 0), stop=(kc == KC - 1)), reads=[wt.b, h.b], writes=[ps.b])
                    tm = tmp[nrl[0] % 2]
                    nrl[0] += 1
                    P.op("act", functools.partial(nc.scalar.activation, out=tm.ap, in_=ps.ap, func=AF.Relu),
                         reads=[ps.b], writes=[tm.b])
                    P.op("pool", functools.partial(nc.gpsimd.tensor_tensor,
                        out=hg3[:, j, tt * TT:(tt + 1) * TT], in0=tm.ap, in1=tm.ap, op=ALU.mult),
                        reads=[tm.b], writes=[hg.b])

        def down(g):
            wt = self.wtile(("dn", L, g), 128, 4 * D,
                            lambda inp, L=L, g=g: inp["mlp_w_down"][L][g * 512:(g + 1) * 512, :]
                            .reshape(4, 128, D).transpose(1, 0, 2).reshape(128, 4 * D))
            w3 = wt.ap.rearrange("p (j d) -> p j d", j=4)
            hg = hh[g % 2]
            hg3 = hg.ap.rearrange("p (j t) -> p j t", j=4)
            for dc in range(KC):
                for tt in range(ntt):
                    ps = self.psa()
                    for j in range(4):
                        P.op("pe", functools.partial(nc.tensor.matmul,
                            ps.ap, lhsT=w3[:, j, dc * 128:(dc + 1) * 128], rhs=hg3[:, j, tt * TT:(tt + 1) * TT],
                            start=(j == 0), stop=(j == 3)), reads=[wt.b, hg.b], writes=[ps.b])
                    P.op("dve", functools.partial(nc.vector.tensor_tensor,
                        out=x3[:, dc, tt * TT:(tt + 1) * TT], in0=ps.ap, in1=x3[:, dc, tt * TT:(tt + 1) * TT],
                        op=ALU.add), reads=[ps.b, xb.b], writes=[xb.b])

        up(0)
        for g in range(NG):
            if g + 1 < NG:
                up(g + 1)
            down(g)
        return xb, m

    def finish_block(self, xb, m, blk, last):
        P, nc = self.P, self.nc
        t0 = blk * TB
        if last and self.final_norm:
            gain = self.dvec(("norm_final",), lambda inp: inp["norm_final"])
            self.final_rms(xb, gain, t0)
        elif last:
            self.store_x(xb, self.yT, self.yT_b, t0, TB)
        else:
            self.store_x(xb, self.xs, self.xs_b, t0, TB)
        self.release(m)

    def final_rms(self, xb, gain, t0):
        P, nc = self.P, self.nc
        x3 = xb.ap.rearrange("p (k t) -> p k t", k=KC)
        m = self.mark()
        sq = [self.alloc("fsq%d" % i, TT, BF16) for i in range(3)]
        rs = self.alloc("frs", TT, F32)
        rs2 = self.alloc("frs2", TT, F32)
        for tt in range(TB // TT):
            sl = slice(tt * TT, (tt + 1) * TT)
            for kc in range(KC):
                s = sq[kc % 3]
                P.op("act", functools.partial(nc.scalar.activation,
                    out=s.ap, in_=x3[:, kc, sl], func=AF.Square), reads=[xb.b], writes=[s.b])
                P.op("pe", functools.partial(nc.tensor.matmul,
                    self.psS[:, :], lhsT=self.ones, rhs=s.ap, start=(kc == 0), stop=(kc == KC - 1)),
                    reads=[s.b, self.cb.b], writes=[self.psS_r.b])
            P.op("dve", functools.partial(nc.vector.tensor_scalar,
                out=rs.ap, in0=self.psS[:, :], scalar1=1.0 / D, scalar2=RMS_EPS,
                op0=ALU.mult, op1=ALU.add), reads=[self.psS_r.b], writes=[rs.b])
            P.op("act", functools.partial(nc.scalar.activation, out=rs2.ap, in_=rs.ap, func=AF.Sqrt),
                 reads=[rs.b], writes=[rs2.b])
            P.op("dve", functools.partial(nc.vector.reciprocal, out=rs.ap, in_=rs2.ap), reads=[rs2.b], writes=[rs.b])
            for kc in range(KC):
                P.op("dve", functools.partial(nc.vector.scalar_tensor_tensor,
                    out=x3[:, kc, sl], in0=x3[:, kc, sl], scalar=gain[:, kc:kc + 1], in1=rs.ap,
                    op0=ALU.mult, op1=ALU.mult), reads=[xb.b, rs.b, self.pp.b], writes=[xb.b])
        self.store_x(xb, self.yT, self.yT_b, t0, TB)
        self.release(m)

    def build(self):
        self.setup_consts()
        self.load_consts()
        self.setup_persist()
        first_layer = True
        for li, spec in enumerate(self.layers):
            last_layer = (li == len(self.layers) - 1)
            mlp_only = isinstance(spec, tuple)
            L = spec[1] if mlp_only else spec
            kind = L % 3
            if self.pair_mode and not mlp_only and kind == 0:
                self.rwkv_layer_pair(L, first_layer)
            for blk in range(self.nblk):
                if mlp_only:
                    pass
                elif kind == 0:
                    if not self.pair_mode:
                        self.rwkv_phase(L, blk, first_layer)
                elif kind == 1:
                    self.swa_phase(L, blk, first_layer)
                elif kind == 2:
                    self.conv_phase(L, blk, first_layer)
                xb, m = self.mlp_phase(L, blk, first=(first_layer and mlp_only))
                self.finish_block(xb, m, blk, last_layer)
            if self.pair_mode and not last_layer:
                nc_, P_ = self.nc, self.P
                nxt = self.layers[li + 1]
                nxt = (nxt[1] if isinstance(nxt, tuple) else nxt) % 3
                if nxt == 0:
                    for kc in range(KC):
                        P_.dma("pool", functools.partial(nc_.gpsimd.collective_compute, "AllGather", ALU.bypass,
                                                         replica_groups=self.groups,
                                                         ins=[self.xs[kc]], outs=[self.xg3[kc * 256:(kc + 1) * 256, :]]),
                               self.xg3_b, reads=[self.xs_b], writes=[self.xg3_b], inc=1)
                else:
                    P_.dma("sp", functools.partial(nc_.sync.dma_start,
                                                   out=self.tail_send.rearrange("(k p) t -> k p t", k=KC),
                                                   in_=self.xs[:, :, self.tok - 128:self.tok]),
                           self.tail_send_b, reads=[self.xs_b], writes=[self.tail_send_b])
                    P_.dma("pool", functools.partial(nc_.gpsimd.collective_compute, "AllGather", ALU.bypass,
                                                     replica_groups=self.groups,
                                                     ins=[self.tail_send], outs=[self.tail_recv]),
                           self.tail_recv_b, reads=[self.tail_send_b], writes=[self.tail_recv_b], inc=1)
            first_layer = False
        nc = self.nc
        self.wpack = nc.dram_tensor("wpack", [max(self.woff, 128)], F32, kind="ExternalInput").ap()
        self.ppack = nc.dram_tensor("ppack", [128, self.NPCOL], F32, kind="ExternalInput").ap()
        self.cpack = nc.dram_tensor("cpack", [128, self.NCCOL], F32, kind="ExternalInput").ap()
        self.P.finalize(hoist_dist=2)
        self.built = True
        return nc

    def pack(self, inp, core=0):
        inp = dict(inp)
        inp["_core"] = core
        wp = np.zeros(max(self.woff, 128), np.float32)
        for off, npart, nelem, g in self.wgetters:
            a = np.asarray(g(inp), np.float32)
            assert a.shape == (npart, nelem), (a.shape, npart, nelem)
            wp[off:off + npart * nelem] = a.reshape(-1)
        pp = np.zeros((128, self.NPCOL), np.float32)
        for c, n, g in self.pgetters:
            a = np.asarray(g(inp), np.float32)
            assert a.shape == (128, n), (a.shape, n)
            pp[:, c:c + n] = a
        cp = np.zeros((128, self.NCCOL), np.float32)
        for c, n, a in self.cgetters:
            cp[:, c:c + n] = a(core) if callable(a) else a
        return wp, pp, cp

    def out_proj(self, key, L, z, n, t0, wget, first, bias=None):
        P, nc = self.P, self.nc
        z3 = z.ap.rearrange("p (k t) -> p k t", k=KC)
        src, srcb = self.x_src(first)
        m = self.mark()
        st = [self.alloc("ost%d" % i, TT, F32) for i in range(4)]
        ntt = n // TT
        k = 0
        for g in range(4):
            wt = self.wtile((key, L, g), 128, KC * 512,
                            lambda inp, g=g: wget(inp)[:, g * 512:(g + 1) * 512]
                            .reshape(KC, 128, 512).transpose(1, 0, 2).reshape(128, KC * 512))
            w3 = wt.ap.rearrange("p (k m) -> p k m", k=KC)
            for j in range(4):
                mc = g * 4 + j
                for tt in range(ntt):
                    s_ = st[k % 4]
                    k += 1
                    sl = slice(t0 + tt * TT, t0 + (tt + 1) * TT)
                    P.dma("sp", functools.partial(nc.sync.dma_start, out=s_.ap, in_=src[mc, :, sl]),
                          s_.b, reads=[srcb], writes=[s_.b])
                    ps = self.psa()
                    for kc in range(KC):
                        P.op("pe", functools.partial(nc.tensor.matmul,
                            ps.ap, lhsT=w3[:, kc, j * 128:(j + 1) * 128], rhs=z3[:, kc, tt * TT:(tt + 1) * TT],
                            start=(kc == 0), stop=(kc == KC - 1)), reads=[wt.b, z.b], writes=[ps.b])
                    if bias is None:
                        P.op("dve", functools.partial(nc.vector.tensor_tensor,
                            out=s_.ap, in0=ps.ap, in1=s_.ap, op=ALU.add), reads=[ps.b, s_.b], writes=[s_.b])
                    else:
                        P.op("dve", functools.partial(nc.vector.scalar_tensor_tensor,
                            out=s_.ap, in0=ps.ap, scalar=bias[:, mc:mc + 1], in1=s_.ap, op0=ALU.add, op1=ALU.add),
                            reads=[ps.b, s_.b, self.pp.b], writes=[s_.b])
                    P.dma("sp", functools.partial(nc.sync.dma_start, out=self.xs[mc, :, sl], in_=s_.ap),
                          self.xs_b, reads=[s_.b], writes=[self.xs_b])
        self.release(m)

    def norm_block(self, L, key, t0, n, first, h, hoff=0, hstride=None, src=None, srcb=None):
        gain = self.dvec((key, L), lambda inp, L=L, key=key: inp[key][L])
        if src is None:
            src, srcb = self.x_src(first)
        for s0 in range(0, n, TT):
            w = min(TT, n - s0)
            m = self.mark()
            xb = self.alloc("xnb", KC * w, F32)
            self.load_x(xb, src, srcb, t0 + s0, w)
            self.rmsnorm(xb, w, gain, h, hoff=hoff + s0, hstride=hstride)
            self.release(m)

    def conv_phase(self, L, blk, first):
        P, nc = self.P, self.nc
        t0 = blk * TB
        ic = L // 3
        HAL = 2 if (self.pair_mode and blk == 0) else 0
        m = self.mark()
        h = self.alloc("h", KC * (HAL + TB), BF16)
        self.norm_block(L, "norm_mix", t0, TB, first, h, hoff=HAL, hstride=HAL + TB)
        if HAL:
            tsrc = self.tail_recv.rearrange("(r k p) t -> r k p t", r=2, k=KC)[0][:, :, 126:128]
            self.norm_block(L, "norm_mix", 0, HAL, first, h, hoff=0, hstride=HAL + TB, src=tsrc, srcb=self.tail_recv_b)
            m01 = self.param(("mask01",), 1, lambda inp: np.full((128, 1), float(inp["_core"] % 2), np.float32))
        z = self.alloc("z", KC * TB, BF16)
        h3 = h.ap.rearrange("p (k t) -> p k t", k=KC)
        z3 = z.ap.rearrange("p (k t) -> p k t", k=KC)
        u = [self.alloc("u%d" % i, TB + 2, F32) for i in range(2)]
        uc = [self.alloc("uc%d" % i, TB, F32) for i in range(2)]
        tcg = [self.alloc("tcg%d" % i, TB + 2, F32) for i in range(2)]
        cw = [self.dvec(("conv_w", ic, tap), lambda inp, ic=ic, tap=tap: inp["conv_w"][ic][tap]) for tap in range(3)]
        uh3 = self.uhalo.ap.rearrange("p (k t) -> p k t", k=KC)
        ntt = TB // TT
        for j in range(KC):
            def getter(inp, j=j, ic=ic):
                W = inp["conv_w_in"][ic]
                cols = np.concatenate([W[:, D + j * 128:D + (j + 1) * 128], W[:, 2 * D + j * 128:2 * D + (j + 1) * 128],
                                       W[:, j * 128:(j + 1) * 128]], axis=1)
                return cols.reshape(KC, 128, 384).transpose(1, 0, 2).reshape(128, KC * 384)
            wt = self.wtile(("cin", L, j), 128, KC * 384, getter)
            w3 = wt.ap.rearrange("p (k m) -> p k m", k=KC)
            uj, ucj, tj = u[j % 2], uc[j % 2], tcg[j % 2]
            if not HAL:
                P.op("pool", functools.partial(nc.gpsimd.tensor_copy, out=uj.ap[:, 0:2], in_=uh3[:, j, :]),
                     reads=[self.uhalo.b], writes=[uj.b])
            for part in range(3):
                if part == 2:
                    P.op("pool", functools.partial(nc.gpsimd.tensor_scalar,
                        out=ucj.ap, in0=uj.ap[:, 2:2 + TB], scalar1=cw[2][:, j:j + 1], scalar2=None, op0=ALU.mult),
                        reads=[uj.b, self.pp.b], writes=[ucj.b])
                    P.op("dve", functools.partial(nc.vector.scalar_tensor_tensor,
                        out=ucj.ap, in0=uj.ap[:, 1:1 + TB], scalar=cw[1][:, j:j + 1], in1=ucj.ap,
                        op0=ALU.mult, op1=ALU.add), reads=[uj.b, ucj.b, self.pp.b], writes=[ucj.b])
                    P.op("dve", functools.partial(nc.vector.scalar_tensor_tensor,
                        out=ucj.ap, in0=uj.ap[:, 0:TB], scalar=cw[0][:, j:j + 1], in1=ucj.ap,
                        op0=ALU.mult, op1=ALU.add), reads=[uj.b, ucj.b, self.pp.b], writes=[ucj.b])
                    P.op("pool", functools.partial(nc.gpsimd.tensor_copy, out=uh3[:, j, :], in_=uj.ap[:, TB:TB + 2]),
                         reads=[uj.b], writes=[self.uhalo.b])
                tiles = [(HAL + tt * TT, TT) for tt in range(ntt)]
                if HAL and part < 2:
                    tiles = [(0, HAL)] + tiles
                for (c0, cw_) in tiles:
                    ps = self.psa()
                    for kc in range(KC):
                        P.op("pe", functools.partial(nc.tensor.matmul,
                            ps.ap[:, 0:cw_], lhsT=w3[:, kc, part * 128:(part + 1) * 128], rhs=h3[:, kc, c0:c0 + cw_],
                            start=(kc == 0), stop=(kc == KC - 1)), reads=[wt.b, h.b], writes=[ps.b])
                    r0 = c0 - HAL
                    if part == 0:
                        P.op("act", functools.partial(nc.scalar.copy, out=tj.ap[:, 2 + r0:2 + r0 + cw_], in_=ps.ap[:, 0:cw_]),
                             reads=[ps.b], writes=[tj.b])
                    elif part == 1:
                        if r0 < 0:
                            P.op("dve", functools.partial(nc.vector.scalar_tensor_tensor,
                                out=uj.ap[:, 0:2], in0=ps.ap[:, 0:2], scalar=m01[:, 0:1], in1=tj.ap[:, 0:2],
                                op0=ALU.mult, op1=ALU.mult), reads=[ps.b, tj.b, self.pp.b], writes=[uj.b])
                        else:
                            P.op("dve", functools.partial(nc.vector.tensor_tensor,
                                out=uj.ap[:, 2 + r0:2 + r0 + cw_], in0=ps.ap[:, 0:cw_], in1=tj.ap[:, 2 + r0:2 + r0 + cw_],
                                op=ALU.mult), reads=[ps.b, tj.b], writes=[uj.b])
                    else:
                        P.op("dve", functools.partial(nc.vector.tensor_tensor,
                            out=z3[:, j, r0:r0 + cw_], in0=ps.ap[:, 0:cw_], in1=ucj.ap[:, r0:r0 + cw_], op=ALU.mult),
                            reads=[ps.b, ucj.b], writes=[z.b])
        self.out_proj("cout", L, z, TB, t0, lambda inp, ic=ic: inp["conv_w_out"][ic], first)
        self.release(m)

    def setup_persist(self):
        P, nc = self.P, self.nc
        kinds = set((l[1] if isinstance(l, tuple) else l) % 3 for l in self.layers if not isinstance(l, tuple))
        if 2 in kinds:
            self.uhalo = self.alloc("uhalo", KC * 2, F32)
            P.op("pool", functools.partial(nc.gpsimd.memset, self.uhalo.ap, 0.0), writes=[self.uhalo.b])
        if 0 in kinds:
            self.hlast = self.alloc("hlast", KC, BF16)

    def swa_phase(self, L, blk, first):
        P, nc = self.P, self.nc
        t0 = blk * TB
        ib = L // 3
        NQB = TB // 128
        if not hasattr(self, "swa_k_st"):
            self.swa_k_st = nc.dram_tensor("swa_k_st", [128, 4 * 128], BF16).ap()
            self.swa_v_st = nc.dram_tensor("swa_v_st", [128, 8 * 128], BF16).ap()
            self.swa_st_b = Buf("swa_st")
            NEG = -30000.0
            qi = np.arange(128)[:, None]
            kj = np.arange(256)[None, :]
            ok = (kj > qi) & (kj <= qi + 128)
            self.c_mask = self.const(256, np.where(ok, 0.0, NEG))
            m_all = np.where(ok, 0.0, NEG)
            m_first = np.where(ok & (kj >= 128), 0.0, NEG)
            if self.pair_mode:
                self.c_mask0 = self.const(256, lambda core: m_first if core % 2 == 0 else m_all)
            else:
                self.c_mask0 = self.const(256, m_first)
        Wq = lambda inp: inp["swa_w_qkv"][ib]
        bq = lambda inp: inp["swa_b_qkv"][ib]
        b_q = self.param(("swa_bq", ib), KC, lambda inp: np.ascontiguousarray(bq(inp)[:D].reshape(KC, 128).T))
        b_k = self.param(("swa_bk", ib), 4, lambda inp: np.stack(
            [np.concatenate([bq(inp)[D + j * 64:D + (j + 1) * 64]] * 2) for j in range(4)], axis=1))
        b_v = self.param(("swa_bv", ib), 2, lambda inp: np.ascontiguousarray(bq(inp)[D + 256:D + 512].reshape(2, 128).T))
        b_o = self.dvec(("swa_bo", ib), lambda inp: inp["swa_b_o"][ib])
        sink = self.param(("swa_sink", ib), NH, lambda inp: np.tile(inp["swa_sinks"][ib][None, :], (128, 1)))
        m = self.mark()
        q_all = self.alloc("q_all", KC * TB, BF16)
        kbuf = self.alloc("kbuf", 4 * (128 + TB), BF16)
        vtp = self.alloc("vtp", (NQB + 1) * 8 * 128, BF16)
        q3 = q_all.ap.rearrange("p (k t) -> p k t", k=KC)
        k3 = kbuf.ap.rearrange("p (j t) -> p j t", j=4)
        v5 = vtp.ap.rearrange("p (b j v d) -> p b j v d", b=NQB + 1, j=4, v=2)
        HAL = 128 if (self.pair_mode and blk == 0) else 0
        if blk == 0:
            P.op("dve", functools.partial(nc.vector.memset, vtp.ap, 0.0), writes=[vtp.b])
            if not HAL:
                P.op("dve", functools.partial(nc.vector.memset, k3[:, :, 0:128], 0.0), writes=[kbuf.b])
        else:
            P.op("dve", functools.partial(nc.vector.memset, vtp.ap[:, 8 * 128:], 0.0), writes=[vtp.b])
            P.dma("sp", functools.partial(nc.sync.dma_start, out=vtp.ap[:, 0:8 * 128], in_=self.swa_v_st), vtp.b,
                  reads=[self.swa_st_b], writes=[vtp.b])
            P.dma("sp", functools.partial(nc.sync.dma_start, out=k3[:, :, 0:128],
                                                  in_=self.swa_k_st.rearrange("p (j t) -> p j t", j=4)), kbuf.b,
                  reads=[self.swa_st_b], writes=[kbuf.b])
        m2 = self.mark()
        h = self.alloc("h", KC * (HAL + TB), BF16)
        self.norm_block(L, "norm_mix", t0, TB, first, h, hoff=HAL, hstride=HAL + TB)
        if HAL:
            tsrc = self.tail_recv.rearrange("(r k p) t -> r k p t", r=2, k=KC)[0]
            self.norm_block(L, "norm_mix", 0, HAL, first, h, hoff=0, hstride=HAL + TB, src=tsrc, srcb=self.tail_recv_b)
        vfm = self.alloc("vfm", 2 * (128 + TB), BF16)
        h3 = h.ap.rearrange("p (k t) -> p k t", k=KC)
        vf3 = vfm.ap.rearrange("p (c t) -> p c t", c=2)
        ntt = TB // TT
        main_tiles = [(HAL + tt * TT, TT) for tt in range(ntt)]
        kv_tiles = ([(0, HAL)] if HAL else []) + main_tiles

        def proj(key, ncols, getter, epi, tiles):
            wt = self.wtile((key, L), 128, KC * ncols,
                            lambda inp: getter(inp).reshape(KC, 128, ncols).transpose(1, 0, 2).reshape(128, KC * ncols))
            w3 = wt.ap.rearrange("p (k m) -> p k m", k=KC)
            for j in range(ncols // 128):
                for (c0, cw) in tiles:
                    ps = self.psa()
                    for kc in range(KC):
                        P.op("pe", functools.partial(nc.tensor.matmul,
                            ps.ap[:, 0:cw], lhsT=w3[:, kc, j * 128:(j + 1) * 128], rhs=h3[:, kc, c0:c0 + cw],
                            start=(kc == 0), stop=(kc == KC - 1)), reads=[wt.b, h.b], writes=[ps.b])
                    epi(j, c0 - HAL, cw, ps)

        for g in range(4):
            def epi_q(j, c0, cw, ps, g=g):
                mc = g * 4 + j
                P.op("dve", functools.partial(nc.vector.tensor_scalar,
                    out=q3[:, mc, c0:c0 + cw], in0=ps.ap[:, 0:cw], scalar1=b_q[:, mc:mc + 1], scalar2=HD ** -0.5,
                    op0=ALU.add, op1=ALU.mult), reads=[ps.b, self.pp.b], writes=[q_all.b])
            proj(("swa_q", g), 512, lambda inp, g=g: Wq(inp)[:, g * 512:(g + 1) * 512], epi_q, main_tiles)

        def epi_k(j, c0, cw, ps):
            P.op("dve", functools.partial(nc.vector.tensor_scalar,
                out=k3[:, j, 128 + c0:128 + c0 + cw], in0=ps.ap[:, 0:cw], scalar1=b_k[:, j:j + 1], scalar2=None,
                op0=ALU.add), reads=[ps.b, self.pp.b], writes=[kbuf.b])
        proj("swa_k", 512, lambda inp: np.concatenate(
            [Wq(inp)[:, D + (j // 2) * 64:D + (j // 2 + 1) * 64] for j in range(8)], axis=1), epi_k, kv_tiles)

        def epi_v(j, c0, cw, ps):
            P.op("dve", functools.partial(nc.vector.tensor_scalar,
                out=vf3[:, j, 128 + c0:128 + c0 + cw], in0=ps.ap[:, 0:cw], scalar1=b_v[:, j:j + 1], scalar2=None,
                op0=ALU.add), reads=[ps.b, self.pp.b], writes=[vfm.b])
        proj("swa_v", 256, lambda inp: Wq(inp)[:, D + 256:D + 512], epi_v, kv_tiles)
        for bb in range(-1 if HAL else 0, NQB):
            for c in range(2):
                P.op("pe", functools.partial(nc.tensor.transpose,
                    self.psT[:, c * 128:(c + 1) * 128], vf3[:, c, 128 + bb * 128:128 + (bb + 1) * 128], self.ident),
                    reads=[vfm.b, self.cb.b], writes=[self.psT_r.b])
            src4 = self.psT[:, 0:256].rearrange("p (j d) -> p j d", j=4)
            P.op("act", functools.partial(nc.scalar.copy, out=v5[:, bb + 1, :, 0, 0:64], in_=src4),
                 reads=[self.psT_r.b], writes=[vtp.b])
            P.op("dve", functools.partial(nc.vector.tensor_copy, out=v5[:, bb + 1, :, 1, 64:128], in_=src4),
                 reads=[self.psT_r.b], writes=[vtp.b])
        self.release(m2)
        o_all = self.alloc("o_all", KC * TB, BF16)
        o3 = o_all.ap.rearrange("p (k t) -> p k t", k=KC)
        NR = 4
        lm = [self.alloc("lm%d" % i, 512, F32) for i in range(NR)]
        pe_ = [self.alloc("pe%d" % i, 512, F32) for i in range(NR)]
        pn = [self.alloc("pn%d" % i, 512, BF16) for i in range(NR)]
        pT = [self.alloc("pT%d" % i, 512, BF16) for i in range(NR)]
        sm = [self.alloc("sm%d" % i, 16, F32) for i in range(NR)]
        def unit(c, bb, it):
            kvh = c // 4
            if True:
                i = it % NR
                u0 = (it % 2) * 512
                lmr, per, pnr, pTr, smr = lm[i], pe_[i], pn[i], pT[i], sm[i]
                if self.rotA % 2:
                    self.rotA += 1
                psl0 = self.psa()
                psl1 = self.psa()
                pbase = ((self.rotA - 2) % 4) * 512
                for hh, psl in ((0, psl0), (1, psl1)):
                    pr = slice(hh * 64, (hh + 1) * 64)
                    P.op("pe", functools.partial(nc.tensor.matmul,
                        psl.ap[:, 0:256], lhsT=q3[pr, c, bb * 128:(bb + 1) * 128],
                        rhs=k3[pr, kvh, bb * 128:bb * 128 + 256], start=True, stop=True),
                        reads=[q_all.b, kbuf.b], writes=[psl.b])
                mk = self.c_mask0 if (blk == 0 and bb == 0) else self.c_mask
                l3 = lmr.ap.rearrange("p (h k) -> p h k", h=2)
                pl3 = self.psA[:, pbase:pbase + 1024].rearrange("p (h k) -> p h k", h=2)[:, :, 0:256]
                P.op("dve", functools.partial(nc.vector.tensor_tensor,
                    out=l3, in0=pl3, in1=mk.unsqueeze(1).to_broadcast([128, 2, 256]), op=ALU.add),
                    reads=[psl0.b, psl1.b, self.cb.b], writes=[lmr.b])
                s_ = smr.ap
                P.op("dve", functools.partial(nc.vector.tensor_reduce,
                    out=s_[:, 0:2], in_=l3, axis=AX.X, op=ALU.max), reads=[lmr.b], writes=[smr.b])
                P.op("dve", functools.partial(nc.vector.tensor_tensor,
                    out=s_[:, 2:4], in0=s_[:, 0:2], in1=sink[:, 2 * c:2 * c + 2], op=ALU.max),
                    reads=[smr.b, self.pp.b], writes=[smr.b])
                P.op("dve", functools.partial(nc.vector.tensor_scalar,
                    out=s_[:, 4:6], in0=s_[:, 2:4], scalar1=-1.0, scalar2=None, op0=ALU.mult),
                    reads=[smr.b], writes=[smr.b])
                p3 = per.ap.rearrange("p (h k) -> p h k", h=2)
                for hh in range(2):
                    P.op("act", functools.partial(nc.scalar.activation,
                        out=p3[:, hh, :], in_=l3[:, hh, :], func=AF.Exp, bias=s_[:, 4 + hh:5 + hh], scale=1.0,
                        accum_out=s_[:, 6 + hh:7 + hh]), reads=[lmr.b, smr.b], writes=[per.b, smr.b])
                P.op("dve", functools.partial(nc.vector.tensor_tensor,
                    out=s_[:, 8:10], in0=s_[:, 4:6], in1=sink[:, 2 * c:2 * c + 2], op=ALU.add),
                    reads=[smr.b, self.pp.b], writes=[smr.b])
                P.op("act", functools.partial(nc.scalar.activation, out=s_[:, 10:12], in_=s_[:, 8:10], func=AF.Exp),
                     reads=[smr.b], writes=[smr.b])
                P.op("dve", functools.partial(nc.vector.tensor_tensor,
                    out=s_[:, 12:14], in0=s_[:, 10:12], in1=s_[:, 6:8], op=ALU.add),
                    reads=[smr.b], writes=[smr.b])
                P.op("dve", functools.partial(nc.vector.reciprocal, out=s_[:, 14:16], in_=s_[:, 12:14]),
                     reads=[smr.b], writes=[smr.b])
                pn3 = pnr.ap.rearrange("p (h k) -> p h k", h=2)
                P.op("dve", functools.partial(nc.vector.tensor_tensor,
                    out=pn3, in0=p3, in1=s_[:, 14:16].unsqueeze(2).to_broadcast([128, 2, 256]), op=ALU.mult),
                    reads=[per.b, smr.b], writes=[pnr.b])
                for hh in range(2):
                    for kb in range(2):
                        P.op("pe", functools.partial(nc.tensor.transpose,
                            self.psT[:, u0 + (hh * 2 + kb) * 128:u0 + (hh * 2 + kb + 1) * 128],
                            pn3[:, hh, kb * 128:(kb + 1) * 128], self.ident),
                            reads=[pnr.b, self.cb.b], writes=[self.psT_r.b])
                P.op("act", functools.partial(nc.scalar.copy, out=pTr.ap, in_=self.psT[:, u0:u0 + 512]),
                     reads=[self.psT_r.b], writes=[pTr.b])
                pso = self.psb()
                n_ = 0
                for hh in range(2):
                    for kb in range(2):
                        P.op("pe", functools.partial(nc.tensor.matmul,
                            pso.ap[:, 0:128], lhsT=v5[:, bb + kb, kvh, hh, :],
                            rhs=pTr.ap[:, (hh * 2 + kb) * 128:(hh * 2 + kb + 1) * 128],
                            start=(n_ == 0), stop=(n_ == 3)), reads=[vtp.b, pTr.b], writes=[pso.b])
                        n_ += 1
                P.op("act", functools.partial(nc.scalar.copy,
                    out=o3[:, c, bb * 128:(bb + 1) * 128], in_=pso.ap[:, 0:128]), reads=[pso.b], writes=[o_all.b])
        units = [(c, bb) for c in range(KC) for bb in range(NQB)]
        for u in range(0, len(units), 2):
            lists = []
            for k_ in range(2):
                saved = P.ops
                P.ops = []
                unit(units[u + k_][0], units[u + k_][1], u + k_)
                lists.append(P.ops)
                P.ops = saved
            for oa, ob in zip(lists[0], lists[1]):
                P.ops.append(oa)
                P.ops.append(ob)
            assert len(lists[0]) == len(lists[1])
        if blk + 1 < self.nblk:
            P.dma("sp", functools.partial(nc.sync.dma_start, out=self.swa_v_st, in_=vtp.ap[:, NQB * 8 * 128:]), self.swa_st_b,
                  reads=[vtp.b], writes=[self.swa_st_b])
            P.dma("sp", functools.partial(nc.sync.dma_start, out=self.swa_k_st.rearrange("p (j t) -> p j t", j=4),
                                                  in_=k3[:, :, TB:TB + 128]), self.swa_st_b,
                  reads=[kbuf.b], writes=[self.swa_st_b])
        self.out_proj("swa_o", L, o_all, TB, t0, lambda inp: inp["swa_w_o"][ib], first, bias=b_o)
        self.release(m)

    def rwkv_phase(self, L, blk, first):
        T2 = 512
        for sub in range(TB // T2):
            self.rwkv_sub(L, blk * TB + sub * T2, T2, first, seq_start=(blk == 0 and sub == 0))

    def rwkv_layer_pair(self, L, first):
        P, nc = self.P, self.nc
        T2 = 512
        for sb in range(self.NSB):
            if first:
                src, srcb = self.xfullT[:, :, sb * T2:(sb + 1) * T2], self.xfull_b
            else:
                half, off = divmod(sb, self.NSB // 2)
                src = self.xg3.rearrange("(k r p) t -> r k p t", k=KC, r=2)[half][:, :, off * T2:(off + 1) * T2]
                srcb = self.xg3_b
            self.rwkv_sub(L, sb * T2, T2, first, seq_start=(sb == 0), xsrc=src, xsrcb=srcb, sb=sb)
        m01 = self.param(("mask01",), 1, lambda inp: np.full((128, 1), float(inp["_core"] % 2), np.float32))
        om01 = self.param(("omask01",), 1, lambda inp: np.full((128, 1), 1.0 - float(inp["_core"] % 2), np.float32))
        ygv = self.ygr.rearrange("(s r q p) t -> s r p q t", s=self.NSB, r=2, q=8)
        hsb = self.NSB // 2
        for blk in range(self.nblk):
            m = self.mark()
            cand = [self.alloc("ygc%d" % i, KC * TB, BF16) for i in range(2)]
            ygb = self.alloc("ygb", KC * TB, BF16)
            for hc in range(2):
                c3 = cand[hc].ap.rearrange("p (k t) -> p k t", k=KC)
                for r in range(2):
                    for s2 in range(TB // T2):
                        sbg = hc * hsb + blk * (TB // T2) + s2
                        P.dma("sp", functools.partial(nc.sync.dma_start, out=c3[:, r * 8:(r + 1) * 8, s2 * T2:(s2 + 1) * T2],
                                                      in_=ygv[sbg][r]), cand[hc].b, reads=[self.ygr_b], writes=[cand[hc].b])
            P.op("dve", functools.partial(nc.vector.tensor_scalar, out=cand[0].ap, in0=cand[0].ap, scalar1=om01[:, 0:1],
                                          scalar2=None, op0=ALU.mult), reads=[cand[0].b, self.pp.b], writes=[cand[0].b])
            P.op("dve", functools.partial(nc.vector.scalar_tensor_tensor, out=ygb.ap, in0=cand[1].ap, scalar=m01[:, 0:1],
                                          in1=cand[0].ap, op0=ALU.mult, op1=ALU.add),
                 reads=[cand[0].b, cand[1].b, self.pp.b], writes=[ygb.b])
            ia = L // 3
            self.out_proj("rwo", L, ygb, TB, blk * TB, lambda inp, ia=ia: inp["rwkv_w_o"][ia], first)
            self.release(m)

    def rwkv_sub(self, L, t0, T2, first, seq_start, xsrc=None, xsrcb=None, sb=None):
        P, nc = self.P, self.nc
        ia = L // 3
        has_vres = ia > 0
        NCH = T2 // 64
        PAIRM = self.pair_mode
        NPAIR = 8 if PAIRM else KC
        LD = 0.6065306597126334
        if not hasattr(self, "zst"):
            self.zst = nc.dram_tensor("zst", [KC, 128, 128], F32).ap()
            self.zst_b = Buf("zst")
            self.vfirst = nc.dram_tensor("vfirst", [KC, 128, self.tok * (2 if PAIRM else 1)], F32).ap()
            self.vfirst_b = Buf("vfirst")
            si = np.arange(64)[:, None]
            tj = np.arange(64)[None, :]
            mab = np.zeros((128, 128))
            mab[:64, :64] = (si < tj)
            mab[:64, 64:] = (si <= tj)
            self.c_mab = self.const(128, mab)
            msl = np.zeros((128, 64))
            msl[:64, :] = (si > tj)
            self.c_msl = self.const(64, msl)
            bd = np.zeros((128, 128))
            bd[:64, :64] = 1.0 / 64
            bd[64:, 64:] = 1.0 / 64
            self.c_blkm = self.const(128, bd)

        def A(fn, reads, writes):
            P.op("act", fn, [x.b for x in reads], [x.b for x in writes])

        def V(fn, reads, writes):
            P.op("dve", fn, [x.b for x in reads], [x.b for x in writes])

        def G(fn, reads, writes):
            P.op("pool", fn, [x.b for x in reads], [x.b for x in writes])

        def M(fn, reads, writes):
            P.op("pe", fn, [x.b for x in reads], [x.b for x in writes])

        CB, PP = self.cb, self.pp
        if PAIRM:
            def pv(name, idx=ia):
                return self.param((name, idx, "half"), 8, lambda inp, name=name, idx=idx: np.ascontiguousarray(
                    np.asarray(inp[name][idx], np.float32).reshape(KC, 128).T[:, (inp["_core"] % 2) * 8:(inp["_core"] % 2) * 8 + 8]))
        else:
            pv = lambda name, idx=ia: self.dvec((name, idx), lambda inp, name=name, idx=idx: inp[name][idx].reshape(-1))
        mu = [self.dvec(("rwkv_mu", ia, i), lambda inp, i=i: inp["rwkv_mu"][ia][i]) for i in range(6)]
        w0c, a0c, kkc, kac, rkc, lwc, lbc = (pv("rwkv_w0"), pv("rwkv_a0"), pv("rwkv_k_k"), pv("rwkv_k_a"),
                                             pv("rwkv_r_k"), pv("rwkv_lnx_w"), pv("rwkv_lnx_b"))
        if has_vres:
            v0c = pv("rwkv_v0", ia - 1)
        m = self.mark()
        xr = self.alloc("xr", KC * T2, BF16)
        xk = self.alloc("xk", KC * T2, BF16)
        xv = self.alloc("xv", KC * T2, BF16)
        inw = self.alloc("inw", T2, BF16)
        ina = self.alloc("ina", T2, BF16)
        ing = self.alloc("ing", 2 * T2, BF16)
        inv = self.alloc("inv", T2, BF16) if has_vres else None
        yg = self.alloc("yg", NPAIR * T2, BF16)
        yg3 = yg.ap.rearrange("p (k t) -> p k t", k=NPAIR)
        xr3, xk3, xv3 = [x.ap.rearrange("p (k t) -> p k t", k=KC) for x in (xr, xk, xv)]
        m2 = self.mark()
        hb = self.alloc("hb", KC * (T2 + 1), BF16)
        h3 = hb.ap.rearrange("p (k t) -> p k t", k=KC)
        if seq_start:
            V(functools.partial(nc.vector.memset, h3[:, :, 0:1], 0.0), [], [hb])
        else:
            V(functools.partial(nc.vector.tensor_copy, out=h3[:, :, 0:1], in_=self.hlast.ap.unsqueeze(2)), [self.hlast], [hb])
        if xsrc is not None:
            self.norm_block(L, "norm_mix", 0, T2, first, hb, hoff=1, hstride=T2 + 1, src=xsrc, srcb=xsrcb)
        else:
            self.norm_block(L, "norm_mix", t0, T2, first, hb, hoff=1, hstride=T2 + 1)
        V(functools.partial(nc.vector.tensor_copy, out=self.hlast.ap.unsqueeze(2), in_=h3[:, :, T2:T2 + 1]), [hb], [self.hlast])
        dx = self.alloc("dx", KC * T2, BF16)
        dx3 = dx.ap.rearrange("p (k t) -> p k t", k=KC)
        V(functools.partial(nc.vector.tensor_tensor, out=dx3, in0=h3[:, :, 0:T2], in1=h3[:, :, 1:T2 + 1], op=ALU.subtract),
          [hb], [dx])

        def mix(i, outr, out3):
            for kc in range(KC):
                V(functools.partial(nc.vector.scalar_tensor_tensor,
                    out=out3[:, kc, :], in0=dx3[:, kc, :], scalar=mu[i][:, kc:kc + 1], in1=h3[:, kc, 1:T2 + 1],
                    op0=ALU.mult, op1=ALU.add), [dx, hb, PP], [outr])

        xm = self.alloc("xm", KC * T2, BF16)
        xm3 = xm.ap.rearrange("p (k t) -> p k t", k=KC)

        def lora1(mi, key, wname, widx, Rr, outr, func):
            mix(mi, xm, xm3)
            wt = self.wtile((key, L), 128, KC * Rr,
                            lambda inp: inp[wname][widx].reshape(KC, 128, Rr).transpose(1, 0, 2).reshape(128, KC * Rr))
            w3 = wt.ap.rearrange("p (k m) -> p k m", k=KC)
            for rc in range((Rr + 127) // 128):
                rr = min(128, Rr - rc * 128)
                ps = self.psa()
                for kc in range(KC):
                    M(functools.partial(nc.tensor.matmul,
                        ps.ap[0:rr, 0:T2], lhsT=w3[:, kc, rc * 128:rc * 128 + rr], rhs=xm3[:, kc, :],
                        start=(kc == 0), stop=(kc == KC - 1)), [wt, xm], [ps])
                A(functools.partial(nc.scalar.activation,
                    out=outr.ap[0:rr, rc * T2:(rc + 1) * T2], in_=ps.ap[0:rr, 0:T2], func=func), [ps], [outr])

        lora1(1, "rw1", "rwkv_w1", ia, 96, inw, AF.Tanh)
        lora1(4, "ra1", "rwkv_a1", ia, 96, ina, AF.Copy)
        lora1(5, "rg1", "rwkv_g1", ia, 256, ing, AF.Sigmoid)
        if has_vres:
            lora1(3, "rv1", "rwkv_v1", ia - 1, 64, inv, AF.Copy)
        mix(0, xr, xr3)
        mix(2, xk, xk3)
        mix(3, xv, xv3)
        self.release(m2)
        f32 = lambda name: self.alloc(name, T2, F32)
        m3 = self.mark()
        r32, k32, v32, sg, alr, kkn, kmod, cs, epos, bonus, t1, t2 = [
            f32(n) for n in ("r32", "k32", "v32", "sg", "alr", "kkn", "kmod", "cs", "epos", "bonus", "t1", "t2")]
        y32, eexc, eneg = r32, k32, v32
        b1 = self.alloc("b1", T2, BF16)
        vb = self.alloc("vb", T2, BF16)
        ar = self.alloc("ar", T2 * 2, BF16)
        bk = self.alloc("bk", T2 * 2, BF16)
        ar3 = ar.ap.rearrange("p (c x t) -> p c x t", c=NCH, x=2)
        bk3 = bk.ap.rearrange("p (c x t) -> p c x t", c=NCH, x=2)
        tT = self.alloc("tT", NCH * 4 * 128, BF16)
        tT4 = tT.ap[0:64].rearrange("p (c k d) -> p c k d", c=NCH, k=4)
        PMb = [self.alloc("PMb%d" % i, NCH * 128, BF16) for i in range(2)]
        PMk = [self.alloc("PMk%d" % i, NCH * 128, BF16) for i in range(2)]
        NHB = 2 if PAIRM else 1
        Lb_h = [[self.alloc("Lb%d_%d" % (i, h_), NCH * 64, BF16) for i in range(2)] for h_ in range(NHB)]
        Nb_h = [[self.alloc("Nb%d_%d" % (i, h_), NCH * 64, BF16) for i in range(2)] for h_ in range(NHB)]
        NI_h = [self.alloc("NI_%d" % h_, NCH * 64, BF16) for h_ in range(NHB)]
        Xb_h = [[self.alloc("Xb%d_%d" % (i, h_), NCH * 128, BF16) for i in range(2)] for h_ in range(NHB)]
        Wp = self.alloc("Wp", NCH * 128, BF16)
        U0p = self.alloc("U0p", NCH * 128, BF16)
        Wpad = [self.alloc("Wpad%d" % i, NCH * 128, BF16) for i in range(2)]
        U0pad = [self.alloc("U0pad%d" % i, NCH * 128, BF16) for i in range(2)]
        vTpad = [self.alloc("vTpad%d" % i, NCH * 128, BF16) for i in range(2)]
        GT = self.alloc("GT", NCH * 128, BF16)
        Hh = self.alloc("Hh", NCH * 128, F32)
        QT = self.alloc("QT", NCH * 64, BF16)
        Zall = self.alloc("Zall", (NCH + 1) * 128, BF16)
        Z32 = self.alloc("Z32", 128, F32)
        tz = self.alloc("tz", 128, F32)
        v3c = lambda r_, w=128: r_.ap[0:64].rearrange("p (c d) -> p c d", c=NCH)
        v3f = lambda r_: r_.ap.rearrange("p (c d) -> p c d", c=NCH)
        for z_ in Wpad + U0pad + vTpad + [GT]:
            V(functools.partial(nc.vector.memset, z_.ap, 0.0), [], [z_])
        V(functools.partial(nc.vector.memset, Hh.ap, 0.0), [], [Hh])
        psA01 = (self.psA_r[0], self.psA_r[1])
        psA23 = (self.psA_r[2], self.psA_r[3])
        pA01 = self.psA[:, 0:1024]
        pA23 = self.psA[:, 1024:2048]
        pB0, pB1 = self.psB_r[0], self.psB_r[1]
        psA01_, pA01_, pB0_, pB1_ = psA01, pA01, pB0, pB1
        psT32 = R(self.psT[:, :].bitcast(F32), self.psT_r.b)
        blk64, blkm, ident, ones = self.blk64, self.c_blkm, self.ident, self.ones
        id64 = ident[0:64, 0:64]
        mab = self.c_mab[0:64, :]
        msl = self.c_msl[0:64, :]
        W_rkv = lambda inp: inp["rwkv_w_rkv"][ia]
        for p in range(NPAIR):
            def getter(inp, p=p):
                pg = (inp["_core"] % 2) * 8 + p if PAIRM else p
                cs_ = slice(pg * 128, (pg + 1) * 128)
                W = W_rkv(inp)
                main = np.concatenate([W[0][:, cs_], W[1][:, cs_], W[2][:, cs_]], axis=1)
                main = main.reshape(KC, 128, 384).transpose(1, 0, 2).reshape(128, KC * 384)
                ext = np.zeros((128, 640), np.float32)
                ext[:96, 0:128] = inp["rwkv_w2"][ia][:, cs_]
                ext[:96, 128:256] = inp["rwkv_a2"][ia][:, cs_]
                g2 = inp["rwkv_g2"][ia][:, cs_]
                ext[:, 256:384] = g2[0:128]
                ext[:, 384:512] = g2[128:256]
                if has_vres:
                    ext[:64, 512:640] = inp["rwkv_v2"][ia - 1][:, cs_]
                return np.concatenate([main, ext], axis=1)

            wt = self.wtile(("rkv", L, p), 128, KC * 384 + 640, getter)
            w3 = wt.ap[:, 0:KC * 384].rearrange("p (k m) -> p k m", k=KC)
            E0 = KC * 384
            w2p = wt.ap[0:96, E0:E0 + 128]
            a2p = wt.ap[0:96, E0 + 128:E0 + 256]
            g2p = [wt.ap[:, E0 + 256:E0 + 384], wt.ap[:, E0 + 384:E0 + 512]]
            v2p = wt.ap[0:64, E0 + 512:E0 + 640]
            for part, (xin, xin3, dst) in enumerate(((xr, xr3, r32), (xk, xk3, k32), (xv, xv3, v32))):
                ps = self.psa()
                for kc in range(KC):
                    M(functools.partial(nc.tensor.matmul,
                        ps.ap, lhsT=w3[:, kc, part * 128:(part + 1) * 128], rhs=xin3[:, kc, :],
                        start=(kc == 0), stop=(kc == KC - 1)), [wt, xin], [ps])
                A(functools.partial(nc.scalar.copy, out=dst.ap, in_=ps.ap), [ps], [dst])
            ps = self.psa()
            M(functools.partial(nc.tensor.matmul, ps.ap, lhsT=w2p, rhs=inw.ap[0:96, :], start=True, stop=True), [wt, inw], [ps])
            A(functools.partial(nc.scalar.activation, out=sg.ap, in_=ps.ap, func=AF.Sigmoid, bias=w0c[:, p:p + 1], scale=1.0),
              [ps, PP], [sg])
            ps = self.psa()
            M(functools.partial(nc.tensor.matmul, ps.ap, lhsT=a2p, rhs=ina.ap[0:96, :], start=True, stop=True), [wt, ina], [ps])
            A(functools.partial(nc.scalar.activation, out=alr.ap, in_=ps.ap, func=AF.Sigmoid, bias=a0c[:, p:p + 1], scale=1.0),
              [ps, PP], [alr])
            if has_vres:
                ps = self.psa()
                M(functools.partial(nc.tensor.matmul, ps.ap, lhsT=v2p, rhs=inv.ap[0:64, :], start=True, stop=True), [wt, inv], [ps])
                A(functools.partial(nc.scalar.activation, out=t1.ap, in_=ps.ap, func=AF.Sigmoid, bias=v0c[:, p:p + 1], scale=1.0),
                  [ps, PP], [t1])
                P.dma("sp", functools.partial(nc.sync.dma_start, out=t2.ap, in_=self.vfirst[p, :, t0:t0 + T2]), t2.b,
                      reads=[self.vfirst_b], writes=[t2.b])
                V(functools.partial(nc.vector.tensor_tensor, out=t2.ap, in0=t2.ap, in1=v32.ap, op=ALU.subtract), [t2, v32], [t2])
                V(functools.partial(nc.vector.tensor_tensor, out=t2.ap, in0=t2.ap, in1=t1.ap, op=ALU.mult), [t2, t1], [t2])
                V(functools.partial(nc.vector.tensor_tensor, out=v32.ap, in0=v32.ap, in1=t2.ap, op=ALU.add), [t2, v32], [v32])
            else:
                P.dma("sp", functools.partial(nc.sync.dma_start, out=self.vfirst[p, :, t0:t0 + T2], in_=v32.ap), self.vfirst_b,
                      reads=[v32.b], writes=[self.vfirst_b])
            V(functools.partial(nc.vector.tensor_scalar, out=t1.ap, in0=k32.ap, scalar1=kkc[:, p:p + 1], scalar2=None, op0=ALU.mult),
              [k32, PP], [t1])
            A(functools.partial(nc.scalar.activation, out=b1.ap, in_=t1.ap, func=AF.Square), [t1], [b1])
            M(functools.partial(nc.tensor.matmul, self.psS[:, :], lhsT=blk64, rhs=b1.ap, start=True, stop=True), [b1, CB], [self.psS_r])
            A(functools.partial(nc.scalar.activation, out=t2.ap, in_=self.psS[:, :], func=AF.Sqrt), [self.psS_r], [t2])
            V(functools.partial(nc.vector.tensor_scalar, out=t2.ap, in0=t2.ap, scalar1=1e-12, scalar2=None, op0=ALU.max), [t2], [t2])
            V(functools.partial(nc.vector.reciprocal, out=t2.ap, in_=t2.ap), [t2], [t2])
            V(functools.partial(nc.vector.tensor_tensor, out=kkn.ap, in0=t1.ap, in1=t2.ap, op=ALU.mult), [t1, t2], [kkn])
            V(functools.partial(nc.vector.tensor_scalar, out=t1.ap, in0=alr.ap, scalar1=-1.0, scalar2=kac[:, p:p + 1],
                                                  op0=ALU.add, op1=ALU.mult), [alr, PP], [t1])
            V(functools.partial(nc.vector.scalar_tensor_tensor, out=kmod.ap, in0=t1.ap, scalar=1.0, in1=k32.ap,
                                                     op0=ALU.add, op1=ALU.mult), [t1, k32], [kmod])
            V(functools.partial(nc.vector.tensor_tensor, out=t1.ap, in0=r32.ap, in1=kmod.ap, op=ALU.mult), [r32, kmod], [t1])
            V(functools.partial(nc.vector.tensor_scalar, out=b1.ap, in0=t1.ap, scalar1=rkc[:, p:p + 1], scalar2=None, op0=ALU.mult),
              [t1, PP], [b1])
            M(functools.partial(nc.tensor.matmul, self.psS[:, :], lhsT=blk64, rhs=b1.ap, start=True, stop=True), [b1, CB], [self.psS_r])
            V(functools.partial(nc.vector.tensor_tensor, out=bonus.ap, in0=self.psS[:, :], in1=v32.ap, op=ALU.mult),
              [self.psS_r, v32], [bonus])
            A(functools.partial(nc.scalar.copy, out=vb.ap, in_=v32.ap), [v32], [vb])
            for c in range(NCH):
                sl = slice(c * 64, (c + 1) * 64)
                V(functools.partial(nc.vector.tensor_tensor_scan, out=cs.ap[:, sl], data0=ones[:, 0:64], data1=sg.ap[:, sl],
                                                             initial=0.0, op0=ALU.mult, op1=ALU.add), [sg, CB], [cs])
            A(functools.partial(nc.scalar.activation, out=eneg.ap, in_=cs.ap, func=AF.Exp, scale=LD), [cs], [eneg])
            A(functools.partial(nc.scalar.activation, out=epos.ap, in_=cs.ap, func=AF.Exp, scale=-LD), [cs], [epos])
            V(functools.partial(nc.vector.tensor_tensor, out=t2.ap, in0=cs.ap, in1=sg.ap, op=ALU.subtract), [cs, sg], [t2])
            A(functools.partial(nc.scalar.activation, out=eexc.ap, in_=t2.ap, func=AF.Exp, scale=-LD), [t2], [eexc])
            c64 = lambda r_: r_.ap.rearrange("p (c t) -> p c t", c=NCH)
            V(functools.partial(nc.vector.scalar_tensor_tensor, out=ar3[:, :, 0, :], in0=c64(kkn), scalar=-1.0, in1=c64(eexc),
                                                     op0=ALU.mult, op1=ALU.mult), [kkn, eexc], [ar])
            V(functools.partial(nc.vector.tensor_tensor, out=ar3[:, :, 1, :], in0=c64(r32), in1=c64(epos), op=ALU.mult),
              [r32, epos], [ar])
            V(functools.partial(nc.vector.tensor_tensor, out=t1.ap, in0=kkn.ap, in1=alr.ap, op=ALU.mult), [kkn, alr], [t1])
            V(functools.partial(nc.vector.tensor_tensor, out=bk3[:, :, 0, :], in0=c64(t1), in1=c64(eneg), op=ALU.mult), [t1, eneg], [bk])
            V(functools.partial(nc.vector.tensor_tensor, out=bk3[:, :, 1, :], in0=c64(kmod), in1=c64(eneg), op=ALU.mult),
              [kmod, eneg], [bk])
            gC = c64(epos)[:, :, 63]
            for c2 in range(NCH // 2):
                for cc in range(2):
                    c = c2 * 2 + cc
                    srcs = (ar3[:, c, 0, :], bk3[:, c, 0, :], bk3[:, c, 1, :], vb.ap[:, c * 64:(c + 1) * 64])
                    for kind in range(4):
                        o_ = (cc * 4 + kind) * 128
                        M(functools.partial(nc.tensor.transpose, self.psT[0:64, o_:o_ + 128], srcs[kind], ident),
                          [ar, bk, vb, CB], [self.psT_r])
                A(functools.partial(nc.scalar.copy,
                    out=tT4[:, c2 * 2:c2 * 2 + 2, :, :],
                    in_=self.psT[0:64, :].rearrange("p (c k d) -> p c k d", c=2, k=4)), [self.psT_r], [tT])
            def head(hh):
                hp = slice(hh * 64, (hh + 1) * 64)
                HI = hh if PAIRM else 0
                Lb, Nb, NI, Xb = Lb_h[HI], Nb_h[HI], NI_h[HI], Xb_h[HI]
                if hh == 0 or not PAIRM:
                    psA01, pA01, pB0, pB1 = psA01_, pA01_, pB0_, pB1_
                else:
                    psA01, pA01, pB0, pB1 = psA23, pA23, self.psS_r, psT32
                psA23x, pA23x = psA01, pA01
                hc = slice(hh * 64, (hh + 1) * 64)
                for c in range(NCH):
                    M(functools.partial(nc.tensor.matmul, pA01[0:64, c * 128:(c + 1) * 128], lhsT=bk3[hp, c, 0, :],
                                                  rhs=ar3[hp, c, :, :], start=True, stop=True), [bk, ar], psA01)
                pmb, pmk = PMb[hh], PMk[hh]
                mab_b = mab.unsqueeze(1).to_broadcast([64, NCH, 128])
                V(functools.partial(nc.vector.tensor_tensor,
                    out=v3c(pmb), in0=pA01[0:64, :].rearrange("p (c d) -> p c d", c=NCH), in1=mab_b, op=ALU.mult),
                  list(psA01) + [CB], [pmb])
                for c in range(NCH):
                    M(functools.partial(nc.tensor.matmul, pA23x[0:64, c * 128:(c + 1) * 128], lhsT=bk3[hp, c, 1, :],
                                                  rhs=ar3[hp, c, :, :], start=True, stop=True), [bk, ar], psA23x)
                V(functools.partial(nc.vector.tensor_tensor,
                    out=v3c(pmk), in0=pA23x[0:64, :].rearrange("p (c d) -> p c d", c=NCH), in1=mab_b, op=ALU.mult),
                  list(psA23x) + [CB], [pmk])
                for c in range(NCH):
                    M(functools.partial(nc.tensor.matmul, pB0.ap[0:64, c * 64:(c + 1) * 64], lhsT=ar3[hp, c, 0, :],
                                                  rhs=bk3[hp, c, 0, :], start=True, stop=True), [bk, ar], [pB0])
                Lc, Nc = Lb[0], pmb
                V(functools.partial(nc.vector.tensor_tensor,
                    out=v3c(Lc), in0=pB0.ap[0:64, :].rearrange("p (c d) -> p c d", c=NCH),
                    in1=msl.unsqueeze(1).to_broadcast([64, NCH, 64]), op=ALU.mult), [pB0, CB], [Lc])
                idb = id64.unsqueeze(1).to_broadcast([64, NCH, 64])
                V(functools.partial(nc.vector.tensor_tensor, out=v3c(NI), in0=v3c(pmb)[:, :, 0:64], in1=idb, op=ALU.add),
                  [pmb, CB], [NI])
                for c in range(NCH):
                    M(functools.partial(nc.tensor.matmul, pB1.ap[0:64, c * 64:(c + 1) * 64], lhsT=v3c(pmk)[:, c, 0:64],
                                                           rhs=tT4[:, c, 3, hc], start=True, stop=True), [pmk, tT], [pB1])
                X = Xb[0]
                A(functools.partial(nc.scalar.copy, out=v3c(X)[:, :, 64:128],
                                             in_=pB1.ap[0:64, :].rearrange("p (c d) -> p c d", c=NCH)), [pB1], [X])
                G(functools.partial(nc.gpsimd.tensor_copy, out=v3c(X)[:, :, 0:64], in_=tT4[:, :, 0, hc]), [tT], [X])
                Nview = lambda r_, isP: (v3c(r_)[:, :, 0:64] if isP else v3c(r_))
                n_isP = True
                for lvl in range(6):
                    for c in range(NCH):
                        M(functools.partial(nc.tensor.matmul, pA01[0:64, c * 128:(c + 1) * 128], lhsT=v3c(NI)[:, c, :],
                                                           rhs=v3c(X)[:, c, :], start=True, stop=True), [NI, X], psA01)
                    px3 = pA01[0:64, :].rearrange("p (c d) -> p c d", c=NCH)
                    if lvl < 5:
                        Xn = Xb[(lvl + 1) % 2]
                        A(functools.partial(nc.scalar.copy, out=v3c(Xn), in_=px3), list(psA01), [Xn])
                        X = Xn
                        nv = Nview(Nc, n_isP)
                        for c in range(NCH):
                            M(functools.partial(nc.tensor.matmul, pB0.ap[0:64, c * 64:(c + 1) * 64], lhsT=v3c(Lc)[:, c, :],
                                                                        rhs=nv[:, c, :], start=True, stop=True), [Lc, Nc], [pB0])
                        if lvl < 4:
                            for c in range(NCH):
                                M(functools.partial(nc.tensor.matmul, pB1.ap[0:64, c * 64:(c + 1) * 64], lhsT=nv[:, c, :],
                                                                            rhs=v3c(Lc)[:, c, :], start=True, stop=True), [Lc, Nc], [pB1])
                        Nn = Nb[lvl % 2]
                        pn3 = pB0.ap[0:64, :].rearrange("p (c d) -> p c d", c=NCH)
                        V(functools.partial(nc.vector.tensor_copy, out=v3c(Nn), in_=pn3), [pB0], [Nn])
                        V(functools.partial(nc.vector.tensor_tensor, out=v3c(NI), in0=pn3, in1=idb, op=ALU.add), [pB0, CB], [NI])
                        if lvl < 4:
                            Ln = Lb[(lvl + 1) % 2]
                            A(functools.partial(nc.scalar.copy, out=v3c(Ln), in_=pB1.ap[0:64, :].rearrange("p (c d) -> p c d", c=NCH)),
                              [pB1], [Ln])
                            Lc = Ln
                        Nc, n_isP = Nn, False
                    else:
                        A(functools.partial(nc.scalar.copy, out=v3c(Wp)[:, :, hc], in_=px3[:, :, 0:64]), list(psA01), [Wp])
                        V(functools.partial(nc.vector.tensor_copy, out=v3c(U0p)[:, :, hc], in_=px3[:, :, 64:128]), list(psA01), [U0p])
                        A(functools.partial(nc.scalar.copy, out=v3c(Wpad[hh])[:, :, hc], in_=px3[:, :, 0:64]),
                          list(psA01), [Wpad[hh]])
                        V(functools.partial(nc.vector.tensor_copy, out=v3c(U0pad[hh])[:, :, hc], in_=px3[:, :, 64:128]),
                          list(psA01), [U0pad[hh]])
                G(functools.partial(nc.gpsimd.tensor_copy, out=v3c(vTpad[hh])[:, :, hc], in_=tT4[:, :, 3, hc]), [tT], [vTpad[hh]])
            hl = []
            for hh in range(2):
                saved = P.ops
                P.ops = []
                head(hh)
                hl.append(P.ops)
                P.ops = saved
            if PAIRM:
                assert len(hl[0]) == len(hl[1])
                for oa, ob in zip(hl[0], hl[1]):
                    P.ops.append(oa)
                    P.ops.append(ob)
            else:
                P.ops.extend(hl[0] + hl[1])
            for c in range(NCH):
                M(functools.partial(nc.tensor.matmul, pA01[:, c * 128:(c + 1) * 128], lhsT=v3c(Wp)[:, c, :], rhs=tT4[:, c, 1, :],
                                              start=True, stop=True), [Wp, tT], psA01)
            pg3 = pA01.rearrange("p (c d) -> p c d", c=NCH)
            V(functools.partial(nc.vector.tensor_copy, out=v3f(GT)[0:64, :, 0:64], in_=pg3[0:64, :, 0:64]), list(psA01), [GT])
            A(functools.partial(nc.scalar.copy, out=v3f(GT)[64:128, :, 64:128], in_=pg3[64:128, :, 64:128]), list(psA01), [GT])
            for c in range(NCH):
                M(functools.partial(nc.tensor.matmul, pA23[:, c * 128:(c + 1) * 128], lhsT=tT4[:, c, 1, :], rhs=v3c(U0p)[:, c, :],
                                              start=True, stop=False), [U0p, tT], psA23)
                M(functools.partial(nc.tensor.matmul, pA23[:, c * 128:(c + 1) * 128], lhsT=tT4[:, c, 2, :], rhs=tT4[:, c, 3, :],
                                              start=False, stop=True), [tT], psA23)
            ph3 = pA23.rearrange("p (c d) -> p c d", c=NCH)
            for hh in range(2):
                hp = slice(hh * 64, (hh + 1) * 64)
                V(functools.partial(nc.vector.tensor_tensor,
                    out=v3f(Hh)[hp, :, hp], in0=ph3[hp, :, hp], in1=gC[hp, :].unsqueeze(2).to_broadcast([64, NCH, 64]),
                    op=ALU.mult), list(psA23) + [epos], [Hh])
            for c in range(NCH):
                for hh in range(2):
                    M(functools.partial(nc.tensor.matmul, pB0.ap[:, c * 64:(c + 1) * 64], lhsT=v3c(Wpad[hh])[:, c, :],
                                                         rhs=v3c(PMb[hh])[:, c, 64:128], start=(hh == 0), stop=(hh == 1)),
                      [Wpad[hh], PMb[hh]], [pB0])
            V(functools.partial(nc.vector.tensor_tensor, out=QT.ap.rearrange("p (c t) -> p c t", c=NCH),
                                              in0=pB0.ap.rearrange("p (c t) -> p c t", c=NCH), in1=ar3[:, :, 1, :],
                                              op=ALU.add), [pB0, ar], [QT])
            if seq_start:
                V(functools.partial(nc.vector.memset, Z32.ap, 0.0), [], [Z32])
            else:
                P.dma("sp", functools.partial(nc.sync.dma_start, out=Z32.ap, in_=self.zst[p]), Z32.b,
                      reads=[self.zst_b], writes=[Z32.b])
            za3 = Zall.ap.rearrange("p (c d) -> p c d", c=NCH + 1)
            A(functools.partial(nc.scalar.copy, out=za3[:, 0, :], in_=Z32.ap), [Z32], [Zall])
            for c in range(NCH):
                M(functools.partial(nc.tensor.matmul, pB1.ap[:, 0:128], lhsT=v3f(GT)[:, c, :], rhs=za3[:, c, :], start=True, stop=True),
                  [GT, Zall], [pB1])
                V(functools.partial(nc.vector.tensor_tensor, out=tz.ap, in0=pB1.ap[:, 0:128], in1=Z32.ap, op=ALU.add), [pB1, Z32], [tz])
                V(functools.partial(nc.vector.scalar_tensor_tensor, out=Z32.ap, in0=tz.ap, scalar=gC[:, c:c + 1], in1=v3f(Hh)[:, c, :],
                                                             op0=ALU.mult, op1=ALU.add), [tz, epos, Hh], [Z32])
                A(functools.partial(nc.scalar.copy, out=za3[:, c + 1, :], in_=Z32.ap), [Z32], [Zall])
            P.dma("sp", functools.partial(nc.sync.dma_start, out=self.zst[p], in_=Z32.ap), self.zst_b,
                  reads=[Z32.b], writes=[self.zst_b])
            q3_ = QT.ap.rearrange("p (c t) -> p c t", c=NCH)
            for c in range(NCH):
                M(functools.partial(nc.tensor.matmul, pB0.ap[:, c * 64:(c + 1) * 64], lhsT=za3[:, c, :], rhs=q3_[:, c, :],
                                              start=True, stop=False), [Zall, QT], [pB0])
                for hh in range(2):
                    M(functools.partial(nc.tensor.matmul, pB0.ap[:, c * 64:(c + 1) * 64], lhsT=v3c(U0pad[hh])[:, c, :],
                                                         rhs=v3c(PMb[hh])[:, c, 64:128], start=False, stop=False),
                      [U0pad[hh], PMb[hh]], [pB0])
                for hh in range(2):
                    M(functools.partial(nc.tensor.matmul, pB0.ap[:, c * 64:(c + 1) * 64], lhsT=v3c(vTpad[hh])[:, c, :],
                                                         rhs=v3c(PMk[hh])[:, c, 64:128], start=False, stop=(hh == 1)),
                      [vTpad[hh], PMk[hh]], [pB0])
            A(functools.partial(nc.scalar.copy, out=y32.ap, in_=pB0.ap), [pB0], [y32])
            A(functools.partial(nc.scalar.copy, out=b1.ap, in_=y32.ap), [y32], [b1])
            M(functools.partial(nc.tensor.matmul, self.psS[:, :], lhsT=blkm, rhs=b1.ap, start=True, stop=True), [b1, CB], [self.psS_r])
            V(functools.partial(nc.vector.tensor_tensor, out=y32.ap, in0=y32.ap, in1=self.psS[:, :], op=ALU.subtract),
              [y32, self.psS_r], [y32])
            A(functools.partial(nc.scalar.activation, out=b1.ap, in_=y32.ap, func=AF.Square), [y32], [b1])
            M(functools.partial(nc.tensor.matmul, self.psS[:, :], lhsT=blkm, rhs=b1.ap, start=True, stop=True), [b1, CB], [self.psS_r])
            V(functools.partial(nc.vector.tensor_scalar, out=t1.ap, in0=self.psS[:, :], scalar1=GN_EPS, scalar2=None, op0=ALU.add),
              [self.psS_r], [t1])
            A(functools.partial(nc.scalar.activation, out=t2.ap, in_=t1.ap, func=AF.Sqrt), [t1], [t2])
            V(functools.partial(nc.vector.reciprocal, out=t1.ap, in_=t2.ap), [t2], [t1])
            V(functools.partial(nc.vector.tensor_tensor, out=y32.ap, in0=y32.ap, in1=t1.ap, op=ALU.mult), [y32, t1], [y32])
            V(functools.partial(nc.vector.tensor_scalar, out=y32.ap, in0=y32.ap, scalar1=lwc[:, p:p + 1], scalar2=lbc[:, p:p + 1],
                                                  op0=ALU.mult, op1=ALU.add), [y32, PP], [y32])
            V(functools.partial(nc.vector.tensor_tensor, out=y32.ap, in0=y32.ap, in1=bonus.ap, op=ALU.add), [y32, bonus], [y32])
            ps = self.psa()
            for kc2 in range(2):
                M(functools.partial(nc.tensor.matmul, ps.ap, lhsT=g2p[kc2], rhs=ing.ap[:, kc2 * T2:(kc2 + 1) * T2],
                                                         start=(kc2 == 0), stop=(kc2 == 1)), [wt, ing], [ps])
            V(functools.partial(nc.vector.tensor_tensor, out=yg3[:, p, :], in0=ps.ap, in1=y32.ap, op=ALU.mult),
              [ps, y32], [yg])
        self.release(m3)
        if PAIRM:
            dst = self.ygs.rearrange("(s q p) t -> s p q t", s=self.NSB, q=8)[sb]
            P.dma("sp", functools.partial(nc.sync.dma_start, out=dst, in_=yg3), self.ygs_b, reads=[yg.b], writes=[self.ygs_b])
            P.dma("pool", functools.partial(nc.gpsimd.collective_compute, "AllGather", ALU.bypass, replica_groups=self.groups,
                                            ins=[self.ygs[sb * 1024:(sb + 1) * 1024, :]],
                                            outs=[self.ygr[sb * 2048:(sb + 1) * 2048, :]]), self.ygr_b,
                  reads=[self.ygs_b], writes=[self.ygr_b], inc=1)
        else:
            self.out_proj("rwo", L, yg, T2, t0, lambda inp: inp["rwkv_w_o"][ia], first)
        self.release(m)


N_CORES = 8
SEQ = 4096
_CACHE = {}


def kernel(**inputs):
    inp = {k: np.asarray(v) for k, v in inputs.items()}
    x = inp["x"].astype(np.float32, copy=False)
    B, T, Dm = x.shape
    assert (2 * B, T, Dm) == (N_CORES, SEQ, D)
    half_t = T // 2
    if "b" not in _CACHE:
        b = Builder(half_t, [0, 1, 2, 3], final_norm=True, pair_mode=True)
        b.build()
        _CACHE["b"] = b
    b = _CACHE["b"]
    packs = [b.pack(inp, core=h) for h in range(2)]
    in_maps = []
    for c in range(N_CORES):
        bi, h = divmod(c, 2)
        xfull = np.ascontiguousarray(x[bi].T).reshape(KC, 128, T)
        xT = np.ascontiguousarray(xfull[:, :, h * half_t:(h + 1) * half_t])
        wp, pp, cp = packs[h]
        in_maps.append({"xT": xT, "xfullT": xfull, "wpack": wp, "ppack": pp, "cpack": cp})
    res = run_bass_kernel_spmd(b.nc, in_maps, core_ids=list(range(N_CORES)))
    out = np.empty((B, T, Dm), np.float32)
    for c in range(N_CORES):
        bi, h = divmod(c, 2)
        out[bi, h * half_t:(h + 1) * half_t] = np.asarray(res.results[c]["yT"]).reshape(Dm, half_t).T
    return out
```

```python
import contextlib
import functools
import math
import numpy as np
import concourse.bass as bass
import concourse.mybir as mybir
from concourse.bass_utils import run_bass_kernel_spmd

F32 = mybir.dt.float32
BF16 = mybir.dt.bfloat16
AF = mybir.ActivationFunctionType
ALU = mybir.AluOpType
AX = mybir.AxisListType

D = 2048
KC = 16
DFF = 8192
NH = 32
HD = 64
TB = 1024
TT = 512
RMS_EPS = 1e-6
import os as _os
SAME_ENGINE_WAITS = bool(int(_os.environ.get('SAME_ENGINE_WAITS', '1')))
GN_EPS = 64e-5


class Buf:
    __slots__ = ("name", "w", "r", "dsem", "dcount", "excl")

    def __init__(self, name, excl=False):
        self.excl = excl
        self.name = name
        self.w = None
        self.r = []
        self.dsem = None
        self.dcount = 0


class Op:
    __slots__ = ("eng", "fn", "reads", "writes", "dma", "dbuf", "tile_idx", "uses_tile",
                 "need_inc", "sem", "val", "ndma", "inc")

    def __init__(self, eng, fn, reads, writes, dma=False, dbuf=None, tile_idx=None, uses_tile=None, ndma=1, inc=16):
        self.inc = inc
        self.eng = eng
        self.fn = fn
        self.reads = reads
        self.writes = writes
        self.dma = dma
        self.dbuf = dbuf
        self.tile_idx = tile_idx
        self.uses_tile = uses_tile
        self.need_inc = False
        self.sem = None
        self.val = None
        self.ndma = ndma


class Prog:
    ENGS = ("pe", "act", "dve", "pool", "sp")

    def __init__(self):
        self.nc = bass.Bass("TRN2", target_bir_lowering=False)
        self.es = contextlib.ExitStack()
        self.ops = []
        self.cur_tile = None
        nc = self.nc
        self.eng = {"pe": nc.tensor, "act": nc.scalar, "dve": nc.vector, "pool": nc.gpsimd, "sp": nc.sync}
        self.esem = {e: self.es.enter_context(nc.semaphore("sem_" + e)) for e in ("pe", "act", "dve", "pool")}
        self.dsems = []

    def op(self, eng, fn, reads=(), writes=(), uses_tile=None):
        reads = list(reads)
        writes = list(writes)
        for b in reads:
            if b.excl and b not in writes:
                writes.append(b)
        o = Op(eng, fn, list(reads), list(writes),
               uses_tile=uses_tile if uses_tile is not None else self.cur_tile)
        self.ops.append(o)
        return o

    def dma(self, eng, fn, dbuf, reads=(), writes=(), tile_idx=None, ndma=1, inc=16):
        o = Op(eng, fn, list(reads), list(writes), dma=True, dbuf=dbuf, tile_idx=tile_idx, ndma=ndma, inc=inc)
        self.ops.append(o)
        return o

    def barrier(self):
        self.ops.append("BARRIER")

    def _hoist(self, dist):
        loads = {}
        rest = []
        for o in self.ops:
            if o != "BARRIER" and o.dma and o.tile_idx is not None:
                loads[o.tile_idx] = o
            else:
                rest.append(o)
        if not loads:
            return
        ntiles = max(loads) + 1
        first_use = {}
        for i, o in enumerate(rest):
            if o != "BARRIER" and o.uses_tile is not None and o.uses_tile not in first_use:
                first_use[o.uses_tile] = i
        inserts = {}
        for j in range(ntiles):
            t = max(0, j - dist)
            while t not in first_use and t < ntiles:
                t += 1
            pos = first_use.get(t, len(rest))
            inserts.setdefault(pos, []).append(loads[j])
        out = []
        for i, o in enumerate(rest):
            if i in inserts:
                out.extend(inserts[i])
            out.append(o)
        if len(rest) in inserts:
            out.extend(inserts[len(rest)])
        self.ops = out

    def finalize(self, hoist_dist=2):
        self._hoist(hoist_dist)
        nc = self.nc
        deps = []
        bufs_seen = {}
        all_bufs = []

        def reg(b):
            if id(b) not in bufs_seen:
                bufs_seen[id(b)] = b
                all_bufs.append(b)

        for o in self.ops:
            if o == "BARRIER":
                deps.append(None)
                continue
            d = []
            for b in o.reads:
                reg(b)
                if b.w is not None:
                    d.append(b.w)
            for b in o.writes:
                reg(b)
                if b.w is not None:
                    d.append(b.w)
                d.extend(b.r)
            dd = []
            seen = set()
            for x in d:
                if id(x) not in seen and x is not o:
                    seen.add(id(x))
                    dd.append(x)
            for x in dd:
                if not x.dma and not (x.eng == o.eng and (o.eng == "pe" or not SAME_ENGINE_WAITS)):
                    x.need_inc = True
            deps.append(dd)
            for b in o.reads:
                if not o.dma:
                    b.r = [x for x in b.r if x.dma or x.eng != o.eng]
                b.r.append(o)
            for b in o.writes:
                b.w = o
                b.r = []
            if o.dma:
                reg(o.dbuf)
        class DS:
            def __init__(self, sem):
                self.sem = sem
                self.count = 0
        ds_by_name = {}
        for o in self.ops:
            if o != "BARRIER" and o.dma and o.dbuf.dsem is None:
                nm = o.dbuf.name
                if nm not in ds_by_name:
                    ds_by_name[nm] = DS(self.es.enter_context(nc.semaphore("ds%d" % len(self.dsems))))
                    self.dsems.append(ds_by_name[nm])
                o.dbuf.dsem = ds_by_name[nm]
        all_ds = list(ds_by_name.values())
        cnt = {e: 0 for e in self.esem}
        known = {e: {} for e in self.ENGS}
        n_wait = 0
        for o, dd in zip(self.ops, deps):
            if o == "BARRIER":
                for e in self.ENGS:
                    eng = self.eng[e]
                    for e2 in self.esem:
                        v = cnt[e2]
                        if v > known[e].get(id(self.esem[e2]), 0):
                            eng.wait_ge(self.esem[e2], v)
                            known[e][id(self.esem[e2])] = v
                    for d_ in all_ds:
                        if d_.count > known[e].get(id(d_.sem), 0):
                            eng.wait_ge(d_.sem, d_.count)
                            known[e][id(d_.sem)] = d_.count
                continue
            e = o.eng
            eng = self.eng[e]
            for x in dd:
                if x.dma:
                    sem, val = x.dbuf.dsem.sem, x.dbuf.dsem.count
                else:
                    if x.eng == e and (e == "pe" or not SAME_ENGINE_WAITS):
                        continue
                    sem, val = x.sem, x.val
                if val > known[e].get(id(sem), 0):
                    eng.wait_ge(sem, val)
                    known[e][id(sem)] = val
                    n_wait += 1
            r = o.fn()
            if o.dma:
                insts = r if isinstance(r, (list, tuple)) else [r]
                assert len(insts) == o.ndma, (len(insts), o.ndma)
                for ins in insts:
                    ins.then_inc(o.dbuf.dsem.sem, o.inc)
                o.dbuf.dsem.count += o.inc * len(insts)
            elif o.need_inc:
                cnt[e] += 1
                r.then_inc(self.esem[e], 1)
                o.sem, o.val = self.esem[e], cnt[e]
        for d_ in all_ds:
            if d_.count > known["sp"].get(id(d_.sem), 0):
                nc.sync.wait_ge(d_.sem, d_.count)
        for e2 in self.esem:
            if cnt[e2] > known["sp"].get(id(self.esem[e2]), 0):
                nc.sync.wait_ge(self.esem[e2], cnt[e2])
        self.stats = dict(nops=len(self.ops), nwait=n_wait, cnt=dict(cnt), ndsem=len(self.dsems))
        self.es.close()
        return nc


class R:
    __slots__ = ("ap", "b")

    def __init__(self, ap, b):
        self.ap = ap
        self.b = b


class Builder:
    def __init__(self, tok, layers, final_norm, pair_mode=False):
        self.P = Prog()
        self.nc = self.P.nc
        self.tok = tok
        self.nblk = tok // TB
        self.layers = layers
        self.final_norm = final_norm
        self.pair_mode = pair_mode
        nc = self.nc
        P = self.P
        if pair_mode:
            self.xfullT = nc.dram_tensor("xfullT", [KC, 128, 2 * tok], F32, kind="ExternalInput").ap()
            self.xfull_b = Buf("xfullT")
            self.xg3 = nc.dram_tensor("xg3", [2 * KC * 128, tok], F32).ap()
            self.xg3_b = Buf("xg3")
            self.tail_send = nc.dram_tensor("tail_send", [KC * 128, 128], F32).ap()
            self.tail_send_b = Buf("tail_send")
            self.tail_recv = nc.dram_tensor("tail_recv", [2 * KC * 128, 128], F32).ap()
            self.tail_recv_b = Buf("tail_recv")
            self.NSB = 2 * tok // 512
            self.ygs = nc.dram_tensor("ygs", [self.NSB * 8 * 128, 512], BF16).ap()
            self.ygs_b = Buf("ygs")
            self.ygr = nc.dram_tensor("ygr", [2 * self.NSB * 8 * 128, 512], BF16).ap()
            self.ygr_b = Buf("ygr")
            self.groups = [[0, 1], [2, 3], [4, 5], [6, 7]]
        self.xT = nc.dram_tensor("xT", [KC, 128, tok], F32, kind="ExternalInput").ap()
        self.yT = nc.dram_tensor("yT", [KC, 128, tok], F32, kind="ExternalOutput").ap()
        self.xs = nc.dram_tensor("xs", [KC, 128, tok], F32).ap()
        self.xT_b = Buf("xT")
        self.xs_b = Buf("xs")
        self.yT_b = Buf("yT")
        self.wgetters = []
        self.wkeys = {}
        self.woff = 0
        self.pgetters = []
        self.pkeys = {}
        self.pcol = 0
        self.cgetters = []
        self.ccol = 0
        self.NPCOL = 1024
        self.NCCOL = 1280
        self.AW = 53200
        self.arena = P.es.enter_context(nc.sbuf_tensor("arena", [128, self.AW], F32))
        self.atop = 0
        self.nbufs = 0
        self.psA = P.es.enter_context(nc.psum_tensor("psA", [128, 2048], F32))
        self.psB = P.es.enter_context(nc.psum_tensor("psB", [128, 1024], F32))
        self.psT = P.es.enter_context(nc.psum_tensor("psT", [128, 1024], BF16))
        self.psS = P.es.enter_context(nc.psum_tensor("psS", [128, 512], F32))
        self.psA_r = [R(self.psA[:, i * 512:(i + 1) * 512], Buf("psA%d" % i, True)) for i in range(4)]
        self.psB_r = [R(self.psB[:, i * 512:(i + 1) * 512], Buf("psB%d" % i, True)) for i in range(2)]
        self.psT_r = R(self.psT[:, :], Buf("psT", True))
        self.psS_r = R(self.psS[:, :], Buf("psS", True))
        self.rotA = 0
        self.rotB = 0
        self.pp = self.alloc("pp", self.NPCOL, F32)
        self.cb = self.alloc("cb", self.NCCOL, BF16)
        self.wslots = [self.alloc("wslot%d" % i, 8192, BF16) for i in range(3)]
        self.wtile_n = 0
        self.built = False

    def alloc(self, name, nelem, dt, npart=128):
        words = nelem if dt == F32 else (nelem + 1) // 2
        off = self.atop
        self.atop += words
        assert self.atop <= self.AW, ("SBUF arena overflow", name, self.atop)
        ap = self.arena[:, off:off + words]
        if dt != F32:
            ap = ap.bitcast(dt)
        if npart != 128:
            ap = ap[0:npart]
        return R(ap, Buf(name))

    def mark(self):
        return self.atop

    def release(self, m):
        self.P.barrier()
        self.atop = m

    def psa(self):
        r = self.psA_r[self.rotA % 4]
        self.rotA += 1
        return r

    def psb(self):
        r = self.psB_r[self.rotB % 2]
        self.rotB += 1
        return r

    def param(self, key, ncols, getter):
        if key not in self.pkeys:
            self.pkeys[key] = self.pcol
            self.pgetters.append((self.pcol, ncols, getter))
            self.pcol += ncols
            assert self.pcol <= self.NPCOL
        c = self.pkeys[key]
        return self.pp.ap[:, c:c + ncols]

    def dvec(self, key, getter):
        return self.param(key, KC, lambda inp, g=getter: np.ascontiguousarray(
            np.asarray(g(inp), np.float32).reshape(KC, 128).T))

    def const(self, ncols, arr):
        c = self.ccol
        self.cgetters.append((c, ncols, arr if callable(arr) else np.asarray(arr, np.float32)))
        self.ccol += ncols
        assert self.ccol <= self.NCCOL
        return self.cb.ap[:, c:c + ncols]

    def wtile(self, key, npart, nelem, getter):
        assert nelem <= 8192
        if key not in self.wkeys:
            self.wkeys[key] = self.woff
            self.wgetters.append((self.woff, npart, nelem, getter))
            self.woff += npart * nelem
        off = self.wkeys[key]
        idx = self.wtile_n
        self.wtile_n += 1
        slot = self.wslots[idx % 3]
        dst = slot.ap[0:npart, 0:nelem]

        def fn(off=off, npart=npart, nelem=nelem, dst=dst):
            src = self.wpack[off:off + npart * nelem].rearrange("(p n) -> p n", p=npart)
            return self.nc.gpsimd.dma_start(out=dst, in_=src)

        self.P.dma("pool", fn, slot.b, writes=[slot.b], tile_idx=idx)
        self.P.cur_tile = idx
        return R(dst, slot.b)

    def setup_consts(self):
        P, nc = self.P, self.nc
        self.ones = self.const(128, np.ones((128, 128)))
        self.ident = self.const(128, np.eye(128))
        bd = np.zeros((128, 128))
        bd[:64, :64] = 1
        bd[64:, 64:] = 1
        self.blk64 = self.const(128, bd)

    def load_consts(self):
        P, nc = self.P, self.nc

        def f1():
            return nc.sync.dma_start(out=self.pp.ap, in_=self.ppack)

        P.dma("sp", f1, self.pp.b, writes=[self.pp.b])

        def f2():
            return nc.gpsimd.dma_start(out=self.cb.ap, in_=self.cpack)

        P.dma("pool", f2, self.cb.b, writes=[self.cb.b])

    def x_src(self, first):
        return (self.xT, self.xT_b) if first else (self.xs, self.xs_b)

    def load_x(self, dst, src, srcb, t0, n):
        nc = self.nc
        d3 = dst.ap.rearrange("p (k t) -> p k t", k=KC)

        def fn():
            return nc.sync.dma_start(out=d3, in_=src.rearrange("k p t -> p k t")[:, :, t0:t0 + n])

        self.P.dma("sp", fn, dst.b, reads=[srcb], writes=[dst.b])

    def store_x(self, srcr, dst, dstb, t0, n):
        nc = self.nc
        s3 = srcr.ap.rearrange("p (k t) -> p k t", k=KC)

        def fn():
            return nc.sync.dma_start(out=dst.rearrange("k p t -> p k t")[:, :, t0:t0 + n], in_=s3)

        self.P.dma("sp", fn, dstb, reads=[srcr.b], writes=[dstb])

    def rmsnorm(self, xr, n, gain, hr, hoff=0, hstride=None):
        P, nc = self.P, self.nc
        hstride = hstride or n
        x3 = xr.ap.rearrange("p (k t) -> p k t", k=KC)
        h3 = hr.ap.rearrange("p (k t) -> p k t", k=KC)
        m = self.mark()
        sq = [self.alloc("sq%d" % i, TT, BF16) for i in range(3)]
        rs = self.alloc("rs", TT, F32)
        rs2 = self.alloc("rs2", TT, F32)
        ntt = (n + TT - 1) // TT
        for tt in range(ntt):
            w = min(TT, n - tt * TT)
            sl = slice(tt * TT, tt * TT + w)
            for kc in range(KC):
                s = sq[kc % 3]
                P.op("act", functools.partial(nc.scalar.activation,
                    out=s.ap[:, 0:w], in_=x3[:, kc, sl], func=AF.Square), reads=[xr.b], writes=[s.b])
                P.op("pe", functools.partial(nc.tensor.matmul,
                    self.psS[:, 0:w], lhsT=self.ones, rhs=s.ap[:, 0:w], start=(kc == 0), stop=(kc == KC - 1)),
                    reads=[s.b, self.cb.b], writes=[self.psS_r.b])
            P.op("dve", functools.partial(nc.vector.tensor_scalar,
                out=rs.ap[:, 0:w], in0=self.psS[:, 0:w], scalar1=1.0 / D, scalar2=RMS_EPS,
                op0=ALU.mult, op1=ALU.add), reads=[self.psS_r.b], writes=[rs.b])
            P.op("act", functools.partial(nc.scalar.activation, out=rs2.ap[:, 0:w], in_=rs.ap[:, 0:w], func=AF.Sqrt),
                 reads=[rs.b], writes=[rs2.b])
            P.op("dve", functools.partial(nc.vector.reciprocal, out=rs.ap[:, 0:w], in_=rs2.ap[:, 0:w]),
                 reads=[rs2.b], writes=[rs.b])
            for kc in range(KC):
                P.op("dve", functools.partial(nc.vector.scalar_tensor_tensor,
                    out=h3[:, kc, hoff + sl.start:hoff + sl.start + w], in0=x3[:, kc, sl],
                    scalar=gain[:, kc:kc + 1], in1=rs.ap[:, 0:w], op0=ALU.mult, op1=ALU.mult),
                    reads=[xr.b, rs.b, self.pp.b], writes=[hr.b])
        self.release(m)

    def mlp_phase(self, L, blk, first):
        P, nc = self.P, self.nc
        t0 = blk * TB
        m = self.mark()
        xb = self.alloc("xblk", KC * TB, F32)
        h = self.alloc("h", KC * TB, BF16)
        hh = [self.alloc("hh%d" % i, 4 * TB, BF16) for i in range(2)]
        tmp = [self.alloc("rl%d" % i, TT, F32) for i in range(2)]
        src, srcb = self.x_src(first)
        self.load_x(xb, src, srcb, t0, TB)
        gain = self.dvec(("norm_ffn", L), lambda inp, L=L: inp["norm_ffn"][L])
        self.rmsnorm(xb, TB, gain, h)
        x3 = xb.ap.rearrange("p (k t) -> p k t", k=KC)
        h3 = h.ap.rearrange("p (k t) -> p k t", k=KC)
        NG = DFF // 512
        ntt = TB // TT
        nrl = [0]

        def up(g):
            wt = self.wtile(("up", L, g), 128, KC * 512,
                            lambda inp, L=L, g=g: inp["mlp_w_up"][L][:, g * 512:(g + 1) * 512]
                            .reshape(KC, 128, 512).transpose(1, 0, 2).reshape(128, KC * 512))
            w3 = wt.ap.rearrange("p (k m) -> p k m", k=KC)
            hg = hh[g % 2]
            hg3 = hg.ap.rearrange("p (j t) -> p j t", j=4)
            for j in range(4):
                for tt in range(ntt):
                    ps = self.psb()
                    for kc in range(KC):
                        P.op("pe", functools.partial(nc.tensor.matmul,
                            ps.ap, lhsT=w3[:, kc, j * 128:(j + 1) * 128], rhs=h3[:, kc, tt * TT:(tt + 1) * TT],
                            start=(kc == 0), stop=(kc == KC - 1)), reads=[wt.b, h.b], writes=[ps.b])
                    tm = tmp[nrl[0] % 2]
                    nrl[0] += 1
                    P.op("act", functools.partial(nc.scalar.activation, out=tm.ap, in_=ps.ap, func=AF.Relu),
                         reads=[ps.b], writes=[tm.b])
                    P.op("pool", functools.partial(nc.gpsimd.tensor_tensor,
                        out=hg3[:, j, tt * TT:(tt + 1) * TT], in0=tm.ap, in1=tm.ap, op=ALU.mult),
                        reads=[tm.b], writes=[hg.b])

        def down(g):
            wt = self.wtile(("dn", L, g), 128, 4 * D,
                            lambda inp, L=L, g=g: inp["mlp_w_down"][L][g * 512:(g + 1) * 512, :]
                            .reshape(4, 128, D).transpose(1, 0, 2).reshape(128, 4 * D))
            w3 = wt.ap.rearrange("p (j d) -> p j d", j=4)
            hg = hh[g % 2]
            hg3 = hg.ap.rearrange("p (j t) -> p j t", j=4)
            for dc in range(KC):
                for tt in range(ntt):
                    ps = self.psa()
                    for j in range(4):
                        P.op("pe", functools.partial(nc.tensor.matmul,
                            ps.ap, lhsT=w3[:, j, dc * 128:(dc + 1) * 128], rhs=hg3[:, j, tt * TT:(tt + 1) * TT],
                            start=(j == 0), stop=(j == 3)), reads=[wt.b, hg.b], writes=[ps.b])
                    P.op("dve", functools.partial(nc.vector.tensor_tensor,
                        out=x3[:, dc, tt * TT:(tt + 1) * TT], in0=ps.ap, in1=x3[:, dc, tt * TT:(tt + 1) * TT],
                        op=ALU.add), reads=[ps.b, xb.b], writes=[xb.b])

        up(0)
        for g in range(NG):
            if g + 1 < NG:
                up(g + 1)
            down(g)
        return xb, m

    def finish_block(self, xb, m, blk, last):
        P, nc = self.P, self.nc
        t0 = blk * TB
        if last and self.final_norm:
            gain = self.dvec(("norm_final",), lambda inp: inp["norm_final"])
            self.final_rms(xb, gain, t0)
        elif last:
            self.store_x(xb, self.yT, self.yT_b, t0, TB)
        else:
            self.store_x(xb, self.xs, self.xs_b, t0, TB)
        self.release(m)

    def final_rms(self, xb, gain, t0):
        P, nc = self.P, self.nc
        x3 = xb.ap.rearrange("p (k t) -> p k t", k=KC)
        m = self.mark()
        sq = [self.alloc("fsq%d" % i, TT, BF16) for i in range(3)]
        rs = self.alloc("frs", TT, F32)
        rs2 = self.alloc("frs2", TT, F32)
        for tt in range(TB // TT):
            sl = slice(tt * TT, (tt + 1) * TT)
            for kc in range(KC):
                s = sq[kc % 3]
                P.op("act", functools.partial(nc.scalar.activation,
                    out=s.ap, in_=x3[:, kc, sl], func=AF.Square), reads=[xb.b], writes=[s.b])
                P.op("pe", functools.partial(nc.tensor.matmul,
                    self.psS[:, :], lhsT=self.ones, rhs=s.ap, start=(kc == 0), stop=(kc == KC - 1)),
                    reads=[s.b, self.cb.b], writes=[self.psS_r.b])
            P.op("dve", functools.partial(nc.vector.tensor_scalar,
                out=rs.ap, in0=self.psS[:, :], scalar1=1.0 / D, scalar2=RMS_EPS,
                op0=ALU.mult, op1=ALU.add), reads=[self.psS_r.b], writes=[rs.b])
            P.op("act", functools.partial(nc.scalar.activation, out=rs2.ap, in_=rs.ap, func=AF.Sqrt),
                 reads=[rs.b], writes=[rs2.b])
            P.op("dve", functools.partial(nc.vector.reciprocal, out=rs.ap, in_=rs2.ap), reads=[rs2.b], writes=[rs.b])
            for kc in range(KC):
                P.op("dve", functools.partial(nc.vector.scalar_tensor_tensor,
                    out=x3[:, kc, sl], in0=x3[:, kc, sl], scalar=gain[:, kc:kc + 1], in1=rs.ap,
                    op0=ALU.mult, op1=ALU.mult), reads=[xb.b, rs.b, self.pp.b], writes=[xb.b])
        self.store_x(xb, self.yT, self.yT_b, t0, TB)
        self.release(m)

    def build(self):
        self.setup_consts()
        self.load_consts()
        self.setup_persist()
        first_layer = True
        for li, spec in enumerate(self.layers):
            last_layer = (li == len(self.layers) - 1)
            mlp_only = isinstance(spec, tuple)
            L = spec[1] if mlp_only else spec
            kind = L % 3
            if self.pair_mode and not mlp_only and kind == 0:
                self.rwkv_layer_pair(L, first_layer)
            for blk in range(self.nblk):
                if mlp_only:
                    pass
                elif kind == 0:
                    if not self.pair_mode:
                        self.rwkv_phase(L, blk, first_layer)
                elif kind == 1:
                    self.swa_phase(L, blk, first_layer)
                elif kind == 2:
                    self.conv_phase(L, blk, first_layer)
                xb, m = self.mlp_phase(L, blk, first=(first_layer and mlp_only))
                self.finish_block(xb, m, blk, last_layer)
            if self.pair_mode and not last_layer:
                nc_, P_ = self.nc, self.P
                nxt = self.layers[li + 1]
                nxt = (nxt[1] if isinstance(nxt, tuple) else nxt) % 3
                if nxt == 0:
                    for kc in range(KC):
                        P_.dma("pool", functools.partial(nc_.gpsimd.collective_compute, "AllGather", ALU.bypass,
                                                         replica_groups=self.groups,
                                                         ins=[self.xs[kc]], outs=[self.xg3[kc * 256:(kc + 1) * 256, :]]),
                               self.xg3_b, reads=[self.xs_b], writes=[self.xg3_b], inc=1)
                else:
                    P_.dma("sp", functools.partial(nc_.sync.dma_start,
                                                   out=self.tail_send.rearrange("(k p) t -> k p t", k=KC),
                                                   in_=self.xs[:, :, self.tok - 128:self.tok]),
                           self.tail_send_b, reads=[self.xs_b], writes=[self.tail_send_b])
                    P_.dma("pool", functools.partial(nc_.gpsimd.collective_compute, "AllGather", ALU.bypass,
                                                     replica_groups=self.groups,
                                                     ins=[self.tail_send], outs=[self.tail_recv]),
                           self.tail_recv_b, reads=[self.tail_send_b], writes=[self.tail_recv_b], inc=1)
            first_layer = False
        nc = self.nc
        self.wpack = nc.dram_tensor("wpack", [max(self.woff, 128)], F32, kind="ExternalInput").ap()
        self.ppack = nc.dram_tensor("ppack", [128, self.NPCOL], F32, kind="ExternalInput").ap()
        self.cpack = nc.dram_tensor("cpack", [128, self.NCCOL], F32, kind="ExternalInput").ap()
        self.P.finalize(hoist_dist=2)
        self.built = True
        return nc

    def pack(self, inp, core=0):
        inp = dict(inp)
        inp["_core"] = core
        wp = np.zeros(max(self.woff, 128), np.float32)
        for off, npart, nelem, g in self.wgetters:
            a = np.asarray(g(inp), np.float32)
            assert a.shape == (npart, nelem), (a.shape, npart, nelem)
            wp[off:off + npart * nelem] = a.reshape(-1)
        pp = np.zeros((128, self.NPCOL), np.float32)
        for c, n, g in self.pgetters:
            a = np.asarray(g(inp), np.float32)
            assert a.shape == (128, n), (a.shape, n)
            pp[:, c:c + n] = a
        cp = np.zeros((128, self.NCCOL), np.float32)
        for c, n, a in self.cgetters:
            cp[:, c:c + n] = a(core) if callable(a) else a
        return wp, pp, cp

    def out_proj(self, key, L, z, n, t0, wget, first, bias=None):
        P, nc = self.P, self.nc
        z3 = z.ap.rearrange("p (k t) -> p k t", k=KC)
        src, srcb = self.x_src(first)
        m = self.mark()
        st = [self.alloc("ost%d" % i, TT, F32) for i in range(4)]
        ntt = n // TT
        k = 0
        for g in range(4):
            wt = self.wtile((key, L, g), 128, KC * 512,
                            lambda inp, g=g: wget(inp)[:, g * 512:(g + 1) * 512]
                            .reshape(KC, 128, 512).transpose(1, 0, 2).reshape(128, KC * 512))
            w3 = wt.ap.rearrange("p (k m) -> p k m", k=KC)
            for j in range(4):
                mc = g * 4 + j
                for tt in range(ntt):
                    s_ = st[k % 4]
                    k += 1
                    sl = slice(t0 + tt * TT, t0 + (tt + 1) * TT)
                    P.dma("sp", functools.partial(nc.sync.dma_start, out=s_.ap, in_=src[mc, :, sl]),
                          s_.b, reads=[srcb], writes=[s_.b])
                    ps = self.psa()
                    for kc in range(KC):
                        P.op("pe", functools.partial(nc.tensor.matmul,
                            ps.ap, lhsT=w3[:, kc, j * 128:(j + 1) * 128], rhs=z3[:, kc, tt * TT:(tt + 1) * TT],
                            start=(kc == 0), stop=(kc == KC - 1)), reads=[wt.b, z.b], writes=[ps.b])
                    if bias is None:
                        P.op("dve", functools.partial(nc.vector.tensor_tensor,
                            out=s_.ap, in0=ps.ap, in1=s_.ap, op=ALU.add), reads=[ps.b, s_.b], writes=[s_.b])
                    else:
                        P.op("dve", functools.partial(nc.vector.scalar_tensor_tensor,
                            out=s_.ap, in0=ps.ap, scalar=bias[:, mc:mc + 1], in1=s_.ap, op0=ALU.add, op1=ALU.add),
                            reads=[ps.b, s_.b, self.pp.b], writes=[s_.b])
                    P.dma("sp", functools.partial(nc.sync.dma_start, out=self.xs[mc, :, sl], in_=s_.ap),
                          self.xs_b, reads=[s_.b], writes=[self.xs_b])
        self.release(m)

    def norm_block(self, L, key, t0, n, first, h, hoff=0, hstride=None, src=None, srcb=None):
        gain = self.dvec((key, L), lambda inp, L=L, key=key: inp[key][L])
        if src is None:
            src, srcb = self.x_src(first)
        for s0 in range(0, n, TT):
            w = min(TT, n - s0)
            m = self.mark()
            xb = self.alloc("xnb", KC * w, F32)
            self.load_x(xb, src, srcb, t0 + s0, w)
            self.rmsnorm(xb, w, gain, h, hoff=hoff + s0, hstride=hstride)
            self.release(m)

    def conv_phase(self, L, blk, first):
        P, nc = self.P, self.nc
        t0 = blk * TB
        ic = L // 3
        HAL = 2 if (self.pair_mode and blk == 0) else 0
        m = self.mark()
        h = self.alloc("h", KC * (HAL + TB), BF16)
        self.norm_block(L, "norm_mix", t0, TB, first, h, hoff=HAL, hstride=HAL + TB)
        if HAL:
            tsrc = self.tail_recv.rearrange("(r k p) t -> r k p t", r=2, k=KC)[0][:, :, 126:128]
            self.norm_block(L, "norm_mix", 0, HAL, first, h, hoff=0, hstride=HAL + TB, src=tsrc, srcb=self.tail_recv_b)
            m01 = self.param(("mask01",), 1, lambda inp: np.full((128, 1), float(inp["_core"] % 2), np.float32))
        z = self.alloc("z", KC * TB, BF16)
        h3 = h.ap.rearrange("p (k t) -> p k t", k=KC)
        z3 = z.ap.rearrange("p (k t) -> p k t", k=KC)
        u = [self.alloc("u%d" % i, TB + 2, F32) for i in range(2)]
        uc = [self.alloc("uc%d" % i, TB, F32) for i in range(2)]
        tcg = [self.alloc("tcg%d" % i, TB + 2, F32) for i in range(2)]
        cw = [self.dvec(("conv_w", ic, tap), lambda inp, ic=ic, tap=tap: inp["conv_w"][ic][tap]) for tap in range(3)]
        uh3 = self.uhalo.ap.rearrange("p (k t) -> p k t", k=KC)
        ntt = TB // TT
        for j in range(KC):
            def getter(inp, j=j, ic=ic):
                W = inp["conv_w_in"][ic]
                cols = np.concatenate([W[:, D + j * 128:D + (j + 1) * 128], W[:, 2 * D + j * 128:2 * D + (j + 1) * 128],
                                       W[:, j * 128:(j + 1) * 128]], axis=1)
                return cols.reshape(KC, 128, 384).transpose(1, 0, 2).reshape(128, KC * 384)
            wt = self.wtile(("cin", L, j), 128, KC * 384, getter)
            w3 = wt.ap.rearrange("p (k m) -> p k m", k=KC)
            uj, ucj, tj = u[j % 2], uc[j % 2], tcg[j % 2]
            if not HAL:
                P.op("pool", functools.partial(nc.gpsimd.tensor_copy, out=uj.ap[:, 0:2], in_=uh3[:, j, :]),
                     reads=[self.uhalo.b], writes=[uj.b])
            for part in range(3):
                if part == 2:
                    P.op("pool", functools.partial(nc.gpsimd.tensor_scalar,
                        out=ucj.ap, in0=uj.ap[:, 2:2 + TB], scalar1=cw[2][:, j:j + 1], scalar2=None, op0=ALU.mult),
                        reads=[uj.b, self.pp.b], writes=[ucj.b])
                    P.op("dve", functools.partial(nc.vector.scalar_tensor_tensor,
                        out=ucj.ap, in0=uj.ap[:, 1:1 + TB], scalar=cw[1][:, j:j + 1], in1=ucj.ap,
                        op0=ALU.mult, op1=ALU.add), reads=[uj.b, ucj.b, self.pp.b], writes=[ucj.b])
                    P.op("dve", functools.partial(nc.vector.scalar_tensor_tensor,
                        out=ucj.ap, in0=uj.ap[:, 0:TB], scalar=cw[0][:, j:j + 1], in1=ucj.ap,
                        op0=ALU.mult, op1=ALU.add), reads=[uj.b, ucj.b, self.pp.b], writes=[ucj.b])
                    P.op("pool", functools.partial(nc.gpsimd.tensor_copy, out=uh3[:, j, :], in_=uj.ap[:, TB:TB + 2]),
                         reads=[uj.b], writes=[self.uhalo.b])
                tiles = [(HAL + tt * TT, TT) for tt in range(ntt)]
                if HAL and part < 2:
                    tiles = [(0, HAL)] + tiles
                for (c0, cw_) in tiles:
                    ps = self.psa()
                    for kc in range(KC):
                        P.op("pe", functools.partial(nc.tensor.matmul,
                            ps.ap[:, 0:cw_], lhsT=w3[:, kc, part * 128:(part + 1) * 128], rhs=h3[:, kc, c0:c0 + cw_],
                            start=(kc == 0), stop=(kc == KC - 1)), reads=[wt.b, h.b], writes=[ps.b])
                    r0 = c0 - HAL
                    if part == 0:
                        P.op("act", functools.partial(nc.scalar.copy, out=tj.ap[:, 2 + r0:2 + r0 + cw_], in_=ps.ap[:, 0:cw_]),
                             reads=[ps.b], writes=[tj.b])
                    elif part == 1:
                        if r0 < 0:
                            P.op("dve", functools.partial(nc.vector.scalar_tensor_tensor,
                                out=uj.ap[:, 0:2], in0=ps.ap[:, 0:2], scalar=m01[:, 0:1], in1=tj.ap[:, 0:2],
                                op0=ALU.mult, op1=ALU.mult), reads=[ps.b, tj.b, self.pp.b], writes=[uj.b])
                        else:
                            P.op("dve", functools.partial(nc.vector.tensor_tensor,
                                out=uj.ap[:, 2 + r0:2 + r0 + cw_], in0=ps.ap[:, 0:cw_], in1=tj.ap[:, 2 + r0:2 + r0 + cw_],
                                op=ALU.mult), reads=[ps.b, tj.b], writes=[uj.b])
                    else:
                        P.op("dve", functools.partial(nc.vector.tensor_tensor,
                            out=z3[:, j, r0:r0 + cw_], in0=ps.ap[:, 0:cw_], in1=ucj.ap[:, r0:r0 + cw_], op=ALU.mult),
                            reads=[ps.b, ucj.b], writes=[z.b])
        self.out_proj("cout", L, z, TB, t0, lambda inp, ic=ic: inp["conv_w_out"][ic], first)
        self.release(m)

    def setup_persist(self):
        P, nc = self.P, self.nc
        kinds = set((l[1] if isinstance(l, tuple) else l) % 3 for l in self.layers if not isinstance(l, tuple))
        if 2 in kinds:
            self.uhalo = self.alloc("uhalo", KC * 2, F32)
            P.op("pool", functools.partial(nc.gpsimd.memset, self.uhalo.ap, 0.0), writes=[self.uhalo.b])
        if 0 in kinds:
            self.hlast = self.alloc("hlast", KC, BF16)

    def swa_phase(self, L, blk, first):
        P, nc = self.P, self.nc
        t0 = blk * TB
        ib = L // 3
        NQB = TB // 128
        if not hasattr(self, "swa_k_st"):
            self.swa_k_st = nc.dram_tensor("swa_k_st", [128, 4 * 128], BF16).ap()
            self.swa_v_st = nc.dram_tensor("swa_v_st", [128, 8 * 128], BF16).ap()
            self.swa_st_b = Buf("swa_st")
            NEG = -30000.0
            qi = np.arange(128)[:, None]
            kj = np.arange(256)[None, :]
            ok = (kj > qi) & (kj <= qi + 128)
            self.c_mask = self.const(256, np.where(ok, 0.0, NEG))
            m_all = np.where(ok, 0.0, NEG)
            m_first = np.where(ok & (kj >= 128), 0.0, NEG)
            if self.pair_mode:
                self.c_mask0 = self.const(256, lambda core: m_first if core % 2 == 0 else m_all)
            else:
                self.c_mask0 = self.const(256, m_first)
        Wq = lambda inp: inp["swa_w_qkv"][ib]
        bq = lambda inp: inp["swa_b_qkv"][ib]
        b_q = self.param(("swa_bq", ib), KC, lambda inp: np.ascontiguousarray(bq(inp)[:D].reshape(KC, 128).T))
        b_k = self.param(("swa_bk", ib), 4, lambda inp: np.stack(
            [np.concatenate([bq(inp)[D + j * 64:D + (j + 1) * 64]] * 2) for j in range(4)], axis=1))
        b_v = self.param(("swa_bv", ib), 2, lambda inp: np.ascontiguousarray(bq(inp)[D + 256:D + 512].reshape(2, 128).T))
        b_o = self.dvec(("swa_bo", ib), lambda inp: inp["swa_b_o"][ib])
        sink = self.param(("swa_sink", ib), NH, lambda inp: np.tile(inp["swa_sinks"][ib][None, :], (128, 1)))
        m = self.mark()
        q_all = self.alloc("q_all", KC * TB, BF16)
        kbuf = self.alloc("kbuf", 4 * (128 + TB), BF16)
        vtp = self.alloc("vtp", (NQB + 1) * 8 * 128, BF16)
        q3 = q_all.ap.rearrange("p (k t) -> p k t", k=KC)
        k3 = kbuf.ap.rearrange("p (j t) -> p j t", j=4)
        v5 = vtp.ap.rearrange("p (b j v d) -> p b j v d", b=NQB + 1, j=4, v=2)
        HAL = 128 if (self.pair_mode and blk == 0) else 0
        if blk == 0:
            P.op("dve", functools.partial(nc.vector.memset, vtp.ap, 0.0), writes=[vtp.b])
            if not HAL:
                P.op("dve", functools.partial(nc.vector.memset, k3[:, :, 0:128], 0.0), writes=[kbuf.b])
        else:
            P.op("dve", functools.partial(nc.vector.memset, vtp.ap[:, 8 * 128:], 0.0), writes=[vtp.b])
            P.dma("sp", functools.partial(nc.sync.dma_start, out=vtp.ap[:, 0:8 * 128], in_=self.swa_v_st), vtp.b,
                  reads=[self.swa_st_b], writes=[vtp.b])
            P.dma("sp", functools.partial(nc.sync.dma_start, out=k3[:, :, 0:128],
                                                  in_=self.swa_k_st.rearrange("p (j t) -> p j t", j=4)), kbuf.b,
                  reads=[self.swa_st_b], writes=[kbuf.b])
        m2 = self.mark()
        h = self.alloc("h", KC * (HAL + TB), BF16)
        self.norm_block(L, "norm_mix", t0, TB, first, h, hoff=HAL, hstride=HAL + TB)
        if HAL:
            tsrc = self.tail_recv.rearrange("(r k p) t -> r k p t", r=2, k=KC)[0]
            self.norm_block(L, "norm_mix", 0, HAL, first, h, hoff=0, hstride=HAL + TB, src=tsrc, srcb=self.tail_recv_b)
        vfm = self.alloc("vfm", 2 * (128 + TB), BF16)
        h3 = h.ap.rearrange("p (k t) -> p k t", k=KC)
        vf3 = vfm.ap.rearrange("p (c t) -> p c t", c=2)
        ntt = TB // TT
        main_tiles = [(HAL + tt * TT, TT) for tt in range(ntt)]
        kv_tiles = ([(0, HAL)] if HAL else []) + main_tiles

        def proj(key, ncols, getter, epi, tiles):
            wt = self.wtile((key, L), 128, KC * ncols,
                            lambda inp: getter(inp).reshape(KC, 128, ncols).transpose(1, 0, 2).reshape(128, KC * ncols))
            w3 = wt.ap.rearrange("p (k m) -> p k m", k=KC)
            for j in range(ncols // 128):
                for (c0, cw) in tiles:
                    ps = self.psa()
                    for kc in range(KC):
                        P.op("pe", functools.partial(nc.tensor.matmul,
                            ps.ap[:, 0:cw], lhsT=w3[:, kc, j * 128:(j + 1) * 128], rhs=h3[:, kc, c0:c0 + cw],
                            start=(kc == 0), stop=(kc == KC - 1)), reads=[wt.b, h.b], writes=[ps.b])
                    epi(j, c0 - HAL, cw, ps)

        for g in range(4):
            def epi_q(j, c0, cw, ps, g=g):
                mc = g * 4 + j
                P.op("dve", functools.partial(nc.vector.tensor_scalar,
                    out=q3[:, mc, c0:c0 + cw], in0=ps.ap[:, 0:cw], scalar1=b_q[:, mc:mc + 1], scalar2=HD ** -0.5,
                    op0=ALU.add, op1=ALU.mult), reads=[ps.b, self.pp.b], writes=[q_all.b])
            proj(("swa_q", g), 512, lambda inp, g=g: Wq(inp)[:, g * 512:(g + 1) * 512], epi_q, main_tiles)

        def epi_k(j, c0, cw, ps):
            P.op("dve", functools.partial(nc.vector.tensor_scalar,
                out=k3[:, j, 128 + c0:128 + c0 + cw], in0=ps.ap[:, 0:cw], scalar1=b_k[:, j:j + 1], scalar2=None,
                op0=ALU.add), reads=[ps.b, self.pp.b], writes=[kbuf.b])
        proj("swa_k", 512, lambda inp: np.concatenate(
            [Wq(inp)[:, D + (j // 2) * 64:D + (j // 2 + 1) * 64] for j in range(8)], axis=1), epi_k, kv_tiles)

        def epi_v(j, c0, cw, ps):
            P.op("dve", functools.partial(nc.vector.tensor_scalar,
                out=vf3[:, j, 128 + c0:128 + c0 + cw], in0=ps.ap[:, 0:cw], scalar1=b_v[:, j:j + 1], scalar2=None,
                op0=ALU.add), reads=[ps.b, self.pp.b], writes=[vfm.b])
        proj("swa_v", 256, lambda inp: Wq(inp)[:, D + 256:D + 512], epi_v, kv_tiles)
        for bb in range(-1 if HAL else 0, NQB):
            for c in range(2):
                P.op("pe", functools.partial(nc.tensor.transpose,
                    self.psT[:, c * 128:(c + 1) * 128], vf3[:, c, 128 + bb * 128:128 + (bb + 1) * 128], self.ident),
                    reads=[vfm.b, self.cb.b], writes=[self.psT_r.b])
            src4 = self.psT[:, 0:256].rearrange("p (j d) -> p j d", j=4)
            P.op("act", functools.partial(nc.scalar.copy, out=v5[:, bb + 1, :, 0, 0:64], in_=src4),
                 reads=[self.psT_r.b], writes=[vtp.b])
            P.op("dve", functools.partial(nc.vector.tensor_copy, out=v5[:, bb + 1, :, 1, 64:128], in_=src4),
                 reads=[self.psT_r.b], writes=[vtp.b])
        self.release(m2)
        o_all = self.alloc("o_all", KC * TB, BF16)
        o3 = o_all.ap.rearrange("p (k t) -> p k t", k=KC)
        NR = 4
        lm = [self.alloc("lm%d" % i, 512, F32) for i in range(NR)]
        pe_ = [self.alloc("pe%d" % i, 512, F32) for i in range(NR)]
        pn = [self.alloc("pn%d" % i, 512, BF16) for i in range(NR)]
        pT = [self.alloc("pT%d" % i, 512, BF16) for i in range(NR)]
        sm = [self.alloc("sm%d" % i, 16, F32) for i in range(NR)]
        def unit(c, bb, it):
            kvh = c // 4
            if True:
                i = it % NR
                u0 = (it % 2) * 512
                lmr, per, pnr, pTr, smr = lm[i], pe_[i], pn[i], pT[i], sm[i]
                if self.rotA % 2:
                    self.rotA += 1
                psl0 = self.psa()
                psl1 = self.psa()
                pbase = ((self.rotA - 2) % 4) * 512
                for hh, psl in ((0, psl0), (1, psl1)):
                    pr = slice(hh * 64, (hh + 1) * 64)
                    P.op("pe", functools.partial(nc.tensor.matmul,
                        psl.ap[:, 0:256], lhsT=q3[pr, c, bb * 128:(bb + 1) * 128],
                        rhs=k3[pr, kvh, bb * 128:bb * 128 + 256], start=True, stop=True),
                        reads=[q_all.b, kbuf.b], writes=[psl.b])
                mk = self.c_mask0 if (blk == 0 and bb == 0) else self.c_mask
                l3 = lmr.ap.rearrange("p (h k) -> p h k", h=2)
                pl3 = self.psA[:, pbase:pbase + 1024].rearrange("p (h k) -> p h k", h=2)[:, :, 0:256]
                P.op("dve", functools.partial(nc.vector.tensor_tensor,
                    out=l3, in0=pl3, in1=mk.unsqueeze(1).to_broadcast([128, 2, 256]), op=ALU.add),
                    reads=[psl0.b, psl1.b, self.cb.b], writes=[lmr.b])
                s_ = smr.ap
                P.op("dve", functools.partial(nc.vector.tensor_reduce,
                    out=s_[:, 0:2], in_=l3, axis=AX.X, op=ALU.max), reads=[lmr.b], writes=[smr.b])
                P.op("dve", functools.partial(nc.vector.tensor_tensor,
                    out=s_[:, 2:4], in0=s_[:, 0:2], in1=sink[:, 2 * c:2 * c + 2], op=ALU.max),
                    reads=[smr.b, self.pp.b], writes=[smr.b])
                P.op("dve", functools.partial(nc.vector.tensor_scalar,
                    out=s_[:, 4:6], in0=s_[:, 2:4], scalar1=-1.0, scalar2=None, op0=ALU.mult),
                    reads=[smr.b], writes=[smr.b])
                p3 = per.ap.rearrange("p (h k) -> p h k", h=2)
                for hh in range(2):
                    P.op("act", functools.partial(nc.scalar.activation,
                        out=p3[:, hh, :], in_=l3[:, hh, :], func=AF.Exp, bias=s_[:, 4 + hh:5 + hh], scale=1.0,
                        accum_out=s_[:, 6 + hh:7 + hh]), reads=[lmr.b, smr.b], writes=[per.b, smr.b])
                P.op("dve", functools.partial(nc.vector.tensor_tensor,
                    out=s_[:, 8:10], in0=s_[:, 4:6], in1=sink[:, 2 * c:2 * c + 2], op=ALU.add),
                    reads=[smr.b, self.pp.b], writes=[smr.b])
                P.op("act", functools.partial(nc.scalar.activation, out=s_[:, 10:12], in_=s_[:, 8:10], func=AF.Exp),
                     reads=[smr.b], writes=[smr.b])
                P.op("dve", functools.partial(nc.vector.tensor_tensor,
                    out=s_[:, 12:14], in0=s_[:, 10:12], in1=s_[:, 6:8], op=ALU.add),
                    reads=[smr.b], writes=[smr.b])
                P.op("dve", functools.partial(nc.vector.reciprocal, out=s_[:, 14:16], in_=s_[:, 12:14]),
                     reads=[smr.b], writes=[smr.b])
                pn3 = pnr.ap.rearrange("p (h k) -> p h k", h=2)
                P.op("dve", functools.partial(nc.vector.tensor_tensor,
                    out=pn3, in0=p3, in1=s_[:, 14:16].unsqueeze(2).to_broadcast([128, 2, 256]), op=ALU.mult),
                    reads=[per.b, smr.b], writes=[pnr.b])
                for hh in range(2):
                    for kb in range(2):
                        P.op("pe", functools.partial(nc.tensor.transpose,
                            self.psT[:, u0 + (hh * 2 + kb) * 128:u0 + (hh * 2 + kb + 1) * 128],
                            pn3[:, hh, kb * 128:(kb + 1) * 128], self.ident),
                            reads=[pnr.b, self.cb.b], writes=[self.psT_r.b])
                P.op("act", functools.partial(nc.scalar.copy, out=pTr.ap, in_=self.psT[:, u0:u0 + 512]),
                     reads=[self.psT_r.b], writes=[pTr.b])
                pso = self.psb()
                n_ = 0
                for hh in range(2):
                    for kb in range(2):
                        P.op("pe", functools.partial(nc.tensor.matmul,
                            pso.ap[:, 0:128], lhsT=v5[:, bb + kb, kvh, hh, :],
                            rhs=pTr.ap[:, (hh * 2 + kb) * 128:(hh * 2 + kb + 1) * 128],
                            start=(n_ == 0), stop=(n_ == 3)), reads=[vtp.b, pTr.b], writes=[pso.b])
                        n_ += 1
                P.op("act", functools.partial(nc.scalar.copy,
                    out=o3[:, c, bb * 128:(bb + 1) * 128], in_=pso.ap[:, 0:128]), reads=[pso.b], writes=[o_all.b])
        units = [(c, bb) for c in range(KC) for bb in range(NQB)]
        for u in range(0, len(units), 2):
            lists = []
            for k_ in range(2):
                saved = P.ops
                P.ops = []
                unit(units[u + k_][0], units[u + k_][1], u + k_)
                lists.append(P.ops)
                P.ops = saved
            for oa, ob in zip(lists[0], lists[1]):
                P.ops.append(oa)
                P.ops.append(ob)
            assert len(lists[0]) == len(lists[1])
        if blk + 1 < self.nblk:
            P.dma("sp", functools.partial(nc.sync.dma_start, out=self.swa_v_st, in_=vtp.ap[:, NQB * 8 * 128:]), self.swa_st_b,
                  reads=[vtp.b], writes=[self.swa_st_b])
            P.dma("sp", functools.partial(nc.sync.dma_start, out=self.swa_k_st.rearrange("p (j t) -> p j t", j=4),
                                                  in_=k3[:, :, TB:TB + 128]), self.swa_st_b,
                  reads=[kbuf.b], writes=[self.swa_st_b])
        self.out_proj("swa_o", L, o_all, TB, t0, lambda inp: inp["swa_w_o"][ib], first, bias=b_o)
        self.release(m)

    def rwkv_phase(self, L, blk, first):
        T2 = 512
        for sub in range(TB // T2):
            self.rwkv_sub(L, blk * TB + sub * T2, T2, first, seq_start=(blk == 0 and sub == 0))

    def rwkv_layer_pair(self, L, first):
        P, nc = self.P, self.nc
        T2 = 512
        for sb in range(self.NSB):
            if first:
                src, srcb = self.xfullT[:, :, sb * T2:(sb + 1) * T2], self.xfull_b
            else:
                half, off = divmod(sb, self.NSB // 2)
                src = self.xg3.rearrange("(k r p) t -> r k p t", k=KC, r=2)[half][:, :, off * T2:(off + 1) * T2]
                srcb = self.xg3_b
            self.rwkv_sub(L, sb * T2, T2, first, seq_start=(sb == 0), xsrc=src, xsrcb=srcb, sb=sb)
        m01 = self.param(("mask01",), 1, lambda inp: np.full((128, 1), float(inp["_core"] % 2), np.float32))
        om01 = self.param(("omask01",), 1, lambda inp: np.full((128, 1), 1.0 - float(inp["_core"] % 2), np.float32))
        ygv = self.ygr.rearrange("(s r q p) t -> s r p q t", s=self.NSB, r=2, q=8)
        hsb = self.NSB // 2
        for blk in range(self.nblk):
            m = self.mark()
            cand = [self.alloc("ygc%d" % i, KC * TB, BF16) for i in range(2)]
            ygb = self.alloc("ygb", KC * TB, BF16)
            for hc in range(2):
                c3 = cand[hc].ap.rearrange("p (k t) -> p k t", k=KC)
                for r in range(2):
                    for s2 in range(TB // T2):
                        sbg = hc * hsb + blk * (TB // T2) + s2
                        P.dma("sp", functools.partial(nc.sync.dma_start, out=c3[:, r * 8:(r + 1) * 8, s2 * T2:(s2 + 1) * T2],
                                                      in_=ygv[sbg][r]), cand[hc].b, reads=[self.ygr_b], writes=[cand[hc].b])
            P.op("dve", functools.partial(nc.vector.tensor_scalar, out=cand[0].ap, in0=cand[0].ap, scalar1=om01[:, 0:1],
                                          scalar2=None, op0=ALU.mult), reads=[cand[0].b, self.pp.b], writes=[cand[0].b])
            P.op("dve", functools.partial(nc.vector.scalar_tensor_tensor, out=ygb.ap, in0=cand[1].ap, scalar=m01[:, 0:1],
                                          in1=cand[0].ap, op0=ALU.mult, op1=ALU.add),
                 reads=[cand[0].b, cand[1].b, self.pp.b], writes=[ygb.b])
            ia = L // 3
            self.out_proj("rwo", L, ygb, TB, blk * TB, lambda inp, ia=ia: inp["rwkv_w_o"][ia], first)
            self.release(m)

    def rwkv_sub(self, L, t0, T2, first, seq_start, xsrc=None, xsrcb=None, sb=None):
        P, nc = self.P, self.nc
        ia = L // 3
        has_vres = ia > 0
        NCH = T2 // 64
        PAIRM = self.pair_mode
        NPAIR = 8 if PAIRM else KC
        LD = 0.6065306597126334
        if not hasattr(self, "zst"):
            self.zst = nc.dram_tensor("zst", [KC, 128, 128], F32).ap()
            self.zst_b = Buf("zst")
            self.vfirst = nc.dram_tensor("vfirst", [KC, 128, self.tok * (2 if PAIRM else 1)], F32).ap()
            self.vfirst_b = Buf("vfirst")
            si = np.arange(64)[:, None]
            tj = np.arange(64)[None, :]
            mab = np.zeros((128, 128))
            mab[:64, :64] = (si < tj)
            mab[:64, 64:] = (si <= tj)
            self.c_mab = self.const(128, mab)
            msl = np.zeros((128, 64))
            msl[:64, :] = (si > tj)
            self.c_msl = self.const(64, msl)
            bd = np.zeros((128, 128))
            bd[:64, :64] = 1.0 / 64
            bd[64:, 64:] = 1.0 / 64
            self.c_blkm = self.const(128, bd)

        def A(fn, reads, writes):
            P.op("act", fn, [x.b for x in reads], [x.b for x in writes])

        def V(fn, reads, writes):
            P.op("dve", fn, [x.b for x in reads], [x.b for x in writes])

        def G(fn, reads, writes):
            P.op("pool", fn, [x.b for x in reads], [x.b for x in writes])

        def M(fn, reads, writes):
            P.op("pe", fn, [x.b for x in reads], [x.b for x in writes])

        CB, PP = self.cb, self.pp
        if PAIRM:
            def pv(name, idx=ia):
                return self.param((name, idx, "half"), 8, lambda inp, name=name, idx=idx: np.ascontiguousarray(
                    np.asarray(inp[name][idx], np.float32).reshape(KC, 128).T[:, (inp["_core"] % 2) * 8:(inp["_core"] % 2) * 8 + 8]))
        else:
            pv = lambda name, idx=ia: self.dvec((name, idx), lambda inp, name=name, idx=idx: inp[name][idx].reshape(-1))
        mu = [self.dvec(("rwkv_mu", ia, i), lambda inp, i=i: inp["rwkv_mu"][ia][i]) for i in range(6)]
        w0c, a0c, kkc, kac, rkc, lwc, lbc = (pv("rwkv_w0"), pv("rwkv_a0"), pv("rwkv_k_k"), pv("rwkv_k_a"),
                                             pv("rwkv_r_k"), pv("rwkv_lnx_w"), pv("rwkv_lnx_b"))
        if has_vres:
            v0c = pv("rwkv_v0", ia - 1)
        m = self.mark()
        xr = self.alloc("xr", KC * T2, BF16)
        xk = self.alloc("xk", KC * T2, BF16)
        xv = self.alloc("xv", KC * T2, BF16)
        inw = self.alloc("inw", T2, BF16)
        ina = self.alloc("ina", T2, BF16)
        ing = self.alloc("ing", 2 * T2, BF16)
        inv = self.alloc("inv", T2, BF16) if has_vres else None
        yg = self.alloc("yg", NPAIR * T2, BF16)
        yg3 = yg.ap.rearrange("p (k t) -> p k t", k=NPAIR)
        xr3, xk3, xv3 = [x.ap.rearrange("p (k t) -> p k t", k=KC) for x in (xr, xk, xv)]
        m2 = self.mark()
        hb = self.alloc("hb", KC * (T2 + 1), BF16)
        h3 = hb.ap.rearrange("p (k t) -> p k t", k=KC)
        if seq_start:
            V(functools.partial(nc.vector.memset, h3[:, :, 0:1], 0.0), [], [hb])
        else:
            V(functools.partial(nc.vector.tensor_copy, out=h3[:, :, 0:1], in_=self.hlast.ap.unsqueeze(2)), [self.hlast], [hb])
        if xsrc is not None:
            self.norm_block(L, "norm_mix", 0, T2, first, hb, hoff=1, hstride=T2 + 1, src=xsrc, srcb=xsrcb)
        else:
            self.norm_block(L, "norm_mix", t0, T2, first, hb, hoff=1, hstride=T2 + 1)
        V(functools.partial(nc.vector.tensor_copy, out=self.hlast.ap.unsqueeze(2), in_=h3[:, :, T2:T2 + 1]), [hb], [self.hlast])
        dx = self.alloc("dx", KC * T2, BF16)
        dx3 = dx.ap.rearrange("p (k t) -> p k t", k=KC)
        V(functools.partial(nc.vector.tensor_tensor, out=dx3, in0=h3[:, :, 0:T2], in1=h3[:, :, 1:T2 + 1], op=ALU.subtract),
          [hb], [dx])

        def mix(i, outr, out3):
            for kc in range(KC):
                V(functools.partial(nc.vector.scalar_tensor_tensor,
                    out=out3[:, kc, :], in0=dx3[:, kc, :], scalar=mu[i][:, kc:kc + 1], in1=h3[:, kc, 1:T2 + 1],
                    op0=ALU.mult, op1=ALU.add), [dx, hb, PP], [outr])

        xm = self.alloc("xm", KC * T2, BF16)
        xm3 = xm.ap.rearrange("p (k t) -> p k t", k=KC)

        def lora1(mi, key, wname, widx, Rr, outr, func):
            mix(mi, xm, xm3)
            wt = self.wtile((key, L), 128, KC * Rr,
                            lambda inp: inp[wname][widx].reshape(KC, 128, Rr).transpose(1, 0, 2).reshape(128, KC * Rr))
            w3 = wt.ap.rearrange("p (k m) -> p k m", k=KC)
            for rc in range((Rr + 127) // 128):
                rr = min(128, Rr - rc * 128)
                ps = self.psa()
                for kc in range(KC):
                    M(functools.partial(nc.tensor.matmul,
                        ps.ap[0:rr, 0:T2], lhsT=w3[:, kc, rc * 128:rc * 128 + rr], rhs=xm3[:, kc, :],
                        start=(kc == 0), stop=(kc == KC - 1)), [wt, xm], [ps])
                A(functools.partial(nc.scalar.activation,
                    out=outr.ap[0:rr, rc * T2:(rc + 1) * T2], in_=ps.ap[0:rr, 0:T2], func=func), [ps], [outr])

        lora1(1, "rw1", "rwkv_w1", ia, 96, inw, AF.Tanh)
        lora1(4, "ra1", "rwkv_a1", ia, 96, ina, AF.Copy)
        lora1(5, "rg1", "rwkv_g1", ia, 256, ing, AF.Sigmoid)
        if has_vres:
            lora1(3, "rv1", "rwkv_v1", ia - 1, 64, inv, AF.Copy)
        mix(0, xr, xr3)
        mix(2, xk, xk3)
        mix(3, xv, xv3)
        self.release(m2)
        f32 = lambda name: self.alloc(name, T2, F32)
        m3 = self.mark()
        r32, k32, v32, sg, alr, kkn, kmod, cs, epos, bonus, t1, t2 = [
            f32(n) for n in ("r32", "k32", "v32", "sg", "alr", "kkn", "kmod", "cs", "epos", "bonus", "t1", "t2")]
        y32, eexc, eneg = r32, k32, v32
        b1 = self.alloc("b1", T2, BF16)
        vb = self.alloc("vb", T2, BF16)
        ar = self.alloc("ar", T2 * 2, BF16)
        bk = self.alloc("bk", T2 * 2, BF16)
        ar3 = ar.ap.rearrange("p (c x t) -> p c x t", c=NCH, x=2)
        bk3 = bk.ap.rearrange("p (c x t) -> p c x t", c=NCH, x=2)
        tT = self.alloc("tT", NCH * 4 * 128, BF16)
        tT4 = tT.ap[0:64].rearrange("p (c k d) -> p c k d", c=NCH, k=4)
        PMb = [self.alloc("PMb%d" % i, NCH * 128, BF16) for i in range(2)]
        PMk = [self.alloc("PMk%d" % i, NCH * 128, BF16) for i in range(2)]
        NHB = 2 if PAIRM else 1
        Lb_h = [[self.alloc("Lb%d_%d" % (i, h_), NCH * 64, BF16) for i in range(2)] for h_ in range(NHB)]
        Nb_h = [[self.alloc("Nb%d_%d" % (i, h_), NCH * 64, BF16) for i in range(2)] for h_ in range(NHB)]
        NI_h = [self.alloc("NI_%d" % h_, NCH * 64, BF16) for h_ in range(NHB)]
        Xb_h = [[self.alloc("Xb%d_%d" % (i, h_), NCH * 128, BF16) for i in range(2)] for h_ in range(NHB)]
        Wp = self.alloc("Wp", NCH * 128, BF16)
        U0p = self.alloc("U0p", NCH * 128, BF16)
        Wpad = [self.alloc("Wpad%d" % i, NCH * 128, BF16) for i in range(2)]
        U0pad = [self.alloc("U0pad%d" % i, NCH * 128, BF16) for i in range(2)]
        vTpad = [self.alloc("vTpad%d" % i, NCH * 128, BF16) for i in range(2)]
        GT = self.alloc("GT", NCH * 128, BF16)
        Hh = self.alloc("Hh", NCH * 128, F32)
        QT = self.alloc("QT", NCH * 64, BF16)
        Zall = self.alloc("Zall", (NCH + 1) * 128, BF16)
        Z32 = self.alloc("Z32", 128, F32)
        tz = self.alloc("tz", 128, F32)
        v3c = lambda r_, w=128: r_.ap[0:64].rearrange("p (c d) -> p c d", c=NCH)
        v3f = lambda r_: r_.ap.rearrange("p (c d) -> p c d", c=NCH)
        for z_ in Wpad + U0pad + vTpad + [GT]:
            V(functools.partial(nc.vector.memset, z_.ap, 0.0), [], [z_])
        V(functools.partial(nc.vector.memset, Hh.ap, 0.0), [], [Hh])
        psA01 = (self.psA_r[0], self.psA_r[1])
        psA23 = (self.psA_r[2], self.psA_r[3])
        pA01 = self.psA[:, 0:1024]
        pA23 = self.psA[:, 1024:2048]
        pB0, pB1 = self.psB_r[0], self.psB_r[1]
        psA01_, pA01_, pB0_, pB1_ = psA01, pA01, pB0, pB1
        psT32 = R(self.psT[:, :].bitcast(F32), self.psT_r.b)
        blk64, blkm, ident, ones = self.blk64, self.c_blkm, self.ident, self.ones
        id64 = ident[0:64, 0:64]
        mab = self.c_mab[0:64, :]
        msl = self.c_msl[0:64, :]
        W_rkv = lambda inp: inp["rwkv_w_rkv"][ia]
        for p in range(NPAIR):
            def getter(inp, p=p):
                pg = (inp["_core"] % 2) * 8 + p if PAIRM else p
                cs_ = slice(pg * 128, (pg + 1) * 128)
                W = W_rkv(inp)
                main = np.concatenate([W[0][:, cs_], W[1][:, cs_], W[2][:, cs_]], axis=1)
                main = main.reshape(KC, 128, 384).transpose(1, 0, 2).reshape(128, KC * 384)
                ext = np.zeros((128, 640), np.float32)
                ext[:96, 0:128] = inp["rwkv_w2"][ia][:, cs_]
                ext[:96, 128:256] = inp["rwkv_a2"][ia][:, cs_]
                g2 = inp["rwkv_g2"][ia][:, cs_]
                ext[:, 256:384] = g2[0:128]
                ext[:, 384:512] = g2[128:256]
                if has_vres:
                    ext[:64, 512:640] = inp["rwkv_v2"][ia - 1][:, cs_]
                return np.concatenate([main, ext], axis=1)

            wt = self.wtile(("rkv", L, p), 128, KC * 384 + 640, getter)
            w3 = wt.ap[:, 0:KC * 384].rearrange("p (k m) -> p k m", k=KC)
            E0 = KC * 384
            w2p = wt.ap[0:96, E0:E0 + 128]
            a2p = wt.ap[0:96, E0 + 128:E0 + 256]
            g2p = [wt.ap[:, E0 + 256:E0 + 384], wt.ap[:, E0 + 384:E0 + 512]]
            v2p = wt.ap[0:64, E0 + 512:E0 + 640]
            for part, (xin, xin3, dst) in enumerate(((xr, xr3, r32), (xk, xk3, k32), (xv, xv3, v32))):
                ps = self.psa()
                for kc in range(KC):
                    M(functools.partial(nc.tensor.matmul,
                        ps.ap, lhsT=w3[:, kc, part * 128:(part + 1) * 128], rhs=xin3[:, kc, :],
                        start=(kc == 0), stop=(kc == KC - 1)), [wt, xin], [ps])
                A(functools.partial(nc.scalar.copy, out=dst.ap, in_=ps.ap), [ps], [dst])
            ps = self.psa()
            M(functools.partial(nc.tensor.matmul, ps.ap, lhsT=w2p, rhs=inw.ap[0:96, :], start=True, stop=True), [wt, inw], [ps])
            A(functools.partial(nc.scalar.activation, out=sg.ap, in_=ps.ap, func=AF.Sigmoid, bias=w0c[:, p:p + 1], scale=1.0),
              [ps, PP], [sg])
            ps = self.psa()
            M(functools.partial(nc.tensor.matmul, ps.ap, lhsT=a2p, rhs=ina.ap[0:96, :], start=True, stop=True), [wt, ina], [ps])
            A(functools.partial(nc.scalar.activation, out=alr.ap, in_=ps.ap, func=AF.Sigmoid, bias=a0c[:, p:p + 1], scale=1.0),
              [ps, PP], [alr])
            if has_vres:
                ps = self.psa()
                M(functools.partial(nc.tensor.matmul, ps.ap, lhsT=v2p, rhs=inv.ap[0:64, :], start=True, stop=True), [wt, inv], [ps])
                A(functools.partial(nc.scalar.activation, out=t1.ap, in_=ps.ap, func=AF.Sigmoid, bias=v0c[:, p:p + 1], scale=1.0),
                  [ps, PP], [t1])
                P.dma("sp", functools.partial(nc.sync.dma_start, out=t2.ap, in_=self.vfirst[p, :, t0:t0 + T2]), t2.b,
                      reads=[self.vfirst_b], writes=[t2.b])
                V(functools.partial(nc.vector.tensor_tensor, out=t2.ap, in0=t2.ap, in1=v32.ap, op=ALU.subtract), [t2, v32], [t2])
                V(functools.partial(nc.vector.tensor_tensor, out=t2.ap, in0=t2.ap, in1=t1.ap, op=ALU.mult), [t2, t1], [t2])
                V(functools.partial(nc.vector.tensor_tensor, out=v32.ap, in0=v32.ap, in1=t2.ap, op=ALU.add), [t2, v32], [v32])
            else:
                P.dma("sp", functools.partial(nc.sync.dma_start, out=self.vfirst[p, :, t0:t0 + T2], in_=v32.ap), self.vfirst_b,
                      reads=[v32.b], writes=[self.vfirst_b])
            V(functools.partial(nc.vector.tensor_scalar, out=t1.ap, in0=k32.ap, scalar1=kkc[:, p:p + 1], scalar2=None, op0=ALU.mult),
              [k32, PP], [t1])
            A(functools.partial(nc.scalar.activation, out=b1.ap, in_=t1.ap, func=AF.Square), [t1], [b1])
            M(functools.partial(nc.tensor.matmul, self.psS[:, :], lhsT=blk64, rhs=b1.ap, start=True, stop=True), [b1, CB], [self.psS_r])
            A(functools.partial(nc.scalar.activation, out=t2.ap, in_=self.psS[:, :], func=AF.Sqrt), [self.psS_r], [t2])
            V(functools.partial(nc.vector.tensor_scalar, out=t2.ap, in0=t2.ap, scalar1=1e-12, scalar2=None, op0=ALU.max), [t2], [t2])
            V(functools.partial(nc.vector.reciprocal, out=t2.ap, in_=t2.ap), [t2], [t2])
            V(functools.partial(nc.vector.tensor_tensor, out=kkn.ap, in0=t1.ap, in1=t2.ap, op=ALU.mult), [t1, t2], [kkn])
            V(functools.partial(nc.vector.tensor_scalar, out=t1.ap, in0=alr.ap, scalar1=-1.0, scalar2=kac[:, p:p + 1],
                                                  op0=ALU.add, op1=ALU.mult), [alr, PP], [t1])
            V(functools.partial(nc.vector.scalar_tensor_tensor, out=kmod.ap, in0=t1.ap, scalar=1.0, in1=k32.ap,
                                                     op0=ALU.add, op1=ALU.mult), [t1, k32], [kmod])
            V(functools.partial(nc.vector.tensor_tensor, out=t1.ap, in0=r32.ap, in1=kmod.ap, op=ALU.mult), [r32, kmod], [t1])
            V(functools.partial(nc.vector.tensor_scalar, out=b1.ap, in0=t1.ap, scalar1=rkc[:, p:p + 1], scalar2=None, op0=ALU.mult),
              [t1, PP], [b1])
            M(functools.partial(nc.tensor.matmul, self.psS[:, :], lhsT=blk64, rhs=b1.ap, start=True, stop=True), [b1, CB], [self.psS_r])
            V(functools.partial(nc.vector.tensor_tensor, out=bonus.ap, in0=self.psS[:, :], in1=v32.ap, op=ALU.mult),
              [self.psS_r, v32], [bonus])
            A(functools.partial(nc.scalar.copy, out=vb.ap, in_=v32.ap), [v32], [vb])
            for c in range(NCH):
                sl = slice(c * 64, (c + 1) * 64)
                V(functools.partial(nc.vector.tensor_tensor_scan, out=cs.ap[:, sl], data0=ones[:, 0:64], data1=sg.ap[:, sl],
                                                             initial=0.0, op0=ALU.mult, op1=ALU.add), [sg, CB], [cs])
            A(functools.partial(nc.scalar.activation, out=eneg.ap, in_=cs.ap, func=AF.Exp, scale=LD), [cs], [eneg])
            A(functools.partial(nc.scalar.activation, out=epos.ap, in_=cs.ap, func=AF.Exp, scale=-LD), [cs], [epos])
            V(functools.partial(nc.vector.tensor_tensor, out=t2.ap, in0=cs.ap, in1=sg.ap, op=ALU.subtract), [cs, sg], [t2])
            A(functools.partial(nc.scalar.activation, out=eexc.ap, in_=t2.ap, func=AF.Exp, scale=-LD), [t2], [eexc])
            c64 = lambda r_: r_.ap.rearrange("p (c t) -> p c t", c=NCH)
            V(functools.partial(nc.vector.scalar_tensor_tensor, out=ar3[:, :, 0, :], in0=c64(kkn), scalar=-1.0, in1=c64(eexc),
                                                     op0=ALU.mult, op1=ALU.mult), [kkn, eexc], [ar])
            V(functools.partial(nc.vector.tensor_tensor, out=ar3[:, :, 1, :], in0=c64(r32), in1=c64(epos), op=ALU.mult),
              [r32, epos], [ar])
            V(functools.partial(nc.vector.tensor_tensor, out=t1.ap, in0=kkn.ap, in1=alr.ap, op=ALU.mult), [kkn, alr], [t1])
            V(functools.partial(nc.vector.tensor_tensor, out=bk3[:, :, 0, :], in0=c64(t1), in1=c64(eneg), op=ALU.mult), [t1, eneg], [bk])
            V(functools.partial(nc.vector.tensor_tensor, out=bk3[:, :, 1, :], in0=c64(kmod), in1=c64(eneg), op=ALU.mult),
              [kmod, eneg], [bk])
            gC = c64(epos)[:, :, 63]
            for c2 in range(NCH // 2):
                for cc in range(2):
                    c = c2 * 2 + cc
                    srcs = (ar3[:, c, 0, :], bk3[:, c, 0, :], bk3[:, c, 1, :], vb.ap[:, c * 64:(c + 1) * 64])
                    for kind in range(4):
                        o_ = (cc * 4 + kind) * 128
                        M(functools.partial(nc.tensor.transpose, self.psT[0:64, o_:o_ + 128], srcs[kind], ident),
                          [ar, bk, vb, CB], [self.psT_r])
                A(functools.partial(nc.scalar.copy,
                    out=tT4[:, c2 * 2:c2 * 2 + 2, :, :],
                    in_=self.psT[0:64, :].rearrange("p (c k d) -> p c k d", c=2, k=4)), [self.psT_r], [tT])
            def head(hh):
                hp = slice(hh * 64, (hh + 1) * 64)
                HI = hh if PAIRM else 0
                Lb, Nb, NI, Xb = Lb_h[HI], Nb_h[HI], NI_h[HI], Xb_h[HI]
                if hh == 0 or not PAIRM:
                    psA01, pA01, pB0, pB1 = psA01_, pA01_, pB0_, pB1_
                else:
                    psA01, pA01, pB0, pB1 = psA23, pA23, self.psS_r, psT32
                psA23x, pA23x = psA01, pA01
                hc = slice(hh * 64, (hh + 1) * 64)
                for c in range(NCH):
                    M(functools.partial(nc.tensor.matmul, pA01[0:64, c * 128:(c + 1) * 128], lhsT=bk3[hp, c, 0, :],
                                                  rhs=ar3[hp, c, :, :], start=True, stop=True), [bk, ar], psA01)
                pmb, pmk = PMb[hh], PMk[hh]
                mab_b = mab.unsqueeze(1).to_broadcast([64, NCH, 128])
                V(functools.partial(nc.vector.tensor_tensor,
                    out=v3c(pmb), in0=pA01[0:64, :].rearrange("p (c d) -> p c d", c=NCH), in1=mab_b, op=ALU.mult),
                  list(psA01) + [CB], [pmb])
                for c in range(NCH):
                    M(functools.partial(nc.tensor.matmul, pA23x[0:64, c * 128:(c + 1) * 128], lhsT=bk3[hp, c, 1, :],
                                                  rhs=ar3[hp, c, :, :], start=True, stop=True), [bk, ar], psA23x)
                V(functools.partial(nc.vector.tensor_tensor,
                    out=v3c(pmk), in0=pA23x[0:64, :].rearrange("p (c d) -> p c d", c=NCH), in1=mab_b, op=ALU.mult),
                  list(psA23x) + [CB], [pmk])
                for c in range(NCH):
                    M(functools.partial(nc.tensor.matmul, pB0.ap[0:64, c * 64:(c + 1) * 64], lhsT=ar3[hp, c, 0, :],
                                                  rhs=bk3[hp, c, 0, :], start=True, stop=True), [bk, ar], [pB0])
                Lc, Nc = Lb[0], pmb
                V(functools.partial(nc.vector.tensor_tensor,
                    out=v3c(Lc), in0=pB0.ap[0:64, :].rearrange("p (c d) -> p c d", c=NCH),
                    in1=msl.unsqueeze(1).to_broadcast([64, NCH, 64]), op=ALU.mult), [pB0, CB], [Lc])
                idb = id64.unsqueeze(1).to_broadcast([64, NCH, 64])
                V(functools.partial(nc.vector.tensor_tensor, out=v3c(NI), in0=v3c(pmb)[:, :, 0:64], in1=idb, op=ALU.add),
                  [pmb, CB], [NI])
                for c in range(NCH):
                    M(functools.partial(nc.tensor.matmul, pB1.ap[0:64, c * 64:(c + 1) * 64], lhsT=v3c(pmk)[:, c, 0:64],
                                                           rhs=tT4[:, c, 3, hc], start=True, stop=True), [pmk, tT], [pB1])
                X = Xb[0]
                A(functools.partial(nc.scalar.copy, out=v3c(X)[:, :, 64:128],
                                             in_=pB1.ap[0:64, :].rearrange("p (c d) -> p c d", c=NCH)), [pB1], [X])
                G(functools.partial(nc.gpsimd.tensor_copy, out=v3c(X)[:, :, 0:64], in_=tT4[:, :, 0, hc]), [tT], [X])
                Nview = lambda r_, isP: (v3c(r_)[:, :, 0:64] if isP else v3c(r_))
                n_isP = True
                for lvl in range(6):
                    for c in range(NCH):
                        M(functools.partial(nc.tensor.matmul, pA01[0:64, c * 128:(c + 1) * 128], lhsT=v3c(NI)[:, c, :],
                                                           rhs=v3c(X)[:, c, :], start=True, stop=True), [NI, X], psA01)
                    px3 = pA01[0:64, :].rearrange("p (c d) -> p c d", c=NCH)
                    if lvl < 5:
                        Xn = Xb[(lvl + 1) % 2]
                        A(functools.partial(nc.scalar.copy, out=v3c(Xn), in_=px3), list(psA01), [Xn])
                        X = Xn
                        nv = Nview(Nc, n_isP)
                        for c in range(NCH):
                            M(functools.partial(nc.tensor.matmul, pB0.ap[0:64, c * 64:(c + 1) * 64], lhsT=v3c(Lc)[:, c, :],
                                                                        rhs=nv[:, c, :], start=True, stop=True), [Lc, Nc], [pB0])
                        if lvl < 4:
                            for c in range(NCH):
                                M(functools.partial(nc.tensor.matmul, pB1.ap[0:64, c * 64:(c + 1) * 64], lhsT=nv[:, c, :],
                                                                            rhs=v3c(Lc)[:, c, :], start=True, stop=True), [Lc, Nc], [pB1])
                        Nn = Nb[lvl % 2]
                        pn3 = pB0.ap[0:64, :].rearrange("p (c d) -> p c d", c=NCH)
                        V(functools.partial(nc.vector.tensor_copy, out=v3c(Nn), in_=pn3), [pB0], [Nn])
                        V(functools.partial(nc.vector.tensor_tensor, out=v3c(NI), in0=pn3, in1=idb, op=ALU.add), [pB0, CB], [NI])
                        if lvl < 4:
                            Ln = Lb[(lvl + 1) % 2]
                            A(functools.partial(nc.scalar.copy, out=v3c(Ln), in_=pB1.ap[0:64, :].rearrange("p (c d) -> p c d", c=NCH)),
                              [pB1], [Ln])
                            Lc = Ln
                        Nc, n_isP = Nn, False
                    else:
                        A(functools.partial(nc.scalar.copy, out=v3c(Wp)[:, :, hc], in_=px3[:, :, 0:64]), list(psA01), [Wp])
                        V(functools.partial(nc.vector.tensor_copy, out=v3c(U0p)[:, :, hc], in_=px3[:, :, 64:128]), list(psA01), [U0p])
                        A(functools.partial(nc.scalar.copy, out=v3c(Wpad[hh])[:, :, hc], in_=px3[:, :, 0:64]),
                          list(psA01), [Wpad[hh]])
                        V(functools.partial(nc.vector.tensor_copy, out=v3c(U0pad[hh])[:, :, hc], in_=px3[:, :, 64:128]),
                          list(psA01), [U0pad[hh]])
                G(functools.partial(nc.gpsimd.tensor_copy, out=v3c(vTpad[hh])[:, :, hc], in_=tT4[:, :, 3, hc]), [tT], [vTpad[hh]])
            hl = []
            for hh in range(2):
                saved = P.ops
                P.ops = []
                head(hh)
                hl.append(P.ops)
                P.ops = saved
            if PAIRM:
                assert len(hl[0]) == len(hl[1])
                for oa, ob in zip(hl[0], hl[1]):
                    P.ops.append(oa)
                    P.ops.append(ob)
            else:
                P.ops.extend(hl[0] + hl[1])
            for c in range(NCH):
                M(functools.partial(nc.tensor.matmul, pA01[:, c * 128:(c + 1) * 128], lhsT=v3c(Wp)[:, c, :], rhs=tT4[:, c, 1, :],
                                              start=True, stop=True), [Wp, tT], psA01)
            pg3 = pA01.rearrange("p (c d) -> p c d", c=NCH)
            V(functools.partial(nc.vector.tensor_copy, out=v3f(GT)[0:64, :, 0:64], in_=pg3[0:64, :, 0:64]), list(psA01), [GT])
            A(functools.partial(nc.scalar.copy, out=v3f(GT)[64:128, :, 64:128], in_=pg3[64:128, :, 64:128]), list(psA01), [GT])
            for c in range(NCH):
                M(functools.partial(nc.tensor.matmul, pA23[:, c * 128:(c + 1) * 128], lhsT=tT4[:, c, 1, :], rhs=v3c(U0p)[:, c, :],
                                              start=True, stop=False), [U0p, tT], psA23)
                M(functools.partial(nc.tensor.matmul, pA23[:, c * 128:(c + 1) * 128], lhsT=tT4[:, c, 2, :], rhs=tT4[:, c, 3, :],
                                              start=False, stop=True), [tT], psA23)
            ph3 = pA23.rearrange("p (c d) -> p c d", c=NCH)
            for hh in range(2):
                hp = slice(hh * 64, (hh + 1) * 64)
                V(functools.partial(nc.vector.tensor_tensor,
                    out=v3f(Hh)[hp, :, hp], in0=ph3[hp, :, hp], in1=gC[hp, :].unsqueeze(2).to_broadcast([64, NCH, 64]),
                    op=ALU.mult), list(psA23) + [epos], [Hh])
            for c in range(NCH):
                for hh in range(2):
                    M(functools.partial(nc.tensor.matmul, pB0.ap[:, c * 64:(c + 1) * 64], lhsT=v3c(Wpad[hh])[:, c, :],
                                                         rhs=v3c(PMb[hh])[:, c, 64:128], start=(hh == 0), stop=(hh == 1)),
                      [Wpad[hh], PMb[hh]], [pB0])
            V(functools.partial(nc.vector.tensor_tensor, out=QT.ap.rearrange("p (c t) -> p c t", c=NCH),
                                              in0=pB0.ap.rearrange("p (c t) -> p c t", c=NCH), in1=ar3[:, :, 1, :],
                                              op=ALU.add), [pB0, ar], [QT])
            if seq_start:
                V(functools.partial(nc.vector.memset, Z32.ap, 0.0), [], [Z32])
            else:
                P.dma("sp", functools.partial(nc.sync.dma_start, out=Z32.ap, in_=self.zst[p]), Z32.b,
                      reads=[self.zst_b], writes=[Z32.b])
            za3 = Zall.ap.rearrange("p (c d) -> p c d", c=NCH + 1)
            A(functools.partial(nc.scalar.copy, out=za3[:, 0, :], in_=Z32.ap), [Z32], [Zall])
            for c in range(NCH):
                M(functools.partial(nc.tensor.matmul, pB1.ap[:, 0:128], lhsT=v3f(GT)[:, c, :], rhs=za3[:, c, :], start=True, stop=True),
                  [GT, Zall], [pB1])
                for hh in range(2):
                    M(functools.partial(nc.tensor.matmul, pB0.ap[:, c * 64:(c + 1) * 64], lhsT=v3c(U0pad[hh])[:, c, :],
                                        rhs=v3c(PMb[hh])[:, c, 64:128], start=(hh == 0), stop=False),
                      [U0pad[hh], PMb[hh]], [pB0])
                for hh in range(2):
                    M(functools.partial(nc.tensor.matmul, pB0.ap[:, c * 64:(c + 1) * 64], lhsT=v3c(vTpad[hh])[:, c, :],
                                        rhs=v3c(PMk[hh])[:, c, 64:128], start=False, stop=(hh == 1)),
                      [vTpad[hh], PMk[hh]], [pB0])
                V(functools.partial(nc.vector.tensor_tensor, out=tz.ap, in0=pB1.ap[:, 0:128], in1=Z32.ap, op=ALU.add), [pB1, Z32], [tz])
                V(functools.partial(nc.vector.scalar_tensor_tensor, out=Z32.ap, in0=tz.ap, scalar=gC[:, c:c + 1], in1=v3f(Hh)[:, c, :],
                                                             op0=ALU.mult, op1=ALU.add), [tz, epos, Hh], [Z32])
                A(functools.partial(nc.scalar.copy, out=za3[:, c + 1, :], in_=Z32.ap), [Z32], [Zall])
            P.dma("sp", functools.partial(nc.sync.dma_start, out=self.zst[p], in_=Z32.ap), self.zst_b,
                  reads=[Z32.b], writes=[self.zst_b])
            q3_ = QT.ap.rearrange("p (c t) -> p c t", c=NCH)
            A(functools.partial(nc.scalar.copy, out=y32.ap, in_=pB0.ap), [pB0], [y32])
            for c in range(NCH):
                M(functools.partial(nc.tensor.matmul, pB0.ap[:, c * 64:(c + 1) * 64], lhsT=za3[:, c, :], rhs=q3_[:, c, :],
                                    start=True, stop=True), [Zall, QT], [pB0])
            V(functools.partial(nc.vector.tensor_tensor, out=y32.ap, in0=pB0.ap, in1=y32.ap, op=ALU.add), [pB0, y32], [y32])
            A(functools.partial(nc.scalar.copy, out=b1.ap, in_=y32.ap), [y32], [b1])
            M(functools.partial(nc.tensor.matmul, self.psS[:, :], lhsT=blkm, rhs=b1.ap, start=True, stop=True), [b1, CB], [self.psS_r])
            V(functools.partial(nc.vector.tensor_tensor, out=y32.ap, in0=y32.ap, in1=self.psS[:, :], op=ALU.subtract),
              [y32, self.psS_r], [y32])
            A(functools.partial(nc.scalar.activation, out=b1.ap, in_=y32.ap, func=AF.Square), [y32], [b1])
            M(functools.partial(nc.tensor.matmul, self.psS[:, :], lhsT=blkm, rhs=b1.ap, start=True, stop=True), [b1, CB], [self.psS_r])
            V(functools.partial(nc.vector.tensor_scalar, out=t1.ap, in0=self.psS[:, :], scalar1=GN_EPS, scalar2=None, op0=ALU.add),
              [self.psS_r], [t1])
            A(functools.partial(nc.scalar.activation, out=t2.ap, in_=t1.ap, func=AF.Sqrt), [t1], [t2])
            V(functools.partial(nc.vector.reciprocal, out=t1.ap, in_=t2.ap), [t2], [t1])
            V(functools.partial(nc.vector.tensor_tensor, out=y32.ap, in0=y32.ap, in1=t1.ap, op=ALU.mult), [y32, t1], [y32])
            V(functools.partial(nc.vector.tensor_scalar, out=y32.ap, in0=y32.ap, scalar1=lwc[:, p:p + 1], scalar2=lbc[:, p:p + 1],
                                                  op0=ALU.mult, op1=ALU.add), [y32, PP], [y32])
            V(functools.partial(nc.vector.tensor_tensor, out=y32.ap, in0=y32.ap, in1=bonus.ap, op=ALU.add), [y32, bonus], [y32])
            ps = self.psa()
            for kc2 in range(2):
                M(functools.partial(nc.tensor.matmul, ps.ap, lhsT=g2p[kc2], rhs=ing.ap[:, kc2 * T2:(kc2 + 1) * T2],
                                                         start=(kc2 == 0), stop=(kc2 == 1)), [wt, ing], [ps])
            V(functools.partial(nc.vector.tensor_tensor, out=yg3[:, p, :], in0=ps.ap, in1=y32.ap, op=ALU.mult),
              [ps, y32], [yg])
        self.release(m3)
        if PAIRM:
            dst = self.ygs.rearrange("(s q p) t -> s p q t", s=self.NSB, q=8)[sb]
            P.dma("sp", functools.partial(nc.sync.dma_start, out=dst, in_=yg3), self.ygs_b, reads=[yg.b], writes=[self.ygs_b])
            P.dma("pool", functools.partial(nc.gpsimd.collective_compute, "AllGather", ALU.bypass, replica_groups=self.groups,
                                            ins=[self.ygs[sb * 1024:(sb + 1) * 1024, :]],
                                            outs=[self.ygr[sb * 2048:(sb + 1) * 2048, :]]), self.ygr_b,
                  reads=[self.ygs_b], writes=[self.ygr_b], inc=1)
        else:
            self.out_proj("rwo", L, yg, T2, t0, lambda inp: inp["rwkv_w_o"][ia], first)
        self.release(m)


N_CORES = 8
SEQ = 4096
_CACHE = {}


def kernel(**inputs):
    inp = {k: np.asarray(v) for k, v in inputs.items()}
    x = inp["x"].astype(np.float32, copy=False)
    B, T, Dm = x.shape
    assert (2 * B, T, Dm) == (N_CORES, SEQ, D)
    half_t = T // 2
    if "b" not in _CACHE:
        b = Builder(half_t, [0, 1, 2, 3], final_norm=True, pair_mode=True)
        b.build()
        _CACHE["b"] = b
    b = _CACHE["b"]
    packs = [b.pack(inp, core=h) for h in range(2)]
    in_maps = []
    for c in range(N_CORES):
        bi, h = divmod(c, 2)
        xfull = np.ascontiguousarray(x[bi].T).reshape(KC, 128, T)
        xT = np.ascontiguousarray(xfull[:, :, h * half_t:(h + 1) * half_t])
        wp, pp, cp = packs[h]
        in_maps.append({"xT": xT, "xfullT": xfull, "wpack": wp, "ppack": pp, "cpack": cp})
    res = run_bass_kernel_spmd(b.nc, in_maps, core_ids=list(range(N_CORES)))
    out = np.empty((B, T, Dm), np.float32)
    for c in range(N_CORES):
        bi, h = divmod(c, 2)
        out[bi, h * half_t:(h + 1) * half_t] = np.asarray(res.results[c]["yT"]).reshape(Dm, half_t).T
    return out
```
